# Optimizing a Trainium2 kernel written in Bass

```python
import math
import jax, jax.numpy as jnp
from jax import lax
import numpy as np

D_MODEL = 1024
BATCH = 8
SEQ = 4096
DEPTH = 4
DEC_BATCH = 4
DEC_SEQ = 8192
PAST_LEN = 128

N_HEADS = 4
HEAD_DIM = 64
V_DIM = 2 * HEAD_DIM
D_QK = N_HEADS * 2 * HEAD_DIM
D_ATTN = N_HEADS * V_DIM
ROPE_THETA = 10000.0
Q_BLOCK = 128
D_SSM = 512
GROUP_CH = 16
N_GROUPS = D_SSM // GROUP_CH
N_STATE = 64
DT_MIN = 0.001
DT_MAX = 0.1
D_FF = -(-8 * D_MODEL // (3 * 256)) * 256
EPS = 1e-6

OFF_Q = 0
OFF_K = OFF_Q + D_QK
OFF_V = OFF_K + D_QK
OFF_U = OFF_V + D_ATTN
OFF_GA = OFF_U + D_SSM
OFF_GS = OFF_GA + D_MODEL
IN_COLS = OFF_GS + D_MODEL

kernel_name = 'hybrid_diffattn_s5_encoder'


def rms_norm(x, g):
    x32 = x.astype(jnp.float32)
    y = x32 * lax.rsqrt(jnp.mean(x32 * x32, axis=-1, keepdims=True) + EPS)
    return (y * g.astype(jnp.float32)).astype(x.dtype)


def rope_tables(seq_len):
    inv = 1.0 / (ROPE_THETA ** (jnp.arange(0, HEAD_DIM, 2, dtype=jnp.float32) / HEAD_DIM))
    ang = jnp.arange(seq_len, dtype=jnp.float32)[:, None] * inv[None, :]
    ang = jnp.concatenate([ang, ang], axis=-1)
    return jnp.cos(ang), jnp.sin(ang)


def apply_rope(x, cos, sin):
    x1, x2 = jnp.split(x, 2, axis=-1)
    rot = jnp.concatenate([-x2, x1], axis=-1)
    c = cos[None, :, None, None, :]
    s = sin[None, :, None, None, :]
    return (x.astype(jnp.float32) * c + rot.astype(jnp.float32) * s).astype(x.dtype)


def lambda_init_fn(layer):
    return 0.8 - 0.6 * math.exp(-0.3 * layer)


def diff_attention(q, k, v, lam):
    b, l, h, _, d = q.shape
    nb = l // Q_BLOCK
    scale = HEAD_DIM ** -0.5
    qb = q.reshape(b, nb, Q_BLOCK, h, 2, d).transpose(1, 0, 2, 3, 4, 5)

    def one_block(q_blk):
        s = jnp.einsum('bqhmd,bkhmd->bhmqk', q_blk, k).astype(jnp.float32) * scale
        p = jax.nn.softmax(s, axis=-1)
        a = p[:, :, 0] - lam * p[:, :, 1]
        return jnp.einsum('bhqk,bkhe->bqhe', a.astype(v.dtype), v)

    out = lax.map(one_block, qb)
    return out.transpose(1, 0, 2, 3, 4).reshape(b, l, h, V_DIM)


def zoh(a_re, a_im, log_dt, b_re, b_im):
    dt = jnp.exp(log_dt.astype(jnp.float32))[:, None]
    ar = a_re.astype(jnp.float32)
    ai = a_im.astype(jnp.float32)
    mag = jnp.exp(dt * ar)
    abr = mag * jnp.cos(dt * ai)
    abi = mag * jnp.sin(dt * ai)
    nr = abr - 1.0
    ni = abi
    den = ar * ar + ai * ai
    fr = (nr * ar + ni * ai) / den
    fi = (ni * ar - nr * ai) / den
    br = b_re.astype(jnp.float32)
    bi = b_im.astype(jnp.float32)
    bbr = fr[..., None] * br - fi[..., None] * bi
    bbi = fr[..., None] * bi + fi[..., None] * br
    return abr, abi, bbr, bbi


def complex_affine_combine(e1, e2):
    a1r, a1i, b1r, b1i = e1
    a2r, a2i, b2r, b2i = e2
    return (a1r * a2r - a1i * a2i,
            a1r * a2i + a1i * a2r,
            a2r * b1r - a2i * b1i + b2r,
            a2r * b1i + a2i * b1r + b2i)


def ssm_direction(u32, a_re, a_im, log_dt, b_re, b_im, c_re, c_im, reverse):
    abr, abi, bbr, bbi = zoh(a_re, a_im, log_dt, b_re, b_im)
    bur = jnp.einsum('blgc,gpc->blgp', u32, bbr)
    bui = jnp.einsum('blgc,gpc->blgp', u32, bbi)
    shape = (1, u32.shape[1]) + abr.shape
    ar_t = jnp.broadcast_to(abr, shape)
    ai_t = jnp.broadcast_to(abi, shape)
    _, _, hr, hi = lax.associative_scan(complex_affine_combine, (ar_t, ai_t, bur, bui),
                                        reverse=reverse, axis=1)
    return (jnp.einsum('blgp,gcp->blgc', hr, c_re.astype(jnp.float32))
            - jnp.einsum('blgp,gcp->blgc', hi, c_im.astype(jnp.float32)))


def ssm_branch(u, a_re, a_im, log_dt, b_re, b_im, c_re, c_im, d_skip, w_glu, b_glu):
    b, l, _ = u.shape
    u32 = u.astype(jnp.float32)
    ug = u32.reshape(b, l, N_GROUPS, GROUP_CH)
    y = (ssm_direction(ug, a_re[0], a_im[0], log_dt[0], b_re[0], b_im[0], c_re[0], c_im[0], False)
         + ssm_direction(ug, a_re[1], a_im[1], log_dt[1], b_re[1], b_im[1], c_re[1], c_im[1], True))
    y = y.reshape(b, l, D_SSM) + d_skip.astype(jnp.float32) * u32
    z = jax.nn.gelu(y).astype(u.dtype)
    lin, gate = jnp.split(z @ w_glu + b_glu, 2, axis=-1)
    return lin * jax.nn.sigmoid(gate)


def layer(x, c, cos, sin, lam_init, w_mod, b_mod, norm1_g, w_in, lam_q1, lam_k1, lam_q2, lam_k2,
          subln_g, w_attn_br, ssm_a_re, ssm_a_im, ssm_log_dt, ssm_b_re, ssm_b_im, ssm_c_re, ssm_c_im,
          ssm_d, w_glu, b_glu, w_o, norm2_g, w_ffn_in, w_ffn_out):
    b, l, _ = x.shape
    mod = jax.nn.silu(c) @ w_mod + b_mod
    sh1, sc1, gt1, sh2, sc2, gt2 = jnp.split(mod[:, None, :], 6, axis=-1)

    h = rms_norm(x, norm1_g) * (1 + sc1) + sh1
    proj = h @ w_in
    q = proj[..., OFF_Q:OFF_K].reshape(b, l, N_HEADS, 2, HEAD_DIM)
    k = proj[..., OFF_K:OFF_V].reshape(b, l, N_HEADS, 2, HEAD_DIM)
    v = proj[..., OFF_V:OFF_U].reshape(b, l, N_HEADS, V_DIM)
    u = proj[..., OFF_U:OFF_GA]
    g_a = proj[..., OFF_GA:OFF_GS]
    g_s = proj[..., OFF_GS:IN_COLS]

    q = apply_rope(q, cos, sin)
    k = apply_rope(k, cos, sin)
    lam = (jnp.exp(jnp.sum(lam_q1.astype(jnp.float32) * lam_k1.astype(jnp.float32)))
           - jnp.exp(jnp.sum(lam_q2.astype(jnp.float32) * lam_k2.astype(jnp.float32)))
           + lam_init)
    o = diff_attention(q, k, v, lam)
    o = rms_norm(o, subln_g) * (1.0 - lam_init)
    y_attn = o.reshape(b, l, D_ATTN) @ w_attn_br

    y_ssm = ssm_branch(u, ssm_a_re, ssm_a_im, ssm_log_dt, ssm_b_re, ssm_b_im, ssm_c_re, ssm_c_im,
                       ssm_d, w_glu, b_glu)

    merged = jax.nn.sigmoid(g_a) * y_attn + jax.nn.sigmoid(g_s) * y_ssm
    x = x + gt1 * (merged @ w_o)

    h2 = rms_norm(x, norm2_g) * (1 + sc2) + sh2
    f_gate, f_up = jnp.split(h2 @ w_ffn_in, 2, axis=-1)
    x = x + gt2 * ((jax.nn.silu(f_gate) * f_up) @ w_ffn_out)
    return x


def trunk(x, c, w_mod, b_mod, norm1_g, w_in, lam_q1, lam_k1, lam_q2, lam_k2, subln_g, w_attn_br,
          ssm_a_re, ssm_a_im, ssm_log_dt, ssm_b_re, ssm_b_im, ssm_c_re, ssm_c_im, ssm_d, w_glu, b_glu,
          w_o, norm2_g, w_ffn_in, w_ffn_out, final_g):
    cos, sin = rope_tables(x.shape[1])
    for i in range(DEPTH):
        x = layer(x, c, cos, sin, lambda_init_fn(i), w_mod[i], b_mod[i], norm1_g[i], w_in[i],
                  lam_q1[i], lam_k1[i], lam_q2[i], lam_k2[i], subln_g[i], w_attn_br[i],
                  ssm_a_re[i], ssm_a_im[i], ssm_log_dt[i], ssm_b_re[i], ssm_b_im[i],
                  ssm_c_re[i], ssm_c_im[i], ssm_d[i], w_glu[i], b_glu[i], w_o[i], norm2_g[i],
                  w_ffn_in[i], w_ffn_out[i])
    return rms_norm(x, final_g)


def setup_inputs(seed: int = 0) -> dict:
    key = jax.random.key(seed)
    ks = jax.random.split(key, 32)
    f32 = jnp.float32

    def nrm(k, shape, scale):
        return jax.random.normal(k, shape, f32) * scale

    n = jnp.arange(N_STATE, dtype=f32)
    ssm_a_re = -0.5 + nrm(ks[12], (DEPTH, 2, N_GROUPS, N_STATE), 0.01)
    ssm_a_im = math.pi * n + nrm(ks[13], (DEPTH, 2, N_GROUPS, N_STATE), 0.01)
    ssm_log_dt = jax.random.uniform(ks[14], (DEPTH, 2, N_GROUPS), f32,
                                    minval=math.log(DT_MIN), maxval=math.log(DT_MAX))
    return {
        'x_prompt': nrm(ks[0], (BATCH, SEQ, D_MODEL), 1.0),
        'x_sample': nrm(ks[1], (DEC_BATCH, DEC_SEQ, D_MODEL), 1.0),
        'c_prompt': nrm(ks[2], (BATCH, D_MODEL), 1.0),
        'c_sample': nrm(ks[3], (DEC_BATCH, D_MODEL), 1.0),
        'w_mod': nrm(ks[4], (DEPTH, D_MODEL, 6 * D_MODEL), 0.5 * D_MODEL ** -0.5),
        'b_mod': nrm(ks[5], (DEPTH, 6 * D_MODEL), 0.01),
        'norm1_g': 1.0 + nrm(ks[6], (DEPTH, D_MODEL), 0.02),
        'w_in': nrm(ks[7], (DEPTH, D_MODEL, IN_COLS), D_MODEL ** -0.5),
        'lam_q1': nrm(ks[8], (DEPTH, HEAD_DIM), 0.1),
        'lam_k1': nrm(ks[9], (DEPTH, HEAD_DIM), 0.1),
        'lam_q2': nrm(ks[10], (DEPTH, HEAD_DIM), 0.1),
        'lam_k2': nrm(ks[11], (DEPTH, HEAD_DIM), 0.1),
        'subln_g': 1.0 + nrm(ks[15], (DEPTH, V_DIM), 0.02),
        'w_attn_br': nrm(ks[16], (DEPTH, D_ATTN, D_MODEL), D_ATTN ** -0.5),
        'ssm_a_re': ssm_a_re,
        'ssm_a_im': ssm_a_im,
        'ssm_log_dt': ssm_log_dt,
        'ssm_b_re': nrm(ks[17], (DEPTH, 2, N_GROUPS, N_STATE, GROUP_CH), (2 * GROUP_CH) ** -0.5),
        'ssm_b_im': nrm(ks[18], (DEPTH, 2, N_GROUPS, N_STATE, GROUP_CH), (2 * GROUP_CH) ** -0.5),
        'ssm_c_re': nrm(ks[19], (DEPTH, 2, N_GROUPS, GROUP_CH, N_STATE), (2 * N_STATE) ** -0.5),
        'ssm_c_im': nrm(ks[20], (DEPTH, 2, N_GROUPS, GROUP_CH, N_STATE), (2 * N_STATE) ** -0.5),
        'ssm_d': nrm(ks[21], (DEPTH, D_SSM), 1.0),
        'w_glu': nrm(ks[22], (DEPTH, D_SSM, 2 * D_MODEL), D_SSM ** -0.5),
        'b_glu': nrm(ks[23], (DEPTH, 2 * D_MODEL), 0.01),
        'w_o': nrm(ks[24], (DEPTH, D_MODEL, D_MODEL), D_MODEL ** -0.5),
        'norm2_g': 1.0 + nrm(ks[25], (DEPTH, D_MODEL), 0.02),
        'w_ffn_in': nrm(ks[26], (DEPTH, D_MODEL, 2 * D_FF), D_MODEL ** -0.5),
        'w_ffn_out': nrm(ks[27], (DEPTH, D_FF, D_MODEL), D_FF ** -0.5),
        'final_g': 1.0 + nrm(ks[28], (D_MODEL,), 0.02),
    }


def reference(x_prompt, x_sample, c_prompt, c_sample, w_mod, b_mod, norm1_g, w_in,
              lam_q1, lam_k1, lam_q2, lam_k2, subln_g, w_attn_br, ssm_a_re, ssm_a_im,
              ssm_log_dt, ssm_b_re, ssm_b_im, ssm_c_re, ssm_c_im, ssm_d, w_glu, b_glu,
              w_o, norm2_g, w_ffn_in, w_ffn_out, final_g):
    params = (w_mod, b_mod, norm1_g, w_in, lam_q1, lam_k1, lam_q2, lam_k2, subln_g, w_attn_br,
              ssm_a_re, ssm_a_im, ssm_log_dt, ssm_b_re, ssm_b_im, ssm_c_re, ssm_c_im, ssm_d,
              w_glu, b_glu, w_o, norm2_g, w_ffn_in, w_ffn_out, final_g)
    y_prompt = trunk(x_prompt, c_prompt, *params)
    y_sample = trunk(x_sample, c_sample, *params)
    return (y_prompt, y_sample)
```

```python
import math
import contextlib
import numpy as np
import concourse.bass as bass
import concourse.mybir as mybir
from concourse.bass_utils import run_bass_kernel_spmd

F32 = mybir.dt.float32
BF16 = mybir.dt.bfloat16
I32 = mybir.dt.int32
ALU = mybir.AluOpType
AF = mybir.ActivationFunctionType

D = 1024
NH = 4
DFF = 2816
DSSM = 512
NG = 32
EPS = 1e-6
TWO_PI = 2.0 * math.pi


def lam_init_fn(layer):
    return 0.8 - 0.6 * math.exp(-0.3 * layer)


class Buf:
    __slots__ = ("name", "w", "r", "sem", "cnt", "excl")

    def __init__(self, name, excl=False):
        self.name = name
        self.excl = excl
        self.w = {}
        self.r = {}
        self.sem = None
        self.cnt = 0


class EngState:
    def __init__(self, eng, sem, self_sync):
        self.eng = eng
        self.sem = sem
        self.cnt = 0
        self.waited = {}
        self.self_sync = self_sync


def _merge(d, src):
    for k, (s, v) in src.items():
        if k not in d or d[k][1] < v:
            d[k] = (s, v)


class KB:
    def __init__(self, nc):
        self.nc = nc
        self.engs = {}
        for name, e in (("pe", nc.tensor), ("dve", nc.vector), ("act", nc.scalar),
                        ("pool", nc.gpsimd), ("sp", nc.sync)):
            self.engs[name] = EngState(e, nc.alloc_semaphore("e_" + name), name != "pe")
        self.dma_bufs = []
        self.nalloc = 0
        self.stacks = []
        self.scope_bufs = []
        self.free_sems = []
        self.retired = {}
        self.nsem = 0
        self.ps = []
        for i in range(8):
            t = nc.alloc_psum_tensor("psb%d" % i, [128, 512], F32)
            self.ps.append((t.ap(), Buf("ps%d" % i, excl=True)))
        self.ps_i = 0

    def sb(self, shape, dt, name=None):
        self.nalloc += 1
        nm = "%s_%d" % (name or "t", self.nalloc)
        if self.stacks:
            t = self.stacks[-1].enter_context(self.nc.sbuf_tensor(nm, list(shape), dt))
        else:
            t = self.nc.alloc_sbuf_tensor(nm, list(shape), dt)
        b = Buf(name or "t")
        if self.scope_bufs:
            self.scope_bufs[-1].append(b)
        return (t.ap() if hasattr(t, "ap") and callable(t.ap) else t[:]), b

    @contextlib.contextmanager
    def scope(self):
        st = contextlib.ExitStack()
        self.stacks.append(st)
        self.scope_bufs.append([])
        try:
            yield
        finally:
            self.stacks.pop()
            for b in self.scope_bufs.pop():
                if b.sem is not None:
                    self.free_sems.append((b.sem, b.cnt))
                    self.dma_bufs.remove(b)
                    self.retired[id(b.sem)] = (b.sem, b.cnt)
                    b.sem = None
            st.close()

    def dram(self, name, shape, dt):
        return self.nc.dram_tensor(name, list(shape), dt, kind="Internal").ap()

    def psum(self):
        p = self.ps.pop(0)
        self.ps.append(p)
        return p

    def psum_hold(self):
        return self.ps.pop(0)

    def psum_release(self, p):
        self.ps.append(p)

    def _deps(self, reads, writes):
        d = {}
        for b in reads:
            _merge(d, b.w)
            if b.excl:
                _merge(d, b.r)
        for b in writes:
            _merge(d, b.w)
            _merge(d, b.r)
        return d

    def _wait(self, E, deps):
        for k, (sem, val) in deps.items():
            if sem is E.sem and not E.self_sync:
                continue
            if E.waited.get(k, 0) < val:
                E.eng.wait_ge(sem, val)
                E.waited[k] = val

    def _record(self, tok, reads, writes):
        k = id(tok[0])
        for b in reads:
            if k not in b.r or b.r[k][1] < tok[1]:
                b.r[k] = tok
        for b in writes:
            b.w = {k: tok}
            b.r = {}

    def op(self, ename, fn, reads=(), writes=()):
        E = self.engs[ename]
        self._wait(E, self._deps(reads, writes))
        ins = fn(E.eng)
        E.cnt += 1
        ins.then_inc(E.sem, 1)
        self._record((E.sem, E.cnt), reads, writes)

    def dma(self, ename, pairs, reads, writes, sbuf):
        E = self.engs[ename]
        if sbuf.sem is None:
            if self.free_sems:
                sbuf.sem, sbuf.cnt = self.free_sems.pop()
                self.retired.pop(id(sbuf.sem), None)
            else:
                self.nsem += 1
                sbuf.sem = self.nc.alloc_semaphore("d_%d" % self.nsem)
            self.dma_bufs.append(sbuf)
        deps = self._deps(reads, writes)
        if sbuf.cnt > 0:
            _merge(deps, {id(sbuf.sem): (sbuf.sem, sbuf.cnt)})
        self._wait(E, deps)
        for (o, i) in pairs:
            E.eng.dma_start(out=o, in_=i).then_inc(sbuf.sem, 16)
            sbuf.cnt += 16
        self._record((sbuf.sem, sbuf.cnt), reads, writes)

    def barrier(self):
        toks = {}
        for E in self.engs.values():
            if E.cnt:
                toks[id(E.sem)] = (E.sem, E.cnt)
        for b in self.dma_bufs:
            if b.cnt:
                toks[id(b.sem)] = (b.sem, b.cnt)
        for k, tok in self.retired.items():
            toks.setdefault(k, tok)
        for E in self.engs.values():
            for k, (sem, val) in toks.items():
                if sem is E.sem:
                    continue
                if E.waited.get(k, 0) < val:
                    E.eng.wait_ge(sem, val)
                    E.waited[k] = val

    def mm(self, out, lhsT, rhs, start, stop, reads, writes):
        self.op("pe", lambda e: e.matmul(out, lhsT, rhs, start=start, stop=stop), reads, writes)

    def tt(self, eng, out, in0, in1, op, reads, writes):
        self.op(eng, lambda e: e.tensor_tensor(out=out, in0=in0, in1=in1, op=op), reads, writes)

    def ts(self, eng, out, in0, s1, s2, op0, op1, reads, writes):
        if op1 is None:
            self.op(eng, lambda e: e.tensor_scalar(out=out, in0=in0, scalar1=s1, scalar2=None, op0=op0), reads, writes)
        else:
            self.op(eng, lambda e: e.tensor_scalar(out=out, in0=in0, scalar1=s1, scalar2=s2, op0=op0, op1=op1), reads, writes)

    def stt(self, out, in0, scalar, in1, op0, op1, reads, writes):
        self.op("dve", lambda e: e.scalar_tensor_tensor(out=out, in0=in0, scalar=scalar, in1=in1, op0=op0, op1=op1), reads, writes)

    def actf(self, out, in_, func, reads, writes, bias=None, scale=None):
        kw = {}
        if bias is not None:
            kw["bias"] = bias
        if scale is not None:
            kw["scale"] = scale
        self.op("act", lambda e: e.activation(out=out, in_=in_, func=func, **kw), reads, writes)

    def copy(self, eng, out, in_, reads, writes):
        if eng == "act":
            self.op("act", lambda e: e.activation(out=out, in_=in_, func=AF.Identity), reads, writes)
        else:
            self.op(eng, lambda e: e.tensor_copy(out=out, in_=in_), reads, writes)

    def memset(self, eng, ap, val, writes):
        self.op(eng, lambda e: e.memset(ap, val), (), writes)


class Cfg:
    def __init__(self, T, L):
        self.T = T
        self.L = L
        self.NT = T // 512
        self.SEG = T // 2
        self.NB = T // 8
        self.NBS = self.NB // 2
        self.NKC = T // 128


def build(cfg):
    T, L, NT, NB, NBS = cfg.T, cfg.L, cfg.NT, cfg.NB, cfg.NBS
    nc = bass.Bass("TRN2", target_bir_lowering=False)
    K = KB(nc)

    def din(name, shape, dt=F32):
        return nc.dram_tensor(name, list(shape), dt, kind="ExternalInput").ap()

    x_in = din("xT", [D, T])
    y_out = nc.dram_tensor("yT", [D, T], F32, kind="ExternalOutput").ap()
    cT_in = din("cT", [128, 8, 2])
    cos_in = din("cosT", [128, T])
    sin_in = din("sinT", [128, T])
    perm_in = din("perm", [128, 128])
    ident_in = din("ident", [128, 128])
    maskf_in = din("maskf", [128, 128])
    maskb_in = din("maskb", [128, 128])
    cb_in = din("crossbias", [128, 1])
    flag_in = din("flag", [128, 1])
    w_mod = din("w_mod", [L, D, 6 * D])
    bmod_in = din("b_modT", [128, L, 48])
    g1_in = din("g1T", [128, L, 8])
    g2_in = din("g2T", [128, L, 8])
    gf_in = din("gfT", [128, 8])
    w_in = din("w_in", [L, D, 4096])
    lam_in = din("lamT", [128, L, 4, 64])
    subg_in = din("subgT", [128, L])
    w_attn = din("w_attn", [L, 512, D])
    are_in = din("a_reP", [128, L, 32])
    aim_in = din("a_imP", [128, L, 32])
    ldt_in = din("ldtP", [128, L, 32])
    bre_in = din("b_reP", [128, L, 32, 16])
    bim_in = din("b_imP", [128, L, 32, 16])
    cre_in = din("c_reP", [128, L, 32, 16])
    cim_in = din("c_imP", [128, L, 32, 16])
    dsk_in = din("ssm_dT", [128, L, 4])
    w_glu = din("w_glu", [L, 512, 2 * D])
    bglu_in = din("b_gluT", [128, L, 16])
    w_o = din("w_o", [L, D, D])
    w_ffi = din("w_ffi", [L, D, 2 * DFF])
    w_ffo = din("w_ffo", [L, DFF, D])

    wb_in = K.dram("wb_in", [L, D, 4096], BF16)
    wb_attn = K.dram("wb_attn", [L, 512, D], BF16)
    wb_glu = K.dram("wb_glu", [L, 512, 2 * D], BF16)
    wb_o = K.dram("wb_o", [L, D, D], BF16)
    wb_ffi = K.dram("wb_ffi", [L, D, 2 * DFF], BF16)
    wb_ffo = K.dram("wb_ffo", [L, DFF, D], BF16)
    xs = K.dram("xs", [D, T], F32)
    QT = K.dram("QT", [4, 128, T], BF16)
    KT = K.dram("KT", [4, 128, T], BF16)
    VS = K.dram("VS", [T // 128, 128, 512], BF16)
    SGA = K.dram("SGA", [D, T], BF16)
    SGS = K.dram("SGS", [D, T], BF16)
    UT = K.dram("UT", [512, T], F32)
    XL = K.dram("XL", [8, 512, NB], BF16)
    YL = K.dram("YL", [8, 512, NB], F32)
    OT = K.dram("OT", [512, T], BF16)
    dbuf = {}

    def DB(name, i=0):
        key = (name, i)
        if key not in dbuf:
            dbuf[key] = Buf("%s%d" % (name, i))
        return dbuf[key]

    ones32, b_ones32 = K.sb([128, 128], F32, "ones32")
    perm_b, b_perm = K.sb([128, 128], BF16, "perm")
    ident, b_ident = K.sb([128, 128], F32, "ident")
    maskf, b_maskf = K.sb([128, 128], F32, "maskf")
    maskb, b_maskb = K.sb([128, 128], F32, "maskb")
    crossb, b_crossb = K.sb([128, 1], F32, "crossb")
    zerob, b_zerob = K.sb([128, 1], F32, "zerob")
    flag, b_flag = K.sb([128, 1], F32, "flag")
    tmpc, b_tmpc = K.sb([128, 128], F32, "tmpc")
    K.memset("dve", ones32, 1.0, [b_ones32])
    K.memset("dve", zerob, 0.0, [b_zerob])
    K.dma("sp", [(tmpc, perm_in)], [], [b_tmpc], b_tmpc)
    K.copy("dve", perm_b, tmpc, [b_tmpc], [b_perm])
    K.dma("sp", [(ident, ident_in)], [], [b_ident], b_ident)
    K.dma("sp", [(maskf, maskf_in)], [], [b_maskf], b_maskf)
    K.dma("sp", [(maskb, maskb_in)], [], [b_maskb], b_maskb)
    K.dma("sp", [(crossb, cb_in)], [], [b_crossb], b_crossb)
    K.dma("sp", [(flag, flag_in)], [], [b_flag], b_flag)

    g1T, b_g1T = K.sb([128, L, 8], F32, "g1T")
    g2T, b_g2T = K.sb([128, L, 8], F32, "g2T")
    gfT, b_gfT = K.sb([128, 8], F32, "gfT")
    bglu, b_bglu = K.sb([128, L, 16], F32, "bglu")
    dsk, b_dsk = K.sb([128, L, 4], F32, "dsk")
    subg, b_subg = K.sb([128, L], F32, "subg")
    bmodT, b_bmodT = K.sb([128, L, 48], F32, "bmodT")
    K.dma("sp", [(g1T, g1_in)], [], [b_g1T], b_g1T)
    K.dma("sp", [(g2T, g2_in)], [], [b_g2T], b_g2T)
    K.dma("sp", [(gfT, gf_in)], [], [b_gfT], b_gfT)
    K.dma("sp", [(bglu, bglu_in)], [], [b_bglu], b_bglu)
    K.dma("sp", [(dsk, dsk_in)], [], [b_dsk], b_dsk)
    K.dma("sp", [(subg, subg_in)], [], [b_subg], b_subg)
    K.dma("sp", [(bmodT, bmod_in)], [], [b_bmodT], b_bmodT)

    cvb = [Buf("cv%d" % i) for i in range(4)]
    cvi = [0]
    b_wb = Buf("wb_all")

    def convert(src, dst, nelem):
        rows = nelem // 2048
        s2 = src.rearrange("(r c) -> r c", c=2048)
        d2 = dst.rearrange("(r c) -> r c", c=2048)
        r0 = 0
        while r0 < rows:
            r1 = min(rows, r0 + 1024)
            cb = cvb[cvi[0] % 4]
            cvi[0] += 1
            K.dma("pool", [(d2[r0:r1, :], s2[r0:r1, :])], [], [cb], cb)
            r0 = r1

    for l in range(L):
        convert(w_in[l].rearrange("a b -> (a b)"), wb_in[l].rearrange("a b -> (a b)"), D * 4096)
    for l in range(L):
        convert(w_attn[l].rearrange("a b -> (a b)"), wb_attn[l].rearrange("a b -> (a b)"), 512 * D)
        convert(w_glu[l].rearrange("a b -> (a b)"), wb_glu[l].rearrange("a b -> (a b)"), 512 * 2 * D)
        convert(w_o[l].rearrange("a b -> (a b)"), wb_o[l].rearrange("a b -> (a b)"), D * D)
        convert(w_ffi[l].rearrange("a b -> (a b)"), wb_ffi[l].rearrange("a b -> (a b)"), D * 2 * DFF)
        convert(w_ffo[l].rearrange("a b -> (a b)"), wb_ffo[l].rearrange("a b -> (a b)"), DFF * D)

    modT, b_modT = K.sb([128, L, 48, 2], F32, "modT")
    A1, b_A1 = K.sb([128, L, 8, 2], F32, "A1")
    A2, b_A2 = K.sb([128, L, 8, 2], F32, "A2")
    cT, b_cT = K.sb([128, 8, 2], F32, "cT")
    sc_, b_sc = K.sb([128, 8, 2], F32, "silu_c")
    K.dma("sp", [(cT, cT_in)], [], [b_cT], b_cT)
    K.actf(sc_, cT, AF.Silu, [b_cT], [b_sc])
    blk = 0
    mod_scope = K.scope()
    mod_scope.__enter__()
    wm = [K.sb([128, 8, 512], F32, "wm%d" % i) for i in range(2)]
    for l in range(L):
        for cb_ in range(12):
            wt, bw = wm[blk % 2]
            blk += 1
            K.dma("sp", [(wt, w_mod[l, :, cb_ * 512:(cb_ + 1) * 512].rearrange("(k p) c -> p k c", p=128))],
                  [], [bw], bw)
            pt, bp = K.psum()
            for c in range(4):
                for kc in range(8):
                    K.mm(pt[:, c * 2:c * 2 + 2], wt[:, kc, c * 128:(c + 1) * 128], sc_[:, kc, :],
                         kc == 0, kc == 7, [bw, b_sc], [bp])
            K.tt("dve", modT[:, l, cb_ * 4:(cb_ + 1) * 4, :],
                 pt[:, 0:8].rearrange("p (c s) -> p c s", s=2),
                 bmodT[:, l, cb_ * 4:(cb_ + 1) * 4].unsqueeze(2).to_broadcast([128, 4, 2]),
                 ALU.add, [bp, b_bmodT], [b_modT])
    K.barrier()
    mod_scope.__exit__(None, None, None)
    for l in range(L):
        K.stt(A1[:, l], modT[:, l, 8:16, :], 1.0, g1T[:, l, :].unsqueeze(2).to_broadcast([128, 8, 2]),
              ALU.add, ALU.mult, [b_modT, b_g1T], [b_A1])
        K.stt(A2[:, l], modT[:, l, 32:40, :], 1.0, g2T[:, l, :].unsqueeze(2).to_broadcast([128, 8, 2]),
              ALU.add, ALU.mult, [b_modT, b_g2T], [b_A2])

    def SH1(l, kc, s):
        return modT[:, l, 0 + kc, s:s + 1]

    def GT1(l, kc, s):
        return modT[:, l, 16 + kc, s:s + 1]

    def SH2(l, kc, s):
        return modT[:, l, 24 + kc, s:s + 1]

    def GT2(l, kc, s):
        return modT[:, l, 40 + kc, s:s + 1]

    lamT, b_lamT = K.sb([128, L, 4, 64], F32, "lamT")
    lamv, b_lamv = K.sb([128, L], F32, "lamv")
    neglam, b_neglam = K.sb([128, L], F32, "neglam")
    lsum, b_lsum = K.sb([128, L, 2], F32, "lsum")
    lprod, b_lprod = K.sb([128, L, 2, 64], F32, "lprod")
    K.dma("sp", [(lamT, lam_in)], [], [b_lamT], b_lamT)
    K.tt("dve", lprod[:, :, 0, :], lamT[:, :, 0, :], lamT[:, :, 1, :], ALU.mult, [b_lamT], [b_lprod])
    K.tt("dve", lprod[:, :, 1, :], lamT[:, :, 2, :], lamT[:, :, 3, :], ALU.mult, [b_lamT], [b_lprod])
    K.op("dve", lambda e: e.tensor_reduce(out=lsum, in_=lprod, op=ALU.add, axis=mybir.AxisListType.X),
         [b_lprod], [b_lsum])
    K.actf(lsum, lsum, AF.Exp, [b_lsum], [b_lsum])
    K.tt("dve", lamv, lsum[:, :, 0], lsum[:, :, 1], ALU.subtract, [b_lsum], [b_lamv])
    for l in range(L):
        K.ts("dve", lamv[:, l:l + 1], lamv[:, l:l + 1], float(lam_init_fn(l)), None, ALU.add, None, [b_lamv], [b_lamv])
    K.ts("dve", neglam, lamv, -1.0, None, ALU.mult, None, [b_lamv], [b_neglam])
    subgs, b_subgs = K.sb([128, L], F32, "subgs")
    for l in range(L):
        K.ts("dve", subgs[:, l:l + 1], subg[:, l:l + 1], float(1.0 - lam_init_fn(l)), None, ALU.mult, None,
             [b_subg], [b_subgs])

    K.barrier()

    class NS:
        pass
    S = NS()

    def alloc_shared():
        S.xt_ = [K.sb([128, 8, 512], F32, "xt%d" % i) for i in range(2)]
        S.hT, S.b_hT = K.sb([128, 8, 512], BF16, "hT")
        S.sq, S.b_sq = K.sb([128, 8, 512], F32, "sq")
        S.rstd, S.b_rstd = K.sb([128, 512], F32, "rstd")
        S.tmpn, S.b_tmpn = K.sb([128, 512], F32, "tmpn")
        S.wsl = [K.sb([128, 11, 512], BF16, "w%d" % i) for i in range(3)]
    wsl_i = [0]

    def wslot():
        s_ = S.wsl[wsl_i[0] % 3]
        wsl_i[0] += 1
        return s_

    def rstd_only(xt, bx):
        K.actf(S.sq, xt, AF.Square, [bx], [S.b_sq])
        pt, bp = K.psum()
        for kc in range(8):
            K.mm(pt, ones32, S.sq[:, kc, :], kc == 0, kc == 7, [b_ones32, S.b_sq], [bp])
        K.ts("dve", S.tmpn, pt, 1.0 / D, EPS, ALU.mult, ALU.add, [bp], [S.b_tmpn])
        K.actf(S.tmpn, S.tmpn, AF.Sqrt, [S.b_tmpn], [S.b_tmpn])
        K.op("dve", lambda e: e.reciprocal(out=S.rstd, in_=S.tmpn), [S.b_tmpn], [S.b_rstd])

    def norm_mod(xt, bx, Acol, Bcol, out_bf, b_out):
        rstd_only(xt, bx)
        for kc in range(8):
            K.stt(S.sq[:, kc, :], xt[:, kc, :], Acol(kc), S.rstd, ALU.mult, ALU.mult,
                  [bx, S.b_rstd, b_A1, b_A2], [S.b_sq])
            K.actf(out_bf[:, kc, :], S.sq[:, kc, :], AF.Identity, [S.b_sq, b_modT], [b_out], bias=Bcol(kc), scale=1.0)

    def load_w(src2d, k0, nk, col_runs):
        wt, bw = wslot()
        pairs = []
        off = 0
        for (c0, n) in col_runs:
            pairs.append((wt[:, 0:nk, off:off + n],
                          src2d[k0 * 128:(k0 + nk) * 128, c0:c0 + n].rearrange("(k p) c -> p k c", p=128)))
            off += n
        K.dma("sp", pairs, [b_wb], [bw], bw)
        return wt, bw

    cnt = {"qo": 0, "vo": 0, "uo": 0, "go": 0, "x": 0, "cs": 0}

    def phaseA(l):
        import os
        stopA = os.environ.get('MK_STOPA', '')
        scA = K.scope()
        scA.__enter__()
        alloc_shared()
        xt_ = S.xt_
        hT, b_hT = S.hT, S.b_hT
        csl = [K.sb([128, 2, 512], F32, "cs%d" % i) for i in range(2)]
        qb_, b_qb = K.sb([128, 512], BF16, "qb")
        qc_, b_qc = K.sb([128, 512], F32, "qc")
        qs_, b_qs = K.sb([128, 512], F32, "qs")
        qo_ = [K.sb([128, 4, 512], BF16, "qo%d" % i) for i in range(2)]
        vo_ = [K.sb([128, 4, 512], BF16, "vo%d" % i) for i in range(2)]
        uo_ = [K.sb([128, 4, 512], F32, "uo%d" % i) for i in range(2)]
        ul_ = [K.sb([128, 4, 8, 64], BF16, "ul%d" % i) for i in range(2)]
        go_ = [K.sb([128, 4, 512], BF16, "go%d" % i) for i in range(2)]
        wsrc = wb_in[l]
        for i in range(NT):
            s = i // (NT // 2)
            t0 = i * 512
            xt, bx = xt_[cnt["x"] % 2]
            cnt["x"] += 1
            src = x_in if l == 0 else xs
            rd = [] if l == 0 else [DB("xs", i)]
            K.dma("sp", [(xt, src[:, t0:t0 + 512].rearrange("(k p) t -> p k t", p=128))], rd, [bx], bx)
            cs, bcs = csl[cnt["cs"] % 2]
            cnt["cs"] += 1
            K.dma("sp", [(cs[:, 0, :], cos_in[:, t0:t0 + 512]), (cs[:, 1, :], sin_in[:, t0:t0 + 512])], [], [bcs], bcs)
            norm_mod(xt, bx, lambda kc: A1[:, l, kc, s:s + 1], lambda kc: SH1(l, kc, s), hT, b_hT)
            if stopA == 'n':
                continue
            for qk in range(2):
                wt, bw = load_w(wsrc, 0, 8, [(qk * 512, 512)])
                qo, bqo = qo_[cnt["qo"] % 2]
                cnt["qo"] += 1
                Y = int(os.environ.get("MK_Y", "9"))
                for c in range(4):
                    if Y < 1:
                        continue
                    pt, bp = K.psum()
                    for kc in range(8):
                        K.mm(pt, wt[:, kc, c * 128:(c + 1) * 128], hT[:, kc, :], kc == 0, kc == 7, [bw, b_hT], [bp])
                    if Y < 2:
                        continue
                    K.copy("act", qb_, pt, [bp], [b_qb])
                    if Y < 3:
                        continue
                    K.tt("dve", qc_, pt, cs[:, 0, :], ALU.mult, [bp, bcs] + ([b_qb] if os.environ.get("MK_Z") == "1" else []), [b_qc])
                    if Y < 4:
                        continue
                    p2, bp2 = K.psum()
                    K.mm(p2, perm_b, qb_, True, True, [b_perm, b_qb], [bp2])
                    if Y < 5:
                        continue
                    K.tt("dve", qs_, p2, cs[:, 1, :], ALU.mult, [bp2, bcs], [b_qs])
                    K.tt(os.environ.get("MK_QE", "pool"), qo[:, c, :], qc_, qs_, ALU.add, [b_qc, b_qs], [bqo])
                dst = QT if qk == 0 else KT
                if os.environ.get("MK_X") != "1":
                    K.dma("sp", [(dst[:, :, t0:t0 + 512].rearrange("h p t -> p h t"), qo)], [bqo],
                          [DB("QT" if qk == 0 else "KT", i)], bqo)
            if stopA == 'qk':
                continue
            wt, bw = load_w(wsrc, 0, 8, [(1024, 512)])
            vo, bvo = vo_[cnt["vo"] % 2]
            cnt["vo"] += 1
            for tc in range(4):
                pt, bp = K.psum()
                for kc in range(8):
                    K.mm(pt, hT[:, kc, tc * 128:(tc + 1) * 128], wt[:, kc, 0:512], kc == 0, kc == 7, [bw, b_hT], [bp])
                K.copy("act", vo[:, tc, :], pt, [bp], [bvo])
            K.dma("sp", [(VS[i * 4:(i + 1) * 4].rearrange("c p e -> p c e"), vo)], [bvo], [DB("VS", i)], bvo)
            if stopA == 'v':
                continue
            wt, bw = load_w(wsrc, 0, 8, [(1536, 512)])
            uo, buo = uo_[cnt["uo"] % 2]
            ul, bul = ul_[cnt["uo"] % 2]
            cnt["uo"] += 1
            for c in range(4):
                pt, bp = K.psum()
                for kc in range(8):
                    K.mm(pt, wt[:, kc, c * 128:(c + 1) * 128], hT[:, kc, :], kc == 0, kc == 7, [bw, b_hT], [bp])
                K.copy("act", uo[:, c, :], pt, [bp], [buo])
                K.copy("dve", ul[:, c], pt.rearrange("p (b s) -> p s b", s=8), [bp], [bul])
            K.dma("sp", [(UT[:, t0:t0 + 512].rearrange("(c p) t -> p c t", p=128), uo)], [buo], [DB("UT", i)], buo)
            K.dma("sp", [(XL[:, c * 128:(c + 1) * 128, i * 64:(i + 1) * 64].rearrange("s p b -> p s b"), ul[:, c])
                         for c in range(4)], [bul], [DB("XL", i)], bul)
            if stopA == 'u':
                continue
            for gb in range(4):
                wt, bw = load_w(wsrc, 0, 8, [(2048 + gb * 512, 512)])
                go, bgo = go_[cnt["go"] % 2]
                cnt["go"] += 1
                for c in range(4):
                    pt, bp = K.psum()
                    for kc in range(8):
                        K.mm(pt, wt[:, kc, c * 128:(c + 1) * 128], hT[:, kc, :], kc == 0, kc == 7, [bw, b_hT], [bp])
                    K.actf(go[:, c, :], pt, AF.Sigmoid, [bp], [bgo])
                dst = SGA if gb < 2 else SGS
                r0 = (gb % 2) * 512
                K.dma("sp", [(dst[r0:r0 + 512, t0:t0 + 512].rearrange("(c p) t -> p c t", p=128), go)], [bgo],
                      [DB("SGA" if gb < 2 else "SGS", i * 2 + gb % 2)], bgo)

        K.barrier()
        scA.__exit__(None, None, None)

    cntB = {"q": 0, "p": 0, "ob": 0}
    scale = 64 ** -0.5

    def phaseB(l):
        scB = K.scope()
        scB.__enter__()
        kt_, b_kt = K.sb([128, T], BF16, "kt")
        vh_, b_vh = K.sb([128, T // 128, 128], BF16, "vh")
        qt_ = [K.sb([128, 512], BF16, "qt%d" % i) for i in range(2)]
        pT_ = [K.sb([128, 512], BF16, "pT%d" % i) for i in range(4)]
        rr_, b_rr = K.sb([128, 512], F32, "rr")
        t1_, b_t1 = K.sb([128, 512], F32, "t1")
        t2_, b_t2 = K.sb([128, 512], F32, "t2")
        od_, b_od = K.sb([128, 512], F32, "od")
        o2_, b_o2 = K.sb([128, 512], F32, "o2")
        ob_ = [K.sb([128, 512], BF16, "ob%d" % i) for i in range(2)]
        acs_, b_acs = K.sb([128, 512], F32, "acs")
        NKC = T // 128
        for h in range(NH):
            K.dma("sp", [(kt_, KT[h])], [DB("KT", i) for i in range(NT)], [b_kt], b_kt)
            K.dma("sp", [(vh_, VS[:, :, h * 128:(h + 1) * 128].rearrange("c p e -> p c e"))],
                  [DB("VS", i) for i in range(NT)], [b_vh], b_vh)
            for j in range(NT):
                sj = j // (NT // 2)
                qt, bq = qt_[cntB["q"] % 2]
                cntB["q"] += 1
                K.dma("sp", [(qt, QT[h, :, j * 512:(j + 1) * 512])], [DB("QT", j)], [bq], bq)
                for m in range(2):
                    hpo = K.psum_hold()
                    hpa = K.psum_hold()
                    po, bpo = hpo
                    pa, bpa = hpa

                    def emit_s(c):
                        ps_, bps = K.psum()
                        K.mm(ps_, kt_[64 * m:64 * m + 64, c * 128:(c + 1) * 128], qt[64 * m:64 * m + 64, :],
                             True, True, [b_kt, bq], [bps])
                        return ps_, bps
                    nxt = emit_s(0)
                    for c in range(NKC):
                        sc = (c * 128) // cfg.SEG
                        ps_, bps = nxt
                        if c + 1 < NKC:
                            nxt = emit_s(c + 1)
                        pT, bpT = pT_[cntB["p"] % 4]
                        cntB["p"] += 1
                        K.actf(pT, ps_, AF.Exp, [bps, b_crossb, b_zerob], [bpT],
                               bias=(zerob if sc == sj else crossb), scale=scale)
                        K.mm(po, vh_[:, c, :], pT, c == 0, c == NKC - 1, [b_vh, bpT], [bpo])
                        if c == 0:
                            K.copy("dve", pa, pT, [bpT], [bpa])
                        else:
                            K.tt("dve", pa, pa, pT, ALU.add, [bpT, bpa], [bpa])
                    K.copy("dve", acs_, pa, [bpa], [b_acs])
                    pr, bpr = K.psum()
                    K.mm(pr, ones32, acs_, True, True, [b_ones32, b_acs], [bpr])
                    K.op("dve", lambda e: e.reciprocal(out=rr_, in_=pr), [bpr], [b_rr])
                    if m == 0:
                        K.tt("dve", t1_, po, rr_, ALU.mult, [bpo, b_rr], [b_t1])
                    else:
                        K.tt("dve", t2_, po, rr_, ALU.mult, [bpo, b_rr], [b_t2])
                    K.psum_release(hpo)
                    K.psum_release(hpa)
                K.stt(od_, t2_, neglam[:, l:l + 1], t1_, ALU.mult, ALU.add, [b_t1, b_t2, b_neglam], [b_od])
                K.actf(o2_, od_, AF.Square, [b_od], [b_o2])
                pq, bpq = K.psum()
                K.mm(pq, ones32, o2_, True, True, [b_ones32, b_o2], [bpq])
                K.ts("dve", rr_, pq, 1.0 / 128, EPS, ALU.mult, ALU.add, [bpq], [b_rr])
                K.actf(rr_, rr_, AF.Sqrt, [b_rr], [b_rr])
                K.op("dve", lambda e: e.reciprocal(out=t1_, in_=rr_), [b_rr], [b_t1])
                ob, bob = ob_[cntB["ob"] % 2]
                cntB["ob"] += 1
                K.stt(ob, od_, subgs[:, l:l + 1], t1_, ALU.mult, ALU.mult, [b_od, b_t1, b_subgs], [bob])
                K.dma("sp", [(OT[h * 128:(h + 1) * 128, j * 512:(j + 1) * 512], ob)], [bob], [DB("OT", j)], bob)

        K.barrier()
        scB.__exit__(None, None, None)

    PG = [128, 32]
    sA = {}

    def pg(name, shape=None, dt=F32):
        if name not in sA:
            sA[name] = K.sb(shape or PG, dt, name)
        return sA[name]

    NLEV = int(math.log2(NB))
    cntC = {"x": 0, "y": 0}

    def ssm_precompute(l):
        sA.clear()
        scP = K.scope()
        scP.__enter__()
        are, b1 = pg("are"); aim, b2 = pg("aim"); ldt, b3 = pg("ldt")
        K.dma("sp", [(are, are_in[:, l, :])], [], [b1], b1)
        K.dma("sp", [(aim, aim_in[:, l, :])], [], [b2], b2)
        K.dma("sp", [(ldt, ldt_in[:, l, :])], [], [b3], b3)
        Br, bBr = pg("Br", [128, 32, 16]); Bi, bBi = pg("Bi", [128, 32, 16])
        Cr, bCr = pg("Cr", [128, 32, 16]); Ci, bCi = pg("Ci", [128, 32, 16])
        K.dma("sp", [(Br, bre_in[:, l])], [], [bBr], bBr)
        K.dma("sp", [(Bi, bim_in[:, l])], [], [bBi], bBi)
        K.dma("sp", [(Cr, cre_in[:, l])], [], [bCr], bCr)
        K.dma("sp", [(Ci, cim_in[:, l])], [], [bCi], bCi)
        dt_, bdt = pg("dt"); xr, bxr = pg("xr"); th, bth = pg("th"); mag, bmag = pg("mag")
        K.actf(dt_, ldt, AF.Exp, [b3], [bdt])
        K.tt("dve", xr, dt_, are, ALU.mult, [bdt, b1], [bxr])
        K.tt("dve", th, dt_, aim, ALU.mult, [bdt, b2], [bth])
        K.actf(mag, xr, AF.Exp, [bxr], [bmag])
        yv, byv = pg("yv"); ki, bki = pg("ki", PG, I32); kf, bkf = pg("kf"); mk, bmk = pg("mk")
        sn, bsn = pg("sn"); cs_, bcs_ = pg("cs")
        for (dst, bdst, off) in ((sn, bsn, 1.5), (cs_, bcs_, 1.75)):
            K.ts("dve", yv, th, 1.0 / TWO_PI, off, ALU.mult, ALU.add, [bth], [byv])
            K.copy("dve", ki, yv, [byv], [bki])
            K.copy("dve", kf, ki, [bki], [bkf])
            K.tt("dve", mk, kf, yv, ALU.is_gt, [bkf, byv], [bmk])
            K.tt("dve", kf, kf, mk, ALU.subtract, [bkf, bmk], [bkf])
            K.tt("dve", yv, yv, kf, ALU.subtract, [byv, bkf], [byv])
            K.ts("dve", yv, yv, -0.5, TWO_PI, ALU.add, ALU.mult, [byv], [byv])
            K.ts("dve", yv, yv, math.pi, -math.pi, ALU.min, ALU.max, [byv], [byv])
            K.actf(dst, yv, AF.Sin, [byv], [bdst])
        Ar, bAr = pg("Ar"); Ai, bAi = pg("Ai")
        stopC = os.environ.get('MK_STOPC', '')
        if stopC == 'sin':
            K.barrier(); scP.__exit__(None, None, None); return
        K.tt("dve", Ar, mag, cs_, ALU.mult, [bmag, bcs_], [bAr])
        K.tt("dve", Ai, mag, sn, ALU.mult, [bmag, bsn], [bAi])
        Par, bPar = pg("Par", [128, 32, 9]); Pai, bPai = pg("Pai", [128, 32, 9])
        Pdr, bPdr = pg("Pdr", [128, 32, 9]); Pdi, bPdi = pg("Pdi", [128, 32, 9])
        Qar, bQar = pg("Qar", [128, 32, 9]); Qai, bQai = pg("Qai", [128, 32, 9])
        Qdr, bQdr = pg("Qdr", [128, 32, 9]); Qdi, bQdi = pg("Qdi", [128, 32, 9])
        ta, bta = pg("ta"); tb, btb = pg("tb")
        K.memset("dve", Par[:, :, 0], 1.0, [bPar])
        K.memset("dve", Pai[:, :, 0], 0.0, [bPai])
        for n in range(1, 9):
            K.tt("dve", ta, Par[:, :, n - 1], Ar, ALU.mult, [bPar, bAr], [bta])
            K.tt("dve", tb, Pai[:, :, n - 1], Ai, ALU.mult, [bPai, bAi], [btb])
            K.tt("dve", Par[:, :, n], ta, tb, ALU.subtract, [bta, btb], [bPar])
            K.tt("dve", ta, Par[:, :, n - 1], Ai, ALU.mult, [bPar, bAi], [bta])
            K.tt("dve", tb, Pai[:, :, n - 1], Ar, ALU.mult, [bPai, bAr], [btb])
            K.tt("dve", Pai[:, :, n], ta, tb, ALU.add, [bta, btb], [bPai])
        e2, be2 = pg("e2")
        for n in range(9):
            K.actf(e2, xr, AF.Exp, [bxr], [be2], scale=-2.0 * n)
            K.tt("dve", Qar[:, :, n], Par[:, :, n], e2, ALU.mult, [bPar, be2], [bQar])
            K.stt(Qai[:, :, n], Pai[:, :, n], -1.0, e2, ALU.mult, ALU.mult, [bPai, be2], [bQai])
        for n in range(9):
            K.copy("pool", Pdr[:, :, 8 - n], Par[:, :, n], [bPar], [bPdr])
            K.copy("pool", Pdi[:, :, 8 - n], Pai[:, :, n], [bPai], [bPdi])
            K.copy("pool", Qdr[:, :, 8 - n], Qar[:, :, n], [bQar], [bQdr])
            K.copy("pool", Qdi[:, :, 8 - n], Qai[:, :, n], [bQai], [bQdi])
        K.copy("dve", S.SS[:, 0, 0, :], Par[:, :, 8], [bPar], [S.b_SS])
        K.copy("dve", S.SS[:, 0, 1, :], Pai[:, :, 8], [bPai], [S.b_SS])
        for k in range(NLEV):
            if k > 0:
                K.tt("dve", ta, S.SS[:, k - 1, 0, :], S.SS[:, k - 1, 0, :], ALU.mult, [S.b_SS], [bta])
                K.tt("dve", tb, S.SS[:, k - 1, 1, :], S.SS[:, k - 1, 1, :], ALU.mult, [S.b_SS], [btb])
                K.tt("dve", S.SS[:, k, 0, :], ta, tb, ALU.subtract, [bta, btb], [S.b_SS])
                K.stt(S.SS[:, k, 1, :], S.SS[:, k - 1, 0, :], 2.0, S.SS[:, k - 1, 1, :], ALU.mult, ALU.mult, [S.b_SS], [S.b_SS])
            K.ts("dve", S.SS[:, k, 2, :], S.SS[:, k, 1, :], -1.0, None, ALU.mult, None, [S.b_SS], [S.b_SS])
        K.ts("dve", S.SF, S.SS, flag[:, 0:1], None, ALU.mult, None, [S.b_SS, b_flag], [S.b_SF])
        if stopC == 'pow':
            K.barrier(); scP.__exit__(None, None, None); return
        nr, bnr = pg("nr"); den, bden = pg("den"); fr, bfr = pg("fr"); fi, bfi = pg("fi")
        K.ts("dve", nr, Ar, -1.0, None, ALU.add, None, [bAr], [bnr])
        K.tt("dve", ta, are, are, ALU.mult, [b1], [bta])
        K.tt("dve", tb, aim, aim, ALU.mult, [b2], [btb])
        K.tt("dve", den, ta, tb, ALU.add, [bta, btb], [bden])
        K.op("dve", lambda e: e.reciprocal(out=den, in_=den), [bden], [bden])
        K.tt("dve", ta, nr, are, ALU.mult, [bnr, b1], [bta])
        K.tt("dve", tb, Ai, aim, ALU.mult, [bAi, b2], [btb])
        K.tt("dve", ta, ta, tb, ALU.add, [bta, btb], [bta])
        K.tt("dve", fr, ta, den, ALU.mult, [bta, bden], [bfr])
        K.tt("dve", ta, Ai, are, ALU.mult, [bAi, b1], [bta])
        K.tt("dve", tb, nr, aim, ALU.mult, [bnr, b2], [btb])
        K.tt("dve", ta, ta, tb, ALU.subtract, [bta, btb], [bta])
        K.tt("dve", fi, ta, den, ALU.mult, [bta, bden], [bfi])
        Bbr, bBbr = pg("Bbr", [128, 32, 16]); Bbi, bBbi = pg("Bbi", [128, 32, 16])
        t16a, bt16a = pg("t16a", [128, 32, 16]); t16b, bt16b = pg("t16b", [128, 32, 16])
        frb = fr.unsqueeze(2).to_broadcast([128, 32, 16])
        fib = fi.unsqueeze(2).to_broadcast([128, 32, 16])
        K.tt("dve", t16a, Br, frb, ALU.mult, [bBr, bfr], [bt16a])
        K.tt("dve", t16b, Bi, fib, ALU.mult, [bBi, bfi], [bt16b])
        K.tt("dve", Bbr, t16a, t16b, ALU.subtract, [bt16a, bt16b], [bBbr])
        K.tt("dve", t16a, Bi, frb, ALU.mult, [bBi, bfr], [bt16a])
        K.tt("dve", t16b, Br, fib, ALU.mult, [bBr, bfi], [bt16b])
        K.tt("dve", Bbi, t16a, t16b, ALU.add, [bt16a, bt16b], [bBbi])
        def wtab(name, src0, bs0, o0, src1, bs1, o1):
            w, bw = pg(name, [128, 16, 2, 8])
            v0 = src0.rearrange("p (a d) n -> p a d n", d=2)
            v1 = src1.rearrange("p (a d) n -> p a d n", d=2)
            K.copy("pool", w[:, :, 0, :], v0[:, :, 0, o0:o0 + 8], [bs0], [bw])
            K.copy("pool", w[:, :, 1, :], v1[:, :, 1, o1:o1 + 8], [bs1], [bw])
            return w.rearrange("p a d n -> p (a d) n"), bw
        WBr, bWBr = wtab("WBr", Pdr, bPdr, 1, Par, bPar, 0)
        WBi, bWBi = wtab("WBi", Pdi, bPdi, 1, Pai, bPai, 0)
        WCr, bWCr = wtab("WCr", Qdr, bQdr, 1, Qar, bQar, 0)
        WCi, bWCi = wtab("WCi", Qdi, bQdi, 1, Qai, bQai, 0)
        WKr, bWKr = wtab("WKr", Par, bPar, 1, Pdr, bPdr, 0)
        WKi, bWKi = wtab("WKi", Pai, bPai, 1, Pdi, bPdi, 0)
        big = [128, 32, 8, 16]
        PBr, bPBr = pg("PBr", big); PBi, bPBi = pg("PBi", big)
        PCr, bPCr = pg("PCr", big); PCi, bPCi = pg("PCi", big)
        tg1, btg1 = pg("tg1", big); tg2, btg2 = pg("tg2", big)

        def cmul(outr, boutr, outi, bouti, Wr, bWr, Wi, bWi, Xr, bXr, Xi, bXi, neg_i=False):
            wr = Wr.unsqueeze(3).to_broadcast(big)
            wi = Wi.unsqueeze(3).to_broadcast(big)
            xr_ = Xr.unsqueeze(2).to_broadcast(big)
            xi_ = Xi.unsqueeze(2).to_broadcast(big)
            K.tt("dve", tg1, wr, xr_, ALU.mult, [bWr, bXr], [btg1])
            K.tt("dve", tg2, wi, xi_, ALU.mult, [bWi, bXi], [btg2])
            K.tt("dve", outr, tg1, tg2, ALU.subtract, [btg1, btg2], [boutr])
            K.tt("dve", tg1, wr, xi_, ALU.mult, [bWr, bXi], [btg1])
            K.tt("dve", tg2, wi, xr_, ALU.mult, [bWi, bXr], [btg2])
            if neg_i:
                K.stt(outi, tg1, -1.0, tg2, ALU.mult, ALU.subtract, [btg1, btg2], [bouti])
            else:
                K.tt("dve", outi, tg1, tg2, ALU.add, [btg1, btg2], [bouti])

        cmul(PBr, bPBr, PBi, bPBi, WBr, bWBr, WBi, bWBi, Bbr, bBbr, Bbi, bBbi)
        cmul(PCr, bPCr, PCi, bPCi, WCr, bWCr, WCi, bWCi, Cr, bCr, Ci, bCi, neg_i=True)
        if stopC == 'cmul':
            K.barrier(); scP.__exit__(None, None, None); return
        PBr3 = PBr.rearrange("p g n c -> p g (n c)")
        PBi3 = PBi.rearrange("p g n c -> p g (n c)")
        PCr3 = PCr.rearrange("p g n c -> p g (n c)")
        PCi3 = PCi.rearrange("p g n c -> p g (n c)")
        for gp in range(16):
            for d in range(2):
                pt, bp = K.psum()
                idx = 0
                for gpar in range(2):
                    for (src, bsrc) in ((PBr3, bPBr), (PBi3, bPBi)):
                        sl_ = slice(gpar * 64, gpar * 64 + 64)
                        K.mm(pt[:, idx * 64:(idx + 1) * 64], src[:, gp * 2 + d, :], ident[:, sl_], True, True,
                             [bsrc, b_ident], [bp])
                        idx += 1
                K.copy("act", S.PBT[:, gp, d].rearrange("p a b c -> p (a b c)"), pt[:, 0:256], [bp], [S.b_PBT])
        mt, bmt = pg("mt", [128, 128])
        tb1 = tg1.rearrange("p g n c -> p (g n c)").bitcast(BF16).rearrange("p (h g f) -> p h g f", h=2, g=32)
        tb2 = tg2.rearrange("p g n c -> p (g n c)").bitcast(BF16).rearrange("p (h g f) -> p h g f", h=2, g=32)
        PBrb, bPBrb, PBib, bPBib = tb1[:, 0], btg1, tb1[:, 1], btg1
        PCrb, bPCrb, PCib, bPCib = tb2[:, 0], btg2, tb2[:, 1], btg2
        K.copy("act", PBrb, PBr3, [bPBr], [bPBrb])
        K.copy("act", PBib, PBi3, [bPBi], [bPBib])
        K.copy("act", PCrb, PCr3, [bPCr], [bPCrb])
        K.copy("act", PCib, PCi3, [bPCi], [bPCib])
        for gp in range(16):
            for gpar in range(2):
                g = gp * 2 + gpar
                sl = slice(gpar * 64, gpar * 64 + 64)
                pt, bp = K.psum()
                for d in range(2):
                    o = pt[:, d * 128:(d + 1) * 128]
                    K.mm(o, PBrb[sl, gp * 2 + d, :], PCrb[sl, gp * 2 + d, :], True, False, [bPBrb, bPCrb], [bp])
                    K.mm(o, PBib[sl, gp * 2 + d, :], PCib[sl, gp * 2 + d, :], False, True, [bPBib, bPCib], [bp])
                K.tt("dve", mt, pt[:, 0:128], maskf, ALU.mult, [bp, b_maskf], [bmt])
                K.tt("dve", tmpc, pt[:, 128:256], maskb, ALU.mult, [bp, b_maskb], [b_tmpc])
                K.tt("dve", S.M0[:, g, :], mt, tmpc, ALU.add, [bmt, b_tmpc], [S.b_M0])

        PKr, bPKr, PKi, bPKi = PCr, bPCr, PCi, bPCi
        cmul(PKr, bPKr, PKi, bPKi, WKr, bWKr, WKi, bWKi, Cr, bCr, Ci, bCi, neg_i=True)
        K.copy("act", S.PCC[:, :, :, 0, :], PKr.rearrange("p (a d) n c -> p a d (n c)", d=2), [bPKr], [S.b_PCC])
        K.copy("act", S.PCC[:, :, :, 1, :], PKi.rearrange("p (a d) n c -> p a d (n c)", d=2), [bPKi], [S.b_PCC])
        K.barrier()
        scP.__exit__(None, None, None)

    def hs_scan(gp, d):
        col = gp * 2 + d
        cur = 0
        for k in range(NLEV):
            s = 1 << k
            src, bsrc = S.Hb[cur]
            dst, bdst = S.Hb[1 - cur]

            def scal(tab, j):
                return tab[:, k, j, col:col + 1]

            def region(lo, hi, tab, btab, two_seg=False):
                sh = -s if d == 0 else s
                if two_seg:
                    def v(t, ri, off):
                        return t[:, ri, :].rearrange("p (g n) -> p g n", g=2)[:, :, lo + off:hi + off]
                else:
                    def v(t, ri, off):
                        return t[:, ri, lo + off:hi + off]
                rd = [bsrc, btab]
                K.stt(v(dst, 0, 0), v(src, 0, sh), scal(tab, 0), v(src, 0, 0), ALU.mult, ALU.add, rd, [bdst])
                K.stt(v(dst, 1, 0), v(src, 1, sh), scal(tab, 0), v(src, 1, 0), ALU.mult, ALU.add, rd, [bdst])
                K.stt(v(dst, 0, 0), v(src, 1, sh), scal(tab, 2), v(dst, 0, 0), ALU.mult, ALU.add, rd + [bdst], [bdst])
                K.stt(v(dst, 1, 0), v(src, 0, sh), scal(tab, 1), v(dst, 1, 0), ALU.mult, ALU.add, rd + [bdst], [bdst])

            if d == 0:
                K.copy("pool", dst[:, :, 0:min(s, NBS)], src[:, :, 0:min(s, NBS)], [bsrc], [bdst])
                if s < NBS:
                    region(s, NBS, S.SS, S.b_SS, two_seg=True)
                region(NBS, min(NBS + s, NB), S.SF, S.b_SF)
                if s >= NBS and s < NB:
                    pass
            else:
                lo0 = max(NB - s, NBS)
                K.copy("pool", dst[:, :, lo0:NB], src[:, :, lo0:NB], [bsrc], [bdst])
                if s < NBS:
                    region(0, NBS - s, S.SS, S.b_SS, two_seg=True)
                region(max(NBS - s, 0), NBS, S.SF, S.b_SF)
            cur = 1 - cur
        return cur

    def phaseC(l):
        stopC = os.environ.get('MK_STOPC', '')
        scC = K.scope()
        scC.__enter__()
        S.PBT, S.b_PBT = K.sb([128, 16, 2, 2, 2, 64], BF16, "PBT")
        S.PCC, S.b_PCC = K.sb([128, 16, 2, 2, 128], BF16, "PCC")
        S.M0, S.b_M0 = K.sb([128, 32, 128], BF16, "M0")
        S.SS, S.b_SS = K.sb([128, 11, 3, 32], F32, "SS")
        S.SF, S.b_SF = K.sb([128, 11, 3, 32], F32, "SF")
        ssm_precompute(l)
        if stopC in ('sin', 'pow', 'cmul', 'tr', 'pre'):
            K.barrier(); scC.__exit__(None, None, None); return
        S.Hb = [K.sb([128, 2, NB], F32, "H%d" % i) for i in range(2)]
        S.Hin, S.b_Hin = K.sb([128, 2, 2, NB], BF16, "Hin")
        S.xg_ = [K.sb([128, NB], BF16, "xg%d" % i) for i in range(4)]
        S.yg_ = [K.sb([128, NB], F32, "yg%d" % i) for i in range(2)]
        NBT = (NB + 511) // 512
        bw = min(512, NB)
        for gp in range(16):
            xg = []
            for gpar in range(2):
                g = gp * 2 + gpar
                xt, bx = S.xg_[cntC["x"] % 4]
                cntC["x"] += 1
                K.dma("sp", [(xt[s2 * 16:(s2 + 1) * 16, :], XL[s2, g * 16:(g + 1) * 16, :]) for s2 in range(8)],
                      [DB("XL", i) for i in range(NT)], [bx], bx)
                xg.append((xt, bx))
            for d in range(2):
                H0, bH0 = S.Hb[0]
                for nt in range(NBT):
                    for ri in range(2):
                        pt, bp = K.psum()
                        for gpar in range(2):
                            K.mm(pt[gpar * 64:(gpar + 1) * 64, 0:bw], S.PBT[:, gp, d, gpar, ri, :],
                                 xg[gpar][0][:, nt * 512:nt * 512 + bw], True, True, [S.b_PBT, xg[gpar][1]], [bp])
                        K.copy("act", H0[:, ri, nt * 512:nt * 512 + bw], pt[:, 0:bw], [bp], [bH0])
                cur = hs_scan(gp, d)
                Hf, bHf = S.Hb[cur]
                if d == 0:
                    K.memset("pool", S.Hin[:, 0, :, 0:1], 0.0, [S.b_Hin])
                    K.copy("act", S.Hin[:, 0].rearrange("p r (g n) -> p r g n", g=2)[:, :, :, 1:NBS],
                           Hf.rearrange("p r (g n) -> p r g n", g=2)[:, :, :, 0:NBS - 1], [bHf], [S.b_Hin])
                    K.ts("dve", S.Hin[:, 0, :, NBS:NBS + 1], Hf[:, :, NBS - 1:NBS], flag[:, 0:1], None, ALU.mult, None,
                         [bHf, b_flag], [S.b_Hin])
                else:
                    K.memset("pool", S.Hin[:, 1, :, NB - 1:NB], 0.0, [S.b_Hin])
                    K.copy("act", S.Hin[:, 1].rearrange("p r (g n) -> p r g n", g=2)[:, :, :, 0:NBS - 1],
                           Hf.rearrange("p r (g n) -> p r g n", g=2)[:, :, :, 1:NBS], [bHf], [S.b_Hin])
                    K.ts("dve", S.Hin[:, 1, :, NBS - 1:NBS], Hf[:, :, NBS:NBS + 1], flag[:, 0:1], None, ALU.mult, None,
                         [bHf, b_flag], [S.b_Hin])
            for gpar in range(2):
                g = gp * 2 + gpar
                sl = slice(gpar * 64, gpar * 64 + 64)
                yt, by = S.yg_[cntC["y"] % 2]
                cntC["y"] += 1
                for nt in range(NBT):
                    cs = slice(nt * 512, nt * 512 + bw)
                    pt, bp = K.psum()
                    K.mm(pt[:, 0:bw], S.M0[:, g, :], xg[gpar][0][:, cs], True, False, [S.b_M0, xg[gpar][1]], [bp])
                    for d in range(2):
                        for ri in range(2):
                            K.mm(pt[:, 0:bw], S.PCC[sl, gp, d, ri, :], S.Hin[sl, d, ri, cs], False,
                                 (d == 1 and ri == 1), [S.b_PCC, S.b_Hin], [bp])
                    K.copy("act", yt[:, cs], pt[:, 0:bw], [bp], [by])
                K.dma("sp", [(YL[t2, g * 16:(g + 1) * 16, :], yt[t2 * 16:(t2 + 1) * 16, :]) for t2 in range(8)],
                      [by], [DB("YL", g)], by)
        K.barrier()
        scC.__exit__(None, None, None)

    cntD = {"i": 0}

    def phaseD(l, last):
        scD = K.scope()
        scD.__enter__()
        alloc_shared()
        xt_ = S.xt_
        oT_ = [K.sb([128, 4, 512], BF16, "oT%d" % i) for i in range(1)]
        sg_ = [K.sb([128, 2, 8, 512], BF16, "sg%d" % i) for i in range(1)]
        yl_ = [K.sb([128, 4, 8, 64], F32, "yl%d" % i) for i in range(1)]
        ud_ = [K.sb([128, 4, 512], F32, "ud%d" % i) for i in range(1)]
        zt_, b_zt = K.sb([128, 4, 512], BF16, "zt")
        m1_, b_m1 = K.sb([128, 8, 512], F32, "m1")
        mg_, b_mg = K.sb([128, 8, 512], BF16, "mg")
        h2_, b_h2 = K.sb([128, 8, 512], BF16, "h2")
        aT_, b_aT = K.sb([128, 22, 512], BF16, "aT")
        ga_, b_ga = K.sb([128, 512], F32, "ga")
        gb_, b_gb = K.sb([128, 512], F32, "gb")
        gc_, b_gc = K.sb([128, 512], F32, "gc")
        for i in range(NT):
            s = i // (NT // 2)
            t0 = i * 512
            par = 0
            cntD["i"] += 1
            xt, bx = xt_[cnt["x"] % 2]
            cnt["x"] += 1
            src = x_in if l == 0 else xs
            rd = [] if l == 0 else [DB("xs", i)]
            K.dma("sp", [(xt, src[:, t0:t0 + 512].rearrange("(k p) t -> p k t", p=128))], rd, [bx], bx)
            oT, boT = oT_[par]
            K.dma("sp", [(oT, OT[:, t0:t0 + 512].rearrange("(c p) t -> p c t", p=128))], [DB("OT", i)], [boT], boT)
            sg, bsg = sg_[par]
            K.dma("sp", [(sg[:, 0], SGA[:, t0:t0 + 512].rearrange("(c p) t -> p c t", p=128)),
                         (sg[:, 1], SGS[:, t0:t0 + 512].rearrange("(c p) t -> p c t", p=128))],
                  [DB("SGA", i * 2), DB("SGA", i * 2 + 1), DB("SGS", i * 2), DB("SGS", i * 2 + 1)], [bsg], bsg)
            yl, byl = yl_[par]
            K.dma("sp", [(yl[:, c], YL[:, c * 128:(c + 1) * 128, i * 64:(i + 1) * 64].rearrange("t p b -> p t b"))
                         for c in range(4)], [DB("YL", g) for g in range(NG)], [byl], byl)
            ud, bud = ud_[par]
            K.dma("sp", [(ud, UT[:, t0:t0 + 512].rearrange("(c p) t -> p c t", p=128))], [DB("UT", i)], [bud], bud)
            for c in range(4):
                yv = yl[:, c].rearrange("p t b -> p b t")
                g3 = ga_.rearrange("p (b t) -> p b t", t=8)
                K.stt(g3, ud[:, c, :].rearrange("p (b t) -> p b t", t=8), dsk[:, l, c:c + 1], yv, ALU.mult, ALU.add,
                      [bud, byl, b_dsk], [b_ga])
                K.actf(gb_, ga_, AF.Square, [b_ga], [b_gb])
                K.ts("dve", gb_, gb_, 0.044715, 1.0, ALU.mult, ALU.add, [b_gb], [b_gb])
                K.tt("dve", gb_, gb_, ga_, ALU.mult, [b_gb, b_ga], [b_gb])
                K.actf(gc_, gb_, AF.Sigmoid, [b_gb], [b_gc], scale=1.5957691216057308)
                K.tt("dve", zt_[:, c, :], ga_, gc_, ALU.mult, [b_ga, b_gc], [b_zt])
            for blk_ in range(2):
                wt, bw = load_w(wb_attn[l], 0, 4, [(blk_ * 512, 512)])
                for c in range(4):
                    pt, bp = K.psum()
                    for kc in range(4):
                        K.mm(pt, wt[:, kc, c * 128:(c + 1) * 128], oT[:, kc, :], kc == 0, kc == 3, [bw, boT], [bp])
                    K.tt("dve", m1_[:, blk_ * 4 + c, :], pt, sg[:, 0, blk_ * 4 + c, :], ALU.mult, [bp, bsg], [b_m1])
            for blk_ in range(4):
                wt, bw = load_w(wb_glu[l], 0, 4, [(blk_ * 256, 256), (1024 + blk_ * 256, 256)])
                for c in range(2):
                    j = blk_ * 2 + c
                    pl, bpl = K.psum()
                    for kc in range(4):
                        K.mm(pl, wt[:, kc, c * 128:(c + 1) * 128], zt_[:, kc, :], kc == 0, kc == 3, [bw, b_zt], [bpl])
                    pg_, bpg = K.psum()
                    for kc in range(4):
                        K.mm(pg_, wt[:, kc, 256 + c * 128:256 + (c + 1) * 128], zt_[:, kc, :], kc == 0, kc == 3,
                             [bw, b_zt], [bpg])
                    K.actf(ga_, pg_, AF.Sigmoid, [bpg, b_bglu], [b_ga], bias=bglu[:, l, 8 + j:9 + j], scale=1.0)
                    K.stt(gb_, pl, bglu[:, l, j:j + 1], ga_, ALU.add, ALU.mult, [bpl, b_bglu, b_ga], [b_gb])
                    K.tt("dve", gb_, gb_, sg[:, 1, j, :], ALU.mult, [b_gb, bsg], [b_gb])
                    K.tt("dve", mg_[:, j, :], gb_, m1_[:, j, :], ALU.add, [b_gb, b_m1], [b_mg])
            for blk_ in range(2):
                wt, bw = load_w(wb_o[l], 0, 8, [(blk_ * 512, 512)])
                for c in range(4):
                    j = blk_ * 4 + c
                    pt, bp = K.psum()
                    for kc in range(8):
                        K.mm(pt, wt[:, kc, c * 128:(c + 1) * 128], mg_[:, kc, :], kc == 0, kc == 7, [bw, b_mg], [bp])
                    K.stt(xt[:, j, :], pt, GT1(l, j, s), xt[:, j, :], ALU.mult, ALU.add, [bp, b_modT, bx], [bx])
            norm_mod(xt, bx, lambda kc: A2[:, l, kc, s:s + 1], lambda kc: SH2(l, kc, s), h2_, b_h2)
            for blk_ in range(11):
                wt, bw = load_w(wb_ffi[l], 0, 8, [(blk_ * 256, 256), (DFF + blk_ * 256, 256)])
                for c in range(2):
                    j = blk_ * 2 + c
                    pgt, bpg = K.psum()
                    for kc in range(8):
                        K.mm(pgt, wt[:, kc, c * 128:(c + 1) * 128], h2_[:, kc, :], kc == 0, kc == 7, [bw, b_h2], [bpg])
                    pu, bpu = K.psum()
                    for kc in range(8):
                        K.mm(pu, wt[:, kc, 256 + c * 128:256 + (c + 1) * 128], h2_[:, kc, :], kc == 0, kc == 7,
                             [bw, b_h2], [bpu])
                    K.actf(ga_, pgt, AF.Silu, [bpg], [b_ga])
                    K.tt("dve", aT_[:, j, :], pu, ga_, ALU.mult, [bpu, b_ga], [b_aT])
            for half in range(2):
                wa, bwa = load_w(wb_ffo[l], 0, 11, [(half * 512, 512)])
                wb2, bwb2 = load_w(wb_ffo[l], 11, 11, [(half * 512, 512)])
                for c in range(4):
                    j = half * 4 + c
                    pt, bp = K.psum()
                    for kc in range(22):
                        w_, bw_ = (wa, bwa) if kc < 11 else (wb2, bwb2)
                        K.mm(pt, w_[:, kc % 11, c * 128:(c + 1) * 128], aT_[:, kc, :], kc == 0, kc == 21,
                             [bw_, b_aT], [bp])
                    K.stt(xt[:, j, :], pt, GT2(l, j, s), xt[:, j, :], ALU.mult, ALU.add, [bp, b_modT, bx], [bx])
            if not last:
                K.dma("sp", [(xs[:, t0:t0 + 512].rearrange("(k p) t -> p k t", p=128), xt)], [bx], [DB("xs", i)], bx)
            else:
                rstd_only(xt, bx)
                for kc in range(8):
                    K.stt(xt[:, kc, :], xt[:, kc, :], gfT[:, kc:kc + 1], S.rstd, ALU.mult, ALU.mult,
                          [bx, S.b_rstd, b_gfT], [bx])
                K.dma("sp", [(y_out[:, t0:t0 + 512].rearrange("(k p) t -> p k t", p=128), xt)], [bx], [DB("y", i)], bx)

        K.barrier()
        scD.__exit__(None, None, None)

    import os
    stop = os.environ.get("MK_STOP", "")
    for l in range(L):
        if l == 0:
            for cb in cvb:
                b_wb.w.update(cb.w)
        if stop == "pro":
            break
        phaseA(l)
        K.barrier()
        if stop == "A":
            break
        phaseB(l)
        K.barrier()
        if stop == "B":
            break
        phaseC(l)
        K.barrier()
        if stop == "C":
            break
        phaseD(l, l == L - 1)
        K.barrier()
    K.barrier()
    return nc


def _rope_tables(T_seq):
    inv = 1.0 / (10000.0 ** (np.arange(0, 64, 2, dtype=np.float32) / 64.0))
    ang = np.arange(T_seq, dtype=np.float32)[:, None] * inv[None, :].astype(np.float32)
    ang = np.concatenate([ang, ang], axis=-1).astype(np.float32)
    return np.cos(ang).astype(np.float32), np.sin(ang).astype(np.float32)


def _consts():
    perm = np.zeros((128, 128), np.float32)
    for m in range(2):
        for d in range(64):
            perm[m * 64 + (d + 32) % 64, m * 64 + d] = 1.0
    ident = np.eye(128, dtype=np.float32)
    s2 = np.arange(128) // 16
    maskf = (s2[None, :] >= s2[:, None]).astype(np.float32)
    maskb = (s2[None, :] <= s2[:, None]).astype(np.float32)
    return perm, ident, maskf, maskb


def _pl(a, L):
    rest = a.shape[4:]
    a = a.reshape((L, 2, 16, 2, 64) + rest)
    a = np.moveaxis(a, (3, 4, 0, 2, 1), (0, 1, 2, 3, 4))
    return np.ascontiguousarray(a.reshape((128, L, 32) + rest)).astype(np.float32)


def make_in_maps(inp, cfg, core_seqs):
    L, T = cfg.L, cfg.T
    perm, ident, maskf, maskb = _consts()

    def fm(v, nch):
        v = np.asarray(v, np.float32)
        lead = v.shape[:-1]
        v = v.reshape(lead + (nch, 128))
        v = np.moveaxis(v, -1, 0)
        return np.ascontiguousarray(v)

    shared = {
        "perm": perm, "ident": ident, "maskf": maskf, "maskb": maskb,
        "w_mod": np.ascontiguousarray(inp["w_mod"][:L], np.float32),
        "b_modT": fm(inp["b_mod"][:L], 48),
        "g1T": fm(inp["norm1_g"][:L], 8), "g2T": fm(inp["norm2_g"][:L], 8), "gfT": fm(inp["final_g"], 8),
        "w_in": np.ascontiguousarray(inp["w_in"][:L], np.float32),
        "lamT": np.ascontiguousarray(np.broadcast_to(
            np.stack([inp["lam_q1"][:L], inp["lam_k1"][:L], inp["lam_q2"][:L], inp["lam_k2"][:L]], axis=1)[None],
            (128, L, 4, 64)), np.float32),
        "subgT": np.ascontiguousarray(np.asarray(inp["subln_g"][:L], np.float32).T),
        "w_attn": np.ascontiguousarray(inp["w_attn_br"][:L], np.float32),
        "ssm_dT": fm(inp["ssm_d"][:L], 4),
        "w_glu": np.ascontiguousarray(inp["w_glu"][:L], np.float32),
        "b_gluT": fm(inp["b_glu"][:L], 16),
        "w_o": np.ascontiguousarray(inp["w_o"][:L], np.float32),
        "w_ffi": np.ascontiguousarray(inp["w_ffn_in"][:L], np.float32),
        "w_ffo": np.ascontiguousarray(inp["w_ffn_out"][:L], np.float32),
    }
    are = np.asarray(inp["ssm_a_re"][:L], np.float32)
    aim = np.asarray(inp["ssm_a_im"][:L], np.float32)
    ldt = np.broadcast_to(np.asarray(inp["ssm_log_dt"][:L], np.float32)[..., None], (L, 2, 32, 64))
    shared["a_reP"] = _pl(are, L)
    shared["a_imP"] = _pl(aim, L)
    shared["ldtP"] = _pl(np.ascontiguousarray(ldt), L)
    shared["b_reP"] = _pl(np.asarray(inp["ssm_b_re"][:L], np.float32), L)
    shared["b_imP"] = _pl(np.asarray(inp["ssm_b_im"][:L], np.float32), L)
    shared["c_reP"] = _pl(np.swapaxes(np.asarray(inp["ssm_c_re"][:L], np.float32), 3, 4), L)
    shared["c_imP"] = _pl(np.swapaxes(np.asarray(inp["ssm_c_im"][:L], np.float32), 3, 4), L)
    maps = []
    for (x, c, pos, split) in core_seqs:
        cos, sin = _rope_tables(int(pos.max()) + 1)
        cosT = np.ascontiguousarray(np.tile(cos[pos].T, (2, 1)))
        sgn = np.where(np.arange(64) < 32, -1.0, 1.0).astype(np.float32)
        sinT = np.ascontiguousarray(np.tile((sin[pos] * sgn[None, :]).T, (2, 1)))
        m = dict(shared)
        m["xT"] = np.ascontiguousarray(x.T)
        m["cT"] = np.ascontiguousarray(np.moveaxis(np.asarray(c, np.float32).reshape(2, 8, 128), (0, 1, 2), (2, 1, 0)))
        m["cosT"] = cosT.astype(np.float32)
        m["sinT"] = sinT.astype(np.float32)
        m["crossbias"] = np.full((128, 1), 0.0 if split else -30000.0, np.float32)
        m["flag"] = np.full((128, 1), 1.0 if split else 0.0, np.float32)
        maps.append(m)
    return maps


_NC_CACHE = {}


def run(inp, cfg, core_seqs):
    key = (cfg.T, cfg.L)
    if key not in _NC_CACHE:
        _NC_CACHE[key] = build(cfg)
    nc = _NC_CACHE[key]
    maps = make_in_maps(inp, cfg, core_seqs)
    res = run_bass_kernel_spmd(nc, maps, core_ids=list(range(len(maps))))
    return [np.asarray(r["yT"]).T for r in res.results]


def kernel(**inp):
    cfg = Cfg(8192, 4)
    xp = np.asarray(inp["x_prompt"], np.float32)
    xsm = np.asarray(inp["x_sample"], np.float32)
    cp = np.asarray(inp["c_prompt"], np.float32)
    csm = np.asarray(inp["c_sample"], np.float32)
    cores = []
    pos_p = np.concatenate([np.arange(4096), np.arange(4096)])
    for c in range(4):
        cores.append((np.concatenate([xp[2 * c], xp[2 * c + 1]], axis=0), np.stack([cp[2 * c], cp[2 * c + 1]]),
                      pos_p, False))
    for c in range(4):
        cores.append((xsm[c], np.stack([csm[c], csm[c]]), np.arange(8192), True))
    outs = run(inp, cfg, cores)
    yp = np.empty((8, 4096, D), np.float32)
    for c in range(4):
        yp[2 * c] = outs[c][:4096]
        yp[2 * c + 1] = outs[c][4096:]
    ys = np.stack([outs[4 + c] for c in range(4)], axis=0).astype(np.float32)
    return (yp, ys)
```

```python
import math
import contextlib
import numpy as np
import concourse.bass as bass
import concourse.mybir as mybir
from concourse.bass_utils import run_bass_kernel_spmd

F32 = mybir.dt.float32
BF16 = mybir.dt.bfloat16
I32 = mybir.dt.int32
ALU = mybir.AluOpType
AF = mybir.ActivationFunctionType

D = 1024
NH = 4
DFF = 2816
DSSM = 512
NG = 32
EPS = 1e-6
TWO_PI = 2.0 * math.pi


def lam_init_fn(layer):
    return 0.8 - 0.6 * math.exp(-0.3 * layer)


class Buf:
    __slots__ = ("name", "w", "r", "sem", "cnt", "excl")

    def __init__(self, name, excl=False):
        self.name = name
        self.excl = excl
        self.w = {}
        self.r = {}
        self.sem = None
        self.cnt = 0


class EngState:
    def __init__(self, eng, sem, self_sync):
        self.eng = eng
        self.sem = sem
        self.cnt = 0
        self.waited = {}
        self.self_sync = self_sync


def _merge(d, src):
    for k, (s, v) in src.items():
        if k not in d or d[k][1] < v:
            d[k] = (s, v)


class KB:
    def __init__(self, nc):
        self.nc = nc
        self.engs = {}
        for name, e in (("pe", nc.tensor), ("dve", nc.vector), ("act", nc.scalar),
                        ("pool", nc.gpsimd), ("sp", nc.sync)):
            self.engs[name] = EngState(e, nc.alloc_semaphore("e_" + name), name != "pe")
        self.dma_bufs = []
        self.nalloc = 0
        self.stacks = []
        self.scope_bufs = []
        self.free_sems = []
        self.retired = {}
        self.nsem = 0
        self.ps = []
        for i in range(8):
            t = nc.alloc_psum_tensor("psb%d" % i, [128, 512], F32)
            self.ps.append((t.ap(), Buf("ps%d" % i, excl=True)))
        self.ps_i = 0

    def sb(self, shape, dt, name=None):
        self.nalloc += 1
        nm = "%s_%d" % (name or "t", self.nalloc)
        if self.stacks:
            t = self.stacks[-1].enter_context(self.nc.sbuf_tensor(nm, list(shape), dt))
        else:
            t = self.nc.alloc_sbuf_tensor(nm, list(shape), dt)
        b = Buf(name or "t")
        if self.scope_bufs:
            self.scope_bufs[-1].append(b)
        return (t.ap() if hasattr(t, "ap") and callable(t.ap) else t[:]), b

    @contextlib.contextmanager
    def scope(self):
        st = contextlib.ExitStack()
        self.stacks.append(st)
        self.scope_bufs.append([])
        try:
            yield
        finally:
            self.stacks.pop()
            for b in self.scope_bufs.pop():
                if b.sem is not None:
                    self.free_sems.append((b.sem, b.cnt))
                    self.dma_bufs.remove(b)
                    self.retired[id(b.sem)] = (b.sem, b.cnt)
                    b.sem = None
            st.close()

    def dram(self, name, shape, dt):
        return self.nc.dram_tensor(name, list(shape), dt, kind="Internal").ap()

    def psum(self):
        p = self.ps.pop(0)
        self.ps.append(p)
        return p

    def psum_hold(self):
        return self.ps.pop(0)

    def psum_release(self, p):
        self.ps.append(p)

    def _deps(self, reads, writes):
        d = {}
        for b in reads:
            _merge(d, b.w)
            if b.excl:
                _merge(d, b.r)
        for b in writes:
            _merge(d, b.w)
            _merge(d, b.r)
        return d

    def _wait(self, E, deps):
        for k, (sem, val) in deps.items():
            if sem is E.sem and not E.self_sync:
                continue
            if E.waited.get(k, 0) < val:
                E.eng.wait_ge(sem, val)
                E.waited[k] = val

    def _record(self, tok, reads, writes):
        k = id(tok[0])
        for b in reads:
            if k not in b.r or b.r[k][1] < tok[1]:
                b.r[k] = tok
        for b in writes:
            b.w = {k: tok}
            b.r = {}

    def op(self, ename, fn, reads=(), writes=()):
        E = self.engs[ename]
        self._wait(E, self._deps(reads, writes))
        ins = fn(E.eng)
        E.cnt += 1
        ins.then_inc(E.sem, 1)
        self._record((E.sem, E.cnt), reads, writes)

    def dma(self, ename, pairs, reads, writes, sbuf):
        E = self.engs[ename]
        if sbuf.sem is None:
            if self.free_sems:
                sbuf.sem, sbuf.cnt = self.free_sems.pop()
                self.retired.pop(id(sbuf.sem), None)
            else:
                self.nsem += 1
                sbuf.sem = self.nc.alloc_semaphore("d_%d" % self.nsem)
            self.dma_bufs.append(sbuf)
        deps = self._deps(reads, writes)
        if sbuf.cnt > 0:
            _merge(deps, {id(sbuf.sem): (sbuf.sem, sbuf.cnt)})
        self._wait(E, deps)
        for (o, i) in pairs:
            E.eng.dma_start(out=o, in_=i).then_inc(sbuf.sem, 16)
            sbuf.cnt += 16
        self._record((sbuf.sem, sbuf.cnt), reads, writes)

    def barrier(self):
        toks = {}
        for E in self.engs.values():
            if E.cnt:
                toks[id(E.sem)] = (E.sem, E.cnt)
        for b in self.dma_bufs:
            if b.cnt:
                toks[id(b.sem)] = (b.sem, b.cnt)
        for k, tok in self.retired.items():
            toks.setdefault(k, tok)
        for E in self.engs.values():
            for k, (sem, val) in toks.items():
                if sem is E.sem:
                    continue
                if E.waited.get(k, 0) < val:
                    E.eng.wait_ge(sem, val)
                    E.waited[k] = val

    def mm(self, out, lhsT, rhs, start, stop, reads, writes):
        self.op("pe", lambda e: e.matmul(out, lhsT, rhs, start=start, stop=stop), reads, writes)

    def tt(self, eng, out, in0, in1, op, reads, writes):
        self.op(eng, lambda e: e.tensor_tensor(out=out, in0=in0, in1=in1, op=op), reads, writes)

    def ts(self, eng, out, in0, s1, s2, op0, op1, reads, writes):
        if op1 is None:
            self.op(eng, lambda e: e.tensor_scalar(out=out, in0=in0, scalar1=s1, scalar2=None, op0=op0), reads, writes)
        else:
            self.op(eng, lambda e: e.tensor_scalar(out=out, in0=in0, scalar1=s1, scalar2=s2, op0=op0, op1=op1), reads, writes)

    def stt(self, out, in0, scalar, in1, op0, op1, reads, writes):
        self.op("dve", lambda e: e.scalar_tensor_tensor(out=out, in0=in0, scalar=scalar, in1=in1, op0=op0, op1=op1), reads, writes)

    def actf(self, out, in_, func, reads, writes, bias=None, scale=None):
        kw = {}
        if bias is not None:
            kw["bias"] = bias
        if scale is not None:
            kw["scale"] = scale
        self.op("act", lambda e: e.activation(out=out, in_=in_, func=func, **kw), reads, writes)

    def copy(self, eng, out, in_, reads, writes):
        if eng == "act":
            self.op("act", lambda e: e.activation(out=out, in_=in_, func=AF.Identity), reads, writes)
        else:
            self.op(eng, lambda e: e.tensor_copy(out=out, in_=in_), reads, writes)

    def memset(self, eng, ap, val, writes):
        self.op(eng, lambda e: e.memset(ap, val), (), writes)


class Cfg:
    def __init__(self, T, L):
        self.T = T
        self.L = L
        self.NT = T // 512
        self.SEG = T // 2
        self.NB = T // 8
        self.NBS = self.NB // 2
        self.NKC = T // 128


def build(cfg):
    T, L, NT, NB, NBS = cfg.T, cfg.L, cfg.NT, cfg.NB, cfg.NBS
    nc = bass.Bass("TRN2", target_bir_lowering=False)
    K = KB(nc)

    def din(name, shape, dt=F32):
        return nc.dram_tensor(name, list(shape), dt, kind="ExternalInput").ap()

    x_in = din("xT", [D, T])
    y_out = nc.dram_tensor("yT", [D, T], F32, kind="ExternalOutput").ap()
    cT_in = din("cT", [128, 8, 2])
    cos_in = din("cosT", [128, T])
    sin_in = din("sinT", [128, T])
    perm_in = din("perm", [128, 128])
    ident_in = din("ident", [128, 128])
    maskf_in = din("maskf", [128, 128])
    maskb_in = din("maskb", [128, 128])
    cb_in = din("crossbias", [128, 1])
    flag_in = din("flag", [128, 1])
    w_mod = din("w_mod", [L, D, 6 * D])
    bmod_in = din("b_modT", [128, L, 48])
    g1_in = din("g1T", [128, L, 8])
    g2_in = din("g2T", [128, L, 8])
    gf_in = din("gfT", [128, 8])
    w_in = din("w_in", [L, D, 4096])
    lam_in = din("lamT", [128, L, 4, 64])
    subg_in = din("subgT", [128, L])
    w_attn = din("w_attn", [L, 512, D])
    are_in = din("a_reP", [128, L, 32])
    aim_in = din("a_imP", [128, L, 32])
    ldt_in = din("ldtP", [128, L, 32])
    bre_in = din("b_reP", [128, L, 32, 16])
    bim_in = din("b_imP", [128, L, 32, 16])
    cre_in = din("c_reP", [128, L, 32, 16])
    cim_in = din("c_imP", [128, L, 32, 16])
    dsk_in = din("ssm_dT", [128, L, 4])
    w_glu = din("w_glu", [L, 512, 2 * D])
    bglu_in = din("b_gluT", [128, L, 16])
    w_o = din("w_o", [L, D, D])
    w_ffi = din("w_ffi", [L, D, 2 * DFF])
    w_ffo = din("w_ffo", [L, DFF, D])

    wb_in = K.dram("wb_in", [L, D, 4096], BF16)
    wb_attn = K.dram("wb_attn", [L, 512, D], BF16)
    wb_glu = K.dram("wb_glu", [L, 512, 2 * D], BF16)
    wb_o = K.dram("wb_o", [L, D, D], BF16)
    wb_ffi = K.dram("wb_ffi", [L, D, 2 * DFF], BF16)
    wb_ffo = K.dram("wb_ffo", [L, DFF, D], BF16)
    xs = K.dram("xs", [D, T], F32)
    QT = K.dram("QT", [4, 128, T], BF16)
    KT = K.dram("KT", [4, 128, T], BF16)
    VS = K.dram("VS", [T // 128, 128, 512], BF16)
    SGA = K.dram("SGA", [D, T], BF16)
    SGS = K.dram("SGS", [D, T], BF16)
    UT = K.dram("UT", [512, T], F32)
    XL = K.dram("XL", [8, 512, NB], BF16)
    YL = K.dram("YL", [8, 512, NB], F32)
    OT = K.dram("OT", [512, T], BF16)
    dbuf = {}

    def DB(name, i=0):
        key = (name, i)
        if key not in dbuf:
            dbuf[key] = Buf("%s%d" % (name, i))
        return dbuf[key]

    ones32, b_ones32 = K.sb([128, 128], F32, "ones32")
    perm_b, b_perm = K.sb([128, 128], BF16, "perm")
    ident, b_ident = K.sb([128, 128], F32, "ident")
    maskf, b_maskf = K.sb([128, 128], F32, "maskf")
    maskb, b_maskb = K.sb([128, 128], F32, "maskb")
    crossb, b_crossb = K.sb([128, 1], F32, "crossb")
    zerob, b_zerob = K.sb([128, 1], F32, "zerob")
    flag, b_flag = K.sb([128, 1], F32, "flag")
    tmpc, b_tmpc = K.sb([128, 128], F32, "tmpc")
    K.memset("dve", ones32, 1.0, [b_ones32])
    K.memset("dve", zerob, 0.0, [b_zerob])
    K.dma("sp", [(tmpc, perm_in)], [], [b_tmpc], b_tmpc)
    K.copy("dve", perm_b, tmpc, [b_tmpc], [b_perm])
    K.dma("sp", [(ident, ident_in)], [], [b_ident], b_ident)
    K.dma("sp", [(maskf, maskf_in)], [], [b_maskf], b_maskf)
    K.dma("sp", [(maskb, maskb_in)], [], [b_maskb], b_maskb)
    K.dma("sp", [(crossb, cb_in)], [], [b_crossb], b_crossb)
    K.dma("sp", [(flag, flag_in)], [], [b_flag], b_flag)

    g1T, b_g1T = K.sb([128, L, 8], F32, "g1T")
    g2T, b_g2T = K.sb([128, L, 8], F32, "g2T")
    gfT, b_gfT = K.sb([128, 8], F32, "gfT")
    bglu, b_bglu = K.sb([128, L, 16], F32, "bglu")
    dsk, b_dsk = K.sb([128, L, 4], F32, "dsk")
    subg, b_subg = K.sb([128, L], F32, "subg")
    bmodT, b_bmodT = K.sb([128, L, 48], F32, "bmodT")
    K.dma("sp", [(g1T, g1_in)], [], [b_g1T], b_g1T)
    K.dma("sp", [(g2T, g2_in)], [], [b_g2T], b_g2T)
    K.dma("sp", [(gfT, gf_in)], [], [b_gfT], b_gfT)
    K.dma("sp", [(bglu, bglu_in)], [], [b_bglu], b_bglu)
    K.dma("sp", [(dsk, dsk_in)], [], [b_dsk], b_dsk)
    K.dma("sp", [(subg, subg_in)], [], [b_subg], b_subg)
    K.dma("sp", [(bmodT, bmod_in)], [], [b_bmodT], b_bmodT)

    cvb = [Buf("cv%d" % i) for i in range(4)]
    cvi = [0]
    b_wb = Buf("wb_all")

    def convert(src, dst, nelem):
        rows = nelem // 2048
        s2 = src.rearrange("(r c) -> r c", c=2048)
        d2 = dst.rearrange("(r c) -> r c", c=2048)
        r0 = 0
        while r0 < rows:
            r1 = min(rows, r0 + 1024)
            cb = cvb[cvi[0] % 4]
            cvi[0] += 1
            K.dma("pool", [(d2[r0:r1, :], s2[r0:r1, :])], [], [cb], cb)
            r0 = r1

    for l in range(L):
        convert(w_in[l].rearrange("a b -> (a b)"), wb_in[l].rearrange("a b -> (a b)"), D * 4096)
    for l in range(L):
        convert(w_attn[l].rearrange("a b -> (a b)"), wb_attn[l].rearrange("a b -> (a b)"), 512 * D)
        convert(w_glu[l].rearrange("a b -> (a b)"), wb_glu[l].rearrange("a b -> (a b)"), 512 * 2 * D)
        convert(w_o[l].rearrange("a b -> (a b)"), wb_o[l].rearrange("a b -> (a b)"), D * D)
        convert(w_ffi[l].rearrange("a b -> (a b)"), wb_ffi[l].rearrange("a b -> (a b)"), D * 2 * DFF)
        convert(w_ffo[l].rearrange("a b -> (a b)"), wb_ffo[l].rearrange("a b -> (a b)"), DFF * D)

    modT, b_modT = K.sb([128, L, 48, 2], F32, "modT")
    A1, b_A1 = K.sb([128, L, 8, 2], F32, "A1")
    A2, b_A2 = K.sb([128, L, 8, 2], F32, "A2")
    cT, b_cT = K.sb([128, 8, 2], F32, "cT")
    sc_, b_sc = K.sb([128, 8, 2], F32, "silu_c")
    K.dma("sp", [(cT, cT_in)], [], [b_cT], b_cT)
    K.actf(sc_, cT, AF.Silu, [b_cT], [b_sc])
    blk = 0
    mod_scope = K.scope()
    mod_scope.__enter__()
    wm = [K.sb([128, 8, 512], F32, "wm%d" % i) for i in range(2)]
    for l in range(L):
        for cb_ in range(12):
            wt, bw = wm[blk % 2]
            blk += 1
            K.dma("sp", [(wt, w_mod[l, :, cb_ * 512:(cb_ + 1) * 512].rearrange("(k p) c -> p k c", p=128))],
                  [], [bw], bw)
            pt, bp = K.psum()
            for c in range(4):
                for kc in range(8):
                    K.mm(pt[:, c * 2:c * 2 + 2], wt[:, kc, c * 128:(c + 1) * 128], sc_[:, kc, :],
                         kc == 0, kc == 7, [bw, b_sc], [bp])
            K.tt("dve", modT[:, l, cb_ * 4:(cb_ + 1) * 4, :],
                 pt[:, 0:8].rearrange("p (c s) -> p c s", s=2),
                 bmodT[:, l, cb_ * 4:(cb_ + 1) * 4].unsqueeze(2).to_broadcast([128, 4, 2]),
                 ALU.add, [bp, b_bmodT], [b_modT])
    K.barrier()
    mod_scope.__exit__(None, None, None)
    for l in range(L):
        K.stt(A1[:, l], modT[:, l, 8:16, :], 1.0, g1T[:, l, :].unsqueeze(2).to_broadcast([128, 8, 2]),
              ALU.add, ALU.mult, [b_modT, b_g1T], [b_A1])
        K.stt(A2[:, l], modT[:, l, 32:40, :], 1.0, g2T[:, l, :].unsqueeze(2).to_broadcast([128, 8, 2]),
              ALU.add, ALU.mult, [b_modT, b_g2T], [b_A2])

    def SH1(l, kc, s):
        return modT[:, l, 0 + kc, s:s + 1]

    def GT1(l, kc, s):
        return modT[:, l, 16 + kc, s:s + 1]

    def SH2(l, kc, s):
        return modT[:, l, 24 + kc, s:s + 1]

    def GT2(l, kc, s):
        return modT[:, l, 40 + kc, s:s + 1]

    lamT, b_lamT = K.sb([128, L, 4, 64], F32, "lamT")
    lamv, b_lamv = K.sb([128, L], F32, "lamv")
    neglam, b_neglam = K.sb([128, L], F32, "neglam")
    lsum, b_lsum = K.sb([128, L, 2], F32, "lsum")
    lprod, b_lprod = K.sb([128, L, 2, 64], F32, "lprod")
    K.dma("sp", [(lamT, lam_in)], [], [b_lamT], b_lamT)
    K.tt("dve", lprod[:, :, 0, :], lamT[:, :, 0, :], lamT[:, :, 1, :], ALU.mult, [b_lamT], [b_lprod])
    K.tt("dve", lprod[:, :, 1, :], lamT[:, :, 2, :], lamT[:, :, 3, :], ALU.mult, [b_lamT], [b_lprod])
    K.op("dve", lambda e: e.tensor_reduce(out=lsum, in_=lprod, op=ALU.add, axis=mybir.AxisListType.X),
         [b_lprod], [b_lsum])
    K.actf(lsum, lsum, AF.Exp, [b_lsum], [b_lsum])
    K.tt("dve", lamv, lsum[:, :, 0], lsum[:, :, 1], ALU.subtract, [b_lsum], [b_lamv])
    for l in range(L):
        K.ts("dve", lamv[:, l:l + 1], lamv[:, l:l + 1], float(lam_init_fn(l)), None, ALU.add, None, [b_lamv], [b_lamv])
    K.ts("dve", neglam, lamv, -1.0, None, ALU.mult, None, [b_lamv], [b_neglam])
    subgs, b_subgs = K.sb([128, L], F32, "subgs")
    for l in range(L):
        K.ts("dve", subgs[:, l:l + 1], subg[:, l:l + 1], float(1.0 - lam_init_fn(l)), None, ALU.mult, None,
             [b_subg], [b_subgs])

    K.barrier()

    class NS:
        pass
    S = NS()

    def alloc_shared():
        S.xt_ = [K.sb([128, 8, 512], F32, "xt%d" % i) for i in range(2)]
        S.hT, S.b_hT = K.sb([128, 8, 512], BF16, "hT")
        S.sq, S.b_sq = K.sb([128, 8, 512], F32, "sq")
        S.rstd, S.b_rstd = K.sb([128, 512], F32, "rstd")
        S.tmpn, S.b_tmpn = K.sb([128, 512], F32, "tmpn")
        S.wsl = [K.sb([128, 11, 512], BF16, "w%d" % i) for i in range(3)]
    wsl_i = [0]

    def wslot():
        s_ = S.wsl[wsl_i[0] % 3]
        wsl_i[0] += 1
        return s_

    def rstd_only(xt, bx):
        K.actf(S.sq, xt, AF.Square, [bx], [S.b_sq])
        pt, bp = K.psum()
        for kc in range(8):
            K.mm(pt, ones32, S.sq[:, kc, :], kc == 0, kc == 7, [b_ones32, S.b_sq], [bp])
        K.ts("dve", S.tmpn, pt, 1.0 / D, EPS, ALU.mult, ALU.add, [bp], [S.b_tmpn])
        K.actf(S.tmpn, S.tmpn, AF.Sqrt, [S.b_tmpn], [S.b_tmpn])
        K.op("dve", lambda e: e.reciprocal(out=S.rstd, in_=S.tmpn), [S.b_tmpn], [S.b_rstd])

    def norm_mod(xt, bx, Acol, Bcol, out_bf, b_out):
        rstd_only(xt, bx)
        for kc in range(8):
            K.stt(S.sq[:, kc, :], xt[:, kc, :], Acol(kc), S.rstd, ALU.mult, ALU.mult,
                  [bx, S.b_rstd, b_A1, b_A2], [S.b_sq])
            K.actf(out_bf[:, kc, :], S.sq[:, kc, :], AF.Identity, [S.b_sq, b_modT], [b_out], bias=Bcol(kc), scale=1.0)

    def load_w(src2d, k0, nk, col_runs):
        wt, bw = wslot()
        pairs = []
        off = 0
        for (c0, n) in col_runs:
            pairs.append((wt[:, 0:nk, off:off + n],
                          src2d[k0 * 128:(k0 + nk) * 128, c0:c0 + n].rearrange("(k p) c -> p k c", p=128)))
            off += n
        K.dma("sp", pairs, [b_wb], [bw], bw)
        return wt, bw

    cnt = {"qo": 0, "vo": 0, "uo": 0, "go": 0, "x": 0, "cs": 0}

    def phaseA(l):
        import os
        stopA = os.environ.get('MK_STOPA', '')
        scA = K.scope()
        scA.__enter__()
        alloc_shared()
        xt_ = S.xt_
        hT, b_hT = S.hT, S.b_hT
        csl = [K.sb([128, 2, 512], F32, "cs%d" % i) for i in range(2)]
        qb_, b_qb = K.sb([128, 512], BF16, "qb")
        qc_, b_qc = K.sb([128, 512], F32, "qc")
        qs_, b_qs = K.sb([128, 512], F32, "qs")
        qo_ = [K.sb([128, 4, 512], BF16, "qo%d" % i) for i in range(2)]
        vo_ = [K.sb([128, 4, 512], BF16, "vo%d" % i) for i in range(2)]
        uo_ = [K.sb([128, 4, 512], F32, "uo%d" % i) for i in range(2)]
        ul_ = [K.sb([128, 4, 8, 64], BF16, "ul%d" % i) for i in range(2)]
        go_ = [K.sb([128, 4, 512], BF16, "go%d" % i) for i in range(2)]
        wsrc = wb_in[l]
        for i in range(NT):
            s = i // (NT // 2)
            t0 = i * 512
            xt, bx = xt_[cnt["x"] % 2]
            cnt["x"] += 1
            src = x_in if l == 0 else xs
            rd = [] if l == 0 else [DB("xs", i)]
            K.dma("sp", [(xt, src[:, t0:t0 + 512].rearrange("(k p) t -> p k t", p=128))], rd, [bx], bx)
            cs, bcs = csl[cnt["cs"] % 2]
            cnt["cs"] += 1
            K.dma("sp", [(cs[:, 0, :], cos_in[:, t0:t0 + 512]), (cs[:, 1, :], sin_in[:, t0:t0 + 512])], [], [bcs], bcs)
            norm_mod(xt, bx, lambda kc: A1[:, l, kc, s:s + 1], lambda kc: SH1(l, kc, s), hT, b_hT)
            if stopA == 'n':
                continue
            for qk in range(2):
                wt, bw = load_w(wsrc, 0, 8, [(qk * 512, 512)])
                qo, bqo = qo_[cnt["qo"] % 2]
                cnt["qo"] += 1
                Y = int(os.environ.get("MK_Y", "9"))
                for c in range(4):
                    if Y < 1:
                        continue
                    pt, bp = K.psum()
                    for kc in range(8):
                        K.mm(pt, wt[:, kc, c * 128:(c + 1) * 128], hT[:, kc, :], kc == 0, kc == 7, [bw, b_hT], [bp])
                    if Y < 2:
                        continue
                    K.copy("act", qb_, pt, [bp], [b_qb])
                    if Y < 3:
                        continue
                    K.tt("dve", qc_, pt, cs[:, 0, :], ALU.mult, [bp, bcs] + ([b_qb] if os.environ.get("MK_Z") == "1" else []), [b_qc])
                    if Y < 4:
                        continue
                    p2, bp2 = K.psum()
                    K.mm(p2, perm_b, qb_, True, True, [b_perm, b_qb], [bp2])
                    if Y < 5:
                        continue
                    K.tt("dve", qs_, p2, cs[:, 1, :], ALU.mult, [bp2, bcs], [b_qs])
                    K.tt(os.environ.get("MK_QE", "pool"), qo[:, c, :], qc_, qs_, ALU.add, [b_qc, b_qs], [bqo])
                dst = QT if qk == 0 else KT
                if os.environ.get("MK_X") != "1":
                    K.dma("sp", [(dst[:, :, t0:t0 + 512].rearrange("h p t -> p h t"), qo)], [bqo],
                          [DB("QT" if qk == 0 else "KT", i)], bqo)
            if stopA == 'qk':
                continue
            wt, bw = load_w(wsrc, 0, 8, [(1024, 512)])
            vo, bvo = vo_[cnt["vo"] % 2]
            cnt["vo"] += 1
            for tc in range(4):
                pt, bp = K.psum()
                for kc in range(8):
                    K.mm(pt, hT[:, kc, tc * 128:(tc + 1) * 128], wt[:, kc, 0:512], kc == 0, kc == 7, [bw, b_hT], [bp])
                K.copy("act", vo[:, tc, :], pt, [bp], [bvo])
            K.dma("sp", [(VS[i * 4:(i + 1) * 4].rearrange("c p e -> p c e"), vo)], [bvo], [DB("VS", i)], bvo)
            if stopA == 'v':
                continue
            wt, bw = load_w(wsrc, 0, 8, [(1536, 512)])
            uo, buo = uo_[cnt["uo"] % 2]
            ul, bul = ul_[cnt["uo"] % 2]
            cnt["uo"] += 1
            for c in range(4):
                pt, bp = K.psum()
                for kc in range(8):
                    K.mm(pt, wt[:, kc, c * 128:(c + 1) * 128], hT[:, kc, :], kc == 0, kc == 7, [bw, b_hT], [bp])
                K.copy("act", uo[:, c, :], pt, [bp], [buo])
                K.copy("dve", ul[:, c], pt.rearrange("p (b s) -> p s b", s=8), [bp], [bul])
            K.dma("sp", [(UT[:, t0:t0 + 512].rearrange("(c p) t -> p c t", p=128), uo)], [buo], [DB("UT", i)], buo)
            K.dma("sp", [(XL[:, c * 128:(c + 1) * 128, i * 64:(i + 1) * 64].rearrange("s p b -> p s b"), ul[:, c])
                         for c in range(4)], [bul], [DB("XL", i)], bul)
            if stopA == 'u':
                continue
            for gb in range(4):
                wt, bw = load_w(wsrc, 0, 8, [(2048 + gb * 512, 512)])
                go, bgo = go_[cnt["go"] % 2]
                cnt["go"] += 1
                for c in range(4):
                    pt, bp = K.psum()
                    for kc in range(8):
                        K.mm(pt, wt[:, kc, c * 128:(c + 1) * 128], hT[:, kc, :], kc == 0, kc == 7, [bw, b_hT], [bp])
                    K.actf(go[:, c, :], pt, AF.Sigmoid, [bp], [bgo])
                dst = SGA if gb < 2 else SGS
                r0 = (gb % 2) * 512
                K.dma("sp", [(dst[r0:r0 + 512, t0:t0 + 512].rearrange("(c p) t -> p c t", p=128), go)], [bgo],
                      [DB("SGA" if gb < 2 else "SGS", i * 2 + gb % 2)], bgo)

        K.barrier()
        scA.__exit__(None, None, None)

    cntB = {"q": 0, "p": 0, "ob": 0}
    scale = 64 ** -0.5

    def phaseB(l):
        scB = K.scope()
        scB.__enter__()
        kz0, b_kz0 = K.sb([128, T], BF16, "kz0")
        kz1, b_kz1 = K.sb([128, T], BF16, "kz1")
        vh_, b_vh = K.sb([128, T // 128, 128], BF16, "vh")
        qt_ = [K.sb([128, 512], BF16, "qt%d" % i) for i in range(2)]
        pT_ = [K.sb([128, 512], BF16, "pT%d" % i) for i in range(4)]
        rr_, b_rr = K.sb([128, 512], F32, "rr")
        t1_, b_t1 = K.sb([128, 512], F32, "t1")
        t2_, b_t2 = K.sb([128, 512], F32, "t2")
        od_, b_od = K.sb([128, 512], F32, "od")
        o2_, b_o2 = K.sb([128, 512], F32, "o2")
        ob_ = [K.sb([128, 512], BF16, "ob%d" % i) for i in range(2)]
        acs_, b_acs = K.sb([128, 512], F32, "acs")
        accP, b_accP = K.sb([128, 512], F32, "accP")
        NKC = T // 128
        for h in range(NH):
            if h == 0:
                K.memset("pool", kz0[64:128, :], 0.0, [b_kz0])
                K.memset("pool", kz1[0:64, :], 0.0, [b_kz1])
            K.dma("sp", [(kz0[0:64, :], KT[h, 0:64, :])], [DB("KT", i) for i in range(NT)], [b_kz0], b_kz0)
            K.dma("sp", [(kz1[64:128, :], KT[h, 64:128, :])], [DB("KT", i) for i in range(NT)], [b_kz1], b_kz1)
            K.dma("sp", [(vh_, VS[:, :, h * 128:(h + 1) * 128].rearrange("c p e -> p c e"))],
                  [DB("VS", i) for i in range(NT)], [b_vh], b_vh)
            for j in range(NT):
                sj = j // (NT // 2)
                qt, bq = qt_[cntB["q"] % 2]
                cntB["q"] += 1
                K.dma("sp", [(qt, QT[h, :, j * 512:(j + 1) * 512])], [DB("QT", j)], [bq], bq)
                for m in range(2):
                    hpo = K.psum_hold()
                    hpa = K.psum_hold()
                    po, bpo = hpo
                    pa, bpa = hpa

                    def emit_s(c):
                        ps_, bps = K.psum()
                        kz, bkz = (kz0, b_kz0) if m == 0 else (kz1, b_kz1)
                        K.mm(ps_, kz[:, c * 128:(c + 1) * 128], qt, True, True, [bkz, bq], [bps])
                        return ps_, bps
                    nxt = emit_s(0)
                    for c in range(NKC):
                        sc = (c * 128) // cfg.SEG
                        ps_, bps = nxt
                        if c + 1 < NKC:
                            nxt = emit_s(c + 1)
                        pT, bpT = pT_[cntB["p"] % 4]
                        cntB["p"] += 1
                        K.actf(pT, ps_, AF.Exp, [bps, b_crossb, b_zerob], [bpT],
                               bias=(zerob if sc == sj else crossb), scale=scale)
                        K.mm(po, vh_[:, c, :], pT, c == 0, c == NKC - 1, [b_vh, bpT], [bpo])
                        if c % 5 in (1, 3):
                            if c == 1:
                                K.copy("pool", accP, pT, [bpT], [b_accP])
                            else:
                                K.tt("pool", accP, accP, pT, ALU.add, [bpT, b_accP], [b_accP])
                        elif c == 0:
                            K.copy("dve", pa, pT, [bpT], [bpa])
                        else:
                            K.tt("dve", pa, pa, pT, ALU.add, [bpT, bpa], [bpa])
                    K.tt("dve", acs_, pa, accP, ALU.add, [bpa, b_accP], [b_acs])
                    pr, bpr = K.psum()
                    K.mm(pr, ones32, acs_, True, True, [b_ones32, b_acs], [bpr])
                    K.op("dve", lambda e: e.reciprocal(out=rr_, in_=pr), [bpr], [b_rr])
                    if m == 0:
                        K.tt("dve", t1_, po, rr_, ALU.mult, [bpo, b_rr], [b_t1])
                    else:
                        K.tt("dve", t2_, po, rr_, ALU.mult, [bpo, b_rr], [b_t2])
                    K.psum_release(hpo)
                    K.psum_release(hpa)
                K.stt(od_, t2_, neglam[:, l:l + 1], t1_, ALU.mult, ALU.add, [b_t1, b_t2, b_neglam], [b_od])
                K.actf(o2_, od_, AF.Square, [b_od], [b_o2])
                pq, bpq = K.psum()
                K.mm(pq, ones32, o2_, True, True, [b_ones32, b_o2], [bpq])
                K.ts("dve", rr_, pq, 1.0 / 128, EPS, ALU.mult, ALU.add, [bpq], [b_rr])
                K.actf(rr_, rr_, AF.Sqrt, [b_rr], [b_rr])
                K.op("dve", lambda e: e.reciprocal(out=t1_, in_=rr_), [b_rr], [b_t1])
                ob, bob = ob_[cntB["ob"] % 2]
                cntB["ob"] += 1
                K.stt(ob, od_, subgs[:, l:l + 1], t1_, ALU.mult, ALU.mult, [b_od, b_t1, b_subgs], [bob])
                K.dma("sp", [(OT[h * 128:(h + 1) * 128, j * 512:(j + 1) * 512], ob)], [bob], [DB("OT", j)], bob)

        K.barrier()
        scB.__exit__(None, None, None)

    PG = [128, 32]
    sA = {}

    def pg(name, shape=None, dt=F32):
        if name not in sA:
            sA[name] = K.sb(shape or PG, dt, name)
        return sA[name]

    NLEV = int(math.log2(NB))
    cntC = {"x": 0, "y": 0}

    def ssm_precompute(l):
        sA.clear()
        scP = K.scope()
        scP.__enter__()
        are, b1 = pg("are"); aim, b2 = pg("aim"); ldt, b3 = pg("ldt")
        K.dma("sp", [(are, are_in[:, l, :])], [], [b1], b1)
        K.dma("sp", [(aim, aim_in[:, l, :])], [], [b2], b2)
        K.dma("sp", [(ldt, ldt_in[:, l, :])], [], [b3], b3)
        Br, bBr = pg("Br", [128, 32, 16]); Bi, bBi = pg("Bi", [128, 32, 16])
        Cr, bCr = pg("Cr", [128, 32, 16]); Ci, bCi = pg("Ci", [128, 32, 16])
        K.dma("sp", [(Br, bre_in[:, l])], [], [bBr], bBr)
        K.dma("sp", [(Bi, bim_in[:, l])], [], [bBi], bBi)
        K.dma("sp", [(Cr, cre_in[:, l])], [], [bCr], bCr)
        K.dma("sp", [(Ci, cim_in[:, l])], [], [bCi], bCi)
        dt_, bdt = pg("dt"); xr, bxr = pg("xr"); th, bth = pg("th"); mag, bmag = pg("mag")
        K.actf(dt_, ldt, AF.Exp, [b3], [bdt])
        K.tt("dve", xr, dt_, are, ALU.mult, [bdt, b1], [bxr])
        K.tt("dve", th, dt_, aim, ALU.mult, [bdt, b2], [bth])
        K.actf(mag, xr, AF.Exp, [bxr], [bmag])
        yv, byv = pg("yv"); ki, bki = pg("ki", PG, I32); kf, bkf = pg("kf"); mk, bmk = pg("mk")
        sn, bsn = pg("sn"); cs_, bcs_ = pg("cs")
        for (dst, bdst, off) in ((sn, bsn, 1.5), (cs_, bcs_, 1.75)):
            K.ts("dve", yv, th, 1.0 / TWO_PI, off, ALU.mult, ALU.add, [bth], [byv])
            K.copy("dve", ki, yv, [byv], [bki])
            K.copy("dve", kf, ki, [bki], [bkf])
            K.tt("dve", mk, kf, yv, ALU.is_gt, [bkf, byv], [bmk])
            K.tt("dve", kf, kf, mk, ALU.subtract, [bkf, bmk], [bkf])
            K.tt("dve", yv, yv, kf, ALU.subtract, [byv, bkf], [byv])
            K.ts("dve", yv, yv, -0.5, TWO_PI, ALU.add, ALU.mult, [byv], [byv])
            K.ts("dve", yv, yv, math.pi, -math.pi, ALU.min, ALU.max, [byv], [byv])
            K.actf(dst, yv, AF.Sin, [byv], [bdst])
        Ar, bAr = pg("Ar"); Ai, bAi = pg("Ai")
        stopC = os.environ.get('MK_STOPC', '')
        if stopC == 'sin':
            K.barrier(); scP.__exit__(None, None, None); return
        K.tt("dve", Ar, mag, cs_, ALU.mult, [bmag, bcs_], [bAr])
        K.tt("dve", Ai, mag, sn, ALU.mult, [bmag, bsn], [bAi])
        Par, bPar = pg("Par", [128, 32, 9]); Pai, bPai = pg("Pai", [128, 32, 9])
        Pdr, bPdr = pg("Pdr", [128, 32, 9]); Pdi, bPdi = pg("Pdi", [128, 32, 9])
        Qar, bQar = pg("Qar", [128, 32, 9]); Qai, bQai = pg("Qai", [128, 32, 9])
        Qdr, bQdr = pg("Qdr", [128, 32, 9]); Qdi, bQdi = pg("Qdi", [128, 32, 9])
        ta, bta = pg("ta"); tb, btb = pg("tb")
        K.memset("dve", Par[:, :, 0], 1.0, [bPar])
        K.memset("dve", Pai[:, :, 0], 0.0, [bPai])
        for n in range(1, 9):
            K.tt("dve", ta, Par[:, :, n - 1], Ar, ALU.mult, [bPar, bAr], [bta])
            K.tt("dve", tb, Pai[:, :, n - 1], Ai, ALU.mult, [bPai, bAi], [btb])
            K.tt("dve", Par[:, :, n], ta, tb, ALU.subtract, [bta, btb], [bPar])
            K.tt("dve", ta, Par[:, :, n - 1], Ai, ALU.mult, [bPar, bAi], [bta])
            K.tt("dve", tb, Pai[:, :, n - 1], Ar, ALU.mult, [bPai, bAr], [btb])
            K.tt("dve", Pai[:, :, n], ta, tb, ALU.add, [bta, btb], [bPai])
        e2, be2 = pg("e2")
        for n in range(9):
            K.actf(e2, xr, AF.Exp, [bxr], [be2], scale=-2.0 * n)
            K.tt("dve", Qar[:, :, n], Par[:, :, n], e2, ALU.mult, [bPar, be2], [bQar])
            K.stt(Qai[:, :, n], Pai[:, :, n], -1.0, e2, ALU.mult, ALU.mult, [bPai, be2], [bQai])
        for n in range(9):
            K.copy("pool", Pdr[:, :, 8 - n], Par[:, :, n], [bPar], [bPdr])
            K.copy("pool", Pdi[:, :, 8 - n], Pai[:, :, n], [bPai], [bPdi])
            K.copy("pool", Qdr[:, :, 8 - n], Qar[:, :, n], [bQar], [bQdr])
            K.copy("pool", Qdi[:, :, 8 - n], Qai[:, :, n], [bQai], [bQdi])
        K.copy("dve", S.SS[:, 0, 0, :], Par[:, :, 8], [bPar], [S.b_SS])
        K.copy("dve", S.SS[:, 0, 1, :], Pai[:, :, 8], [bPai], [S.b_SS])
        for k in range(NLEV):
            if k > 0:
                K.tt("dve", ta, S.SS[:, k - 1, 0, :], S.SS[:, k - 1, 0, :], ALU.mult, [S.b_SS], [bta])
                K.tt("dve", tb, S.SS[:, k - 1, 1, :], S.SS[:, k - 1, 1, :], ALU.mult, [S.b_SS], [btb])
                K.tt("dve", S.SS[:, k, 0, :], ta, tb, ALU.subtract, [bta, btb], [S.b_SS])
                K.stt(S.SS[:, k, 1, :], S.SS[:, k - 1, 0, :], 2.0, S.SS[:, k - 1, 1, :], ALU.mult, ALU.mult, [S.b_SS], [S.b_SS])
            K.ts("dve", S.SS[:, k, 2, :], S.SS[:, k, 1, :], -1.0, None, ALU.mult, None, [S.b_SS], [S.b_SS])
        K.ts("dve", S.SF, S.SS, flag[:, 0:1], None, ALU.mult, None, [S.b_SS, b_flag], [S.b_SF])
        if stopC == 'pow':
            K.barrier(); scP.__exit__(None, None, None); return
        nr, bnr = pg("nr"); den, bden = pg("den"); fr, bfr = pg("fr"); fi, bfi = pg("fi")
        K.ts("dve", nr, Ar, -1.0, None, ALU.add, None, [bAr], [bnr])
        K.tt("dve", ta, are, are, ALU.mult, [b1], [bta])
        K.tt("dve", tb, aim, aim, ALU.mult, [b2], [btb])
        K.tt("dve", den, ta, tb, ALU.add, [bta, btb], [bden])
        K.op("dve", lambda e: e.reciprocal(out=den, in_=den), [bden], [bden])
        K.tt("dve", ta, nr, are, ALU.mult, [bnr, b1], [bta])
        K.tt("dve", tb, Ai, aim, ALU.mult, [bAi, b2], [btb])
        K.tt("dve", ta, ta, tb, ALU.add, [bta, btb], [bta])
        K.tt("dve", fr, ta, den, ALU.mult, [bta, bden], [bfr])
        K.tt("dve", ta, Ai, are, ALU.mult, [bAi, b1], [bta])
        K.tt("dve", tb, nr, aim, ALU.mult, [bnr, b2], [btb])
        K.tt("dve", ta, ta, tb, ALU.subtract, [bta, btb], [bta])
        K.tt("dve", fi, ta, den, ALU.mult, [bta, bden], [bfi])
        Bbr, bBbr = pg("Bbr", [128, 32, 16]); Bbi, bBbi = pg("Bbi", [128, 32, 16])
        t16a, bt16a = pg("t16a", [128, 32, 16]); t16b, bt16b = pg("t16b", [128, 32, 16])
        frb = fr.unsqueeze(2).to_broadcast([128, 32, 16])
        fib = fi.unsqueeze(2).to_broadcast([128, 32, 16])
        K.tt("dve", t16a, Br, frb, ALU.mult, [bBr, bfr], [bt16a])
        K.tt("dve", t16b, Bi, fib, ALU.mult, [bBi, bfi], [bt16b])
        K.tt("dve", Bbr, t16a, t16b, ALU.subtract, [bt16a, bt16b], [bBbr])
        K.tt("dve", t16a, Bi, frb, ALU.mult, [bBi, bfr], [bt16a])
        K.tt("dve", t16b, Br, fib, ALU.mult, [bBr, bfi], [bt16b])
        K.tt("dve", Bbi, t16a, t16b, ALU.add, [bt16a, bt16b], [bBbi])
        def wtab(name, src0, bs0, o0, src1, bs1, o1):
            w, bw = pg(name, [128, 16, 2, 8])
            v0 = src0.rearrange("p (a d) n -> p a d n", d=2)
            v1 = src1.rearrange("p (a d) n -> p a d n", d=2)
            K.copy("pool", w[:, :, 0, :], v0[:, :, 0, o0:o0 + 8], [bs0], [bw])
            K.copy("pool", w[:, :, 1, :], v1[:, :, 1, o1:o1 + 8], [bs1], [bw])
            return w.rearrange("p a d n -> p (a d) n"), bw
        WBr, bWBr = wtab("WBr", Pdr, bPdr, 1, Par, bPar, 0)
        WBi, bWBi = wtab("WBi", Pdi, bPdi, 1, Pai, bPai, 0)
        WCr, bWCr = wtab("WCr", Qdr, bQdr, 1, Qar, bQar, 0)
        WCi, bWCi = wtab("WCi", Qdi, bQdi, 1, Qai, bQai, 0)
        WKr, bWKr = wtab("WKr", Par, bPar, 1, Pdr, bPdr, 0)
        WKi, bWKi = wtab("WKi", Pai, bPai, 1, Pdi, bPdi, 0)
        big = [128, 32, 8, 16]
        PBr, bPBr = pg("PBr", big); PBi, bPBi = pg("PBi", big)
        PCr, bPCr = pg("PCr", big); PCi, bPCi = pg("PCi", big)
        tg1, btg1 = pg("tg1", big); tg2, btg2 = pg("tg2", big)

        def cmul(outr, boutr, outi, bouti, Wr, bWr, Wi, bWi, Xr, bXr, Xi, bXi, neg_i=False):
            wr = Wr.unsqueeze(3).to_broadcast(big)
            wi = Wi.unsqueeze(3).to_broadcast(big)
            xr_ = Xr.unsqueeze(2).to_broadcast(big)
            xi_ = Xi.unsqueeze(2).to_broadcast(big)
            K.tt("dve", tg1, wr, xr_, ALU.mult, [bWr, bXr], [btg1])
            K.tt("dve", tg2, wi, xi_, ALU.mult, [bWi, bXi], [btg2])
            K.tt("dve", outr, tg1, tg2, ALU.subtract, [btg1, btg2], [boutr])
            K.tt("dve", tg1, wr, xi_, ALU.mult, [bWr, bXi], [btg1])
            K.tt("dve", tg2, wi, xr_, ALU.mult, [bWi, bXr], [btg2])
            if neg_i:
                K.stt(outi, tg1, -1.0, tg2, ALU.mult, ALU.subtract, [btg1, btg2], [bouti])
            else:
                K.tt("dve", outi, tg1, tg2, ALU.add, [btg1, btg2], [bouti])

        cmul(PBr, bPBr, PBi, bPBi, WBr, bWBr, WBi, bWBi, Bbr, bBbr, Bbi, bBbi)
        cmul(PCr, bPCr, PCi, bPCi, WCr, bWCr, WCi, bWCi, Cr, bCr, Ci, bCi, neg_i=True)
        if stopC == 'cmul':
            K.barrier(); scP.__exit__(None, None, None); return
        PBr3 = PBr.rearrange("p g n c -> p g (n c)")
        PBi3 = PBi.rearrange("p g n c -> p g (n c)")
        PCr3 = PCr.rearrange("p g n c -> p g (n c)")
        PCi3 = PCi.rearrange("p g n c -> p g (n c)")
        for gp in range(16):
            for d in range(2):
                pt, bp = K.psum()
                idx = 0
                for gpar in range(2):
                    for (src, bsrc) in ((PBr3, bPBr), (PBi3, bPBi)):
                        sl_ = slice(gpar * 64, gpar * 64 + 64)
                        K.mm(pt[:, idx * 64:(idx + 1) * 64], src[:, gp * 2 + d, :], ident[:, sl_], True, True,
                             [bsrc, b_ident], [bp])
                        idx += 1
                K.copy("act", S.PBT[:, gp, d].rearrange("p a b c -> p (a b c)"), pt[:, 0:256], [bp], [S.b_PBT])
        mt, bmt = pg("mt", [128, 128])
        tb1 = tg1.rearrange("p g n c -> p (g n c)").bitcast(BF16).rearrange("p (h g f) -> p h g f", h=2, g=32)
        tb2 = tg2.rearrange("p g n c -> p (g n c)").bitcast(BF16).rearrange("p (h g f) -> p h g f", h=2, g=32)
        PBrb, bPBrb, PBib, bPBib = tb1[:, 0], btg1, tb1[:, 1], btg1
        PCrb, bPCrb, PCib, bPCib = tb2[:, 0], btg2, tb2[:, 1], btg2
        K.copy("act", PBrb, PBr3, [bPBr], [bPBrb])
        K.copy("act", PBib, PBi3, [bPBi], [bPBib])
        K.copy("act", PCrb, PCr3, [bPCr], [bPCrb])
        K.copy("act", PCib, PCi3, [bPCi], [bPCib])
        for gp in range(16):
            for gpar in range(2):
                g = gp * 2 + gpar
                sl = slice(gpar * 64, gpar * 64 + 64)
                pt, bp = K.psum()
                for d in range(2):
                    o = pt[:, d * 128:(d + 1) * 128]
                    K.mm(o, PBrb[sl, gp * 2 + d, :], PCrb[sl, gp * 2 + d, :], True, False, [bPBrb, bPCrb], [bp])
                    K.mm(o, PBib[sl, gp * 2 + d, :], PCib[sl, gp * 2 + d, :], False, True, [bPBib, bPCib], [bp])
                K.tt("dve", mt, pt[:, 0:128], maskf, ALU.mult, [bp, b_maskf], [bmt])
                K.tt("dve", tmpc, pt[:, 128:256], maskb, ALU.mult, [bp, b_maskb], [b_tmpc])
                K.tt("dve", S.M0[:, g, :], mt, tmpc, ALU.add, [bmt, b_tmpc], [S.b_M0])

        PKr, bPKr, PKi, bPKi = PCr, bPCr, PCi, bPCi
        cmul(PKr, bPKr, PKi, bPKi, WKr, bWKr, WKi, bWKi, Cr, bCr, Ci, bCi, neg_i=True)
        K.copy("act", S.PCC[:, :, :, 0, :], PKr.rearrange("p (a d) n c -> p a d (n c)", d=2), [bPKr], [S.b_PCC])
        K.copy("act", S.PCC[:, :, :, 1, :], PKi.rearrange("p (a d) n c -> p a d (n c)", d=2), [bPKi], [S.b_PCC])
        K.barrier()
        scP.__exit__(None, None, None)

    def hs_scan(gp, d):
        col = gp * 2 + d
        cur = 0
        for k in range(NLEV):
            s = 1 << k
            src, bsrc = S.Hb[cur]
            dst, bdst = S.Hb[1 - cur]

            def scal(tab, j):
                return tab[:, k, j, col:col + 1]

            def region(lo, hi, tab, btab, two_seg=False):
                sh = -s if d == 0 else s
                if two_seg:
                    def v(t, ri, off):
                        return t[:, ri, :].rearrange("p (g n) -> p g n", g=2)[:, :, lo + off:hi + off]
                else:
                    def v(t, ri, off):
                        return t[:, ri, lo + off:hi + off]
                rd = [bsrc, btab]
                K.stt(v(dst, 0, 0), v(src, 0, sh), scal(tab, 0), v(src, 0, 0), ALU.mult, ALU.add, rd, [bdst])
                K.stt(v(dst, 1, 0), v(src, 1, sh), scal(tab, 0), v(src, 1, 0), ALU.mult, ALU.add, rd, [bdst])
                K.stt(v(dst, 0, 0), v(src, 1, sh), scal(tab, 2), v(dst, 0, 0), ALU.mult, ALU.add, rd + [bdst], [bdst])
                K.stt(v(dst, 1, 0), v(src, 0, sh), scal(tab, 1), v(dst, 1, 0), ALU.mult, ALU.add, rd + [bdst], [bdst])

            if d == 0:
                K.copy("pool", dst[:, :, 0:min(s, NBS)], src[:, :, 0:min(s, NBS)], [bsrc], [bdst])
                if s < NBS:
                    region(s, NBS, S.SS, S.b_SS, two_seg=True)
                region(NBS, min(NBS + s, NB), S.SF, S.b_SF)
                if s >= NBS and s < NB:
                    pass
            else:
                lo0 = max(NB - s, NBS)
                K.copy("pool", dst[:, :, lo0:NB], src[:, :, lo0:NB], [bsrc], [bdst])
                if s < NBS:
                    region(0, NBS - s, S.SS, S.b_SS, two_seg=True)
                region(max(NBS - s, 0), NBS, S.SF, S.b_SF)
            cur = 1 - cur
        return cur

    def phaseC(l):
        stopC = os.environ.get('MK_STOPC', '')
        scC = K.scope()
        scC.__enter__()
        S.PBT, S.b_PBT = K.sb([128, 16, 2, 2, 2, 64], BF16, "PBT")
        S.PCC, S.b_PCC = K.sb([128, 16, 2, 2, 128], BF16, "PCC")
        S.M0, S.b_M0 = K.sb([128, 32, 128], BF16, "M0")
        S.SS, S.b_SS = K.sb([128, 11, 3, 32], F32, "SS")
        S.SF, S.b_SF = K.sb([128, 11, 3, 32], F32, "SF")
        ssm_precompute(l)
        if stopC in ('sin', 'pow', 'cmul', 'tr', 'pre'):
            K.barrier(); scC.__exit__(None, None, None); return
        S.Hb = [K.sb([128, 2, NB], F32, "H%d" % i) for i in range(2)]
        S.Hin, S.b_Hin = K.sb([128, 2, 2, NB], BF16, "Hin")
        S.xg_ = [K.sb([128, NB], BF16, "xg%d" % i) for i in range(4)]
        S.yg_ = [K.sb([128, NB], F32, "yg%d" % i) for i in range(2)]
        NBT = (NB + 511) // 512
        bw = min(512, NB)
        for gp in range(16):
            xg = []
            for gpar in range(2):
                g = gp * 2 + gpar
                xt, bx = S.xg_[cntC["x"] % 4]
                cntC["x"] += 1
                K.dma("sp", [(xt[s2 * 16:(s2 + 1) * 16, :], XL[s2, g * 16:(g + 1) * 16, :]) for s2 in range(8)],
                      [DB("XL", i) for i in range(NT)], [bx], bx)
                xg.append((xt, bx))
            for d in range(2):
                H0, bH0 = S.Hb[0]
                for nt in range(NBT):
                    for ri in range(2):
                        pt, bp = K.psum()
                        for gpar in range(2):
                            K.mm(pt[gpar * 64:(gpar + 1) * 64, 0:bw], S.PBT[:, gp, d, gpar, ri, :],
                                 xg[gpar][0][:, nt * 512:nt * 512 + bw], True, True, [S.b_PBT, xg[gpar][1]], [bp])
                        K.copy("act", H0[:, ri, nt * 512:nt * 512 + bw], pt[:, 0:bw], [bp], [bH0])
                cur = hs_scan(gp, d)
                Hf, bHf = S.Hb[cur]
                if d == 0:
                    K.memset("pool", S.Hin[:, 0, :, 0:1], 0.0, [S.b_Hin])
                    K.copy("act", S.Hin[:, 0].rearrange("p r (g n) -> p r g n", g=2)[:, :, :, 1:NBS],
                           Hf.rearrange("p r (g n) -> p r g n", g=2)[:, :, :, 0:NBS - 1], [bHf], [S.b_Hin])
                    K.ts("dve", S.Hin[:, 0, :, NBS:NBS + 1], Hf[:, :, NBS - 1:NBS], flag[:, 0:1], None, ALU.mult, None,
                         [bHf, b_flag], [S.b_Hin])
                else:
                    K.memset("pool", S.Hin[:, 1, :, NB - 1:NB], 0.0, [S.b_Hin])
                    K.copy("act", S.Hin[:, 1].rearrange("p r (g n) -> p r g n", g=2)[:, :, :, 0:NBS - 1],
                           Hf.rearrange("p r (g n) -> p r g n", g=2)[:, :, :, 1:NBS], [bHf], [S.b_Hin])
                    K.ts("dve", S.Hin[:, 1, :, NBS - 1:NBS], Hf[:, :, NBS:NBS + 1], flag[:, 0:1], None, ALU.mult, None,
                         [bHf, b_flag], [S.b_Hin])
            for gpar in range(2):
                g = gp * 2 + gpar
                sl = slice(gpar * 64, gpar * 64 + 64)
                yt, by = S.yg_[cntC["y"] % 2]
                cntC["y"] += 1
                for nt in range(NBT):
                    cs = slice(nt * 512, nt * 512 + bw)
                    pt, bp = K.psum()
                    K.mm(pt[:, 0:bw], S.M0[:, g, :], xg[gpar][0][:, cs], True, False, [S.b_M0, xg[gpar][1]], [bp])
                    for d in range(2):
                        for ri in range(2):
                            K.mm(pt[:, 0:bw], S.PCC[sl, gp, d, ri, :], S.Hin[sl, d, ri, cs], False,
                                 (d == 1 and ri == 1), [S.b_PCC, S.b_Hin], [bp])
                    K.copy("act", yt[:, cs], pt[:, 0:bw], [bp], [by])
                K.dma("sp", [(YL[t2, g * 16:(g + 1) * 16, :], yt[t2 * 16:(t2 + 1) * 16, :]) for t2 in range(8)],
                      [by], [DB("YL", g)], by)
        K.barrier()
        scC.__exit__(None, None, None)

    cntD = {"i": 0}

    def phaseD(l, last):
        scD = K.scope()
        scD.__enter__()
        alloc_shared()
        xt_ = S.xt_
        oT_ = [K.sb([128, 4, 512], BF16, "oT%d" % i) for i in range(1)]
        sg_ = [K.sb([128, 2, 8, 512], BF16, "sg%d" % i) for i in range(1)]
        yl_ = [K.sb([128, 4, 8, 64], F32, "yl%d" % i) for i in range(1)]
        ud_ = [K.sb([128, 4, 512], F32, "ud%d" % i) for i in range(1)]
        zt_, b_zt = K.sb([128, 4, 512], BF16, "zt")
        m1_, b_m1 = K.sb([128, 8, 512], F32, "m1")
        mg_, b_mg = K.sb([128, 8, 512], BF16, "mg")
        h2_, b_h2 = K.sb([128, 8, 512], BF16, "h2")
        aT_, b_aT = K.sb([128, 22, 512], BF16, "aT")
        ga_, b_ga = K.sb([128, 512], F32, "ga")
        gb_, b_gb = K.sb([128, 512], F32, "gb")
        gc_, b_gc = K.sb([128, 512], F32, "gc")
        for i in range(NT):
            s = i // (NT // 2)
            t0 = i * 512
            par = 0
            cntD["i"] += 1
            xt, bx = xt_[cnt["x"] % 2]
            cnt["x"] += 1
            src = x_in if l == 0 else xs
            rd = [] if l == 0 else [DB("xs", i)]
            K.dma("sp", [(xt, src[:, t0:t0 + 512].rearrange("(k p) t -> p k t", p=128))], rd, [bx], bx)
            oT, boT = oT_[par]
            K.dma("sp", [(oT, OT[:, t0:t0 + 512].rearrange("(c p) t -> p c t", p=128))], [DB("OT", i)], [boT], boT)
            sg, bsg = sg_[par]
            K.dma("sp", [(sg[:, 0], SGA[:, t0:t0 + 512].rearrange("(c p) t -> p c t", p=128)),
                         (sg[:, 1], SGS[:, t0:t0 + 512].rearrange("(c p) t -> p c t", p=128))],
                  [DB("SGA", i * 2), DB("SGA", i * 2 + 1), DB("SGS", i * 2), DB("SGS", i * 2 + 1)], [bsg], bsg)
            yl, byl = yl_[par]
            K.dma("sp", [(yl[:, c], YL[:, c * 128:(c + 1) * 128, i * 64:(i + 1) * 64].rearrange("t p b -> p t b"))
                         for c in range(4)], [DB("YL", g) for g in range(NG)], [byl], byl)
            ud, bud = ud_[par]
            K.dma("sp", [(ud, UT[:, t0:t0 + 512].rearrange("(c p) t -> p c t", p=128))], [DB("UT", i)], [bud], bud)
            for c in range(4):
                yv = yl[:, c].rearrange("p t b -> p b t")
                g3 = ga_.rearrange("p (b t) -> p b t", t=8)
                K.stt(g3, ud[:, c, :].rearrange("p (b t) -> p b t", t=8), dsk[:, l, c:c + 1], yv, ALU.mult, ALU.add,
                      [bud, byl, b_dsk], [b_ga])
                K.actf(gb_, ga_, AF.Square, [b_ga], [b_gb])
                K.ts("dve", gb_, gb_, 0.044715, 1.0, ALU.mult, ALU.add, [b_gb], [b_gb])
                K.tt("dve", gb_, gb_, ga_, ALU.mult, [b_gb, b_ga], [b_gb])
                K.actf(gc_, gb_, AF.Sigmoid, [b_gb], [b_gc], scale=1.5957691216057308)
                K.tt("dve", zt_[:, c, :], ga_, gc_, ALU.mult, [b_ga, b_gc], [b_zt])
            for blk_ in range(2):
                wt, bw = load_w(wb_attn[l], 0, 4, [(blk_ * 512, 512)])
                for c in range(4):
                    pt, bp = K.psum()
                    for kc in range(4):
                        K.mm(pt, wt[:, kc, c * 128:(c + 1) * 128], oT[:, kc, :], kc == 0, kc == 3, [bw, boT], [bp])
                    K.tt("dve", m1_[:, blk_ * 4 + c, :], pt, sg[:, 0, blk_ * 4 + c, :], ALU.mult, [bp, bsg], [b_m1])
            for blk_ in range(4):
                wt, bw = load_w(wb_glu[l], 0, 4, [(blk_ * 256, 256), (1024 + blk_ * 256, 256)])
                for c in range(2):
                    j = blk_ * 2 + c
                    pl, bpl = K.psum()
                    for kc in range(4):
                        K.mm(pl, wt[:, kc, c * 128:(c + 1) * 128], zt_[:, kc, :], kc == 0, kc == 3, [bw, b_zt], [bpl])
                    pg_, bpg = K.psum()
                    for kc in range(4):
                        K.mm(pg_, wt[:, kc, 256 + c * 128:256 + (c + 1) * 128], zt_[:, kc, :], kc == 0, kc == 3,
                             [bw, b_zt], [bpg])
                    K.actf(ga_, pg_, AF.Sigmoid, [bpg, b_bglu], [b_ga], bias=bglu[:, l, 8 + j:9 + j], scale=1.0)
                    K.stt(gb_, pl, bglu[:, l, j:j + 1], ga_, ALU.add, ALU.mult, [bpl, b_bglu, b_ga], [b_gb])
                    K.tt("dve", gb_, gb_, sg[:, 1, j, :], ALU.mult, [b_gb, bsg], [b_gb])
                    K.tt("dve", mg_[:, j, :], gb_, m1_[:, j, :], ALU.add, [b_gb, b_m1], [b_mg])
            for blk_ in range(2):
                wt, bw = load_w(wb_o[l], 0, 8, [(blk_ * 512, 512)])
                for c in range(4):
                    j = blk_ * 4 + c
                    pt, bp = K.psum()
                    for kc in range(8):
                        K.mm(pt, wt[:, kc, c * 128:(c + 1) * 128], mg_[:, kc, :], kc == 0, kc == 7, [bw, b_mg], [bp])
                    K.stt(xt[:, j, :], pt, GT1(l, j, s), xt[:, j, :], ALU.mult, ALU.add, [bp, b_modT, bx], [bx])
            norm_mod(xt, bx, lambda kc: A2[:, l, kc, s:s + 1], lambda kc: SH2(l, kc, s), h2_, b_h2)
            for blk_ in range(11):
                wt, bw = load_w(wb_ffi[l], 0, 8, [(blk_ * 256, 256), (DFF + blk_ * 256, 256)])
                for c in range(2):
                    j = blk_ * 2 + c
                    pgt, bpg = K.psum()
                    for kc in range(8):
                        K.mm(pgt, wt[:, kc, c * 128:(c + 1) * 128], h2_[:, kc, :], kc == 0, kc == 7, [bw, b_h2], [bpg])
                    pu, bpu = K.psum()
                    for kc in range(8):
                        K.mm(pu, wt[:, kc, 256 + c * 128:256 + (c + 1) * 128], h2_[:, kc, :], kc == 0, kc == 7,
                             [bw, b_h2], [bpu])
                    K.actf(ga_, pgt, AF.Silu, [bpg], [b_ga])
                    K.tt("dve", aT_[:, j, :], pu, ga_, ALU.mult, [bpu, b_ga], [b_aT])
            for half in range(2):
                wa, bwa = load_w(wb_ffo[l], 0, 11, [(half * 512, 512)])
                wb2, bwb2 = load_w(wb_ffo[l], 11, 11, [(half * 512, 512)])
                for c in range(4):
                    j = half * 4 + c
                    pt, bp = K.psum()
                    for kc in range(22):
                        w_, bw_ = (wa, bwa) if kc < 11 else (wb2, bwb2)
                        K.mm(pt, w_[:, kc % 11, c * 128:(c + 1) * 128], aT_[:, kc, :], kc == 0, kc == 21,
                             [bw_, b_aT], [bp])
                    K.stt(xt[:, j, :], pt, GT2(l, j, s), xt[:, j, :], ALU.mult, ALU.add, [bp, b_modT, bx], [bx])
            if not last:
                K.dma("sp", [(xs[:, t0:t0 + 512].rearrange("(k p) t -> p k t", p=128), xt)], [bx], [DB("xs", i)], bx)
            else:
                rstd_only(xt, bx)
                for kc in range(8):
                    K.stt(xt[:, kc, :], xt[:, kc, :], gfT[:, kc:kc + 1], S.rstd, ALU.mult, ALU.mult,
                          [bx, S.b_rstd, b_gfT], [bx])
                K.dma("sp", [(y_out[:, t0:t0 + 512].rearrange("(k p) t -> p k t", p=128), xt)], [bx], [DB("y", i)], bx)

        K.barrier()
        scD.__exit__(None, None, None)

    import os
    stop = os.environ.get("MK_STOP", "")
    for l in range(L):
        if l == 0:
            for cb in cvb:
                b_wb.w.update(cb.w)
        if stop == "pro":
            break
        phaseA(l)
        K.barrier()
        if stop == "A":
            break
        phaseB(l)
        K.barrier()
        if stop == "B":
            break
        phaseC(l)
        K.barrier()
        if stop == "C":
            break
        phaseD(l, l == L - 1)
        K.barrier()
    K.barrier()
    return nc


def _rope_tables(T_seq):
    inv = 1.0 / (10000.0 ** (np.arange(0, 64, 2, dtype=np.float32) / 64.0))
    ang = np.arange(T_seq, dtype=np.float32)[:, None] * inv[None, :].astype(np.float32)
    ang = np.concatenate([ang, ang], axis=-1).astype(np.float32)
    return np.cos(ang).astype(np.float32), np.sin(ang).astype(np.float32)


def _consts():
    perm = np.zeros((128, 128), np.float32)
    for m in range(2):
        for d in range(64):
            perm[m * 64 + (d + 32) % 64, m * 64 + d] = 1.0
    ident = np.eye(128, dtype=np.float32)
    s2 = np.arange(128) // 16
    maskf = (s2[None, :] >= s2[:, None]).astype(np.float32)
    maskb = (s2[None, :] <= s2[:, None]).astype(np.float32)
    return perm, ident, maskf, maskb


def _pl(a, L):
    rest = a.shape[4:]
    a = a.reshape((L, 2, 16, 2, 64) + rest)
    a = np.moveaxis(a, (3, 4, 0, 2, 1), (0, 1, 2, 3, 4))
    return np.ascontiguousarray(a.reshape((128, L, 32) + rest)).astype(np.float32)


def make_in_maps(inp, cfg, core_seqs):
    L, T = cfg.L, cfg.T
    perm, ident, maskf, maskb = _consts()

    def fm(v, nch):
        v = np.asarray(v, np.float32)
        lead = v.shape[:-1]
        v = v.reshape(lead + (nch, 128))
        v = np.moveaxis(v, -1, 0)
        return np.ascontiguousarray(v)

    shared = {
        "perm": perm, "ident": ident, "maskf": maskf, "maskb": maskb,
        "w_mod": np.ascontiguousarray(inp["w_mod"][:L], np.float32),
        "b_modT": fm(inp["b_mod"][:L], 48),
        "g1T": fm(inp["norm1_g"][:L], 8), "g2T": fm(inp["norm2_g"][:L], 8), "gfT": fm(inp["final_g"], 8),
        "w_in": np.ascontiguousarray(inp["w_in"][:L], np.float32),
        "lamT": np.ascontiguousarray(np.broadcast_to(
            np.stack([inp["lam_q1"][:L], inp["lam_k1"][:L], inp["lam_q2"][:L], inp["lam_k2"][:L]], axis=1)[None],
            (128, L, 4, 64)), np.float32),
        "subgT": np.ascontiguousarray(np.asarray(inp["subln_g"][:L], np.float32).T),
        "w_attn": np.ascontiguousarray(inp["w_attn_br"][:L], np.float32),
        "ssm_dT": fm(inp["ssm_d"][:L], 4),
        "w_glu": np.ascontiguousarray(inp["w_glu"][:L], np.float32),
        "b_gluT": fm(inp["b_glu"][:L], 16),
        "w_o": np.ascontiguousarray(inp["w_o"][:L], np.float32),
        "w_ffi": np.ascontiguousarray(inp["w_ffn_in"][:L], np.float32),
        "w_ffo": np.ascontiguousarray(inp["w_ffn_out"][:L], np.float32),
    }
    are = np.asarray(inp["ssm_a_re"][:L], np.float32)
    aim = np.asarray(inp["ssm_a_im"][:L], np.float32)
    ldt = np.broadcast_to(np.asarray(inp["ssm_log_dt"][:L], np.float32)[..., None], (L, 2, 32, 64))
    shared["a_reP"] = _pl(are, L)
    shared["a_imP"] = _pl(aim, L)
    shared["ldtP"] = _pl(np.ascontiguousarray(ldt), L)
    shared["b_reP"] = _pl(np.asarray(inp["ssm_b_re"][:L], np.float32), L)
    shared["b_imP"] = _pl(np.asarray(inp["ssm_b_im"][:L], np.float32), L)
    shared["c_reP"] = _pl(np.swapaxes(np.asarray(inp["ssm_c_re"][:L], np.float32), 3, 4), L)
    shared["c_imP"] = _pl(np.swapaxes(np.asarray(inp["ssm_c_im"][:L], np.float32), 3, 4), L)
    maps = []
    for (x, c, pos, split) in core_seqs:
        cos, sin = _rope_tables(int(pos.max()) + 1)
        cosT = np.ascontiguousarray(np.tile(cos[pos].T, (2, 1)))
        sgn = np.where(np.arange(64) < 32, -1.0, 1.0).astype(np.float32)
        sinT = np.ascontiguousarray(np.tile((sin[pos] * sgn[None, :]).T, (2, 1)))
        m = dict(shared)
        m["xT"] = np.ascontiguousarray(x.T)
        m["cT"] = np.ascontiguousarray(np.moveaxis(np.asarray(c, np.float32).reshape(2, 8, 128), (0, 1, 2), (2, 1, 0)))
        m["cosT"] = cosT.astype(np.float32)
        m["sinT"] = sinT.astype(np.float32)
        m["crossbias"] = np.full((128, 1), 0.0 if split else -30000.0, np.float32)
        m["flag"] = np.full((128, 1), 1.0 if split else 0.0, np.float32)
        maps.append(m)
    return maps


_NC_CACHE = {}


def run(inp, cfg, core_seqs):
    key = (cfg.T, cfg.L)
    if key not in _NC_CACHE:
        _NC_CACHE[key] = build(cfg)
    nc = _NC_CACHE[key]
    maps = make_in_maps(inp, cfg, core_seqs)
    res = run_bass_kernel_spmd(nc, maps, core_ids=list(range(len(maps))))
    return [np.asarray(r["yT"]).T for r in res.results]


def kernel(**inp):
    cfg = Cfg(8192, 4)
    xp = np.asarray(inp["x_prompt"], np.float32)
    xsm = np.asarray(inp["x_sample"], np.float32)
    cp = np.asarray(inp["c_prompt"], np.float32)
    csm = np.asarray(inp["c_sample"], np.float32)
    cores = []
    pos_p = np.concatenate([np.arange(4096), np.arange(4096)])
    for c in range(4):
        cores.append((np.concatenate([xp[2 * c], xp[2 * c + 1]], axis=0), np.stack([cp[2 * c], cp[2 * c + 1]]),
                      pos_p, False))
    for c in range(4):
        cores.append((xsm[c], np.stack([csm[c], csm[c]]), np.arange(8192), True))
    outs = run(inp, cfg, cores)
    yp = np.empty((8, 4096, D), np.float32)
    for c in range(4):
        yp[2 * c] = outs[c][:4096]
        yp[2 * c + 1] = outs[c][4096:]
    ys = np.stack([outs[4 + c] for c in range(4)], axis=0).astype(np.float32)
    return (yp, ys)
```

```python
import math
import contextlib
import numpy as np
import concourse.bass as bass
import concourse.mybir as mybir
from concourse.bass_utils import run_bass_kernel_spmd

F32 = mybir.dt.float32
BF16 = mybir.dt.bfloat16
I32 = mybir.dt.int32
ALU = mybir.AluOpType
AF = mybir.ActivationFunctionType

D = 1024
NH = 4
DFF = 2816
DSSM = 512
NG = 32
EPS = 1e-6
TWO_PI = 2.0 * math.pi


def lam_init_fn(layer):
    return 0.8 - 0.6 * math.exp(-0.3 * layer)


class Buf:
    __slots__ = ("name", "w", "r", "sem", "cnt", "excl")

    def __init__(self, name, excl=False):
        self.name = name
        self.excl = excl
        self.w = {}
        self.r = {}
        self.sem = None
        self.cnt = 0


class EngState:
    def __init__(self, eng, sem, self_sync):
        self.eng = eng
        self.sem = sem
        self.cnt = 0
        self.waited = {}
        self.self_sync = self_sync


def _merge(d, src):
    for k, (s, v) in src.items():
        if k not in d or d[k][1] < v:
            d[k] = (s, v)


class KB:
    def __init__(self, nc):
        self.nc = nc
        self.engs = {}
        for name, e in (("pe", nc.tensor), ("dve", nc.vector), ("act", nc.scalar),
                        ("pool", nc.gpsimd), ("sp", nc.sync)):
            self.engs[name] = EngState(e, nc.alloc_semaphore("e_" + name), name != "pe")
        self.dma_bufs = []
        self.nalloc = 0
        self.stacks = []
        self.scope_bufs = []
        self.free_sems = []
        self.retired = {}
        self.nsem = 0
        self.ps = []
        for i in range(8):
            t = nc.alloc_psum_tensor("psb%d" % i, [128, 512], F32)
            self.ps.append((t.ap(), Buf("ps%d" % i, excl=True)))
        self.ps_i = 0

    def sb(self, shape, dt, name=None):
        self.nalloc += 1
        nm = "%s_%d" % (name or "t", self.nalloc)
        if self.stacks:
            t = self.stacks[-1].enter_context(self.nc.sbuf_tensor(nm, list(shape), dt))
        else:
            t = self.nc.alloc_sbuf_tensor(nm, list(shape), dt)
        b = Buf(name or "t")
        if self.scope_bufs:
            self.scope_bufs[-1].append(b)
        return (t.ap() if hasattr(t, "ap") and callable(t.ap) else t[:]), b

    @contextlib.contextmanager
    def scope(self):
        st = contextlib.ExitStack()
        self.stacks.append(st)
        self.scope_bufs.append([])
        try:
            yield
        finally:
            self.stacks.pop()
            for b in self.scope_bufs.pop():
                if b.sem is not None:
                    self.free_sems.append((b.sem, b.cnt))
                    self.dma_bufs.remove(b)
                    self.retired[id(b.sem)] = (b.sem, b.cnt)
                    b.sem = None
            st.close()

    def dram(self, name, shape, dt):
        return self.nc.dram_tensor(name, list(shape), dt, kind="Internal").ap()

    def psum(self):
        p = self.ps.pop(0)
        self.ps.append(p)
        return p

    def psum_hold(self):
        return self.ps.pop(0)

    def psum_release(self, p):
        self.ps.append(p)

    def _deps(self, reads, writes):
        d = {}
        for b in reads:
            _merge(d, b.w)
            if b.excl:
                _merge(d, b.r)
        for b in writes:
            _merge(d, b.w)
            _merge(d, b.r)
        return d

    def _wait(self, E, deps):
        for k, (sem, val) in deps.items():
            if sem is E.sem and not E.self_sync:
                continue
            if E.waited.get(k, 0) < val:
                E.eng.wait_ge(sem, val)
                E.waited[k] = val

    def _record(self, tok, reads, writes):
        k = id(tok[0])
        for b in reads:
            if k not in b.r or b.r[k][1] < tok[1]:
                b.r[k] = tok
        for b in writes:
            b.w = {k: tok}
            b.r = {}

    def op(self, ename, fn, reads=(), writes=()):
        E = self.engs[ename]
        self._wait(E, self._deps(reads, writes))
        ins = fn(E.eng)
        E.cnt += 1
        ins.then_inc(E.sem, 1)
        self._record((E.sem, E.cnt), reads, writes)

    def dma(self, ename, pairs, reads, writes, sbuf):
        E = self.engs[ename]
        if sbuf.sem is None:
            if self.free_sems:
                sbuf.sem, sbuf.cnt = self.free_sems.pop()
                self.retired.pop(id(sbuf.sem), None)
            else:
                self.nsem += 1
                sbuf.sem = self.nc.alloc_semaphore("d_%d" % self.nsem)
            self.dma_bufs.append(sbuf)
        deps = self._deps(reads, writes)
        if sbuf.cnt > 0:
            _merge(deps, {id(sbuf.sem): (sbuf.sem, sbuf.cnt)})
        self._wait(E, deps)
        for (o, i) in pairs:
            E.eng.dma_start(out=o, in_=i).then_inc(sbuf.sem, 16)
            sbuf.cnt += 16
        self._record((sbuf.sem, sbuf.cnt), reads, writes)

    def barrier(self):
        toks = {}
        for E in self.engs.values():
            if E.cnt:
                toks[id(E.sem)] = (E.sem, E.cnt)
        for b in self.dma_bufs:
            if b.cnt:
                toks[id(b.sem)] = (b.sem, b.cnt)
        for k, tok in self.retired.items():
            toks.setdefault(k, tok)
        for E in self.engs.values():
            for k, (sem, val) in toks.items():
                if sem is E.sem:
                    continue
                if E.waited.get(k, 0) < val:
                    E.eng.wait_ge(sem, val)
                    E.waited[k] = val

    def mm(self, out, lhsT, rhs, start, stop, reads, writes):
        self.op("pe", lambda e: e.matmul(out, lhsT, rhs, start=start, stop=stop), reads, writes)

    def tt(self, eng, out, in0, in1, op, reads, writes):
        self.op(eng, lambda e: e.tensor_tensor(out=out, in0=in0, in1=in1, op=op), reads, writes)

    def ts(self, eng, out, in0, s1, s2, op0, op1, reads, writes):
        if op1 is None:
            self.op(eng, lambda e: e.tensor_scalar(out=out, in0=in0, scalar1=s1, scalar2=None, op0=op0), reads, writes)
        else:
            self.op(eng, lambda e: e.tensor_scalar(out=out, in0=in0, scalar1=s1, scalar2=s2, op0=op0, op1=op1), reads, writes)

    def stt(self, out, in0, scalar, in1, op0, op1, reads, writes):
        self.op("dve", lambda e: e.scalar_tensor_tensor(out=out, in0=in0, scalar=scalar, in1=in1, op0=op0, op1=op1), reads, writes)

    def actf(self, out, in_, func, reads, writes, bias=None, scale=None):
        kw = {}
        if bias is not None:
            kw["bias"] = bias
        if scale is not None:
            kw["scale"] = scale
        self.op("act", lambda e: e.activation(out=out, in_=in_, func=func, **kw), reads, writes)

    def copy(self, eng, out, in_, reads, writes):
        if eng == "act":
            self.op("act", lambda e: e.activation(out=out, in_=in_, func=AF.Identity), reads, writes)
        else:
            self.op(eng, lambda e: e.tensor_copy(out=out, in_=in_), reads, writes)

    def memset(self, eng, ap, val, writes):
        self.op(eng, lambda e: e.memset(ap, val), (), writes)


class Cfg:
    def __init__(self, T, L):
        self.T = T
        self.L = L
        self.NT = T // 512
        self.SEG = T // 2
        self.NB = T // 8
        self.NBS = self.NB // 2
        self.NKC = T // 128


def build(cfg):
    T, L, NT, NB, NBS = cfg.T, cfg.L, cfg.NT, cfg.NB, cfg.NBS
    nc = bass.Bass("TRN2", target_bir_lowering=False)
    K = KB(nc)

    def din(name, shape, dt=F32):
        return nc.dram_tensor(name, list(shape), dt, kind="ExternalInput").ap()

    x_in = din("xT", [D, T])
    y_out = nc.dram_tensor("yT", [D, T], F32, kind="ExternalOutput").ap()
    cT_in = din("cT", [128, 8, 2])
    cos_in = din("cosT", [128, T])
    sin_in = din("sinT", [128, T])
    perm_in = din("perm", [128, 128])
    ident_in = din("ident", [128, 128])
    maskf_in = din("maskf", [128, 128])
    maskb_in = din("maskb", [128, 128])
    cb_in = din("crossbias", [128, 1])
    flag_in = din("flag", [128, 1])
    w_mod = din("w_mod", [L, D, 6 * D])
    bmod_in = din("b_modT", [128, L, 48])
    g1_in = din("g1T", [128, L, 8])
    g2_in = din("g2T", [128, L, 8])
    gf_in = din("gfT", [128, 8])
    w_in = din("w_in", [L, D, 4096])
    lam_in = din("lamT", [128, L, 4, 64])
    subg_in = din("subgT", [128, L])
    w_attn = din("w_attn", [L, 512, D])
    are_in = din("a_reP", [128, L, 32])
    aim_in = din("a_imP", [128, L, 32])
    ldt_in = din("ldtP", [128, L, 32])
    bre_in = din("b_reP", [128, L, 32, 16])
    bim_in = din("b_imP", [128, L, 32, 16])
    cre_in = din("c_reP", [128, L, 32, 16])
    cim_in = din("c_imP", [128, L, 32, 16])
    dsk_in = din("ssm_dT", [128, L, 4])
    w_glu = din("w_glu", [L, 512, 2 * D])
    bglu_in = din("b_gluT", [128, L, 16])
    w_o = din("w_o", [L, D, D])
    w_ffi = din("w_ffi", [L, D, 2 * DFF])
    w_ffo = din("w_ffo", [L, DFF, D])

    wb_in = K.dram("wb_in", [L, D, 4096], BF16)
    wb_attn = K.dram("wb_attn", [L, 512, D], BF16)
    wb_glu = K.dram("wb_glu", [L, 512, 2 * D], BF16)
    wb_o = K.dram("wb_o", [L, D, D], BF16)
    wb_ffi = K.dram("wb_ffi", [L, D, 2 * DFF], BF16)
    wb_ffo = K.dram("wb_ffo", [L, DFF, D], BF16)
    xs = K.dram("xs", [D, T], F32)
    QT = K.dram("QT", [4, 128, T], BF16)
    KT = K.dram("KT", [4, 128, T], BF16)
    VS = K.dram("VS", [T // 128, 128, 512], BF16)
    SGA = K.dram("SGA", [D, T], BF16)
    SGS = K.dram("SGS", [D, T], BF16)
    UT = K.dram("UT", [512, T], F32)
    XL = K.dram("XL", [8, 512, NB], BF16)
    YL = K.dram("YL", [8, 512, NB], F32)
    OT = K.dram("OT", [512, T], BF16)
    dbuf = {}

    def DB(name, i=0):
        key = (name, i)
        if key not in dbuf:
            dbuf[key] = Buf("%s%d" % (name, i))
        return dbuf[key]

    ones32, b_ones32 = K.sb([128, 128], F32, "ones32")
    perm_b, b_perm = K.sb([128, 128], BF16, "perm")
    ident, b_ident = K.sb([128, 128], F32, "ident")
    maskf, b_maskf = K.sb([128, 128], F32, "maskf")
    maskb, b_maskb = K.sb([128, 128], F32, "maskb")
    crossb, b_crossb = K.sb([128, 1], F32, "crossb")
    zerob, b_zerob = K.sb([128, 1], F32, "zerob")
    flag, b_flag = K.sb([128, 1], F32, "flag")
    tmpc, b_tmpc = K.sb([128, 128], F32, "tmpc")
    K.memset("dve", ones32, 1.0, [b_ones32])
    K.memset("dve", zerob, 0.0, [b_zerob])
    K.dma("sp", [(tmpc, perm_in)], [], [b_tmpc], b_tmpc)
    K.copy("dve", perm_b, tmpc, [b_tmpc], [b_perm])
    K.dma("sp", [(ident, ident_in)], [], [b_ident], b_ident)
    K.dma("sp", [(maskf, maskf_in)], [], [b_maskf], b_maskf)
    K.dma("sp", [(maskb, maskb_in)], [], [b_maskb], b_maskb)
    K.dma("sp", [(crossb, cb_in)], [], [b_crossb], b_crossb)
    K.dma("sp", [(flag, flag_in)], [], [b_flag], b_flag)

    g1T, b_g1T = K.sb([128, L, 8], F32, "g1T")
    g2T, b_g2T = K.sb([128, L, 8], F32, "g2T")
    gfT, b_gfT = K.sb([128, 8], F32, "gfT")
    bglu, b_bglu = K.sb([128, L, 16], F32, "bglu")
    dsk, b_dsk = K.sb([128, L, 4], F32, "dsk")
    subg, b_subg = K.sb([128, L], F32, "subg")
    bmodT, b_bmodT = K.sb([128, L, 48], F32, "bmodT")
    K.dma("sp", [(g1T, g1_in)], [], [b_g1T], b_g1T)
    K.dma("sp", [(g2T, g2_in)], [], [b_g2T], b_g2T)
    K.dma("sp", [(gfT, gf_in)], [], [b_gfT], b_gfT)
    K.dma("sp", [(bglu, bglu_in)], [], [b_bglu], b_bglu)
    K.dma("sp", [(dsk, dsk_in)], [], [b_dsk], b_dsk)
    K.dma("sp", [(subg, subg_in)], [], [b_subg], b_subg)
    K.dma("sp", [(bmodT, bmod_in)], [], [b_bmodT], b_bmodT)

    cvb = [Buf("cv%d" % i) for i in range(4)]
    cvi = [0]
    b_wb = Buf("wb_all")

    def convert(src, dst, nelem):
        rows = nelem // 2048
        s2 = src.rearrange("(r c) -> r c", c=2048)
        d2 = dst.rearrange("(r c) -> r c", c=2048)
        r0 = 0
        while r0 < rows:
            r1 = min(rows, r0 + 1024)
            cb = cvb[cvi[0] % 4]
            cvi[0] += 1
            K.dma("pool", [(d2[r0:r1, :], s2[r0:r1, :])], [], [cb], cb)
            r0 = r1

    for l in range(L):
        convert(w_in[l].rearrange("a b -> (a b)"), wb_in[l].rearrange("a b -> (a b)"), D * 4096)
    for l in range(L):
        convert(w_attn[l].rearrange("a b -> (a b)"), wb_attn[l].rearrange("a b -> (a b)"), 512 * D)
        convert(w_glu[l].rearrange("a b -> (a b)"), wb_glu[l].rearrange("a b -> (a b)"), 512 * 2 * D)
        convert(w_o[l].rearrange("a b -> (a b)"), wb_o[l].rearrange("a b -> (a b)"), D * D)
        convert(w_ffi[l].rearrange("a b -> (a b)"), wb_ffi[l].rearrange("a b -> (a b)"), D * 2 * DFF)
        convert(w_ffo[l].rearrange("a b -> (a b)"), wb_ffo[l].rearrange("a b -> (a b)"), DFF * D)

    modT, b_modT = K.sb([128, L, 48, 2], F32, "modT")
    A1, b_A1 = K.sb([128, L, 8, 2], F32, "A1")
    A2, b_A2 = K.sb([128, L, 8, 2], F32, "A2")
    cT, b_cT = K.sb([128, 8, 2], F32, "cT")
    sc_, b_sc = K.sb([128, 8, 2], F32, "silu_c")
    K.dma("sp", [(cT, cT_in)], [], [b_cT], b_cT)
    K.actf(sc_, cT, AF.Silu, [b_cT], [b_sc])
    blk = 0
    mod_scope = K.scope()
    mod_scope.__enter__()
    wm = [K.sb([128, 8, 512], F32, "wm%d" % i) for i in range(2)]
    for l in range(L):
        for cb_ in range(12):
            wt, bw = wm[blk % 2]
            blk += 1
            K.dma("sp", [(wt, w_mod[l, :, cb_ * 512:(cb_ + 1) * 512].rearrange("(k p) c -> p k c", p=128))],
                  [], [bw], bw)
            pt, bp = K.psum()
            for c in range(4):
                for kc in range(8):
                    K.mm(pt[:, c * 2:c * 2 + 2], wt[:, kc, c * 128:(c + 1) * 128], sc_[:, kc, :],
                         kc == 0, kc == 7, [bw, b_sc], [bp])
            K.tt("dve", modT[:, l, cb_ * 4:(cb_ + 1) * 4, :],
                 pt[:, 0:8].rearrange("p (c s) -> p c s", s=2),
                 bmodT[:, l, cb_ * 4:(cb_ + 1) * 4].unsqueeze(2).to_broadcast([128, 4, 2]),
                 ALU.add, [bp, b_bmodT], [b_modT])
    K.barrier()
    mod_scope.__exit__(None, None, None)
    for l in range(L):
        K.stt(A1[:, l], modT[:, l, 8:16, :], 1.0, g1T[:, l, :].unsqueeze(2).to_broadcast([128, 8, 2]),
              ALU.add, ALU.mult, [b_modT, b_g1T], [b_A1])
        K.stt(A2[:, l], modT[:, l, 32:40, :], 1.0, g2T[:, l, :].unsqueeze(2).to_broadcast([128, 8, 2]),
              ALU.add, ALU.mult, [b_modT, b_g2T], [b_A2])

    def SH1(l, kc, s):
        return modT[:, l, 0 + kc, s:s + 1]

    def GT1(l, kc, s):
        return modT[:, l, 16 + kc, s:s + 1]

    def SH2(l, kc, s):
        return modT[:, l, 24 + kc, s:s + 1]

    def GT2(l, kc, s):
        return modT[:, l, 40 + kc, s:s + 1]

    lamT, b_lamT = K.sb([128, L, 4, 64], F32, "lamT")
    lamv, b_lamv = K.sb([128, L], F32, "lamv")
    neglam, b_neglam = K.sb([128, L], F32, "neglam")
    lsum, b_lsum = K.sb([128, L, 2], F32, "lsum")
    lprod, b_lprod = K.sb([128, L, 2, 64], F32, "lprod")
    K.dma("sp", [(lamT, lam_in)], [], [b_lamT], b_lamT)
    K.tt("dve", lprod[:, :, 0, :], lamT[:, :, 0, :], lamT[:, :, 1, :], ALU.mult, [b_lamT], [b_lprod])
    K.tt("dve", lprod[:, :, 1, :], lamT[:, :, 2, :], lamT[:, :, 3, :], ALU.mult, [b_lamT], [b_lprod])
    K.op("dve", lambda e: e.tensor_reduce(out=lsum, in_=lprod, op=ALU.add, axis=mybir.AxisListType.X),
         [b_lprod], [b_lsum])
    K.actf(lsum, lsum, AF.Exp, [b_lsum], [b_lsum])
    K.tt("dve", lamv, lsum[:, :, 0], lsum[:, :, 1], ALU.subtract, [b_lsum], [b_lamv])
    for l in range(L):
        K.ts("dve", lamv[:, l:l + 1], lamv[:, l:l + 1], float(lam_init_fn(l)), None, ALU.add, None, [b_lamv], [b_lamv])
    K.ts("dve", neglam, lamv, -1.0, None, ALU.mult, None, [b_lamv], [b_neglam])
    subgs, b_subgs = K.sb([128, L], F32, "subgs")
    for l in range(L):
        K.ts("dve", subgs[:, l:l + 1], subg[:, l:l + 1], float(1.0 - lam_init_fn(l)), None, ALU.mult, None,
             [b_subg], [b_subgs])

    K.barrier()

    class NS:
        pass
    S = NS()

    def alloc_shared():
        S.xt_ = [K.sb([128, 8, 512], F32, "xt%d" % i) for i in range(2)]
        S.hT, S.b_hT = K.sb([128, 8, 512], BF16, "hT")
        S.sq, S.b_sq = K.sb([128, 8, 512], F32, "sq")
        S.rstd, S.b_rstd = K.sb([128, 512], F32, "rstd")
        S.tmpn, S.b_tmpn = K.sb([128, 512], F32, "tmpn")
        S.wsl = [K.sb([128, 11, 512], BF16, "w%d" % i) for i in range(3)]
    wsl_i = [0]

    def wslot():
        s_ = S.wsl[wsl_i[0] % 3]
        wsl_i[0] += 1
        return s_

    def rstd_only(xt, bx):
        K.actf(S.sq, xt, AF.Square, [bx], [S.b_sq])
        pt, bp = K.psum()
        for kc in range(8):
            K.mm(pt, ones32, S.sq[:, kc, :], kc == 0, kc == 7, [b_ones32, S.b_sq], [bp])
        K.ts("dve", S.tmpn, pt, 1.0 / D, EPS, ALU.mult, ALU.add, [bp], [S.b_tmpn])
        K.actf(S.tmpn, S.tmpn, AF.Sqrt, [S.b_tmpn], [S.b_tmpn])
        K.op("dve", lambda e: e.reciprocal(out=S.rstd, in_=S.tmpn), [S.b_tmpn], [S.b_rstd])

    def norm_mod(xt, bx, Acol, Bcol, out_bf, b_out):
        rstd_only(xt, bx)
        for kc in range(8):
            K.stt(S.sq[:, kc, :], xt[:, kc, :], Acol(kc), S.rstd, ALU.mult, ALU.mult,
                  [bx, S.b_rstd, b_A1, b_A2], [S.b_sq])
            K.actf(out_bf[:, kc, :], S.sq[:, kc, :], AF.Identity, [S.b_sq, b_modT], [b_out], bias=Bcol(kc), scale=1.0)

    def load_w(src2d, k0, nk, col_runs):
        wt, bw = wslot()
        pairs = []
        off = 0
        for (c0, n) in col_runs:
            pairs.append((wt[:, 0:nk, off:off + n],
                          src2d[k0 * 128:(k0 + nk) * 128, c0:c0 + n].rearrange("(k p) c -> p k c", p=128)))
            off += n
        K.dma("sp", pairs, [b_wb], [bw], bw)
        return wt, bw

    cnt = {"qo": 0, "vo": 0, "uo": 0, "go": 0, "x": 0, "cs": 0}

    def phaseA(l):
        import os
        stopA = os.environ.get('MK_STOPA', '')
        scA = K.scope()
        scA.__enter__()
        alloc_shared()
        xt_ = S.xt_
        hT, b_hT = S.hT, S.b_hT
        csl = [K.sb([128, 2, 512], F32, "cs%d" % i) for i in range(2)]
        qb_, b_qb = K.sb([128, 512], BF16, "qb")
        qc_, b_qc = K.sb([128, 512], F32, "qc")
        qs_, b_qs = K.sb([128, 512], F32, "qs")
        qo_ = [K.sb([128, 4, 512], BF16, "qo%d" % i) for i in range(2)]
        vo_ = [K.sb([128, 4, 512], BF16, "vo%d" % i) for i in range(2)]
        uo_ = [K.sb([128, 4, 512], F32, "uo%d" % i) for i in range(2)]
        ul_ = [K.sb([128, 4, 8, 64], BF16, "ul%d" % i) for i in range(2)]
        go_ = [K.sb([128, 4, 512], BF16, "go%d" % i) for i in range(2)]
        wsrc = wb_in[l]
        for i in range(NT):
            s = i // (NT // 2)
            t0 = i * 512
            xt, bx = xt_[cnt["x"] % 2]
            cnt["x"] += 1
            src = x_in if l == 0 else xs
            rd = [] if l == 0 else [DB("xs", i)]
            K.dma("sp", [(xt, src[:, t0:t0 + 512].rearrange("(k p) t -> p k t", p=128))], rd, [bx], bx)
            cs, bcs = csl[cnt["cs"] % 2]
            cnt["cs"] += 1
            K.dma("sp", [(cs[:, 0, :], cos_in[:, t0:t0 + 512]), (cs[:, 1, :], sin_in[:, t0:t0 + 512])], [], [bcs], bcs)
            norm_mod(xt, bx, lambda kc: A1[:, l, kc, s:s + 1], lambda kc: SH1(l, kc, s), hT, b_hT)
            if stopA == 'n':
                continue
            for qk in range(2):
                wt, bw = load_w(wsrc, 0, 8, [(qk * 512, 512)])
                qo, bqo = qo_[cnt["qo"] % 2]
                cnt["qo"] += 1
                Y = int(os.environ.get("MK_Y", "9"))
                for c in range(4):
                    if Y < 1:
                        continue
                    pt, bp = K.psum()
                    for kc in range(8):
                        K.mm(pt, wt[:, kc, c * 128:(c + 1) * 128], hT[:, kc, :], kc == 0, kc == 7, [bw, b_hT], [bp])
                    if Y < 2:
                        continue
                    K.copy("act", qb_, pt, [bp], [b_qb])
                    if Y < 3:
                        continue
                    K.tt("dve", qc_, pt, cs[:, 0, :], ALU.mult, [bp, bcs] + ([b_qb] if os.environ.get("MK_Z") == "1" else []), [b_qc])
                    if Y < 4:
                        continue
                    p2, bp2 = K.psum()
                    K.mm(p2, perm_b, qb_, True, True, [b_perm, b_qb], [bp2])
                    if Y < 5:
                        continue
                    K.tt("dve", qs_, p2, cs[:, 1, :], ALU.mult, [bp2, bcs], [b_qs])
                    K.tt(os.environ.get("MK_QE", "pool"), qo[:, c, :], qc_, qs_, ALU.add, [b_qc, b_qs], [bqo])
                dst = QT if qk == 0 else KT
                if os.environ.get("MK_X") != "1":
                    K.dma("act", [(dst[:, :, t0:t0 + 512].rearrange("h p t -> p h t"), qo)], [bqo],
                          [DB("QT" if qk == 0 else "KT", i)], bqo)
            if stopA == 'qk':
                continue
            wt, bw = load_w(wsrc, 0, 8, [(1024, 512)])
            vo, bvo = vo_[cnt["vo"] % 2]
            cnt["vo"] += 1
            for tc in range(4):
                pt, bp = K.psum()
                for kc in range(8):
                    K.mm(pt, hT[:, kc, tc * 128:(tc + 1) * 128], wt[:, kc, 0:512], kc == 0, kc == 7, [bw, b_hT], [bp])
                K.copy("act", vo[:, tc, :], pt, [bp], [bvo])
            K.dma("act", [(VS[i * 4:(i + 1) * 4].rearrange("c p e -> p c e"), vo)], [bvo], [DB("VS", i)], bvo)
            if stopA == 'v':
                continue
            wt, bw = load_w(wsrc, 0, 8, [(1536, 512)])
            uo, buo = uo_[cnt["uo"] % 2]
            ul, bul = ul_[cnt["uo"] % 2]
            cnt["uo"] += 1
            for c in range(4):
                pt, bp = K.psum()
                for kc in range(8):
                    K.mm(pt, wt[:, kc, c * 128:(c + 1) * 128], hT[:, kc, :], kc == 0, kc == 7, [bw, b_hT], [bp])
                K.copy("act", uo[:, c, :], pt, [bp], [buo])
                K.copy("dve", ul[:, c], pt.rearrange("p (b s) -> p s b", s=8), [bp], [bul])
            K.dma("act", [(UT[:, t0:t0 + 512].rearrange("(c p) t -> p c t", p=128), uo)], [buo], [DB("UT", i)], buo)
            K.dma("act", [(XL[:, c * 128:(c + 1) * 128, i * 64:(i + 1) * 64].rearrange("s p b -> p s b"), ul[:, c])
                         for c in range(4)], [bul], [DB("XL", i)], bul)
            if stopA == 'u':
                continue
            for gb in range(4):
                wt, bw = load_w(wsrc, 0, 8, [(2048 + gb * 512, 512)])
                go, bgo = go_[cnt["go"] % 2]
                cnt["go"] += 1
                for c in range(4):
                    pt, bp = K.psum()
                    for kc in range(8):
                        K.mm(pt, wt[:, kc, c * 128:(c + 1) * 128], hT[:, kc, :], kc == 0, kc == 7, [bw, b_hT], [bp])
                    K.actf(go[:, c, :], pt, AF.Sigmoid, [bp], [bgo])
                dst = SGA if gb < 2 else SGS
                r0 = (gb % 2) * 512
                K.dma("act", [(dst[r0:r0 + 512, t0:t0 + 512].rearrange("(c p) t -> p c t", p=128), go)], [bgo],
                      [DB("SGA" if gb < 2 else "SGS", i * 2 + gb % 2)], bgo)

        K.barrier()
        scA.__exit__(None, None, None)

    cntB = {"q": 0, "p": 0, "ob": 0}
    scale = 64 ** -0.5

    def phaseB(l):
        scB = K.scope()
        scB.__enter__()
        kz0, b_kz0 = K.sb([128, T], BF16, "kz0")
        kz1, b_kz1 = K.sb([128, T], BF16, "kz1")
        vh_, b_vh = K.sb([128, T // 128, 128], BF16, "vh")
        qt_ = [K.sb([128, 512], BF16, "qt%d" % i) for i in range(2)]
        pT_ = [K.sb([128, 512], BF16, "pT%d" % i) for i in range(4)]
        rr_, b_rr = K.sb([128, 512], F32, "rr")
        t1_, b_t1 = K.sb([128, 512], F32, "t1")
        t2_, b_t2 = K.sb([128, 512], F32, "t2")
        od_, b_od = K.sb([128, 512], F32, "od")
        o2_, b_o2 = K.sb([128, 512], F32, "o2")
        ob_ = [K.sb([128, 512], BF16, "ob%d" % i) for i in range(2)]
        acs_, b_acs = K.sb([128, 512], F32, "acs")
        accP, b_accP = K.sb([128, 512], F32, "accP")
        NKC = T // 128
        for h in range(NH):
            if h == 0:
                K.memset("pool", kz0[64:128, :], 0.0, [b_kz0])
                K.memset("pool", kz1[0:64, :], 0.0, [b_kz1])
            K.dma("sp", [(kz0[0:64, :], KT[h, 0:64, :])], [DB("KT", i) for i in range(NT)], [b_kz0], b_kz0)
            K.dma("sp", [(kz1[64:128, :], KT[h, 64:128, :])], [DB("KT", i) for i in range(NT)], [b_kz1], b_kz1)
            K.dma("sp", [(vh_, VS[:, :, h * 128:(h + 1) * 128].rearrange("c p e -> p c e"))],
                  [DB("VS", i) for i in range(NT)], [b_vh], b_vh)
            for j in range(NT):
                sj = j // (NT // 2)
                qt, bq = qt_[cntB["q"] % 2]
                cntB["q"] += 1
                K.dma("sp", [(qt, QT[h, :, j * 512:(j + 1) * 512])], [DB("QT", j)], [bq], bq)
                for m in range(2):
                    hpo = K.psum_hold()
                    hpa = K.psum_hold()
                    po, bpo = hpo
                    pa, bpa = hpa

                    def emit_s(c):
                        ps_, bps = K.psum()
                        kz, bkz = (kz0, b_kz0) if m == 0 else (kz1, b_kz1)
                        K.mm(ps_, kz[:, c * 128:(c + 1) * 128], qt, True, True, [bkz, bq], [bps])
                        return ps_, bps
                    nxt = emit_s(0)
                    for c in range(NKC):
                        sc = (c * 128) // cfg.SEG
                        ps_, bps = nxt
                        if c + 1 < NKC:
                            nxt = emit_s(c + 1)
                        pT, bpT = pT_[cntB["p"] % 4]
                        cntB["p"] += 1
                        K.actf(pT, ps_, AF.Exp, [bps, b_crossb, b_zerob], [bpT],
                               bias=(zerob if sc == sj else crossb), scale=scale)
                        K.mm(po, vh_[:, c, :], pT, c == 0, c == NKC - 1, [b_vh, bpT], [bpo])
                        if c % 5 in (1, 3):
                            if c == 1:
                                K.copy("pool", accP, pT, [bpT], [b_accP])
                            else:
                                K.tt("pool", accP, accP, pT, ALU.add, [bpT, b_accP], [b_accP])
                        elif c == 0:
                            K.copy("dve", pa, pT, [bpT], [bpa])
                        else:
                            K.tt("dve", pa, pa, pT, ALU.add, [bpT, bpa], [bpa])
                    K.tt("dve", acs_, pa, accP, ALU.add, [bpa, b_accP], [b_acs])
                    pr, bpr = K.psum()
                    K.mm(pr, ones32, acs_, True, True, [b_ones32, b_acs], [bpr])
                    K.op("dve", lambda e: e.reciprocal(out=rr_, in_=pr), [bpr], [b_rr])
                    if m == 0:
                        K.tt("dve", t1_, po, rr_, ALU.mult, [bpo, b_rr], [b_t1])
                    else:
                        K.tt("dve", t2_, po, rr_, ALU.mult, [bpo, b_rr], [b_t2])
                    K.psum_release(hpo)
                    K.psum_release(hpa)
                K.stt(od_, t2_, neglam[:, l:l + 1], t1_, ALU.mult, ALU.add, [b_t1, b_t2, b_neglam], [b_od])
                K.actf(o2_, od_, AF.Square, [b_od], [b_o2])
                pq, bpq = K.psum()
                K.mm(pq, ones32, o2_, True, True, [b_ones32, b_o2], [bpq])
                K.ts("dve", rr_, pq, 1.0 / 128, EPS, ALU.mult, ALU.add, [bpq], [b_rr])
                K.actf(rr_, rr_, AF.Sqrt, [b_rr], [b_rr])
                K.op("dve", lambda e: e.reciprocal(out=t1_, in_=rr_), [b_rr], [b_t1])
                ob, bob = ob_[cntB["ob"] % 2]
                cntB["ob"] += 1
                K.stt(ob, od_, subgs[:, l:l + 1], t1_, ALU.mult, ALU.mult, [b_od, b_t1, b_subgs], [bob])
                K.dma("act", [(OT[h * 128:(h + 1) * 128, j * 512:(j + 1) * 512], ob)], [bob], [DB("OT", j)], bob)

        K.barrier()
        scB.__exit__(None, None, None)

    PG = [128, 32]
    sA = {}

    def pg(name, shape=None, dt=F32):
        if name not in sA:
            sA[name] = K.sb(shape or PG, dt, name)
        return sA[name]

    NLEV = int(math.log2(NB))
    cntC = {"x": 0, "y": 0}

    def ssm_precompute(l):
        sA.clear()
        scP = K.scope()
        scP.__enter__()
        are, b1 = pg("are"); aim, b2 = pg("aim"); ldt, b3 = pg("ldt")
        K.dma("sp", [(are, are_in[:, l, :])], [], [b1], b1)
        K.dma("sp", [(aim, aim_in[:, l, :])], [], [b2], b2)
        K.dma("sp", [(ldt, ldt_in[:, l, :])], [], [b3], b3)
        Br, bBr = pg("Br", [128, 32, 16]); Bi, bBi = pg("Bi", [128, 32, 16])
        Cr, bCr = pg("Cr", [128, 32, 16]); Ci, bCi = pg("Ci", [128, 32, 16])
        K.dma("sp", [(Br, bre_in[:, l])], [], [bBr], bBr)
        K.dma("sp", [(Bi, bim_in[:, l])], [], [bBi], bBi)
        K.dma("sp", [(Cr, cre_in[:, l])], [], [bCr], bCr)
        K.dma("sp", [(Ci, cim_in[:, l])], [], [bCi], bCi)
        dt_, bdt = pg("dt"); xr, bxr = pg("xr"); th, bth = pg("th"); mag, bmag = pg("mag")
        K.actf(dt_, ldt, AF.Exp, [b3], [bdt])
        K.tt("dve", xr, dt_, are, ALU.mult, [bdt, b1], [bxr])
        K.tt("dve", th, dt_, aim, ALU.mult, [bdt, b2], [bth])
        K.actf(mag, xr, AF.Exp, [bxr], [bmag])
        yv, byv = pg("yv"); ki, bki = pg("ki", PG, I32); kf, bkf = pg("kf"); mk, bmk = pg("mk")
        sn, bsn = pg("sn"); cs_, bcs_ = pg("cs")
        for (dst, bdst, off) in ((sn, bsn, 1.5), (cs_, bcs_, 1.75)):
            K.ts("dve", yv, th, 1.0 / TWO_PI, off, ALU.mult, ALU.add, [bth], [byv])
            K.copy("dve", ki, yv, [byv], [bki])
            K.copy("dve", kf, ki, [bki], [bkf])
            K.tt("dve", mk, kf, yv, ALU.is_gt, [bkf, byv], [bmk])
            K.tt("dve", kf, kf, mk, ALU.subtract, [bkf, bmk], [bkf])
            K.tt("dve", yv, yv, kf, ALU.subtract, [byv, bkf], [byv])
            K.ts("dve", yv, yv, -0.5, TWO_PI, ALU.add, ALU.mult, [byv], [byv])
            K.ts("dve", yv, yv, math.pi, -math.pi, ALU.min, ALU.max, [byv], [byv])
            K.actf(dst, yv, AF.Sin, [byv], [bdst])
        Ar, bAr = pg("Ar"); Ai, bAi = pg("Ai")
        stopC = os.environ.get('MK_STOPC', '')
        if stopC == 'sin':
            K.barrier(); scP.__exit__(None, None, None); return
        K.tt("dve", Ar, mag, cs_, ALU.mult, [bmag, bcs_], [bAr])
        K.tt("dve", Ai, mag, sn, ALU.mult, [bmag, bsn], [bAi])
        Par, bPar = pg("Par", [128, 32, 9]); Pai, bPai = pg("Pai", [128, 32, 9])
        Pdr, bPdr = pg("Pdr", [128, 32, 9]); Pdi, bPdi = pg("Pdi", [128, 32, 9])
        Qar, bQar = pg("Qar", [128, 32, 9]); Qai, bQai = pg("Qai", [128, 32, 9])
        Qdr, bQdr = pg("Qdr", [128, 32, 9]); Qdi, bQdi = pg("Qdi", [128, 32, 9])
        ta, bta = pg("ta"); tb, btb = pg("tb")
        K.memset("dve", Par[:, :, 0], 1.0, [bPar])
        K.memset("dve", Pai[:, :, 0], 0.0, [bPai])
        for n in range(1, 9):
            K.tt("dve", ta, Par[:, :, n - 1], Ar, ALU.mult, [bPar, bAr], [bta])
            K.tt("dve", tb, Pai[:, :, n - 1], Ai, ALU.mult, [bPai, bAi], [btb])
            K.tt("dve", Par[:, :, n], ta, tb, ALU.subtract, [bta, btb], [bPar])
            K.tt("dve", ta, Par[:, :, n - 1], Ai, ALU.mult, [bPar, bAi], [bta])
            K.tt("dve", tb, Pai[:, :, n - 1], Ar, ALU.mult, [bPai, bAr], [btb])
            K.tt("dve", Pai[:, :, n], ta, tb, ALU.add, [bta, btb], [bPai])
        e2, be2 = pg("e2")
        for n in range(9):
            K.actf(e2, xr, AF.Exp, [bxr], [be2], scale=-2.0 * n)
            K.tt("dve", Qar[:, :, n], Par[:, :, n], e2, ALU.mult, [bPar, be2], [bQar])
            K.stt(Qai[:, :, n], Pai[:, :, n], -1.0, e2, ALU.mult, ALU.mult, [bPai, be2], [bQai])
        for n in range(9):
            K.copy("pool", Pdr[:, :, 8 - n], Par[:, :, n], [bPar], [bPdr])
            K.copy("pool", Pdi[:, :, 8 - n], Pai[:, :, n], [bPai], [bPdi])
            K.copy("pool", Qdr[:, :, 8 - n], Qar[:, :, n], [bQar], [bQdr])
            K.copy("pool", Qdi[:, :, 8 - n], Qai[:, :, n], [bQai], [bQdi])
        K.copy("dve", S.SS[:, 0, 0, :], Par[:, :, 8], [bPar], [S.b_SS])
        K.copy("dve", S.SS[:, 0, 1, :], Pai[:, :, 8], [bPai], [S.b_SS])
        for k in range(NLEV):
            if k > 0:
                K.tt("dve", ta, S.SS[:, k - 1, 0, :], S.SS[:, k - 1, 0, :], ALU.mult, [S.b_SS], [bta])
                K.tt("dve", tb, S.SS[:, k - 1, 1, :], S.SS[:, k - 1, 1, :], ALU.mult, [S.b_SS], [btb])
                K.tt("dve", S.SS[:, k, 0, :], ta, tb, ALU.subtract, [bta, btb], [S.b_SS])
                K.stt(S.SS[:, k, 1, :], S.SS[:, k - 1, 0, :], 2.0, S.SS[:, k - 1, 1, :], ALU.mult, ALU.mult, [S.b_SS], [S.b_SS])
            K.ts("dve", S.SS[:, k, 2, :], S.SS[:, k, 1, :], -1.0, None, ALU.mult, None, [S.b_SS], [S.b_SS])
        K.ts("dve", S.SF, S.SS, flag[:, 0:1], None, ALU.mult, None, [S.b_SS, b_flag], [S.b_SF])
        if stopC == 'pow':
            K.barrier(); scP.__exit__(None, None, None); return
        nr, bnr = pg("nr"); den, bden = pg("den"); fr, bfr = pg("fr"); fi, bfi = pg("fi")
        K.ts("dve", nr, Ar, -1.0, None, ALU.add, None, [bAr], [bnr])
        K.tt("dve", ta, are, are, ALU.mult, [b1], [bta])
        K.tt("dve", tb, aim, aim, ALU.mult, [b2], [btb])
        K.tt("dve", den, ta, tb, ALU.add, [bta, btb], [bden])
        K.op("dve", lambda e: e.reciprocal(out=den, in_=den), [bden], [bden])
        K.tt("dve", ta, nr, are, ALU.mult, [bnr, b1], [bta])
        K.tt("dve", tb, Ai, aim, ALU.mult, [bAi, b2], [btb])
        K.tt("dve", ta, ta, tb, ALU.add, [bta, btb], [bta])
        K.tt("dve", fr, ta, den, ALU.mult, [bta, bden], [bfr])
        K.tt("dve", ta, Ai, are, ALU.mult, [bAi, b1], [bta])
        K.tt("dve", tb, nr, aim, ALU.mult, [bnr, b2], [btb])
        K.tt("dve", ta, ta, tb, ALU.subtract, [bta, btb], [bta])
        K.tt("dve", fi, ta, den, ALU.mult, [bta, bden], [bfi])
        Bbr, bBbr = pg("Bbr", [128, 32, 16]); Bbi, bBbi = pg("Bbi", [128, 32, 16])
        t16a, bt16a = pg("t16a", [128, 32, 16]); t16b, bt16b = pg("t16b", [128, 32, 16])
        frb = fr.unsqueeze(2).to_broadcast([128, 32, 16])
        fib = fi.unsqueeze(2).to_broadcast([128, 32, 16])
        K.tt("dve", t16a, Br, frb, ALU.mult, [bBr, bfr], [bt16a])
        K.tt("dve", t16b, Bi, fib, ALU.mult, [bBi, bfi], [bt16b])
        K.tt("dve", Bbr, t16a, t16b, ALU.subtract, [bt16a, bt16b], [bBbr])
        K.tt("dve", t16a, Bi, frb, ALU.mult, [bBi, bfr], [bt16a])
        K.tt("dve", t16b, Br, fib, ALU.mult, [bBr, bfi], [bt16b])
        K.tt("dve", Bbi, t16a, t16b, ALU.add, [bt16a, bt16b], [bBbi])
        def wtab(name, src0, bs0, o0, src1, bs1, o1):
            w, bw = pg(name, [128, 16, 2, 8])
            v0 = src0.rearrange("p (a d) n -> p a d n", d=2)
            v1 = src1.rearrange("p (a d) n -> p a d n", d=2)
            K.copy("pool", w[:, :, 0, :], v0[:, :, 0, o0:o0 + 8], [bs0], [bw])
            K.copy("pool", w[:, :, 1, :], v1[:, :, 1, o1:o1 + 8], [bs1], [bw])
            return w.rearrange("p a d n -> p (a d) n"), bw
        WBr, bWBr = wtab("WBr", Pdr, bPdr, 1, Par, bPar, 0)
        WBi, bWBi = wtab("WBi", Pdi, bPdi, 1, Pai, bPai, 0)
        WCr, bWCr = wtab("WCr", Qdr, bQdr, 1, Qar, bQar, 0)
        WCi, bWCi = wtab("WCi", Qdi, bQdi, 1, Qai, bQai, 0)
        WKr, bWKr = wtab("WKr", Par, bPar, 1, Pdr, bPdr, 0)
        WKi, bWKi = wtab("WKi", Pai, bPai, 1, Pdi, bPdi, 0)
        big = [128, 32, 8, 16]
        PBr, bPBr = pg("PBr", big); PBi, bPBi = pg("PBi", big)
        PCr, bPCr = pg("PCr", big); PCi, bPCi = pg("PCi", big)
        tg1, btg1 = pg("tg1", big); tg2, btg2 = pg("tg2", big)

        def cmul(outr, boutr, outi, bouti, Wr, bWr, Wi, bWi, Xr, bXr, Xi, bXi, neg_i=False):
            wr = Wr.unsqueeze(3).to_broadcast(big)
            wi = Wi.unsqueeze(3).to_broadcast(big)
            xr_ = Xr.unsqueeze(2).to_broadcast(big)
            xi_ = Xi.unsqueeze(2).to_broadcast(big)
            K.tt("dve", tg1, wr, xr_, ALU.mult, [bWr, bXr], [btg1])
            K.tt("dve", tg2, wi, xi_, ALU.mult, [bWi, bXi], [btg2])
            K.tt("dve", outr, tg1, tg2, ALU.subtract, [btg1, btg2], [boutr])
            K.tt("dve", tg1, wr, xi_, ALU.mult, [bWr, bXi], [btg1])
            K.tt("dve", tg2, wi, xr_, ALU.mult, [bWi, bXr], [btg2])
            if neg_i:
                K.stt(outi, tg1, -1.0, tg2, ALU.mult, ALU.subtract, [btg1, btg2], [bouti])
            else:
                K.tt("dve", outi, tg1, tg2, ALU.add, [btg1, btg2], [bouti])

        cmul(PBr, bPBr, PBi, bPBi, WBr, bWBr, WBi, bWBi, Bbr, bBbr, Bbi, bBbi)
        cmul(PCr, bPCr, PCi, bPCi, WCr, bWCr, WCi, bWCi, Cr, bCr, Ci, bCi, neg_i=True)
        if stopC == 'cmul':
            K.barrier(); scP.__exit__(None, None, None); return
        PBr3 = PBr.rearrange("p g n c -> p g (n c)")
        PBi3 = PBi.rearrange("p g n c -> p g (n c)")
        PCr3 = PCr.rearrange("p g n c -> p g (n c)")
        PCi3 = PCi.rearrange("p g n c -> p g (n c)")
        for gp in range(16):
            for d in range(2):
                pt, bp = K.psum()
                idx = 0
                for gpar in range(2):
                    for (src, bsrc) in ((PBr3, bPBr), (PBi3, bPBi)):
                        sl_ = slice(gpar * 64, gpar * 64 + 64)
                        K.mm(pt[:, idx * 64:(idx + 1) * 64], src[:, gp * 2 + d, :], ident[:, sl_], True, True,
                             [bsrc, b_ident], [bp])
                        idx += 1
                K.copy("act", S.PBT[:, gp, d].rearrange("p a b c -> p (a b c)"), pt[:, 0:256], [bp], [S.b_PBT])
        mt, bmt = pg("mt", [128, 128])
        tb1 = tg1.rearrange("p g n c -> p (g n c)").bitcast(BF16).rearrange("p (h g f) -> p h g f", h=2, g=32)
        tb2 = tg2.rearrange("p g n c -> p (g n c)").bitcast(BF16).rearrange("p (h g f) -> p h g f", h=2, g=32)
        PBrb, bPBrb, PBib, bPBib = tb1[:, 0], btg1, tb1[:, 1], btg1
        PCrb, bPCrb, PCib, bPCib = tb2[:, 0], btg2, tb2[:, 1], btg2
        K.copy("act", PBrb, PBr3, [bPBr], [bPBrb])
        K.copy("act", PBib, PBi3, [bPBi], [bPBib])
        K.copy("act", PCrb, PCr3, [bPCr], [bPCrb])
        K.copy("act", PCib, PCi3, [bPCi], [bPCib])
        for gp in range(16):
            for gpar in range(2):
                g = gp * 2 + gpar
                sl = slice(gpar * 64, gpar * 64 + 64)
                pt, bp = K.psum()
                for d in range(2):
                    o = pt[:, d * 128:(d + 1) * 128]
                    K.mm(o, PBrb[sl, gp * 2 + d, :], PCrb[sl, gp * 2 + d, :], True, False, [bPBrb, bPCrb], [bp])
                    K.mm(o, PBib[sl, gp * 2 + d, :], PCib[sl, gp * 2 + d, :], False, True, [bPBib, bPCib], [bp])
                K.tt("dve", mt, pt[:, 0:128], maskf, ALU.mult, [bp, b_maskf], [bmt])
                K.tt("dve", tmpc, pt[:, 128:256], maskb, ALU.mult, [bp, b_maskb], [b_tmpc])
                K.tt("dve", S.M0[:, g, :], mt, tmpc, ALU.add, [bmt, b_tmpc], [S.b_M0])

        PKr, bPKr, PKi, bPKi = PCr, bPCr, PCi, bPCi
        cmul(PKr, bPKr, PKi, bPKi, WKr, bWKr, WKi, bWKi, Cr, bCr, Ci, bCi, neg_i=True)
        K.copy("act", S.PCC[:, :, :, 0, :], PKr.rearrange("p (a d) n c -> p a d (n c)", d=2), [bPKr], [S.b_PCC])
        K.copy("act", S.PCC[:, :, :, 1, :], PKi.rearrange("p (a d) n c -> p a d (n c)", d=2), [bPKi], [S.b_PCC])
        K.barrier()
        scP.__exit__(None, None, None)

    def hs_scan(gp, d):
        col = gp * 2 + d
        cur = 0
        for k in range(NLEV):
            s = 1 << k
            src, bsrc = S.Hb[cur]
            dst, bdst = S.Hb[1 - cur]

            def scal(tab, j):
                return tab[:, k, j, col:col + 1]

            def region(lo, hi, tab, btab, two_seg=False):
                sh = -s if d == 0 else s
                if two_seg:
                    def v(t, ri, off):
                        return t[:, ri, :].rearrange("p (g n) -> p g n", g=2)[:, :, lo + off:hi + off]
                else:
                    def v(t, ri, off):
                        return t[:, ri, lo + off:hi + off]
                rd = [bsrc, btab]
                K.stt(v(dst, 0, 0), v(src, 0, sh), scal(tab, 0), v(src, 0, 0), ALU.mult, ALU.add, rd, [bdst])
                K.stt(v(dst, 1, 0), v(src, 1, sh), scal(tab, 0), v(src, 1, 0), ALU.mult, ALU.add, rd, [bdst])
                K.stt(v(dst, 0, 0), v(src, 1, sh), scal(tab, 2), v(dst, 0, 0), ALU.mult, ALU.add, rd + [bdst], [bdst])
                K.stt(v(dst, 1, 0), v(src, 0, sh), scal(tab, 1), v(dst, 1, 0), ALU.mult, ALU.add, rd + [bdst], [bdst])

            if d == 0:
                K.copy("pool", dst[:, :, 0:min(s, NBS)], src[:, :, 0:min(s, NBS)], [bsrc], [bdst])
                if s < NBS:
                    region(s, NBS, S.SS, S.b_SS, two_seg=True)
                region(NBS, min(NBS + s, NB), S.SF, S.b_SF)
                if s >= NBS and s < NB:
                    pass
            else:
                lo0 = max(NB - s, NBS)
                K.copy("pool", dst[:, :, lo0:NB], src[:, :, lo0:NB], [bsrc], [bdst])
                if s < NBS:
                    region(0, NBS - s, S.SS, S.b_SS, two_seg=True)
                region(max(NBS - s, 0), NBS, S.SF, S.b_SF)
            cur = 1 - cur
        return cur

    def phaseC(l):
        stopC = os.environ.get('MK_STOPC', '')
        scC = K.scope()
        scC.__enter__()
        S.PBT, S.b_PBT = K.sb([128, 16, 2, 2, 2, 64], BF16, "PBT")
        S.PCC, S.b_PCC = K.sb([128, 16, 2, 2, 128], BF16, "PCC")
        S.M0, S.b_M0 = K.sb([128, 32, 128], BF16, "M0")
        S.SS, S.b_SS = K.sb([128, 11, 3, 32], F32, "SS")
        S.SF, S.b_SF = K.sb([128, 11, 3, 32], F32, "SF")
        ssm_precompute(l)
        if stopC in ('sin', 'pow', 'cmul', 'tr', 'pre'):
            K.barrier(); scC.__exit__(None, None, None); return
        S.Hb = [K.sb([128, 2, NB], F32, "H%d" % i) for i in range(2)]
        S.Hin, S.b_Hin = K.sb([128, 2, 2, NB], BF16, "Hin")
        S.xg_ = [K.sb([128, NB], BF16, "xg%d" % i) for i in range(4)]
        S.yg_ = [K.sb([128, NB], F32, "yg%d" % i) for i in range(2)]
        NBT = (NB + 511) // 512
        bw = min(512, NB)
        for gp in range(16):
            xg = []
            for gpar in range(2):
                g = gp * 2 + gpar
                xt, bx = S.xg_[cntC["x"] % 4]
                cntC["x"] += 1
                K.dma("sp", [(xt[s2 * 16:(s2 + 1) * 16, :], XL[s2, g * 16:(g + 1) * 16, :]) for s2 in range(8)],
                      [DB("XL", i) for i in range(NT)], [bx], bx)
                xg.append((xt, bx))
            for d in range(2):
                H0, bH0 = S.Hb[0]
                for nt in range(NBT):
                    for ri in range(2):
                        pt, bp = K.psum()
                        for gpar in range(2):
                            K.mm(pt[gpar * 64:(gpar + 1) * 64, 0:bw], S.PBT[:, gp, d, gpar, ri, :],
                                 xg[gpar][0][:, nt * 512:nt * 512 + bw], True, True, [S.b_PBT, xg[gpar][1]], [bp])
                        K.copy("act", H0[:, ri, nt * 512:nt * 512 + bw], pt[:, 0:bw], [bp], [bH0])
                cur = hs_scan(gp, d)
                Hf, bHf = S.Hb[cur]
                if d == 0:
                    K.memset("pool", S.Hin[:, 0, :, 0:1], 0.0, [S.b_Hin])
                    K.copy("act", S.Hin[:, 0].rearrange("p r (g n) -> p r g n", g=2)[:, :, :, 1:NBS],
                           Hf.rearrange("p r (g n) -> p r g n", g=2)[:, :, :, 0:NBS - 1], [bHf], [S.b_Hin])
                    K.ts("dve", S.Hin[:, 0, :, NBS:NBS + 1], Hf[:, :, NBS - 1:NBS], flag[:, 0:1], None, ALU.mult, None,
                         [bHf, b_flag], [S.b_Hin])
                else:
                    K.memset("pool", S.Hin[:, 1, :, NB - 1:NB], 0.0, [S.b_Hin])
                    K.copy("act", S.Hin[:, 1].rearrange("p r (g n) -> p r g n", g=2)[:, :, :, 0:NBS - 1],
                           Hf.rearrange("p r (g n) -> p r g n", g=2)[:, :, :, 1:NBS], [bHf], [S.b_Hin])
                    K.ts("dve", S.Hin[:, 1, :, NBS - 1:NBS], Hf[:, :, NBS:NBS + 1], flag[:, 0:1], None, ALU.mult, None,
                         [bHf, b_flag], [S.b_Hin])
            for gpar in range(2):
                g = gp * 2 + gpar
                sl = slice(gpar * 64, gpar * 64 + 64)
                yt, by = S.yg_[cntC["y"] % 2]
                cntC["y"] += 1
                for nt in range(NBT):
                    cs = slice(nt * 512, nt * 512 + bw)
                    pt, bp = K.psum()
                    K.mm(pt[:, 0:bw], S.M0[:, g, :], xg[gpar][0][:, cs], True, False, [S.b_M0, xg[gpar][1]], [bp])
                    for d in range(2):
                        for ri in range(2):
                            K.mm(pt[:, 0:bw], S.PCC[sl, gp, d, ri, :], S.Hin[sl, d, ri, cs], False,
                                 (d == 1 and ri == 1), [S.b_PCC, S.b_Hin], [bp])
                    K.copy("act", yt[:, cs], pt[:, 0:bw], [bp], [by])
                K.dma("act", [(YL[t2, g * 16:(g + 1) * 16, :], yt[t2 * 16:(t2 + 1) * 16, :]) for t2 in range(8)],
                      [by], [DB("YL", g)], by)
        K.barrier()
        scC.__exit__(None, None, None)

    cntD = {"i": 0}

    def phaseD(l, last):
        scD = K.scope()
        scD.__enter__()
        alloc_shared()
        xt_ = S.xt_
        oT_ = [K.sb([128, 4, 512], BF16, "oT%d" % i) for i in range(1)]
        sg_ = [K.sb([128, 2, 8, 512], BF16, "sg%d" % i) for i in range(1)]
        yl_ = [K.sb([128, 4, 8, 64], F32, "yl%d" % i) for i in range(1)]
        ud_ = [K.sb([128, 4, 512], F32, "ud%d" % i) for i in range(1)]
        zt_, b_zt = K.sb([128, 4, 512], BF16, "zt")
        m1_, b_m1 = K.sb([128, 8, 512], F32, "m1")
        mg_, b_mg = K.sb([128, 8, 512], BF16, "mg")
        h2_, b_h2 = K.sb([128, 8, 512], BF16, "h2")
        aT_, b_aT = K.sb([128, 22, 512], BF16, "aT")
        ga_, b_ga = K.sb([128, 512], F32, "ga")
        gb_, b_gb = K.sb([128, 512], F32, "gb")
        gc_, b_gc = K.sb([128, 512], F32, "gc")
        for i in range(NT):
            s = i // (NT // 2)
            t0 = i * 512
            par = 0
            cntD["i"] += 1
            xt, bx = xt_[cnt["x"] % 2]
            cnt["x"] += 1
            src = x_in if l == 0 else xs
            rd = [] if l == 0 else [DB("xs", i)]
            K.dma("sp", [(xt, src[:, t0:t0 + 512].rearrange("(k p) t -> p k t", p=128))], rd, [bx], bx)
            oT, boT = oT_[par]
            K.dma("sp", [(oT, OT[:, t0:t0 + 512].rearrange("(c p) t -> p c t", p=128))], [DB("OT", i)], [boT], boT)
            sg, bsg = sg_[par]
            K.dma("sp", [(sg[:, 0], SGA[:, t0:t0 + 512].rearrange("(c p) t -> p c t", p=128)),
                         (sg[:, 1], SGS[:, t0:t0 + 512].rearrange("(c p) t -> p c t", p=128))],
                  [DB("SGA", i * 2), DB("SGA", i * 2 + 1), DB("SGS", i * 2), DB("SGS", i * 2 + 1)], [bsg], bsg)
            yl, byl = yl_[par]
            K.dma("sp", [(yl[:, c], YL[:, c * 128:(c + 1) * 128, i * 64:(i + 1) * 64].rearrange("t p b -> p t b"))
                         for c in range(4)], [DB("YL", g) for g in range(NG)], [byl], byl)
            ud, bud = ud_[par]
            K.dma("sp", [(ud, UT[:, t0:t0 + 512].rearrange("(c p) t -> p c t", p=128))], [DB("UT", i)], [bud], bud)
            for c in range(4):
                yv = yl[:, c].rearrange("p t b -> p b t")
                g3 = ga_.rearrange("p (b t) -> p b t", t=8)
                K.stt(g3, ud[:, c, :].rearrange("p (b t) -> p b t", t=8), dsk[:, l, c:c + 1], yv, ALU.mult, ALU.add,
                      [bud, byl, b_dsk], [b_ga])
                K.actf(gb_, ga_, AF.Square, [b_ga], [b_gb])
                K.ts("dve", gb_, gb_, 0.044715, 1.0, ALU.mult, ALU.add, [b_gb], [b_gb])
                K.tt("dve", gb_, gb_, ga_, ALU.mult, [b_gb, b_ga], [b_gb])
                K.actf(gc_, gb_, AF.Sigmoid, [b_gb], [b_gc], scale=1.5957691216057308)
                K.tt("dve", zt_[:, c, :], ga_, gc_, ALU.mult, [b_ga, b_gc], [b_zt])
            for blk_ in range(2):
                wt, bw = load_w(wb_attn[l], 0, 4, [(blk_ * 512, 512)])
                for c in range(4):
                    pt, bp = K.psum()
                    for kc in range(4):
                        K.mm(pt, wt[:, kc, c * 128:(c + 1) * 128], oT[:, kc, :], kc == 0, kc == 3, [bw, boT], [bp])
                    K.tt("dve", m1_[:, blk_ * 4 + c, :], pt, sg[:, 0, blk_ * 4 + c, :], ALU.mult, [bp, bsg], [b_m1])
            for blk_ in range(4):
                wt, bw = load_w(wb_glu[l], 0, 4, [(blk_ * 256, 256), (1024 + blk_ * 256, 256)])
                for c in range(2):
                    j = blk_ * 2 + c
                    pl, bpl = K.psum()
                    for kc in range(4):
                        K.mm(pl, wt[:, kc, c * 128:(c + 1) * 128], zt_[:, kc, :], kc == 0, kc == 3, [bw, b_zt], [bpl])
                    pg_, bpg = K.psum()
                    for kc in range(4):
                        K.mm(pg_, wt[:, kc, 256 + c * 128:256 + (c + 1) * 128], zt_[:, kc, :], kc == 0, kc == 3,
                             [bw, b_zt], [bpg])
                    K.actf(ga_, pg_, AF.Sigmoid, [bpg, b_bglu], [b_ga], bias=bglu[:, l, 8 + j:9 + j], scale=1.0)
                    K.stt(gb_, pl, bglu[:, l, j:j + 1], ga_, ALU.add, ALU.mult, [bpl, b_bglu, b_ga], [b_gb])
                    K.tt("dve", gb_, gb_, sg[:, 1, j, :], ALU.mult, [b_gb, bsg], [b_gb])
                    K.tt("dve", mg_[:, j, :], gb_, m1_[:, j, :], ALU.add, [b_gb, b_m1], [b_mg])
            for blk_ in range(2):
                wt, bw = load_w(wb_o[l], 0, 8, [(blk_ * 512, 512)])
                for c in range(4):
                    j = blk_ * 4 + c
                    pt, bp = K.psum()
                    for kc in range(8):
                        K.mm(pt, wt[:, kc, c * 128:(c + 1) * 128], mg_[:, kc, :], kc == 0, kc == 7, [bw, b_mg], [bp])
                    K.stt(xt[:, j, :], pt, GT1(l, j, s), xt[:, j, :], ALU.mult, ALU.add, [bp, b_modT, bx], [bx])
            norm_mod(xt, bx, lambda kc: A2[:, l, kc, s:s + 1], lambda kc: SH2(l, kc, s), h2_, b_h2)
            for blk_ in range(11):
                wt, bw = load_w(wb_ffi[l], 0, 8, [(blk_ * 256, 256), (DFF + blk_ * 256, 256)])
                for c in range(2):
                    j = blk_ * 2 + c
                    pgt, bpg = K.psum()
                    for kc in range(8):
                        K.mm(pgt, wt[:, kc, c * 128:(c + 1) * 128], h2_[:, kc, :], kc == 0, kc == 7, [bw, b_h2], [bpg])
                    pu, bpu = K.psum()
                    for kc in range(8):
                        K.mm(pu, wt[:, kc, 256 + c * 128:256 + (c + 1) * 128], h2_[:, kc, :], kc == 0, kc == 7,
                             [bw, b_h2], [bpu])
                    K.actf(ga_, pgt, AF.Silu, [bpg], [b_ga])
                    K.tt("dve", aT_[:, j, :], pu, ga_, ALU.mult, [bpu, b_ga], [b_aT])
            for half in range(2):
                wa, bwa = load_w(wb_ffo[l], 0, 11, [(half * 512, 512)])
                wb2, bwb2 = load_w(wb_ffo[l], 11, 11, [(half * 512, 512)])
                for c in range(4):
                    j = half * 4 + c
                    pt, bp = K.psum()
                    for kc in range(22):
                        w_, bw_ = (wa, bwa) if kc < 11 else (wb2, bwb2)
                        K.mm(pt, w_[:, kc % 11, c * 128:(c + 1) * 128], aT_[:, kc, :], kc == 0, kc == 21,
                             [bw_, b_aT], [bp])
                    K.stt(xt[:, j, :], pt, GT2(l, j, s), xt[:, j, :], ALU.mult, ALU.add, [bp, b_modT, bx], [bx])
            if not last:
                K.dma("act", [(xs[:, t0:t0 + 512].rearrange("(k p) t -> p k t", p=128), xt)], [bx], [DB("xs", i)], bx)
            else:
                rstd_only(xt, bx)
                for kc in range(8):
                    K.stt(xt[:, kc, :], xt[:, kc, :], gfT[:, kc:kc + 1], S.rstd, ALU.mult, ALU.mult,
                          [bx, S.b_rstd, b_gfT], [bx])
                K.dma("act", [(y_out[:, t0:t0 + 512].rearrange("(k p) t -> p k t", p=128), xt)], [bx], [DB("y", i)], bx)

        K.barrier()
        scD.__exit__(None, None, None)

    import os
    stop = os.environ.get("MK_STOP", "")
    for l in range(L):
        if l == 0:
            for cb in cvb:
                b_wb.w.update(cb.w)
        if stop == "pro":
            break
        phaseA(l)
        K.barrier()
        if stop == "A":
            break
        phaseB(l)
        K.barrier()
        if stop == "B":
            break
        phaseC(l)
        K.barrier()
        if stop == "C":
            break
        phaseD(l, l == L - 1)
        K.barrier()
    K.barrier()
    return nc


def _rope_tables(T_seq):
    inv = 1.0 / (10000.0 ** (np.arange(0, 64, 2, dtype=np.float32) / 64.0))
    ang = np.arange(T_seq, dtype=np.float32)[:, None] * inv[None, :].astype(np.float32)
    ang = np.concatenate([ang, ang], axis=-1).astype(np.float32)
    return np.cos(ang).astype(np.float32), np.sin(ang).astype(np.float32)


def _consts():
    perm = np.zeros((128, 128), np.float32)
    for m in range(2):
        for d in range(64):
            perm[m * 64 + (d + 32) % 64, m * 64 + d] = 1.0
    ident = np.eye(128, dtype=np.float32)
    s2 = np.arange(128) // 16
    maskf = (s2[None, :] >= s2[:, None]).astype(np.float32)
    maskb = (s2[None, :] <= s2[:, None]).astype(np.float32)
    return perm, ident, maskf, maskb


def _pl(a, L):
    rest = a.shape[4:]
    a = a.reshape((L, 2, 16, 2, 64) + rest)
    a = np.moveaxis(a, (3, 4, 0, 2, 1), (0, 1, 2, 3, 4))
    return np.ascontiguousarray(a.reshape((128, L, 32) + rest)).astype(np.float32)


def make_in_maps(inp, cfg, core_seqs):
    L, T = cfg.L, cfg.T
    perm, ident, maskf, maskb = _consts()

    def fm(v, nch):
        v = np.asarray(v, np.float32)
        lead = v.shape[:-1]
        v = v.reshape(lead + (nch, 128))
        v = np.moveaxis(v, -1, 0)
        return np.ascontiguousarray(v)

    shared = {
        "perm": perm, "ident": ident, "maskf": maskf, "maskb": maskb,
        "w_mod": np.ascontiguousarray(inp["w_mod"][:L], np.float32),
        "b_modT": fm(inp["b_mod"][:L], 48),
        "g1T": fm(inp["norm1_g"][:L], 8), "g2T": fm(inp["norm2_g"][:L], 8), "gfT": fm(inp["final_g"], 8),
        "w_in": np.ascontiguousarray(inp["w_in"][:L], np.float32),
        "lamT": np.ascontiguousarray(np.broadcast_to(
            np.stack([inp["lam_q1"][:L], inp["lam_k1"][:L], inp["lam_q2"][:L], inp["lam_k2"][:L]], axis=1)[None],
            (128, L, 4, 64)), np.float32),
        "subgT": np.ascontiguousarray(np.asarray(inp["subln_g"][:L], np.float32).T),
        "w_attn": np.ascontiguousarray(inp["w_attn_br"][:L], np.float32),
        "ssm_dT": fm(inp["ssm_d"][:L], 4),
        "w_glu": np.ascontiguousarray(inp["w_glu"][:L], np.float32),
        "b_gluT": fm(inp["b_glu"][:L], 16),
        "w_o": np.ascontiguousarray(inp["w_o"][:L], np.float32),
        "w_ffi": np.ascontiguousarray(inp["w_ffn_in"][:L], np.float32),
        "w_ffo": np.ascontiguousarray(inp["w_ffn_out"][:L], np.float32),
    }
    are = np.asarray(inp["ssm_a_re"][:L], np.float32)
    aim = np.asarray(inp["ssm_a_im"][:L], np.float32)
    ldt = np.broadcast_to(np.asarray(inp["ssm_log_dt"][:L], np.float32)[..., None], (L, 2, 32, 64))
    shared["a_reP"] = _pl(are, L)
    shared["a_imP"] = _pl(aim, L)
    shared["ldtP"] = _pl(np.ascontiguousarray(ldt), L)
    shared["b_reP"] = _pl(np.asarray(inp["ssm_b_re"][:L], np.float32), L)
    shared["b_imP"] = _pl(np.asarray(inp["ssm_b_im"][:L], np.float32), L)
    shared["c_reP"] = _pl(np.swapaxes(np.asarray(inp["ssm_c_re"][:L], np.float32), 3, 4), L)
    shared["c_imP"] = _pl(np.swapaxes(np.asarray(inp["ssm_c_im"][:L], np.float32), 3, 4), L)
    maps = []
    for (x, c, pos, split) in core_seqs:
        cos, sin = _rope_tables(int(pos.max()) + 1)
        cosT = np.ascontiguousarray(np.tile(cos[pos].T, (2, 1)))
        sgn = np.where(np.arange(64) < 32, -1.0, 1.0).astype(np.float32)
        sinT = np.ascontiguousarray(np.tile((sin[pos] * sgn[None, :]).T, (2, 1)))
        m = dict(shared)
        m["xT"] = np.ascontiguousarray(x.T)
        m["cT"] = np.ascontiguousarray(np.moveaxis(np.asarray(c, np.float32).reshape(2, 8, 128), (0, 1, 2), (2, 1, 0)))
        m["cosT"] = cosT.astype(np.float32)
        m["sinT"] = sinT.astype(np.float32)
        m["crossbias"] = np.full((128, 1), 0.0 if split else -30000.0, np.float32)
        m["flag"] = np.full((128, 1), 1.0 if split else 0.0, np.float32)
        maps.append(m)
    return maps


_NC_CACHE = {}


def run(inp, cfg, core_seqs):
    key = (cfg.T, cfg.L)
    if key not in _NC_CACHE:
        _NC_CACHE[key] = build(cfg)
    nc = _NC_CACHE[key]
    maps = make_in_maps(inp, cfg, core_seqs)
    res = run_bass_kernel_spmd(nc, maps, core_ids=list(range(len(maps))))
    return [np.asarray(r["yT"]).T for r in res.results]


def kernel(**inp):
    cfg = Cfg(8192, 4)
    xp = np.asarray(inp["x_prompt"], np.float32)
    xsm = np.asarray(inp["x_sample"], np.float32)
    cp = np.asarray(inp["c_prompt"], np.float32)
    csm = np.asarray(inp["c_sample"], np.float32)
    cores = []
    pos_p = np.concatenate([np.arange(4096), np.arange(4096)])
    for c in range(4):
        cores.append((np.concatenate([xp[2 * c], xp[2 * c + 1]], axis=0), np.stack([cp[2 * c], cp[2 * c + 1]]),
                      pos_p, False))
    for c in range(4):
        cores.append((xsm[c], np.stack([csm[c], csm[c]]), np.arange(8192), True))
    outs = run(inp, cfg, cores)
    yp = np.empty((8, 4096, D), np.float32)
    for c in range(4):
        yp[2 * c] = outs[c][:4096]
        yp[2 * c + 1] = outs[c][4096:]
    ys = np.stack([outs[4 + c] for c in range(4)], axis=0).astype(np.float32)
    return (yp, ys)
```

```python
import math
import contextlib
import numpy as np
import concourse.bass as bass
import concourse.mybir as mybir
from concourse.bass_utils import run_bass_kernel_spmd

F32 = mybir.dt.float32
BF16 = mybir.dt.bfloat16
I32 = mybir.dt.int32
ALU = mybir.AluOpType
AF = mybir.ActivationFunctionType

D = 1024
NH = 4
DFF = 2816
DSSM = 512
NG = 32
EPS = 1e-6
TWO_PI = 2.0 * math.pi


def lam_init_fn(layer):
    return 0.8 - 0.6 * math.exp(-0.3 * layer)


class Buf:
    __slots__ = ("name", "w", "r", "sem", "cnt", "excl")

    def __init__(self, name, excl=False):
        self.name = name
        self.excl = excl
        self.w = {}
        self.r = {}
        self.sem = None
        self.cnt = 0


class EngState:
    def __init__(self, eng, sem, self_sync):
        self.eng = eng
        self.sem = sem
        self.cnt = 0
        self.waited = {}
        self.self_sync = self_sync


def _merge(d, src):
    for k, (s, v) in src.items():
        if k not in d or d[k][1] < v:
            d[k] = (s, v)


class KB:
    def __init__(self, nc):
        self.nc = nc
        self.engs = {}
        for name, e in (("pe", nc.tensor), ("dve", nc.vector), ("act", nc.scalar),
                        ("pool", nc.gpsimd), ("sp", nc.sync)):
            self.engs[name] = EngState(e, nc.alloc_semaphore("e_" + name), name != "pe")
        self.dma_bufs = []
        self.nalloc = 0
        self.stacks = []
        self.scope_bufs = []
        self.free_sems = []
        self.retired = {}
        self.nsem = 0
        self.ps = []
        for i in range(8):
            t = nc.alloc_psum_tensor("psb%d" % i, [128, 512], F32)
            self.ps.append((t.ap(), Buf("ps%d" % i, excl=True)))
        self.ps_i = 0

    def sb(self, shape, dt, name=None):
        self.nalloc += 1
        nm = "%s_%d" % (name or "t", self.nalloc)
        if self.stacks:
            t = self.stacks[-1].enter_context(self.nc.sbuf_tensor(nm, list(shape), dt))
        else:
            t = self.nc.alloc_sbuf_tensor(nm, list(shape), dt)
        b = Buf(name or "t")
        if self.scope_bufs:
            self.scope_bufs[-1].append(b)
        return (t.ap() if hasattr(t, "ap") and callable(t.ap) else t[:]), b

    @contextlib.contextmanager
    def scope(self):
        st = contextlib.ExitStack()
        self.stacks.append(st)
        self.scope_bufs.append([])
        try:
            yield
        finally:
            self.stacks.pop()
            for b in self.scope_bufs.pop():
                if b.sem is not None:
                    self.free_sems.append((b.sem, b.cnt))
                    self.dma_bufs.remove(b)
                    self.retired[id(b.sem)] = (b.sem, b.cnt)
                    b.sem = None
            st.close()

    def dram(self, name, shape, dt):
        return self.nc.dram_tensor(name, list(shape), dt, kind="Internal").ap()

    def psum(self):
        p = self.ps.pop(0)
        self.ps.append(p)
        return p

    def psum_hold(self):
        return self.ps.pop(0)

    def psum_release(self, p):
        self.ps.append(p)

    def _deps(self, reads, writes):
        d = {}
        for b in reads:
            _merge(d, b.w)
            if b.excl:
                _merge(d, b.r)
        for b in writes:
            _merge(d, b.w)
            _merge(d, b.r)
        return d

    def _wait(self, E, deps):
        for k, (sem, val) in deps.items():
            if sem is E.sem and not E.self_sync:
                continue
            if E.waited.get(k, 0) < val:
                E.eng.wait_ge(sem, val)
                E.waited[k] = val

    def _record(self, tok, reads, writes):
        k = id(tok[0])
        for b in reads:
            if k not in b.r or b.r[k][1] < tok[1]:
                b.r[k] = tok
        for b in writes:
            b.w = {k: tok}
            b.r = {}

    def op(self, ename, fn, reads=(), writes=()):
        E = self.engs[ename]
        self._wait(E, self._deps(reads, writes))
        ins = fn(E.eng)
        E.cnt += 1
        ins.then_inc(E.sem, 1)
        self._record((E.sem, E.cnt), reads, writes)

    def dma(self, ename, pairs, reads, writes, sbuf):
        E = self.engs[ename]
        if sbuf.sem is None:
            if self.free_sems:
                sbuf.sem, sbuf.cnt = self.free_sems.pop()
                self.retired.pop(id(sbuf.sem), None)
            else:
                self.nsem += 1
                sbuf.sem = self.nc.alloc_semaphore("d_%d" % self.nsem)
            self.dma_bufs.append(sbuf)
        deps = self._deps(reads, writes)
        if sbuf.cnt > 0:
            _merge(deps, {id(sbuf.sem): (sbuf.sem, sbuf.cnt)})
        self._wait(E, deps)
        for (o, i) in pairs:
            E.eng.dma_start(out=o, in_=i).then_inc(sbuf.sem, 16)
            sbuf.cnt += 16
        self._record((sbuf.sem, sbuf.cnt), reads, writes)

    def barrier(self):
        toks = {}
        for E in self.engs.values():
            if E.cnt:
                toks[id(E.sem)] = (E.sem, E.cnt)
        for b in self.dma_bufs:
            if b.cnt:
                toks[id(b.sem)] = (b.sem, b.cnt)
        for k, tok in self.retired.items():
            toks.setdefault(k, tok)
        for E in self.engs.values():
            for k, (sem, val) in toks.items():
                if sem is E.sem:
                    continue
                if E.waited.get(k, 0) < val:
                    E.eng.wait_ge(sem, val)
                    E.waited[k] = val

    def mm(self, out, lhsT, rhs, start, stop, reads, writes):
        self.op("pe", lambda e: e.matmul(out, lhsT, rhs, start=start, stop=stop), reads, writes)

    def tt(self, eng, out, in0, in1, op, reads, writes):
        self.op(eng, lambda e: e.tensor_tensor(out=out, in0=in0, in1=in1, op=op), reads, writes)

    def ts(self, eng, out, in0, s1, s2, op0, op1, reads, writes):
        if op1 is None:
            self.op(eng, lambda e: e.tensor_scalar(out=out, in0=in0, scalar1=s1, scalar2=None, op0=op0), reads, writes)
        else:
            self.op(eng, lambda e: e.tensor_scalar(out=out, in0=in0, scalar1=s1, scalar2=s2, op0=op0, op1=op1), reads, writes)

    def stt(self, out, in0, scalar, in1, op0, op1, reads, writes):
        self.op("dve", lambda e: e.scalar_tensor_tensor(out=out, in0=in0, scalar=scalar, in1=in1, op0=op0, op1=op1), reads, writes)

    def actf(self, out, in_, func, reads, writes, bias=None, scale=None):
        kw = {}
        if bias is not None:
            kw["bias"] = bias
        if scale is not None:
            kw["scale"] = scale
        self.op("act", lambda e: e.activation(out=out, in_=in_, func=func, **kw), reads, writes)

    def copy(self, eng, out, in_, reads, writes):
        if eng == "act":
            self.op("act", lambda e: e.activation(out=out, in_=in_, func=AF.Identity), reads, writes)
        else:
            self.op(eng, lambda e: e.tensor_copy(out=out, in_=in_), reads, writes)

    def memset(self, eng, ap, val, writes):
        self.op(eng, lambda e: e.memset(ap, val), (), writes)


class Cfg:
    def __init__(self, T, L):
        self.T = T
        self.L = L
        self.NT = T // 512
        self.SEG = T // 2
        self.NB = T // 8
        self.NBS = self.NB // 2
        self.NKC = T // 128


def build(cfg):
    T, L, NT, NB, NBS = cfg.T, cfg.L, cfg.NT, cfg.NB, cfg.NBS
    nc = bass.Bass("TRN2", target_bir_lowering=False)
    K = KB(nc)

    def din(name, shape, dt=F32):
        return nc.dram_tensor(name, list(shape), dt, kind="ExternalInput").ap()

    x_in = din("xT", [D, T])
    y_out = nc.dram_tensor("yT", [D, T], F32, kind="ExternalOutput").ap()
    cT_in = din("cT", [128, 8, 2])
    cos_in = din("cosT", [128, T])
    sin_in = din("sinT", [128, T])
    perm_in = din("perm", [128, 128])
    ident_in = din("ident", [128, 128])
    maskf_in = din("maskf", [128, 128])
    maskb_in = din("maskb", [128, 128])
    cb_in = din("crossbias", [128, 1])
    flag_in = din("flag", [128, 1])
    w_mod = din("w_mod", [L, D, 6 * D])
    bmod_in = din("b_modT", [128, L, 48])
    g1_in = din("g1T", [128, L, 8])
    g2_in = din("g2T", [128, L, 8])
    gf_in = din("gfT", [128, 8])
    w_in = din("w_in", [L, D, 4096])
    lam_in = din("lamT", [128, L, 4, 64])
    subg_in = din("subgT", [128, L])
    w_attn = din("w_attn", [L, 512, D])
    are_in = din("a_reP", [128, L, 32])
    aim_in = din("a_imP", [128, L, 32])
    ldt_in = din("ldtP", [128, L, 32])
    bre_in = din("b_reP", [128, L, 32, 16])
    bim_in = din("b_imP", [128, L, 32, 16])
    cre_in = din("c_reP", [128, L, 32, 16])
    cim_in = din("c_imP", [128, L, 32, 16])
    dsk_in = din("ssm_dT", [128, L, 4])
    w_glu = din("w_glu", [L, 512, 2 * D])
    bglu_in = din("b_gluT", [128, L, 16])
    w_o = din("w_o", [L, D, D])
    w_ffi = din("w_ffi", [L, D, 2 * DFF])
    w_ffo = din("w_ffo", [L, DFF, D])

    wb_in = K.dram("wb_in", [L, D, 4096], BF16)
    wb_attn = K.dram("wb_attn", [L, 512, D], BF16)
    wb_glu = K.dram("wb_glu", [L, 512, 2 * D], BF16)
    wb_o = K.dram("wb_o", [L, D, D], BF16)
    wb_ffi = K.dram("wb_ffi", [L, D, 2 * DFF], BF16)
    wb_ffo = K.dram("wb_ffo", [L, DFF, D], BF16)
    xs = K.dram("xs", [D, T], F32)
    QT = K.dram("QT", [4, 128, T], BF16)
    KT = K.dram("KT", [4, 128, T], BF16)
    VS = K.dram("VS", [T // 128, 128, 512], BF16)
    SGA = K.dram("SGA", [D, T], BF16)
    SGS = K.dram("SGS", [D, T], BF16)
    UT = K.dram("UT", [512, T], F32)
    XL = K.dram("XL", [8, 512, NB], BF16)
    YL = K.dram("YL", [8, 512, NB], F32)
    OT = K.dram("OT", [512, T], BF16)
    dbuf = {}

    def DB(name, i=0):
        key = (name, i)
        if key not in dbuf:
            dbuf[key] = Buf("%s%d" % (name, i))
        return dbuf[key]

    ones32, b_ones32 = K.sb([128, 128], F32, "ones32")
    perm_b, b_perm = K.sb([128, 128], BF16, "perm")
    ident, b_ident = K.sb([128, 128], F32, "ident")
    maskf, b_maskf = K.sb([128, 128], F32, "maskf")
    maskb, b_maskb = K.sb([128, 128], F32, "maskb")
    crossb, b_crossb = K.sb([128, 1], F32, "crossb")
    zerob, b_zerob = K.sb([128, 1], F32, "zerob")
    flag, b_flag = K.sb([128, 1], F32, "flag")
    tmpc, b_tmpc = K.sb([128, 128], F32, "tmpc")
    K.memset("dve", ones32, 1.0, [b_ones32])
    K.memset("dve", zerob, 0.0, [b_zerob])
    K.dma("sp", [(tmpc, perm_in)], [], [b_tmpc], b_tmpc)
    K.copy("dve", perm_b, tmpc, [b_tmpc], [b_perm])
    K.dma("sp", [(ident, ident_in)], [], [b_ident], b_ident)
    K.dma("sp", [(maskf, maskf_in)], [], [b_maskf], b_maskf)
    K.dma("sp", [(maskb, maskb_in)], [], [b_maskb], b_maskb)
    K.dma("sp", [(crossb, cb_in)], [], [b_crossb], b_crossb)
    K.dma("sp", [(flag, flag_in)], [], [b_flag], b_flag)

    g1T, b_g1T = K.sb([128, L, 8], F32, "g1T")
    g2T, b_g2T = K.sb([128, L, 8], F32, "g2T")
    gfT, b_gfT = K.sb([128, 8], F32, "gfT")
    bglu, b_bglu = K.sb([128, L, 16], F32, "bglu")
    dsk, b_dsk = K.sb([128, L, 4], F32, "dsk")
    subg, b_subg = K.sb([128, L], F32, "subg")
    bmodT, b_bmodT = K.sb([128, L, 48], F32, "bmodT")
    K.dma("sp", [(g1T, g1_in)], [], [b_g1T], b_g1T)
    K.dma("sp", [(g2T, g2_in)], [], [b_g2T], b_g2T)
    K.dma("sp", [(gfT, gf_in)], [], [b_gfT], b_gfT)
    K.dma("sp", [(bglu, bglu_in)], [], [b_bglu], b_bglu)
    K.dma("sp", [(dsk, dsk_in)], [], [b_dsk], b_dsk)
    K.dma("sp", [(subg, subg_in)], [], [b_subg], b_subg)
    K.dma("sp", [(bmodT, bmod_in)], [], [b_bmodT], b_bmodT)

    cvb = [Buf("cv%d" % i) for i in range(4)]
    cvi = [0]
    b_wb = Buf("wb_all")

    def convert(src, dst, nelem):
        rows = nelem // 2048
        s2 = src.rearrange("(r c) -> r c", c=2048)
        d2 = dst.rearrange("(r c) -> r c", c=2048)
        r0 = 0
        while r0 < rows:
            r1 = min(rows, r0 + 1024)
            cb = cvb[cvi[0] % 4]
            cvi[0] += 1
            K.dma("pool", [(d2[r0:r1, :], s2[r0:r1, :])], [], [cb], cb)
            r0 = r1

    for l in range(L):
        convert(w_in[l].rearrange("a b -> (a b)"), wb_in[l].rearrange("a b -> (a b)"), D * 4096)
    for l in range(L):
        convert(w_attn[l].rearrange("a b -> (a b)"), wb_attn[l].rearrange("a b -> (a b)"), 512 * D)
        convert(w_glu[l].rearrange("a b -> (a b)"), wb_glu[l].rearrange("a b -> (a b)"), 512 * 2 * D)
        convert(w_o[l].rearrange("a b -> (a b)"), wb_o[l].rearrange("a b -> (a b)"), D * D)
        convert(w_ffi[l].rearrange("a b -> (a b)"), wb_ffi[l].rearrange("a b -> (a b)"), D * 2 * DFF)
        convert(w_ffo[l].rearrange("a b -> (a b)"), wb_ffo[l].rearrange("a b -> (a b)"), DFF * D)

    modT, b_modT = K.sb([128, L, 48, 2], F32, "modT")
    A1, b_A1 = K.sb([128, L, 8, 2], F32, "A1")
    A2, b_A2 = K.sb([128, L, 8, 2], F32, "A2")
    cT, b_cT = K.sb([128, 8, 2], F32, "cT")
    sc_, b_sc = K.sb([128, 8, 2], F32, "silu_c")
    K.dma("sp", [(cT, cT_in)], [], [b_cT], b_cT)
    K.actf(sc_, cT, AF.Silu, [b_cT], [b_sc])
    blk = 0
    mod_scope = K.scope()
    mod_scope.__enter__()
    wm = [K.sb([128, 8, 512], F32, "wm%d" % i) for i in range(2)]
    for l in range(L):
        for cb_ in range(12):
            wt, bw = wm[blk % 2]
            blk += 1
            K.dma("sp", [(wt, w_mod[l, :, cb_ * 512:(cb_ + 1) * 512].rearrange("(k p) c -> p k c", p=128))],
                  [], [bw], bw)
            pt, bp = K.psum()
            for c in range(4):
                for kc in range(8):
                    K.mm(pt[:, c * 2:c * 2 + 2], wt[:, kc, c * 128:(c + 1) * 128], sc_[:, kc, :],
                         kc == 0, kc == 7, [bw, b_sc], [bp])
            K.tt("dve", modT[:, l, cb_ * 4:(cb_ + 1) * 4, :],
                 pt[:, 0:8].rearrange("p (c s) -> p c s", s=2),
                 bmodT[:, l, cb_ * 4:(cb_ + 1) * 4].unsqueeze(2).to_broadcast([128, 4, 2]),
                 ALU.add, [bp, b_bmodT], [b_modT])
    K.barrier()
    mod_scope.__exit__(None, None, None)
    for l in range(L):
        K.stt(A1[:, l], modT[:, l, 8:16, :], 1.0, g1T[:, l, :].unsqueeze(2).to_broadcast([128, 8, 2]),
              ALU.add, ALU.mult, [b_modT, b_g1T], [b_A1])
        K.stt(A2[:, l], modT[:, l, 32:40, :], 1.0, g2T[:, l, :].unsqueeze(2).to_broadcast([128, 8, 2]),
              ALU.add, ALU.mult, [b_modT, b_g2T], [b_A2])

    def SH1(l, kc, s):
        return modT[:, l, 0 + kc, s:s + 1]

    def GT1(l, kc, s):
        return modT[:, l, 16 + kc, s:s + 1]

    def SH2(l, kc, s):
        return modT[:, l, 24 + kc, s:s + 1]

    def GT2(l, kc, s):
        return modT[:, l, 40 + kc, s:s + 1]

    lamT, b_lamT = K.sb([128, L, 4, 64], F32, "lamT")
    lamv, b_lamv = K.sb([128, L], F32, "lamv")
    neglam, b_neglam = K.sb([128, L], F32, "neglam")
    lsum, b_lsum = K.sb([128, L, 2], F32, "lsum")
    lprod, b_lprod = K.sb([128, L, 2, 64], F32, "lprod")
    K.dma("sp", [(lamT, lam_in)], [], [b_lamT], b_lamT)
    K.tt("dve", lprod[:, :, 0, :], lamT[:, :, 0, :], lamT[:, :, 1, :], ALU.mult, [b_lamT], [b_lprod])
    K.tt("dve", lprod[:, :, 1, :], lamT[:, :, 2, :], lamT[:, :, 3, :], ALU.mult, [b_lamT], [b_lprod])
    K.op("dve", lambda e: e.tensor_reduce(out=lsum, in_=lprod, op=ALU.add, axis=mybir.AxisListType.X),
         [b_lprod], [b_lsum])
    K.actf(lsum, lsum, AF.Exp, [b_lsum], [b_lsum])
    K.tt("dve", lamv, lsum[:, :, 0], lsum[:, :, 1], ALU.subtract, [b_lsum], [b_lamv])
    for l in range(L):
        K.ts("dve", lamv[:, l:l + 1], lamv[:, l:l + 1], float(lam_init_fn(l)), None, ALU.add, None, [b_lamv], [b_lamv])
    K.ts("dve", neglam, lamv, -1.0, None, ALU.mult, None, [b_lamv], [b_neglam])
    subgs, b_subgs = K.sb([128, L], F32, "subgs")
    for l in range(L):
        K.ts("dve", subgs[:, l:l + 1], subg[:, l:l + 1], float(1.0 - lam_init_fn(l)), None, ALU.mult, None,
             [b_subg], [b_subgs])

    K.barrier()

    class NS:
        pass
    S = NS()

    def alloc_shared():
        S.xt_ = [K.sb([128, 8, 512], F32, "xt%d" % i) for i in range(2)]
        S.hT, S.b_hT = K.sb([128, 8, 512], BF16, "hT")
        S.sq, S.b_sq = K.sb([128, 8, 512], F32, "sq")
        S.rstd, S.b_rstd = K.sb([128, 512], F32, "rstd")
        S.tmpn, S.b_tmpn = K.sb([128, 512], F32, "tmpn")
        S.wsl = [K.sb([128, 11, 512], BF16, "w%d" % i) for i in range(3)]
    wsl_i = [0]

    def wslot():
        s_ = S.wsl[wsl_i[0] % 3]
        wsl_i[0] += 1
        return s_

    def rstd_only(xt, bx):
        K.actf(S.sq, xt, AF.Square, [bx], [S.b_sq])
        pt, bp = K.psum()
        for kc in range(8):
            K.mm(pt, ones32, S.sq[:, kc, :], kc == 0, kc == 7, [b_ones32, S.b_sq], [bp])
        K.ts("dve", S.tmpn, pt, 1.0 / D, EPS, ALU.mult, ALU.add, [bp], [S.b_tmpn])
        K.actf(S.tmpn, S.tmpn, AF.Sqrt, [S.b_tmpn], [S.b_tmpn])
        K.op("dve", lambda e: e.reciprocal(out=S.rstd, in_=S.tmpn), [S.b_tmpn], [S.b_rstd])

    def norm_mod(xt, bx, Acol, Bcol, out_bf, b_out):
        rstd_only(xt, bx)
        for kc in range(8):
            K.stt(S.sq[:, kc, :], xt[:, kc, :], Acol(kc), S.rstd, ALU.mult, ALU.mult,
                  [bx, S.b_rstd, b_A1, b_A2], [S.b_sq])
            K.actf(out_bf[:, kc, :], S.sq[:, kc, :], AF.Identity, [S.b_sq, b_modT], [b_out], bias=Bcol(kc), scale=1.0)

    def load_w(src2d, k0, nk, col_runs):
        wt, bw = wslot()
        pairs = []
        off = 0
        for (c0, n) in col_runs:
            pairs.append((wt[:, 0:nk, off:off + n],
                          src2d[k0 * 128:(k0 + nk) * 128, c0:c0 + n].rearrange("(k p) c -> p k c", p=128)))
            off += n
        K.dma("sp", pairs, [b_wb], [bw], bw)
        return wt, bw

    cnt = {"qo": 0, "vo": 0, "uo": 0, "go": 0, "x": 0, "cs": 0}

    def phaseA(l):
        import os
        stopA = os.environ.get('MK_STOPA', '')
        scA = K.scope()
        scA.__enter__()
        alloc_shared()
        xt_ = S.xt_
        hT, b_hT = S.hT, S.b_hT
        csl = [K.sb([128, 2, 512], F32, "cs%d" % i) for i in range(2)]
        qb_, b_qb = K.sb([128, 512], BF16, "qb")
        qc_, b_qc = K.sb([128, 512], F32, "qc")
        qs_, b_qs = K.sb([128, 512], F32, "qs")
        qo_ = [K.sb([128, 4, 512], BF16, "qo%d" % i) for i in range(2)]
        vo_ = [K.sb([128, 4, 512], BF16, "vo%d" % i) for i in range(2)]
        uo_ = [K.sb([128, 4, 512], F32, "uo%d" % i) for i in range(2)]
        ul_ = [K.sb([128, 4, 8, 64], BF16, "ul%d" % i) for i in range(2)]
        go_ = [K.sb([128, 4, 512], BF16, "go%d" % i) for i in range(2)]
        wsrc = wb_in[l]
        for i in range(NT):
            s = i // (NT // 2)
            t0 = i * 512
            xt, bx = xt_[cnt["x"] % 2]
            cnt["x"] += 1
            src = x_in if l == 0 else xs
            rd = [] if l == 0 else [DB("xs", i)]
            K.dma("sp", [(xt, src[:, t0:t0 + 512].rearrange("(k p) t -> p k t", p=128))], rd, [bx], bx)
            cs, bcs = csl[cnt["cs"] % 2]
            cnt["cs"] += 1
            K.dma("sp", [(cs[:, 0, :], cos_in[:, t0:t0 + 512]), (cs[:, 1, :], sin_in[:, t0:t0 + 512])], [], [bcs], bcs)
            norm_mod(xt, bx, lambda kc: A1[:, l, kc, s:s + 1], lambda kc: SH1(l, kc, s), hT, b_hT)
            if stopA == 'n':
                continue
            for qk in range(2):
                wt, bw = load_w(wsrc, 0, 8, [(qk * 512, 512)])
                qo, bqo = qo_[cnt["qo"] % 2]
                cnt["qo"] += 1
                Y = int(os.environ.get("MK_Y", "9"))
                for c in range(4):
                    if Y < 1:
                        continue
                    pt, bp = K.psum()
                    for kc in range(8):
                        K.mm(pt, wt[:, kc, c * 128:(c + 1) * 128], hT[:, kc, :], kc == 0, kc == 7, [bw, b_hT], [bp])
                    if Y < 2:
                        continue
                    K.copy("act", qb_, pt, [bp], [b_qb])
                    if Y < 3:
                        continue
                    K.tt("dve", qc_, pt, cs[:, 0, :], ALU.mult, [bp, bcs] + ([b_qb] if os.environ.get("MK_Z") == "1" else []), [b_qc])
                    if Y < 4:
                        continue
                    p2, bp2 = K.psum()
                    K.mm(p2, perm_b, qb_, True, True, [b_perm, b_qb], [bp2])
                    if Y < 5:
                        continue
                    K.tt("dve", qs_, p2, cs[:, 1, :], ALU.mult, [bp2, bcs], [b_qs])
                    K.tt(os.environ.get("MK_QE", "pool"), qo[:, c, :], qc_, qs_, ALU.add, [b_qc, b_qs], [bqo])
                dst = QT if qk == 0 else KT
                if os.environ.get("MK_X") != "1":
                    K.dma("act", [(dst[:, :, t0:t0 + 512].rearrange("h p t -> p h t"), qo)], [bqo],
                          [DB("QT" if qk == 0 else "KT", i)], bqo)
            if stopA == 'qk':
                continue
            wt, bw = load_w(wsrc, 0, 8, [(1024, 512)])
            vo, bvo = vo_[cnt["vo"] % 2]
            cnt["vo"] += 1
            for tc in range(4):
                pt, bp = K.psum()
                for kc in range(8):
                    K.mm(pt, hT[:, kc, tc * 128:(tc + 1) * 128], wt[:, kc, 0:512], kc == 0, kc == 7, [bw, b_hT], [bp])
                K.copy("act", vo[:, tc, :], pt, [bp], [bvo])
            K.dma("act", [(VS[i * 4:(i + 1) * 4].rearrange("c p e -> p c e"), vo)], [bvo], [DB("VS", i)], bvo)
            if stopA == 'v':
                continue
            wt, bw = load_w(wsrc, 0, 8, [(1536, 512)])
            uo, buo = uo_[cnt["uo"] % 2]
            ul, bul = ul_[cnt["uo"] % 2]
            cnt["uo"] += 1
            for c in range(4):
                pt, bp = K.psum()
                for kc in range(8):
                    K.mm(pt, wt[:, kc, c * 128:(c + 1) * 128], hT[:, kc, :], kc == 0, kc == 7, [bw, b_hT], [bp])
                K.copy("act", uo[:, c, :], pt, [bp], [buo])
                K.copy("dve", ul[:, c], pt.rearrange("p (b s) -> p s b", s=8), [bp], [bul])
            K.dma("act", [(UT[:, t0:t0 + 512].rearrange("(c p) t -> p c t", p=128), uo)], [buo], [DB("UT", i)], buo)
            K.dma("act", [(XL[:, c * 128:(c + 1) * 128, i * 64:(i + 1) * 64].rearrange("s p b -> p s b"), ul[:, c])
                         for c in range(4)], [bul], [DB("XL", i)], bul)
            if stopA == 'u':
                continue
            for gb in range(4):
                wt, bw = load_w(wsrc, 0, 8, [(2048 + gb * 512, 512)])
                go, bgo = go_[cnt["go"] % 2]
                cnt["go"] += 1
                for c in range(4):
                    pt, bp = K.psum()
                    for kc in range(8):
                        K.mm(pt, wt[:, kc, c * 128:(c + 1) * 128], hT[:, kc, :], kc == 0, kc == 7, [bw, b_hT], [bp])
                    K.actf(go[:, c, :], pt, AF.Sigmoid, [bp], [bgo])
                dst = SGA if gb < 2 else SGS
                r0 = (gb % 2) * 512
                K.dma("act", [(dst[r0:r0 + 512, t0:t0 + 512].rearrange("(c p) t -> p c t", p=128), go)], [bgo],
                      [DB("SGA" if gb < 2 else "SGS", i * 2 + gb % 2)], bgo)

        K.barrier()
        scA.__exit__(None, None, None)

    cntB = {"q": 0, "p": 0, "ob": 0}
    scale = 64 ** -0.5

    def phaseB(l):
        scB = K.scope()
        scB.__enter__()
        kz0, b_kz0 = K.sb([128, T], BF16, "kz0")
        kz1, b_kz1 = K.sb([128, T], BF16, "kz1")
        vh_, b_vh = K.sb([128, T // 128, 128], BF16, "vh")
        qt_ = [K.sb([128, 512], BF16, "qt%d" % i) for i in range(2)]
        pT_ = [K.sb([128, 512], BF16, "pT%d" % i) for i in range(6)]
        rr_, b_rr = K.sb([128, 512], F32, "rr")
        t1_, b_t1 = K.sb([128, 512], F32, "t1")
        t2_, b_t2 = K.sb([128, 512], F32, "t2")
        od_, b_od = K.sb([128, 512], F32, "od")
        o2_, b_o2 = K.sb([128, 512], F32, "o2")
        ob_ = [K.sb([128, 512], BF16, "ob%d" % i) for i in range(2)]
        acs_, b_acs = K.sb([128, 512], F32, "acs")
        accP, b_accP = K.sb([128, 512], F32, "accP")
        NKC = T // 128
        for h in range(NH):
            if h == 0:
                K.memset("pool", kz0[64:128, :], 0.0, [b_kz0])
                K.memset("pool", kz1[0:64, :], 0.0, [b_kz1])
            K.dma("sp", [(kz0[0:64, :], KT[h, 0:64, :])], [DB("KT", i) for i in range(NT)], [b_kz0], b_kz0)
            K.dma("sp", [(kz1[64:128, :], KT[h, 64:128, :])], [DB("KT", i) for i in range(NT)], [b_kz1], b_kz1)
            K.dma("sp", [(vh_, VS[:, :, h * 128:(h + 1) * 128].rearrange("c p e -> p c e"))],
                  [DB("VS", i) for i in range(NT)], [b_vh], b_vh)
            for j in range(NT):
                sj = j // (NT // 2)
                qt, bq = qt_[cntB["q"] % 2]
                cntB["q"] += 1
                K.dma("sp", [(qt, QT[h, :, j * 512:(j + 1) * 512])], [DB("QT", j)], [bq], bq)
                for m in range(2):
                    hpo = K.psum_hold()
                    hpa = K.psum_hold()
                    po, bpo = hpo
                    pa, bpa = hpa

                    def emit_s(c):
                        ps_, bps = K.psum()
                        kz, bkz = (kz0, b_kz0) if m == 0 else (kz1, b_kz1)
                        K.mm(ps_, kz[:, c * 128:(c + 1) * 128], qt, True, True, [bkz, bq], [bps])
                        return ps_, bps
                    LA = 2
                    pend = [emit_s(c_) for c_ in range(min(LA, NKC))]
                    for c in range(NKC):
                        sc = (c * 128) // cfg.SEG
                        ps_, bps = pend.pop(0)
                        if c + LA < NKC:
                            pend.append(emit_s(c + LA))
                        pT, bpT = pT_[cntB["p"] % 6]
                        cntB["p"] += 1
                        K.actf(pT, ps_, AF.Exp, [bps, b_crossb, b_zerob], [bpT],
                               bias=(zerob if sc == sj else crossb), scale=scale)
                        K.mm(po, vh_[:, c, :], pT, c == 0, c == NKC - 1, [b_vh, bpT], [bpo])
                        if c % 5 in (1, 3):
                            if c == 1:
                                K.copy("pool", accP, pT, [bpT], [b_accP])
                            else:
                                K.tt("pool", accP, accP, pT, ALU.add, [bpT, b_accP], [b_accP])
                        elif c == 0:
                            K.copy("dve", pa, pT, [bpT], [bpa])
                        else:
                            K.tt("dve", pa, pa, pT, ALU.add, [bpT, bpa], [bpa])
                    K.tt("dve", acs_, pa, accP, ALU.add, [bpa, b_accP], [b_acs])
                    pr, bpr = K.psum()
                    K.mm(pr, ones32, acs_, True, True, [b_ones32, b_acs], [bpr])
                    K.op("dve", lambda e: e.reciprocal(out=rr_, in_=pr), [bpr], [b_rr])
                    if m == 0:
                        K.tt("dve", t1_, po, rr_, ALU.mult, [bpo, b_rr], [b_t1])
                    else:
                        K.tt("dve", t2_, po, rr_, ALU.mult, [bpo, b_rr], [b_t2])
                    K.psum_release(hpo)
                    K.psum_release(hpa)
                K.stt(od_, t2_, neglam[:, l:l + 1], t1_, ALU.mult, ALU.add, [b_t1, b_t2, b_neglam], [b_od])
                K.actf(o2_, od_, AF.Square, [b_od], [b_o2])
                pq, bpq = K.psum()
                K.mm(pq, ones32, o2_, True, True, [b_ones32, b_o2], [bpq])
                K.ts("dve", rr_, pq, 1.0 / 128, EPS, ALU.mult, ALU.add, [bpq], [b_rr])
                K.actf(rr_, rr_, AF.Sqrt, [b_rr], [b_rr])
                K.op("dve", lambda e: e.reciprocal(out=t1_, in_=rr_), [b_rr], [b_t1])
                ob, bob = ob_[cntB["ob"] % 2]
                cntB["ob"] += 1
                K.stt(ob, od_, subgs[:, l:l + 1], t1_, ALU.mult, ALU.mult, [b_od, b_t1, b_subgs], [bob])
                K.dma("act", [(OT[h * 128:(h + 1) * 128, j * 512:(j + 1) * 512], ob)], [bob], [DB("OT", j)], bob)

        K.barrier()
        scB.__exit__(None, None, None)

    PG = [128, 32]
    sA = {}

    def pg(name, shape=None, dt=F32):
        if name not in sA:
            sA[name] = K.sb(shape or PG, dt, name)
        return sA[name]

    NLEV = int(math.log2(NB))
    cntC = {"x": 0, "y": 0}

    def ssm_precompute(l):
        sA.clear()
        scP = K.scope()
        scP.__enter__()
        are, b1 = pg("are"); aim, b2 = pg("aim"); ldt, b3 = pg("ldt")
        K.dma("sp", [(are, are_in[:, l, :])], [], [b1], b1)
        K.dma("sp", [(aim, aim_in[:, l, :])], [], [b2], b2)
        K.dma("sp", [(ldt, ldt_in[:, l, :])], [], [b3], b3)
        Br, bBr = pg("Br", [128, 32, 16]); Bi, bBi = pg("Bi", [128, 32, 16])
        Cr, bCr = pg("Cr", [128, 32, 16]); Ci, bCi = pg("Ci", [128, 32, 16])
        K.dma("sp", [(Br, bre_in[:, l])], [], [bBr], bBr)
        K.dma("sp", [(Bi, bim_in[:, l])], [], [bBi], bBi)
        K.dma("sp", [(Cr, cre_in[:, l])], [], [bCr], bCr)
        K.dma("sp", [(Ci, cim_in[:, l])], [], [bCi], bCi)
        dt_, bdt = pg("dt"); xr, bxr = pg("xr"); th, bth = pg("th"); mag, bmag = pg("mag")
        K.actf(dt_, ldt, AF.Exp, [b3], [bdt])
        K.tt("dve", xr, dt_, are, ALU.mult, [bdt, b1], [bxr])
        K.tt("dve", th, dt_, aim, ALU.mult, [bdt, b2], [bth])
        K.actf(mag, xr, AF.Exp, [bxr], [bmag])
        yv, byv = pg("yv"); ki, bki = pg("ki", PG, I32); kf, bkf = pg("kf"); mk, bmk = pg("mk")
        sn, bsn = pg("sn"); cs_, bcs_ = pg("cs")
        for (dst, bdst, off) in ((sn, bsn, 1.5), (cs_, bcs_, 1.75)):
            K.ts("dve", yv, th, 1.0 / TWO_PI, off, ALU.mult, ALU.add, [bth], [byv])
            K.copy("dve", ki, yv, [byv], [bki])
            K.copy("dve", kf, ki, [bki], [bkf])
            K.tt("dve", mk, kf, yv, ALU.is_gt, [bkf, byv], [bmk])
            K.tt("dve", kf, kf, mk, ALU.subtract, [bkf, bmk], [bkf])
            K.tt("dve", yv, yv, kf, ALU.subtract, [byv, bkf], [byv])
            K.ts("dve", yv, yv, -0.5, TWO_PI, ALU.add, ALU.mult, [byv], [byv])
            K.ts("dve", yv, yv, math.pi, -math.pi, ALU.min, ALU.max, [byv], [byv])
            K.actf(dst, yv, AF.Sin, [byv], [bdst])
        Ar, bAr = pg("Ar"); Ai, bAi = pg("Ai")
        stopC = os.environ.get('MK_STOPC', '')
        if stopC == 'sin':
            K.barrier(); scP.__exit__(None, None, None); return
        K.tt("dve", Ar, mag, cs_, ALU.mult, [bmag, bcs_], [bAr])
        K.tt("dve", Ai, mag, sn, ALU.mult, [bmag, bsn], [bAi])
        Par, bPar = pg("Par", [128, 32, 9]); Pai, bPai = pg("Pai", [128, 32, 9])
        Pdr, bPdr = pg("Pdr", [128, 32, 9]); Pdi, bPdi = pg("Pdi", [128, 32, 9])
        Qar, bQar = pg("Qar", [128, 32, 9]); Qai, bQai = pg("Qai", [128, 32, 9])
        Qdr, bQdr = pg("Qdr", [128, 32, 9]); Qdi, bQdi = pg("Qdi", [128, 32, 9])
        ta, bta = pg("ta"); tb, btb = pg("tb")
        K.memset("dve", Par[:, :, 0], 1.0, [bPar])
        K.memset("dve", Pai[:, :, 0], 0.0, [bPai])
        for n in range(1, 9):
            K.tt("dve", ta, Par[:, :, n - 1], Ar, ALU.mult, [bPar, bAr], [bta])
            K.tt("dve", tb, Pai[:, :, n - 1], Ai, ALU.mult, [bPai, bAi], [btb])
            K.tt("dve", Par[:, :, n], ta, tb, ALU.subtract, [bta, btb], [bPar])
            K.tt("dve", ta, Par[:, :, n - 1], Ai, ALU.mult, [bPar, bAi], [bta])
            K.tt("dve", tb, Pai[:, :, n - 1], Ar, ALU.mult, [bPai, bAr], [btb])
            K.tt("dve", Pai[:, :, n], ta, tb, ALU.add, [bta, btb], [bPai])
        e2, be2 = pg("e2")
        for n in range(9):
            K.actf(e2, xr, AF.Exp, [bxr], [be2], scale=-2.0 * n)
            K.tt("dve", Qar[:, :, n], Par[:, :, n], e2, ALU.mult, [bPar, be2], [bQar])
            K.stt(Qai[:, :, n], Pai[:, :, n], -1.0, e2, ALU.mult, ALU.mult, [bPai, be2], [bQai])
        for n in range(9):
            K.copy("pool", Pdr[:, :, 8 - n], Par[:, :, n], [bPar], [bPdr])
            K.copy("pool", Pdi[:, :, 8 - n], Pai[:, :, n], [bPai], [bPdi])
            K.copy("pool", Qdr[:, :, 8 - n], Qar[:, :, n], [bQar], [bQdr])
            K.copy("pool", Qdi[:, :, 8 - n], Qai[:, :, n], [bQai], [bQdi])
        K.copy("dve", S.SS[:, 0, 0, :], Par[:, :, 8], [bPar], [S.b_SS])
        K.copy("dve", S.SS[:, 0, 1, :], Pai[:, :, 8], [bPai], [S.b_SS])
        for k in range(NLEV):
            if k > 0:
                K.tt("dve", ta, S.SS[:, k - 1, 0, :], S.SS[:, k - 1, 0, :], ALU.mult, [S.b_SS], [bta])
                K.tt("dve", tb, S.SS[:, k - 1, 1, :], S.SS[:, k - 1, 1, :], ALU.mult, [S.b_SS], [btb])
                K.tt("dve", S.SS[:, k, 0, :], ta, tb, ALU.subtract, [bta, btb], [S.b_SS])
                K.stt(S.SS[:, k, 1, :], S.SS[:, k - 1, 0, :], 2.0, S.SS[:, k - 1, 1, :], ALU.mult, ALU.mult, [S.b_SS], [S.b_SS])
            K.ts("dve", S.SS[:, k, 2, :], S.SS[:, k, 1, :], -1.0, None, ALU.mult, None, [S.b_SS], [S.b_SS])
        K.ts("dve", S.SF, S.SS, flag[:, 0:1], None, ALU.mult, None, [S.b_SS, b_flag], [S.b_SF])
        if stopC == 'pow':
            K.barrier(); scP.__exit__(None, None, None); return
        nr, bnr = pg("nr"); den, bden = pg("den"); fr, bfr = pg("fr"); fi, bfi = pg("fi")
        K.ts("dve", nr, Ar, -1.0, None, ALU.add, None, [bAr], [bnr])
        K.tt("dve", ta, are, are, ALU.mult, [b1], [bta])
        K.tt("dve", tb, aim, aim, ALU.mult, [b2], [btb])
        K.tt("dve", den, ta, tb, ALU.add, [bta, btb], [bden])
        K.op("dve", lambda e: e.reciprocal(out=den, in_=den), [bden], [bden])
        K.tt("dve", ta, nr, are, ALU.mult, [bnr, b1], [bta])
        K.tt("dve", tb, Ai, aim, ALU.mult, [bAi, b2], [btb])
        K.tt("dve", ta, ta, tb, ALU.add, [bta, btb], [bta])
        K.tt("dve", fr, ta, den, ALU.mult, [bta, bden], [bfr])
        K.tt("dve", ta, Ai, are, ALU.mult, [bAi, b1], [bta])
        K.tt("dve", tb, nr, aim, ALU.mult, [bnr, b2], [btb])
        K.tt("dve", ta, ta, tb, ALU.subtract, [bta, btb], [bta])
        K.tt("dve", fi, ta, den, ALU.mult, [bta, bden], [bfi])
        Bbr, bBbr = pg("Bbr", [128, 32, 16]); Bbi, bBbi = pg("Bbi", [128, 32, 16])
        t16a, bt16a = pg("t16a", [128, 32, 16]); t16b, bt16b = pg("t16b", [128, 32, 16])
        frb = fr.unsqueeze(2).to_broadcast([128, 32, 16])
        fib = fi.unsqueeze(2).to_broadcast([128, 32, 16])
        K.tt("dve", t16a, Br, frb, ALU.mult, [bBr, bfr], [bt16a])
        K.tt("dve", t16b, Bi, fib, ALU.mult, [bBi, bfi], [bt16b])
        K.tt("dve", Bbr, t16a, t16b, ALU.subtract, [bt16a, bt16b], [bBbr])
        K.tt("dve", t16a, Bi, frb, ALU.mult, [bBi, bfr], [bt16a])
        K.tt("dve", t16b, Br, fib, ALU.mult, [bBr, bfi], [bt16b])
        K.tt("dve", Bbi, t16a, t16b, ALU.add, [bt16a, bt16b], [bBbi])
        def wtab(name, src0, bs0, o0, src1, bs1, o1):
            w, bw = pg(name, [128, 16, 2, 8])
            v0 = src0.rearrange("p (a d) n -> p a d n", d=2)
            v1 = src1.rearrange("p (a d) n -> p a d n", d=2)
            K.copy("pool", w[:, :, 0, :], v0[:, :, 0, o0:o0 + 8], [bs0], [bw])
            K.copy("pool", w[:, :, 1, :], v1[:, :, 1, o1:o1 + 8], [bs1], [bw])
            return w.rearrange("p a d n -> p (a d) n"), bw
        WBr, bWBr = wtab("WBr", Pdr, bPdr, 1, Par, bPar, 0)
        WBi, bWBi = wtab("WBi", Pdi, bPdi, 1, Pai, bPai, 0)
        WCr, bWCr = wtab("WCr", Qdr, bQdr, 1, Qar, bQar, 0)
        WCi, bWCi = wtab("WCi", Qdi, bQdi, 1, Qai, bQai, 0)
        WKr, bWKr = wtab("WKr", Par, bPar, 1, Pdr, bPdr, 0)
        WKi, bWKi = wtab("WKi", Pai, bPai, 1, Pdi, bPdi, 0)
        big = [128, 32, 8, 16]
        PBr, bPBr = pg("PBr", big); PBi, bPBi = pg("PBi", big)
        PCr, bPCr = pg("PCr", big); PCi, bPCi = pg("PCi", big)
        tg1, btg1 = pg("tg1", big); tg2, btg2 = pg("tg2", big)

        def cmul(outr, boutr, outi, bouti, Wr, bWr, Wi, bWi, Xr, bXr, Xi, bXi, neg_i=False):
            wr = Wr.unsqueeze(3).to_broadcast(big)
            wi = Wi.unsqueeze(3).to_broadcast(big)
            xr_ = Xr.unsqueeze(2).to_broadcast(big)
            xi_ = Xi.unsqueeze(2).to_broadcast(big)
            K.tt("dve", tg1, wr, xr_, ALU.mult, [bWr, bXr], [btg1])
            K.tt("dve", tg2, wi, xi_, ALU.mult, [bWi, bXi], [btg2])
            K.tt("dve", outr, tg1, tg2, ALU.subtract, [btg1, btg2], [boutr])
            K.tt("dve", tg1, wr, xi_, ALU.mult, [bWr, bXi], [btg1])
            K.tt("dve", tg2, wi, xr_, ALU.mult, [bWi, bXr], [btg2])
            if neg_i:
                K.stt(outi, tg1, -1.0, tg2, ALU.mult, ALU.subtract, [btg1, btg2], [bouti])
            else:
                K.tt("dve", outi, tg1, tg2, ALU.add, [btg1, btg2], [bouti])

        cmul(PBr, bPBr, PBi, bPBi, WBr, bWBr, WBi, bWBi, Bbr, bBbr, Bbi, bBbi)
        cmul(PCr, bPCr, PCi, bPCi, WCr, bWCr, WCi, bWCi, Cr, bCr, Ci, bCi, neg_i=True)
        if stopC == 'cmul':
            K.barrier(); scP.__exit__(None, None, None); return
        PBr3 = PBr.rearrange("p g n c -> p g (n c)")
        PBi3 = PBi.rearrange("p g n c -> p g (n c)")
        PCr3 = PCr.rearrange("p g n c -> p g (n c)")
        PCi3 = PCi.rearrange("p g n c -> p g (n c)")
        for gp in range(16):
            for d in range(2):
                pt, bp = K.psum()
                idx = 0
                for gpar in range(2):
                    for (src, bsrc) in ((PBr3, bPBr), (PBi3, bPBi)):
                        sl_ = slice(gpar * 64, gpar * 64 + 64)
                        K.mm(pt[:, idx * 64:(idx + 1) * 64], src[:, gp * 2 + d, :], ident[:, sl_], True, True,
                             [bsrc, b_ident], [bp])
                        idx += 1
                K.copy("act", S.PBT[:, gp, d].rearrange("p a b c -> p (a b c)"), pt[:, 0:256], [bp], [S.b_PBT])
        mt, bmt = pg("mt", [128, 128])
        tb1 = tg1.rearrange("p g n c -> p (g n c)").bitcast(BF16).rearrange("p (h g f) -> p h g f", h=2, g=32)
        tb2 = tg2.rearrange("p g n c -> p (g n c)").bitcast(BF16).rearrange("p (h g f) -> p h g f", h=2, g=32)
        PBrb, bPBrb, PBib, bPBib = tb1[:, 0], btg1, tb1[:, 1], btg1
        PCrb, bPCrb, PCib, bPCib = tb2[:, 0], btg2, tb2[:, 1], btg2
        K.copy("act", PBrb, PBr3, [bPBr], [bPBrb])
        K.copy("act", PBib, PBi3, [bPBi], [bPBib])
        K.copy("act", PCrb, PCr3, [bPCr], [bPCrb])
        K.copy("act", PCib, PCi3, [bPCi], [bPCib])
        for gp in range(16):
            for gpar in range(2):
                g = gp * 2 + gpar
                sl = slice(gpar * 64, gpar * 64 + 64)
                pt, bp = K.psum()
                for d in range(2):
                    o = pt[:, d * 128:(d + 1) * 128]
                    K.mm(o, PBrb[sl, gp * 2 + d, :], PCrb[sl, gp * 2 + d, :], True, False, [bPBrb, bPCrb], [bp])
                    K.mm(o, PBib[sl, gp * 2 + d, :], PCib[sl, gp * 2 + d, :], False, True, [bPBib, bPCib], [bp])
                K.tt("dve", mt, pt[:, 0:128], maskf, ALU.mult, [bp, b_maskf], [bmt])
                K.tt("dve", tmpc, pt[:, 128:256], maskb, ALU.mult, [bp, b_maskb], [b_tmpc])
                K.tt("dve", S.M0[:, g, :], mt, tmpc, ALU.add, [bmt, b_tmpc], [S.b_M0])

        PKr, bPKr, PKi, bPKi = PCr, bPCr, PCi, bPCi
        cmul(PKr, bPKr, PKi, bPKi, WKr, bWKr, WKi, bWKi, Cr, bCr, Ci, bCi, neg_i=True)
        K.copy("act", S.PCC[:, :, :, 0, :], PKr.rearrange("p (a d) n c -> p a d (n c)", d=2), [bPKr], [S.b_PCC])
        K.copy("act", S.PCC[:, :, :, 1, :], PKi.rearrange("p (a d) n c -> p a d (n c)", d=2), [bPKi], [S.b_PCC])
        K.barrier()
        scP.__exit__(None, None, None)

    def hs_scan(gp, d):
        col = gp * 2 + d
        cur = 0
        for k in range(NLEV):
            s = 1 << k
            src, bsrc = S.Hb[cur]
            dst, bdst = S.Hb[1 - cur]

            def scal(tab, j):
                return tab[:, k, j, col:col + 1]

            def region(lo, hi, tab, btab, two_seg=False):
                sh = -s if d == 0 else s
                if two_seg:
                    def v(t, ri, off):
                        return t[:, ri, :].rearrange("p (g n) -> p g n", g=2)[:, :, lo + off:hi + off]
                else:
                    def v(t, ri, off):
                        return t[:, ri, lo + off:hi + off]
                rd = [bsrc, btab]
                K.stt(v(dst, 0, 0), v(src, 0, sh), scal(tab, 0), v(src, 0, 0), ALU.mult, ALU.add, rd, [bdst])
                K.stt(v(dst, 1, 0), v(src, 1, sh), scal(tab, 0), v(src, 1, 0), ALU.mult, ALU.add, rd, [bdst])
                K.stt(v(dst, 0, 0), v(src, 1, sh), scal(tab, 2), v(dst, 0, 0), ALU.mult, ALU.add, rd + [bdst], [bdst])
                K.stt(v(dst, 1, 0), v(src, 0, sh), scal(tab, 1), v(dst, 1, 0), ALU.mult, ALU.add, rd + [bdst], [bdst])

            if d == 0:
                K.copy("pool", dst[:, :, 0:min(s, NBS)], src[:, :, 0:min(s, NBS)], [bsrc], [bdst])
                if s < NBS:
                    region(s, NBS, S.SS, S.b_SS, two_seg=True)
                region(NBS, min(NBS + s, NB), S.SF, S.b_SF)
                if s >= NBS and s < NB:
                    pass
            else:
                lo0 = max(NB - s, NBS)
                K.copy("pool", dst[:, :, lo0:NB], src[:, :, lo0:NB], [bsrc], [bdst])
                if s < NBS:
                    region(0, NBS - s, S.SS, S.b_SS, two_seg=True)
                region(max(NBS - s, 0), NBS, S.SF, S.b_SF)
            cur = 1 - cur
        return cur

    def phaseC(l):
        stopC = os.environ.get('MK_STOPC', '')
        scC = K.scope()
        scC.__enter__()
        S.PBT, S.b_PBT = K.sb([128, 16, 2, 2, 2, 64], BF16, "PBT")
        S.PCC, S.b_PCC = K.sb([128, 16, 2, 2, 128], BF16, "PCC")
        S.M0, S.b_M0 = K.sb([128, 32, 128], BF16, "M0")
        S.SS, S.b_SS = K.sb([128, 11, 3, 32], F32, "SS")
        S.SF, S.b_SF = K.sb([128, 11, 3, 32], F32, "SF")
        ssm_precompute(l)
        if stopC in ('sin', 'pow', 'cmul', 'tr', 'pre'):
            K.barrier(); scC.__exit__(None, None, None); return
        S.Hb = [K.sb([128, 2, NB], F32, "H%d" % i) for i in range(2)]
        S.Hin, S.b_Hin = K.sb([128, 2, 2, NB], BF16, "Hin")
        S.xg_ = [K.sb([128, NB], BF16, "xg%d" % i) for i in range(4)]
        S.yg_ = [K.sb([128, NB], F32, "yg%d" % i) for i in range(2)]
        NBT = (NB + 511) // 512
        bw = min(512, NB)
        for gp in range(16):
            xg = []
            for gpar in range(2):
                g = gp * 2 + gpar
                xt, bx = S.xg_[cntC["x"] % 4]
                cntC["x"] += 1
                K.dma("sp", [(xt[s2 * 16:(s2 + 1) * 16, :], XL[s2, g * 16:(g + 1) * 16, :]) for s2 in range(8)],
                      [DB("XL", i) for i in range(NT)], [bx], bx)
                xg.append((xt, bx))
            for d in range(2):
                H0, bH0 = S.Hb[0]
                for nt in range(NBT):
                    for ri in range(2):
                        pt, bp = K.psum()
                        for gpar in range(2):
                            K.mm(pt[gpar * 64:(gpar + 1) * 64, 0:bw], S.PBT[:, gp, d, gpar, ri, :],
                                 xg[gpar][0][:, nt * 512:nt * 512 + bw], True, True, [S.b_PBT, xg[gpar][1]], [bp])
                        K.copy("act", H0[:, ri, nt * 512:nt * 512 + bw], pt[:, 0:bw], [bp], [bH0])
                cur = hs_scan(gp, d)
                Hf, bHf = S.Hb[cur]
                if d == 0:
                    K.memset("pool", S.Hin[:, 0, :, 0:1], 0.0, [S.b_Hin])
                    K.copy("act", S.Hin[:, 0].rearrange("p r (g n) -> p r g n", g=2)[:, :, :, 1:NBS],
                           Hf.rearrange("p r (g n) -> p r g n", g=2)[:, :, :, 0:NBS - 1], [bHf], [S.b_Hin])
                    K.ts("dve", S.Hin[:, 0, :, NBS:NBS + 1], Hf[:, :, NBS - 1:NBS], flag[:, 0:1], None, ALU.mult, None,
                         [bHf, b_flag], [S.b_Hin])
                else:
                    K.memset("pool", S.Hin[:, 1, :, NB - 1:NB], 0.0, [S.b_Hin])
                    K.copy("act", S.Hin[:, 1].rearrange("p r (g n) -> p r g n", g=2)[:, :, :, 0:NBS - 1],
                           Hf.rearrange("p r (g n) -> p r g n", g=2)[:, :, :, 1:NBS], [bHf], [S.b_Hin])
                    K.ts("dve", S.Hin[:, 1, :, NBS - 1:NBS], Hf[:, :, NBS:NBS + 1], flag[:, 0:1], None, ALU.mult, None,
                         [bHf, b_flag], [S.b_Hin])
            for gpar in range(2):
                g = gp * 2 + gpar
                sl = slice(gpar * 64, gpar * 64 + 64)
                yt, by = S.yg_[cntC["y"] % 2]
                cntC["y"] += 1
                for nt in range(NBT):
                    cs = slice(nt * 512, nt * 512 + bw)
                    pt, bp = K.psum()
                    K.mm(pt[:, 0:bw], S.M0[:, g, :], xg[gpar][0][:, cs], True, False, [S.b_M0, xg[gpar][1]], [bp])
                    for d in range(2):
                        for ri in range(2):
                            K.mm(pt[:, 0:bw], S.PCC[sl, gp, d, ri, :], S.Hin[sl, d, ri, cs], False,
                                 (d == 1 and ri == 1), [S.b_PCC, S.b_Hin], [bp])
                    K.copy("act", yt[:, cs], pt[:, 0:bw], [bp], [by])
                K.dma("act", [(YL[t2, g * 16:(g + 1) * 16, :], yt[t2 * 16:(t2 + 1) * 16, :]) for t2 in range(8)],
                      [by], [DB("YL", g)], by)
        K.barrier()
        scC.__exit__(None, None, None)

    cntD = {"i": 0}

    def phaseD(l, last):
        scD = K.scope()
        scD.__enter__()
        alloc_shared()
        xt_ = S.xt_
        oT_ = [K.sb([128, 4, 512], BF16, "oT%d" % i) for i in range(1)]
        sg_ = [K.sb([128, 2, 8, 512], BF16, "sg%d" % i) for i in range(1)]
        yl_ = [K.sb([128, 4, 8, 64], F32, "yl%d" % i) for i in range(1)]
        ud_ = [K.sb([128, 4, 512], F32, "ud%d" % i) for i in range(1)]
        zt_, b_zt = K.sb([128, 4, 512], BF16, "zt")
        m1_, b_m1 = K.sb([128, 8, 512], F32, "m1")
        mg_, b_mg = K.sb([128, 8, 512], BF16, "mg")
        h2_, b_h2 = K.sb([128, 8, 512], BF16, "h2")
        aT_, b_aT = K.sb([128, 22, 512], BF16, "aT")
        ga_, b_ga = K.sb([128, 512], F32, "ga")
        gb_, b_gb = K.sb([128, 512], F32, "gb")
        gc_, b_gc = K.sb([128, 512], F32, "gc")
        for i in range(NT):
            s = i // (NT // 2)
            t0 = i * 512
            par = 0
            cntD["i"] += 1
            xt, bx = xt_[cnt["x"] % 2]
            cnt["x"] += 1
            src = x_in if l == 0 else xs
            rd = [] if l == 0 else [DB("xs", i)]
            K.dma("sp", [(xt, src[:, t0:t0 + 512].rearrange("(k p) t -> p k t", p=128))], rd, [bx], bx)
            oT, boT = oT_[par]
            K.dma("sp", [(oT, OT[:, t0:t0 + 512].rearrange("(c p) t -> p c t", p=128))], [DB("OT", i)], [boT], boT)
            sg, bsg = sg_[par]
            K.dma("sp", [(sg[:, 0], SGA[:, t0:t0 + 512].rearrange("(c p) t -> p c t", p=128)),
                         (sg[:, 1], SGS[:, t0:t0 + 512].rearrange("(c p) t -> p c t", p=128))],
                  [DB("SGA", i * 2), DB("SGA", i * 2 + 1), DB("SGS", i * 2), DB("SGS", i * 2 + 1)], [bsg], bsg)
            yl, byl = yl_[par]
            K.dma("sp", [(yl[:, c], YL[:, c * 128:(c + 1) * 128, i * 64:(i + 1) * 64].rearrange("t p b -> p t b"))
                         for c in range(4)], [DB("YL", g) for g in range(NG)], [byl], byl)
            ud, bud = ud_[par]
            K.dma("sp", [(ud, UT[:, t0:t0 + 512].rearrange("(c p) t -> p c t", p=128))], [DB("UT", i)], [bud], bud)
            for c in range(4):
                yv = yl[:, c].rearrange("p t b -> p b t")
                g3 = ga_.rearrange("p (b t) -> p b t", t=8)
                K.stt(g3, ud[:, c, :].rearrange("p (b t) -> p b t", t=8), dsk[:, l, c:c + 1], yv, ALU.mult, ALU.add,
                      [bud, byl, b_dsk], [b_ga])
                K.actf(gb_, ga_, AF.Square, [b_ga], [b_gb])
                K.ts("dve", gb_, gb_, 0.044715, 1.0, ALU.mult, ALU.add, [b_gb], [b_gb])
                K.tt("dve", gb_, gb_, ga_, ALU.mult, [b_gb, b_ga], [b_gb])
                K.actf(gc_, gb_, AF.Sigmoid, [b_gb], [b_gc], scale=1.5957691216057308)
                K.tt("dve", zt_[:, c, :], ga_, gc_, ALU.mult, [b_ga, b_gc], [b_zt])
            for blk_ in range(2):
                wt, bw = load_w(wb_attn[l], 0, 4, [(blk_ * 512, 512)])
                for c in range(4):
                    pt, bp = K.psum()
                    for kc in range(4):
                        K.mm(pt, wt[:, kc, c * 128:(c + 1) * 128], oT[:, kc, :], kc == 0, kc == 3, [bw, boT], [bp])
                    K.tt("dve", m1_[:, blk_ * 4 + c, :], pt, sg[:, 0, blk_ * 4 + c, :], ALU.mult, [bp, bsg], [b_m1])
            for blk_ in range(4):
                wt, bw = load_w(wb_glu[l], 0, 4, [(blk_ * 256, 256), (1024 + blk_ * 256, 256)])
                for c in range(2):
                    j = blk_ * 2 + c
                    pl, bpl = K.psum()
                    for kc in range(4):
                        K.mm(pl, wt[:, kc, c * 128:(c + 1) * 128], zt_[:, kc, :], kc == 0, kc == 3, [bw, b_zt], [bpl])
                    pg_, bpg = K.psum()
                    for kc in range(4):
                        K.mm(pg_, wt[:, kc, 256 + c * 128:256 + (c + 1) * 128], zt_[:, kc, :], kc == 0, kc == 3,
                             [bw, b_zt], [bpg])
                    K.actf(ga_, pg_, AF.Sigmoid, [bpg, b_bglu], [b_ga], bias=bglu[:, l, 8 + j:9 + j], scale=1.0)
                    K.stt(gb_, pl, bglu[:, l, j:j + 1], ga_, ALU.add, ALU.mult, [bpl, b_bglu, b_ga], [b_gb])
                    K.tt("dve", gb_, gb_, sg[:, 1, j, :], ALU.mult, [b_gb, bsg], [b_gb])
                    K.tt("dve", mg_[:, j, :], gb_, m1_[:, j, :], ALU.add, [b_gb, b_m1], [b_mg])
            for blk_ in range(2):
                wt, bw = load_w(wb_o[l], 0, 8, [(blk_ * 512, 512)])
                for c in range(4):
                    j = blk_ * 4 + c
                    pt, bp = K.psum()
                    for kc in range(8):
                        K.mm(pt, wt[:, kc, c * 128:(c + 1) * 128], mg_[:, kc, :], kc == 0, kc == 7, [bw, b_mg], [bp])
                    K.stt(xt[:, j, :], pt, GT1(l, j, s), xt[:, j, :], ALU.mult, ALU.add, [bp, b_modT, bx], [bx])
            norm_mod(xt, bx, lambda kc: A2[:, l, kc, s:s + 1], lambda kc: SH2(l, kc, s), h2_, b_h2)
            for blk_ in range(11):
                wt, bw = load_w(wb_ffi[l], 0, 8, [(blk_ * 256, 256), (DFF + blk_ * 256, 256)])
                for c in range(2):
                    j = blk_ * 2 + c
                    pgt, bpg = K.psum()
                    for kc in range(8):
                        K.mm(pgt, wt[:, kc, c * 128:(c + 1) * 128], h2_[:, kc, :], kc == 0, kc == 7, [bw, b_h2], [bpg])
                    pu, bpu = K.psum()
                    for kc in range(8):
                        K.mm(pu, wt[:, kc, 256 + c * 128:256 + (c + 1) * 128], h2_[:, kc, :], kc == 0, kc == 7,
                             [bw, b_h2], [bpu])
                    K.actf(ga_, pgt, AF.Silu, [bpg], [b_ga])
                    K.tt("dve", aT_[:, j, :], pu, ga_, ALU.mult, [bpu, b_ga], [b_aT])
            for half in range(2):
                wa, bwa = load_w(wb_ffo[l], 0, 11, [(half * 512, 512)])
                wb2, bwb2 = load_w(wb_ffo[l], 11, 11, [(half * 512, 512)])
                for c in range(4):
                    j = half * 4 + c
                    pt, bp = K.psum()
                    for kc in range(22):
                        w_, bw_ = (wa, bwa) if kc < 11 else (wb2, bwb2)
                        K.mm(pt, w_[:, kc % 11, c * 128:(c + 1) * 128], aT_[:, kc, :], kc == 0, kc == 21,
                             [bw_, b_aT], [bp])
                    K.stt(xt[:, j, :], pt, GT2(l, j, s), xt[:, j, :], ALU.mult, ALU.add, [bp, b_modT, bx], [bx])
            if not last:
                K.dma("act", [(xs[:, t0:t0 + 512].rearrange("(k p) t -> p k t", p=128), xt)], [bx], [DB("xs", i)], bx)
            else:
                rstd_only(xt, bx)
                for kc in range(8):
                    K.stt(xt[:, kc, :], xt[:, kc, :], gfT[:, kc:kc + 1], S.rstd, ALU.mult, ALU.mult,
                          [bx, S.b_rstd, b_gfT], [bx])
                K.dma("act", [(y_out[:, t0:t0 + 512].rearrange("(k p) t -> p k t", p=128), xt)], [bx], [DB("y", i)], bx)

        K.barrier()
        scD.__exit__(None, None, None)

    import os
    stop = os.environ.get("MK_STOP", "")
    for l in range(L):
        if l == 0:
            for cb in cvb:
                b_wb.w.update(cb.w)
        if stop == "pro":
            break
        phaseA(l)
        K.barrier()
        if stop == "A":
            break
        phaseB(l)
        K.barrier()
        if stop == "B":
            break
        phaseC(l)
        K.barrier()
        if stop == "C":
            break
        phaseD(l, l == L - 1)
        K.barrier()
    K.barrier()
    return nc


def _rope_tables(T_seq):
    inv = 1.0 / (10000.0 ** (np.arange(0, 64, 2, dtype=np.float32) / 64.0))
    ang = np.arange(T_seq, dtype=np.float32)[:, None] * inv[None, :].astype(np.float32)
    ang = np.concatenate([ang, ang], axis=-1).astype(np.float32)
    return np.cos(ang).astype(np.float32), np.sin(ang).astype(np.float32)


def _consts():
    perm = np.zeros((128, 128), np.float32)
    for m in range(2):
        for d in range(64):
            perm[m * 64 + (d + 32) % 64, m * 64 + d] = 1.0
    ident = np.eye(128, dtype=np.float32)
    s2 = np.arange(128) // 16
    maskf = (s2[None, :] >= s2[:, None]).astype(np.float32)
    maskb = (s2[None, :] <= s2[:, None]).astype(np.float32)
    return perm, ident, maskf, maskb


def _pl(a, L):
    rest = a.shape[4:]
    a = a.reshape((L, 2, 16, 2, 64) + rest)
    a = np.moveaxis(a, (3, 4, 0, 2, 1), (0, 1, 2, 3, 4))
    return np.ascontiguousarray(a.reshape((128, L, 32) + rest)).astype(np.float32)


def make_in_maps(inp, cfg, core_seqs):
    L, T = cfg.L, cfg.T
    perm, ident, maskf, maskb = _consts()

    def fm(v, nch):
        v = np.asarray(v, np.float32)
        lead = v.shape[:-1]
        v = v.reshape(lead + (nch, 128))
        v = np.moveaxis(v, -1, 0)
        return np.ascontiguousarray(v)

    shared = {
        "perm": perm, "ident": ident, "maskf": maskf, "maskb": maskb,
        "w_mod": np.ascontiguousarray(inp["w_mod"][:L], np.float32),
        "b_modT": fm(inp["b_mod"][:L], 48),
        "g1T": fm(inp["norm1_g"][:L], 8), "g2T": fm(inp["norm2_g"][:L], 8), "gfT": fm(inp["final_g"], 8),
        "w_in": np.ascontiguousarray(inp["w_in"][:L], np.float32),
        "lamT": np.ascontiguousarray(np.broadcast_to(
            np.stack([inp["lam_q1"][:L], inp["lam_k1"][:L], inp["lam_q2"][:L], inp["lam_k2"][:L]], axis=1)[None],
            (128, L, 4, 64)), np.float32),
        "subgT": np.ascontiguousarray(np.asarray(inp["subln_g"][:L], np.float32).T),
        "w_attn": np.ascontiguousarray(inp["w_attn_br"][:L], np.float32),
        "ssm_dT": fm(inp["ssm_d"][:L], 4),
        "w_glu": np.ascontiguousarray(inp["w_glu"][:L], np.float32),
        "b_gluT": fm(inp["b_glu"][:L], 16),
        "w_o": np.ascontiguousarray(inp["w_o"][:L], np.float32),
        "w_ffi": np.ascontiguousarray(inp["w_ffn_in"][:L], np.float32),
        "w_ffo": np.ascontiguousarray(inp["w_ffn_out"][:L], np.float32),
    }
    are = np.asarray(inp["ssm_a_re"][:L], np.float32)
    aim = np.asarray(inp["ssm_a_im"][:L], np.float32)
    ldt = np.broadcast_to(np.asarray(inp["ssm_log_dt"][:L], np.float32)[..., None], (L, 2, 32, 64))
    shared["a_reP"] = _pl(are, L)
    shared["a_imP"] = _pl(aim, L)
    shared["ldtP"] = _pl(np.ascontiguousarray(ldt), L)
    shared["b_reP"] = _pl(np.asarray(inp["ssm_b_re"][:L], np.float32), L)
    shared["b_imP"] = _pl(np.asarray(inp["ssm_b_im"][:L], np.float32), L)
    shared["c_reP"] = _pl(np.swapaxes(np.asarray(inp["ssm_c_re"][:L], np.float32), 3, 4), L)
    shared["c_imP"] = _pl(np.swapaxes(np.asarray(inp["ssm_c_im"][:L], np.float32), 3, 4), L)
    maps = []
    for (x, c, pos, split) in core_seqs:
        cos, sin = _rope_tables(int(pos.max()) + 1)
        cosT = np.ascontiguousarray(np.tile(cos[pos].T, (2, 1)))
        sgn = np.where(np.arange(64) < 32, -1.0, 1.0).astype(np.float32)
        sinT = np.ascontiguousarray(np.tile((sin[pos] * sgn[None, :]).T, (2, 1)))
        m = dict(shared)
        m["xT"] = np.ascontiguousarray(x.T)
        m["cT"] = np.ascontiguousarray(np.moveaxis(np.asarray(c, np.float32).reshape(2, 8, 128), (0, 1, 2), (2, 1, 0)))
        m["cosT"] = cosT.astype(np.float32)
        m["sinT"] = sinT.astype(np.float32)
        m["crossbias"] = np.full((128, 1), 0.0 if split else -30000.0, np.float32)
        m["flag"] = np.full((128, 1), 1.0 if split else 0.0, np.float32)
        maps.append(m)
    return maps


_NC_CACHE = {}


def run(inp, cfg, core_seqs):
    key = (cfg.T, cfg.L)
    if key not in _NC_CACHE:
        _NC_CACHE[key] = build(cfg)
    nc = _NC_CACHE[key]
    maps = make_in_maps(inp, cfg, core_seqs)
    res = run_bass_kernel_spmd(nc, maps, core_ids=list(range(len(maps))))
    return [np.asarray(r["yT"]).T for r in res.results]


def kernel(**inp):
    cfg = Cfg(8192, 4)
    xp = np.asarray(inp["x_prompt"], np.float32)
    xsm = np.asarray(inp["x_sample"], np.float32)
    cp = np.asarray(inp["c_prompt"], np.float32)
    csm = np.asarray(inp["c_sample"], np.float32)
    cores = []
    pos_p = np.concatenate([np.arange(4096), np.arange(4096)])
    for c in range(4):
        cores.append((np.concatenate([xp[2 * c], xp[2 * c + 1]], axis=0), np.stack([cp[2 * c], cp[2 * c + 1]]),
                      pos_p, False))
    for c in range(4):
        cores.append((xsm[c], np.stack([csm[c], csm[c]]), np.arange(8192), True))
    outs = run(inp, cfg, cores)
    yp = np.empty((8, 4096, D), np.float32)
    for c in range(4):
        yp[2 * c] = outs[c][:4096]
        yp[2 * c + 1] = outs[c][4096:]
    ys = np.stack([outs[4 + c] for c in range(4)], axis=0).astype(np.float32)
    return (yp, ys)
```

```python
import math
import contextlib
import numpy as np
import concourse.bass as bass
import concourse.mybir as mybir
from concourse.bass_utils import run_bass_kernel_spmd

F32 = mybir.dt.float32
BF16 = mybir.dt.bfloat16
I32 = mybir.dt.int32
ALU = mybir.AluOpType
AF = mybir.ActivationFunctionType

D = 1024
NH = 4
DFF = 2816
DSSM = 512
NG = 32
EPS = 1e-6
TWO_PI = 2.0 * math.pi


def lam_init_fn(layer):
    return 0.8 - 0.6 * math.exp(-0.3 * layer)


class Buf:
    __slots__ = ("name", "w", "r", "sem", "cnt", "excl")

    def __init__(self, name, excl=False):
        self.name = name
        self.excl = excl
        self.w = {}
        self.r = {}
        self.sem = None
        self.cnt = 0


class EngState:
    def __init__(self, eng, sem, self_sync):
        self.eng = eng
        self.sem = sem
        self.cnt = 0
        self.waited = {}
        self.self_sync = self_sync


def _merge(d, src):
    for k, (s, v) in src.items():
        if k not in d or d[k][1] < v:
            d[k] = (s, v)


class KB:
    def __init__(self, nc):
        self.nc = nc
        self.engs = {}
        for name, e in (("pe", nc.tensor), ("dve", nc.vector), ("act", nc.scalar),
                        ("pool", nc.gpsimd), ("sp", nc.sync)):
            self.engs[name] = EngState(e, nc.alloc_semaphore("e_" + name), name != "pe")
        self.dma_bufs = []
        self.nalloc = 0
        self.stacks = []
        self.scope_bufs = []
        self.free_sems = []
        self.retired = {}
        self.nsem = 0
        self.ps = []
        for i in range(8):
            t = nc.alloc_psum_tensor("psb%d" % i, [128, 512], F32)
            self.ps.append((t.ap(), Buf("ps%d" % i, excl=True)))
        self.ps_i = 0

    def sb(self, shape, dt, name=None):
        self.nalloc += 1
        nm = "%s_%d" % (name or "t", self.nalloc)
        if self.stacks:
            t = self.stacks[-1].enter_context(self.nc.sbuf_tensor(nm, list(shape), dt))
        else:
            t = self.nc.alloc_sbuf_tensor(nm, list(shape), dt)
        b = Buf(name or "t")
        if self.scope_bufs:
            self.scope_bufs[-1].append(b)
        return (t.ap() if hasattr(t, "ap") and callable(t.ap) else t[:]), b

    @contextlib.contextmanager
    def scope(self):
        st = contextlib.ExitStack()
        self.stacks.append(st)
        self.scope_bufs.append([])
        try:
            yield
        finally:
            self.stacks.pop()
            for b in self.scope_bufs.pop():
                if b.sem is not None:
                    self.free_sems.append((b.sem, b.cnt))
                    self.dma_bufs.remove(b)
                    self.retired[id(b.sem)] = (b.sem, b.cnt)
                    b.sem = None
            st.close()

    def dram(self, name, shape, dt):
        return self.nc.dram_tensor(name, list(shape), dt, kind="Internal").ap()

    def psum(self):
        p = self.ps.pop(0)
        self.ps.append(p)
        return p

    def psum_hold(self):
        return self.ps.pop(0)

    def psum_release(self, p):
        self.ps.append(p)

    def _deps(self, reads, writes):
        d = {}
        for b in reads:
            _merge(d, b.w)
            if b.excl:
                _merge(d, b.r)
        for b in writes:
            _merge(d, b.w)
            _merge(d, b.r)
        return d

    def _wait(self, E, deps):
        for k, (sem, val) in deps.items():
            if sem is E.sem and not E.self_sync:
                continue
            if E.waited.get(k, 0) < val:
                E.eng.wait_ge(sem, val)
                E.waited[k] = val

    def _record(self, tok, reads, writes):
        k = id(tok[0])
        for b in reads:
            if k not in b.r or b.r[k][1] < tok[1]:
                b.r[k] = tok
        for b in writes:
            b.w = {k: tok}
            b.r = {}

    def op(self, ename, fn, reads=(), writes=()):
        E = self.engs[ename]
        self._wait(E, self._deps(reads, writes))
        ins = fn(E.eng)
        E.cnt += 1
        ins.then_inc(E.sem, 1)
        self._record((E.sem, E.cnt), reads, writes)

    def dma(self, ename, pairs, reads, writes, sbuf):
        E = self.engs[ename]
        if sbuf.sem is None:
            if self.free_sems:
                sbuf.sem, sbuf.cnt = self.free_sems.pop()
                self.retired.pop(id(sbuf.sem), None)
            else:
                self.nsem += 1
                sbuf.sem = self.nc.alloc_semaphore("d_%d" % self.nsem)
            self.dma_bufs.append(sbuf)
        deps = self._deps(reads, writes)
        if sbuf.cnt > 0:
            _merge(deps, {id(sbuf.sem): (sbuf.sem, sbuf.cnt)})
        self._wait(E, deps)
        for (o, i) in pairs:
            E.eng.dma_start(out=o, in_=i).then_inc(sbuf.sem, 16)
            sbuf.cnt += 16
        self._record((sbuf.sem, sbuf.cnt), reads, writes)

    def barrier(self):
        toks = {}
        for E in self.engs.values():
            if E.cnt:
                toks[id(E.sem)] = (E.sem, E.cnt)
        for b in self.dma_bufs:
            if b.cnt:
                toks[id(b.sem)] = (b.sem, b.cnt)
        for k, tok in self.retired.items():
            toks.setdefault(k, tok)
        for E in self.engs.values():
            for k, (sem, val) in toks.items():
                if sem is E.sem:
                    continue
                if E.waited.get(k, 0) < val:
                    E.eng.wait_ge(sem, val)
                    E.waited[k] = val

    def mm(self, out, lhsT, rhs, start, stop, reads, writes):
        self.op("pe", lambda e: e.matmul(out, lhsT, rhs, start=start, stop=stop), reads, writes)

    def tt(self, eng, out, in0, in1, op, reads, writes):
        self.op(eng, lambda e: e.tensor_tensor(out=out, in0=in0, in1=in1, op=op), reads, writes)

    def ts(self, eng, out, in0, s1, s2, op0, op1, reads, writes):
        if op1 is None:
            self.op(eng, lambda e: e.tensor_scalar(out=out, in0=in0, scalar1=s1, scalar2=None, op0=op0), reads, writes)
        else:
            self.op(eng, lambda e: e.tensor_scalar(out=out, in0=in0, scalar1=s1, scalar2=s2, op0=op0, op1=op1), reads, writes)

    def stt(self, out, in0, scalar, in1, op0, op1, reads, writes):
        self.op("dve", lambda e: e.scalar_tensor_tensor(out=out, in0=in0, scalar=scalar, in1=in1, op0=op0, op1=op1), reads, writes)

    def actf(self, out, in_, func, reads, writes, bias=None, scale=None):
        kw = {}
        if bias is not None:
            kw["bias"] = bias
        if scale is not None:
            kw["scale"] = scale
        self.op("act", lambda e: e.activation(out=out, in_=in_, func=func, **kw), reads, writes)

    def copy(self, eng, out, in_, reads, writes):
        if eng == "act":
            self.op("act", lambda e: e.activation(out=out, in_=in_, func=AF.Identity), reads, writes)
        else:
            self.op(eng, lambda e: e.tensor_copy(out=out, in_=in_), reads, writes)

    def memset(self, eng, ap, val, writes):
        self.op(eng, lambda e: e.memset(ap, val), (), writes)


class Cfg:
    def __init__(self, T, L):
        self.T = T
        self.L = L
        self.NT = T // 512
        self.SEG = T // 2
        self.NB = T // 8
        self.NBS = self.NB // 2
        self.NKC = T // 128


def build(cfg):
    T, L, NT, NB, NBS = cfg.T, cfg.L, cfg.NT, cfg.NB, cfg.NBS
    nc = bass.Bass("TRN2", target_bir_lowering=False)
    K = KB(nc)

    def din(name, shape, dt=F32):
        return nc.dram_tensor(name, list(shape), dt, kind="ExternalInput").ap()

    x_in = din("xT", [D, T])
    y_out = nc.dram_tensor("yT", [D, T], F32, kind="ExternalOutput").ap()
    cT_in = din("cT", [128, 8, 2])
    cos_in = din("cosT", [128, T])
    sin_in = din("sinT", [128, T])
    perm_in = din("perm", [128, 128])
    ident_in = din("ident", [128, 128])
    maskf_in = din("maskf", [128, 128])
    maskb_in = din("maskb", [128, 128])
    cb_in = din("crossbias", [128, 1])
    flag_in = din("flag", [128, 1])
    w_mod = din("w_mod", [L, D, 6 * D])
    bmod_in = din("b_modT", [128, L, 48])
    g1_in = din("g1T", [128, L, 8])
    g2_in = din("g2T", [128, L, 8])
    gf_in = din("gfT", [128, 8])
    w_in = din("w_in", [L, D, 4096])
    lam_in = din("lamT", [128, L, 4, 64])
    subg_in = din("subgT", [128, L])
    w_attn = din("w_attn", [L, 512, D])
    are_in = din("a_reP", [128, L, 32])
    aim_in = din("a_imP", [128, L, 32])
    ldt_in = din("ldtP", [128, L, 32])
    bre_in = din("b_reP", [128, L, 32, 16])
    bim_in = din("b_imP", [128, L, 32, 16])
    cre_in = din("c_reP", [128, L, 32, 16])
    cim_in = din("c_imP", [128, L, 32, 16])
    dsk_in = din("ssm_dT", [128, L, 4])
    w_glu = din("w_glu", [L, 512, 2 * D])
    bglu_in = din("b_gluT", [128, L, 16])
    w_o = din("w_o", [L, D, D])
    w_ffi = din("w_ffi", [L, D, 2 * DFF])
    w_ffo = din("w_ffo", [L, DFF, D])

    wb_in = K.dram("wb_in", [L, D, 4096], BF16)
    wb_attn = K.dram("wb_attn", [L, 512, D], BF16)
    wb_glu = K.dram("wb_glu", [L, 512, 2 * D], BF16)
    wb_o = K.dram("wb_o", [L, D, D], BF16)
    wb_ffi = K.dram("wb_ffi", [L, D, 2 * DFF], BF16)
    wb_ffo = K.dram("wb_ffo", [L, DFF, D], BF16)
    xs = K.dram("xs", [D, T], F32)
    QT = K.dram("QT", [4, 128, T], BF16)
    KT = K.dram("KT", [4, 128, T], BF16)
    VS = K.dram("VS", [T // 128, 128, 512], BF16)
    SGA = K.dram("SGA", [D, T], BF16)
    SGS = K.dram("SGS", [D, T], BF16)
    UT = K.dram("UT", [512, T], F32)
    XL = K.dram("XL", [8, 512, NB], BF16)
    YL = K.dram("YL", [8, 512, NB], F32)
    OT = K.dram("OT", [512, T], BF16)
    dbuf = {}

    def DB(name, i=0):
        key = (name, i)
        if key not in dbuf:
            dbuf[key] = Buf("%s%d" % (name, i))
        return dbuf[key]

    ones32, b_ones32 = K.sb([128, 128], F32, "ones32")
    perm_b, b_perm = K.sb([128, 128], BF16, "perm")
    ident, b_ident = K.sb([128, 128], F32, "ident")
    maskf, b_maskf = K.sb([128, 128], F32, "maskf")
    maskb, b_maskb = K.sb([128, 128], F32, "maskb")
    crossb, b_crossb = K.sb([128, 1], F32, "crossb")
    zerob, b_zerob = K.sb([128, 1], F32, "zerob")
    flag, b_flag = K.sb([128, 1], F32, "flag")
    tmpc, b_tmpc = K.sb([128, 128], F32, "tmpc")
    K.memset("dve", ones32, 1.0, [b_ones32])
    K.memset("dve", zerob, 0.0, [b_zerob])
    K.dma("sp", [(tmpc, perm_in)], [], [b_tmpc], b_tmpc)
    K.copy("dve", perm_b, tmpc, [b_tmpc], [b_perm])
    K.dma("sp", [(ident, ident_in)], [], [b_ident], b_ident)
    K.dma("sp", [(maskf, maskf_in)], [], [b_maskf], b_maskf)
    K.dma("sp", [(maskb, maskb_in)], [], [b_maskb], b_maskb)
    K.dma("sp", [(crossb, cb_in)], [], [b_crossb], b_crossb)
    K.dma("sp", [(flag, flag_in)], [], [b_flag], b_flag)

    g1T, b_g1T = K.sb([128, L, 8], F32, "g1T")
    g2T, b_g2T = K.sb([128, L, 8], F32, "g2T")
    gfT, b_gfT = K.sb([128, 8], F32, "gfT")
    bglu, b_bglu = K.sb([128, L, 16], F32, "bglu")
    dsk, b_dsk = K.sb([128, L, 4], F32, "dsk")
    subg, b_subg = K.sb([128, L], F32, "subg")
    bmodT, b_bmodT = K.sb([128, L, 48], F32, "bmodT")
    K.dma("sp", [(g1T, g1_in)], [], [b_g1T], b_g1T)
    K.dma("sp", [(g2T, g2_in)], [], [b_g2T], b_g2T)
    K.dma("sp", [(gfT, gf_in)], [], [b_gfT], b_gfT)
    K.dma("sp", [(bglu, bglu_in)], [], [b_bglu], b_bglu)
    K.dma("sp", [(dsk, dsk_in)], [], [b_dsk], b_dsk)
    K.dma("sp", [(subg, subg_in)], [], [b_subg], b_subg)
    K.dma("sp", [(bmodT, bmod_in)], [], [b_bmodT], b_bmodT)

    cvb = [Buf("cv%d" % i) for i in range(4)]
    cvi = [0]
    b_wb = Buf("wb_all")

    def convert(src, dst, nelem):
        rows = nelem // 2048
        s2 = src.rearrange("(r c) -> r c", c=2048)
        d2 = dst.rearrange("(r c) -> r c", c=2048)
        r0 = 0
        while r0 < rows:
            r1 = min(rows, r0 + 1024)
            cb = cvb[cvi[0] % 4]
            cvi[0] += 1
            K.dma("pool", [(d2[r0:r1, :], s2[r0:r1, :])], [], [cb], cb)
            r0 = r1

    for l in range(L):
        convert(w_in[l].rearrange("a b -> (a b)"), wb_in[l].rearrange("a b -> (a b)"), D * 4096)
    for l in range(L):
        convert(w_attn[l].rearrange("a b -> (a b)"), wb_attn[l].rearrange("a b -> (a b)"), 512 * D)
        convert(w_glu[l].rearrange("a b -> (a b)"), wb_glu[l].rearrange("a b -> (a b)"), 512 * 2 * D)
        convert(w_o[l].rearrange("a b -> (a b)"), wb_o[l].rearrange("a b -> (a b)"), D * D)
        convert(w_ffi[l].rearrange("a b -> (a b)"), wb_ffi[l].rearrange("a b -> (a b)"), D * 2 * DFF)
        convert(w_ffo[l].rearrange("a b -> (a b)"), wb_ffo[l].rearrange("a b -> (a b)"), DFF * D)

    modT, b_modT = K.sb([128, L, 48, 2], F32, "modT")
    A1, b_A1 = K.sb([128, L, 8, 2], F32, "A1")
    A2, b_A2 = K.sb([128, L, 8, 2], F32, "A2")
    cT, b_cT = K.sb([128, 8, 2], F32, "cT")
    sc_, b_sc = K.sb([128, 8, 2], F32, "silu_c")
    K.dma("sp", [(cT, cT_in)], [], [b_cT], b_cT)
    K.actf(sc_, cT, AF.Silu, [b_cT], [b_sc])
    blk = 0
    mod_scope = K.scope()
    mod_scope.__enter__()
    wm = [K.sb([128, 8, 512], F32, "wm%d" % i) for i in range(2)]
    for l in range(L):
        for cb_ in range(12):
            wt, bw = wm[blk % 2]
            blk += 1
            K.dma("sp", [(wt, w_mod[l, :, cb_ * 512:(cb_ + 1) * 512].rearrange("(k p) c -> p k c", p=128))],
                  [], [bw], bw)
            pt, bp = K.psum()
            for c in range(4):
                for kc in range(8):
                    K.mm(pt[:, c * 2:c * 2 + 2], wt[:, kc, c * 128:(c + 1) * 128], sc_[:, kc, :],
                         kc == 0, kc == 7, [bw, b_sc], [bp])
            K.tt("dve", modT[:, l, cb_ * 4:(cb_ + 1) * 4, :],
                 pt[:, 0:8].rearrange("p (c s) -> p c s", s=2),
                 bmodT[:, l, cb_ * 4:(cb_ + 1) * 4].unsqueeze(2).to_broadcast([128, 4, 2]),
                 ALU.add, [bp, b_bmodT], [b_modT])
    K.barrier()
    mod_scope.__exit__(None, None, None)
    for l in range(L):
        K.stt(A1[:, l], modT[:, l, 8:16, :], 1.0, g1T[:, l, :].unsqueeze(2).to_broadcast([128, 8, 2]),
              ALU.add, ALU.mult, [b_modT, b_g1T], [b_A1])
        K.stt(A2[:, l], modT[:, l, 32:40, :], 1.0, g2T[:, l, :].unsqueeze(2).to_broadcast([128, 8, 2]),
              ALU.add, ALU.mult, [b_modT, b_g2T], [b_A2])

    def SH1(l, kc, s):
        return modT[:, l, 0 + kc, s:s + 1]

    def GT1(l, kc, s):
        return modT[:, l, 16 + kc, s:s + 1]

    def SH2(l, kc, s):
        return modT[:, l, 24 + kc, s:s + 1]

    def GT2(l, kc, s):
        return modT[:, l, 40 + kc, s:s + 1]

    lamT, b_lamT = K.sb([128, L, 4, 64], F32, "lamT")
    lamv, b_lamv = K.sb([128, L], F32, "lamv")
    neglam, b_neglam = K.sb([128, L], F32, "neglam")
    lsum, b_lsum = K.sb([128, L, 2], F32, "lsum")
    lprod, b_lprod = K.sb([128, L, 2, 64], F32, "lprod")
    K.dma("sp", [(lamT, lam_in)], [], [b_lamT], b_lamT)
    K.tt("dve", lprod[:, :, 0, :], lamT[:, :, 0, :], lamT[:, :, 1, :], ALU.mult, [b_lamT], [b_lprod])
    K.tt("dve", lprod[:, :, 1, :], lamT[:, :, 2, :], lamT[:, :, 3, :], ALU.mult, [b_lamT], [b_lprod])
    K.op("dve", lambda e: e.tensor_reduce(out=lsum, in_=lprod, op=ALU.add, axis=mybir.AxisListType.X),
         [b_lprod], [b_lsum])
    K.actf(lsum, lsum, AF.Exp, [b_lsum], [b_lsum])
    K.tt("dve", lamv, lsum[:, :, 0], lsum[:, :, 1], ALU.subtract, [b_lsum], [b_lamv])
    for l in range(L):
        K.ts("dve", lamv[:, l:l + 1], lamv[:, l:l + 1], float(lam_init_fn(l)), None, ALU.add, None, [b_lamv], [b_lamv])
    K.ts("dve", neglam, lamv, -1.0, None, ALU.mult, None, [b_lamv], [b_neglam])
    subgs, b_subgs = K.sb([128, L], F32, "subgs")
    for l in range(L):
        K.ts("dve", subgs[:, l:l + 1], subg[:, l:l + 1], float(1.0 - lam_init_fn(l)), None, ALU.mult, None,
             [b_subg], [b_subgs])

    K.barrier()

    class NS:
        pass
    S = NS()

    def alloc_shared():
        S.xt_ = [K.sb([128, 8, 512], F32, "xt%d" % i) for i in range(2)]
        S.hT, S.b_hT = K.sb([128, 8, 512], BF16, "hT")
        S.sq, S.b_sq = K.sb([128, 8, 512], F32, "sq")
        S.rstd, S.b_rstd = K.sb([128, 512], F32, "rstd")
        S.tmpn, S.b_tmpn = K.sb([128, 512], F32, "tmpn")
        S.wsl = [K.sb([128, 11, 512], BF16, "w%d" % i) for i in range(3)]
    wsl_i = [0]

    def wslot():
        s_ = S.wsl[wsl_i[0] % 3]
        wsl_i[0] += 1
        return s_

    def rstd_only(xt, bx):
        K.actf(S.sq, xt, AF.Square, [bx], [S.b_sq])
        pt, bp = K.psum()
        for kc in range(8):
            K.mm(pt, ones32, S.sq[:, kc, :], kc == 0, kc == 7, [b_ones32, S.b_sq], [bp])
        K.ts("dve", S.tmpn, pt, 1.0 / D, EPS, ALU.mult, ALU.add, [bp], [S.b_tmpn])
        K.actf(S.tmpn, S.tmpn, AF.Sqrt, [S.b_tmpn], [S.b_tmpn])
        K.op("dve", lambda e: e.reciprocal(out=S.rstd, in_=S.tmpn), [S.b_tmpn], [S.b_rstd])

    def norm_mod(xt, bx, Acol, Bcol, out_bf, b_out):
        rstd_only(xt, bx)
        for kc in range(8):
            K.stt(S.sq[:, kc, :], xt[:, kc, :], Acol(kc), S.rstd, ALU.mult, ALU.mult,
                  [bx, S.b_rstd, b_A1, b_A2], [S.b_sq])
            K.actf(out_bf[:, kc, :], S.sq[:, kc, :], AF.Identity, [S.b_sq, b_modT], [b_out], bias=Bcol(kc), scale=1.0)

    def load_w(src2d, k0, nk, col_runs):
        wt, bw = wslot()
        pairs = []
        off = 0
        for (c0, n) in col_runs:
            pairs.append((wt[:, 0:nk, off:off + n],
                          src2d[k0 * 128:(k0 + nk) * 128, c0:c0 + n].rearrange("(k p) c -> p k c", p=128)))
            off += n
        K.dma("sp", pairs, [b_wb], [bw], bw)
        return wt, bw

    cnt = {"qo": 0, "vo": 0, "uo": 0, "go": 0, "x": 0, "cs": 0}

    def phaseA(l):
        import os
        stopA = os.environ.get('MK_STOPA', '')
        scA = K.scope()
        scA.__enter__()
        alloc_shared()
        xt_ = S.xt_
        hT, b_hT = S.hT, S.b_hT
        csl = [K.sb([128, 2, 512], F32, "cs%d" % i) for i in range(2)]
        qb_, b_qb = K.sb([128, 512], BF16, "qb")
        qc_, b_qc = K.sb([128, 512], F32, "qc")
        qs_, b_qs = K.sb([128, 512], F32, "qs")
        qo_ = [K.sb([128, 4, 512], BF16, "qo%d" % i) for i in range(2)]
        vo_ = [K.sb([128, 4, 512], BF16, "vo%d" % i) for i in range(2)]
        uo_ = [K.sb([128, 4, 512], F32, "uo%d" % i) for i in range(2)]
        ul_ = [K.sb([128, 4, 8, 64], BF16, "ul%d" % i) for i in range(2)]
        go_ = [K.sb([128, 4, 512], BF16, "go%d" % i) for i in range(2)]
        wsrc = wb_in[l]
        for i in range(NT):
            s = i // (NT // 2)
            t0 = i * 512
            xt, bx = xt_[cnt["x"] % 2]
            cnt["x"] += 1
            src = x_in if l == 0 else xs
            rd = [] if l == 0 else [DB("xs", i)]
            K.dma("sp", [(xt, src[:, t0:t0 + 512].rearrange("(k p) t -> p k t", p=128))], rd, [bx], bx)
            cs, bcs = csl[cnt["cs"] % 2]
            cnt["cs"] += 1
            K.dma("sp", [(cs[:, 0, :], cos_in[:, t0:t0 + 512]), (cs[:, 1, :], sin_in[:, t0:t0 + 512])], [], [bcs], bcs)
            norm_mod(xt, bx, lambda kc: A1[:, l, kc, s:s + 1], lambda kc: SH1(l, kc, s), hT, b_hT)
            if stopA == 'n':
                continue
            for qk in range(2):
                wt, bw = load_w(wsrc, 0, 8, [(qk * 512, 512)])
                qo, bqo = qo_[cnt["qo"] % 2]
                cnt["qo"] += 1
                Y = int(os.environ.get("MK_Y", "9"))
                for c in range(4):
                    if Y < 1:
                        continue
                    pt, bp = K.psum()
                    for kc in range(8):
                        K.mm(pt, wt[:, kc, c * 128:(c + 1) * 128], hT[:, kc, :], kc == 0, kc == 7, [bw, b_hT], [bp])
                    if Y < 2:
                        continue
                    K.copy("act", qb_, pt, [bp], [b_qb])
                    if Y < 3:
                        continue
                    K.tt("dve", qc_, pt, cs[:, 0, :], ALU.mult, [bp, bcs] + ([b_qb] if os.environ.get("MK_Z") == "1" else []), [b_qc])
                    if Y < 4:
                        continue
                    p2, bp2 = K.psum()
                    K.mm(p2, perm_b, qb_, True, True, [b_perm, b_qb], [bp2])
                    if Y < 5:
                        continue
                    K.tt("dve", qs_, p2, cs[:, 1, :], ALU.mult, [bp2, bcs], [b_qs])
                    K.tt(os.environ.get("MK_QE", "pool"), qo[:, c, :], qc_, qs_, ALU.add, [b_qc, b_qs], [bqo])
                dst = QT if qk == 0 else KT
                if os.environ.get("MK_X") != "1":
                    K.dma("act", [(dst[:, :, t0:t0 + 512].rearrange("h p t -> p h t"), qo)], [bqo],
                          [DB("QT" if qk == 0 else "KT", i)], bqo)
            if stopA == 'qk':
                continue
            wt, bw = load_w(wsrc, 0, 8, [(1024, 512)])
            vo, bvo = vo_[cnt["vo"] % 2]
            cnt["vo"] += 1
            for tc in range(4):
                pt, bp = K.psum()
                for kc in range(8):
                    K.mm(pt, hT[:, kc, tc * 128:(tc + 1) * 128], wt[:, kc, 0:512], kc == 0, kc == 7, [bw, b_hT], [bp])
                K.copy("act", vo[:, tc, :], pt, [bp], [bvo])
            K.dma("act", [(VS[i * 4:(i + 1) * 4].rearrange("c p e -> p c e"), vo)], [bvo], [DB("VS", i)], bvo)
            if stopA == 'v':
                continue
            wt, bw = load_w(wsrc, 0, 8, [(1536, 512)])
            uo, buo = uo_[cnt["uo"] % 2]
            ul, bul = ul_[cnt["uo"] % 2]
            cnt["uo"] += 1
            for c in range(4):
                pt, bp = K.psum()
                for kc in range(8):
                    K.mm(pt, wt[:, kc, c * 128:(c + 1) * 128], hT[:, kc, :], kc == 0, kc == 7, [bw, b_hT], [bp])
                K.copy("act", uo[:, c, :], pt, [bp], [buo])
                K.copy("dve", ul[:, c], pt.rearrange("p (b s) -> p s b", s=8), [bp], [bul])
            K.dma("act", [(UT[:, t0:t0 + 512].rearrange("(c p) t -> p c t", p=128), uo)], [buo], [DB("UT", i)], buo)
            K.dma("act", [(XL[:, c * 128:(c + 1) * 128, i * 64:(i + 1) * 64].rearrange("s p b -> p s b"), ul[:, c])
                         for c in range(4)], [bul], [DB("XL", i)], bul)
            if stopA == 'u':
                continue
            for gb in range(4):
                wt, bw = load_w(wsrc, 0, 8, [(2048 + gb * 512, 512)])
                go, bgo = go_[cnt["go"] % 2]
                cnt["go"] += 1
                for c in range(4):
                    pt, bp = K.psum()
                    for kc in range(8):
                        K.mm(pt, wt[:, kc, c * 128:(c + 1) * 128], hT[:, kc, :], kc == 0, kc == 7, [bw, b_hT], [bp])
                    K.actf(go[:, c, :], pt, AF.Sigmoid, [bp], [bgo])
                dst = SGA if gb < 2 else SGS
                r0 = (gb % 2) * 512
                K.dma("act", [(dst[r0:r0 + 512, t0:t0 + 512].rearrange("(c p) t -> p c t", p=128), go)], [bgo],
                      [DB("SGA" if gb < 2 else "SGS", i * 2 + gb % 2)], bgo)

        K.barrier()
        scA.__exit__(None, None, None)

    cntB = {"q": 0, "p": 0, "ob": 0}
    scale = 64 ** -0.5

    def phaseB(l):
        scB = K.scope()
        scB.__enter__()
        kz0, b_kz0 = K.sb([128, T], BF16, "kz0")
        kz1, b_kz1 = K.sb([128, T], BF16, "kz1")
        vh_, b_vh = K.sb([128, T // 128, 128], BF16, "vh")
        qt_ = [K.sb([128, 512], BF16, "qt%d" % i) for i in range(2)]
        pT_ = [K.sb([128, 512], BF16, "pT%d" % i) for i in range(6)]
        rr_, b_rr = K.sb([128, 512], F32, "rr")
        t1_, b_t1 = K.sb([128, 512], F32, "t1")
        t2_, b_t2 = K.sb([128, 512], F32, "t2")
        od_, b_od = K.sb([128, 512], F32, "od")
        o2_, b_o2 = K.sb([128, 512], F32, "o2")
        ob_ = [K.sb([128, 512], BF16, "ob%d" % i) for i in range(2)]
        acs_, b_acs = K.sb([128, 512], F32, "acs")
        accP, b_accP = K.sb([128, 512], F32, "accP")
        NKC = T // 128
        for h in range(NH):
            if h == 0:
                K.memset("pool", kz0[64:128, :], 0.0, [b_kz0])
                K.memset("pool", kz1[0:64, :], 0.0, [b_kz1])
            K.dma("sp", [(kz0[0:64, :], KT[h, 0:64, :])], [DB("KT", i) for i in range(NT)], [b_kz0], b_kz0)
            K.dma("sp", [(kz1[64:128, :], KT[h, 64:128, :])], [DB("KT", i) for i in range(NT)], [b_kz1], b_kz1)
            K.dma("sp", [(vh_, VS[:, :, h * 128:(h + 1) * 128].rearrange("c p e -> p c e"))],
                  [DB("VS", i) for i in range(NT)], [b_vh], b_vh)
            for j in range(NT):
                sj = j // (NT // 2)
                qt, bq = qt_[cntB["q"] % 2]
                cntB["q"] += 1
                K.dma("sp", [(qt, QT[h, :, j * 512:(j + 1) * 512])], [DB("QT", j)], [bq], bq)
                for m in range(2):
                    hpo = K.psum_hold()
                    hpa = K.psum_hold()
                    po, bpo = hpo
                    pa, bpa = hpa

                    def emit_s(c):
                        ps_, bps = K.psum()
                        kz, bkz = (kz0, b_kz0) if m == 0 else (kz1, b_kz1)
                        K.mm(ps_, kz[:, c * 128:(c + 1) * 128], qt, True, True, [bkz, bq], [bps])
                        return ps_, bps
                    LA = 2
                    pend = [emit_s(c_) for c_ in range(min(LA, NKC))]
                    for c in range(NKC):
                        sc = (c * 128) // cfg.SEG
                        ps_, bps = pend.pop(0)
                        if c + LA < NKC:
                            pend.append(emit_s(c + LA))
                        pT, bpT = pT_[cntB["p"] % 6]
                        cntB["p"] += 1
                        K.actf(pT, ps_, AF.Exp, [bps, b_crossb, b_zerob], [bpT],
                               bias=(zerob if sc == sj else crossb), scale=scale)
                        K.mm(po, vh_[:, c, :], pT, c == 0, c == NKC - 1, [b_vh, bpT], [bpo])
                        if c % 5 in (1, 3):
                            if c == 1:
                                K.copy("pool", accP, pT, [bpT], [b_accP])
                            else:
                                K.tt("pool", accP, accP, pT, ALU.add, [bpT, b_accP], [b_accP])
                        elif c == 0:
                            K.copy("dve", pa, pT, [bpT], [bpa])
                        else:
                            K.tt("dve", pa, pa, pT, ALU.add, [bpT, bpa], [bpa])
                    K.tt("dve", acs_, pa, accP, ALU.add, [bpa, b_accP], [b_acs])
                    pr, bpr = K.psum()
                    K.mm(pr, ones32, acs_, True, True, [b_ones32, b_acs], [bpr])
                    K.op("dve", lambda e: e.reciprocal(out=rr_, in_=pr), [bpr], [b_rr])
                    if m == 0:
                        K.tt("dve", t1_, po, rr_, ALU.mult, [bpo, b_rr], [b_t1])
                    else:
                        K.tt("dve", t2_, po, rr_, ALU.mult, [bpo, b_rr], [b_t2])
                    K.psum_release(hpo)
                    K.psum_release(hpa)
                K.stt(od_, t2_, neglam[:, l:l + 1], t1_, ALU.mult, ALU.add, [b_t1, b_t2, b_neglam], [b_od])
                K.actf(o2_, od_, AF.Square, [b_od], [b_o2])
                pq, bpq = K.psum()
                K.mm(pq, ones32, o2_, True, True, [b_ones32, b_o2], [bpq])
                K.ts("dve", rr_, pq, 1.0 / 128, EPS, ALU.mult, ALU.add, [bpq], [b_rr])
                K.actf(rr_, rr_, AF.Sqrt, [b_rr], [b_rr])
                K.op("dve", lambda e: e.reciprocal(out=t1_, in_=rr_), [b_rr], [b_t1])
                ob, bob = ob_[cntB["ob"] % 2]
                cntB["ob"] += 1
                K.stt(ob, od_, subgs[:, l:l + 1], t1_, ALU.mult, ALU.mult, [b_od, b_t1, b_subgs], [bob])
                K.dma("act", [(OT[h * 128:(h + 1) * 128, j * 512:(j + 1) * 512], ob)], [bob], [DB("OT", j)], bob)

        K.barrier()
        scB.__exit__(None, None, None)

    PG = [128, 32]
    sA = {}

    def pg(name, shape=None, dt=F32):
        if name not in sA:
            sA[name] = K.sb(shape or PG, dt, name)
        return sA[name]

    NLEV = int(math.log2(NB))
    cntC = {"x": 0, "y": 0}

    def ssm_precompute(l):
        sA.clear()
        scP = K.scope()
        scP.__enter__()
        are, b1 = pg("are"); aim, b2 = pg("aim"); ldt, b3 = pg("ldt")
        K.dma("sp", [(are, are_in[:, l, :])], [], [b1], b1)
        K.dma("sp", [(aim, aim_in[:, l, :])], [], [b2], b2)
        K.dma("sp", [(ldt, ldt_in[:, l, :])], [], [b3], b3)
        Br, bBr = pg("Br", [128, 32, 16]); Bi, bBi = pg("Bi", [128, 32, 16])
        Cr, bCr = pg("Cr", [128, 32, 16]); Ci, bCi = pg("Ci", [128, 32, 16])
        K.dma("sp", [(Br, bre_in[:, l])], [], [bBr], bBr)
        K.dma("sp", [(Bi, bim_in[:, l])], [], [bBi], bBi)
        K.dma("sp", [(Cr, cre_in[:, l])], [], [bCr], bCr)
        K.dma("sp", [(Ci, cim_in[:, l])], [], [bCi], bCi)
        dt_, bdt = pg("dt"); xr, bxr = pg("xr"); th, bth = pg("th"); mag, bmag = pg("mag")
        K.actf(dt_, ldt, AF.Exp, [b3], [bdt])
        K.tt("dve", xr, dt_, are, ALU.mult, [bdt, b1], [bxr])
        K.tt("dve", th, dt_, aim, ALU.mult, [bdt, b2], [bth])
        K.actf(mag, xr, AF.Exp, [bxr], [bmag])
        yv, byv = pg("yv"); ki, bki = pg("ki", PG, I32); kf, bkf = pg("kf"); mk, bmk = pg("mk")
        sn, bsn = pg("sn"); cs_, bcs_ = pg("cs")
        for (dst, bdst, off) in ((sn, bsn, 1.5), (cs_, bcs_, 1.75)):
            K.ts("dve", yv, th, 1.0 / TWO_PI, off, ALU.mult, ALU.add, [bth], [byv])
            K.copy("dve", ki, yv, [byv], [bki])
            K.copy("dve", kf, ki, [bki], [bkf])
            K.tt("dve", mk, kf, yv, ALU.is_gt, [bkf, byv], [bmk])
            K.tt("dve", kf, kf, mk, ALU.subtract, [bkf, bmk], [bkf])
            K.tt("dve", yv, yv, kf, ALU.subtract, [byv, bkf], [byv])
            K.ts("dve", yv, yv, -0.5, TWO_PI, ALU.add, ALU.mult, [byv], [byv])
            K.ts("dve", yv, yv, math.pi, -math.pi, ALU.min, ALU.max, [byv], [byv])
            K.actf(dst, yv, AF.Sin, [byv], [bdst])
        Ar, bAr = pg("Ar"); Ai, bAi = pg("Ai")
        stopC = os.environ.get('MK_STOPC', '')
        if stopC == 'sin':
            K.barrier(); scP.__exit__(None, None, None); return
        K.tt("dve", Ar, mag, cs_, ALU.mult, [bmag, bcs_], [bAr])
        K.tt("dve", Ai, mag, sn, ALU.mult, [bmag, bsn], [bAi])
        Par, bPar = pg("Par", [128, 32, 9]); Pai, bPai = pg("Pai", [128, 32, 9])
        Pdr, bPdr = pg("Pdr", [128, 32, 9]); Pdi, bPdi = pg("Pdi", [128, 32, 9])
        Qar, bQar = pg("Qar", [128, 32, 9]); Qai, bQai = pg("Qai", [128, 32, 9])
        Qdr, bQdr = pg("Qdr", [128, 32, 9]); Qdi, bQdi = pg("Qdi", [128, 32, 9])
        ta, bta = pg("ta"); tb, btb = pg("tb")
        K.memset("dve", Par[:, :, 0], 1.0, [bPar])
        K.memset("dve", Pai[:, :, 0], 0.0, [bPai])
        for n in range(1, 9):
            K.tt("dve", ta, Par[:, :, n - 1], Ar, ALU.mult, [bPar, bAr], [bta])
            K.tt("dve", tb, Pai[:, :, n - 1], Ai, ALU.mult, [bPai, bAi], [btb])
            K.tt("dve", Par[:, :, n], ta, tb, ALU.subtract, [bta, btb], [bPar])
            K.tt("dve", ta, Par[:, :, n - 1], Ai, ALU.mult, [bPar, bAi], [bta])
            K.tt("dve", tb, Pai[:, :, n - 1], Ar, ALU.mult, [bPai, bAr], [btb])
            K.tt("dve", Pai[:, :, n], ta, tb, ALU.add, [bta, btb], [bPai])
        e2, be2 = pg("e2")
        for n in range(9):
            K.actf(e2, xr, AF.Exp, [bxr], [be2], scale=-2.0 * n)
            K.tt("dve", Qar[:, :, n], Par[:, :, n], e2, ALU.mult, [bPar, be2], [bQar])
            K.stt(Qai[:, :, n], Pai[:, :, n], -1.0, e2, ALU.mult, ALU.mult, [bPai, be2], [bQai])
        for n in range(9):
            K.copy("pool", Pdr[:, :, 8 - n], Par[:, :, n], [bPar], [bPdr])
            K.copy("pool", Pdi[:, :, 8 - n], Pai[:, :, n], [bPai], [bPdi])
            K.copy("pool", Qdr[:, :, 8 - n], Qar[:, :, n], [bQar], [bQdr])
            K.copy("pool", Qdi[:, :, 8 - n], Qai[:, :, n], [bQai], [bQdi])
        K.copy("dve", S.SS[:, 0, 0, :], Par[:, :, 8], [bPar], [S.b_SS])
        K.copy("dve", S.SS[:, 0, 1, :], Pai[:, :, 8], [bPai], [S.b_SS])
        for k in range(NLEV):
            if k > 0:
                K.tt("dve", ta, S.SS[:, k - 1, 0, :], S.SS[:, k - 1, 0, :], ALU.mult, [S.b_SS], [bta])
                K.tt("dve", tb, S.SS[:, k - 1, 1, :], S.SS[:, k - 1, 1, :], ALU.mult, [S.b_SS], [btb])
                K.tt("dve", S.SS[:, k, 0, :], ta, tb, ALU.subtract, [bta, btb], [S.b_SS])
                K.stt(S.SS[:, k, 1, :], S.SS[:, k - 1, 0, :], 2.0, S.SS[:, k - 1, 1, :], ALU.mult, ALU.mult, [S.b_SS], [S.b_SS])
            K.ts("dve", S.SS[:, k, 2, :], S.SS[:, k, 1, :], -1.0, None, ALU.mult, None, [S.b_SS], [S.b_SS])
        K.ts("dve", S.SF, S.SS, flag[:, 0:1], None, ALU.mult, None, [S.b_SS, b_flag], [S.b_SF])
        if stopC == 'pow':
            K.barrier(); scP.__exit__(None, None, None); return
        nr, bnr = pg("nr"); den, bden = pg("den"); fr, bfr = pg("fr"); fi, bfi = pg("fi")
        K.ts("dve", nr, Ar, -1.0, None, ALU.add, None, [bAr], [bnr])
        K.tt("dve", ta, are, are, ALU.mult, [b1], [bta])
        K.tt("dve", tb, aim, aim, ALU.mult, [b2], [btb])
        K.tt("dve", den, ta, tb, ALU.add, [bta, btb], [bden])
        K.op("dve", lambda e: e.reciprocal(out=den, in_=den), [bden], [bden])
        K.tt("dve", ta, nr, are, ALU.mult, [bnr, b1], [bta])
        K.tt("dve", tb, Ai, aim, ALU.mult, [bAi, b2], [btb])
        K.tt("dve", ta, ta, tb, ALU.add, [bta, btb], [bta])
        K.tt("dve", fr, ta, den, ALU.mult, [bta, bden], [bfr])
        K.tt("dve", ta, Ai, are, ALU.mult, [bAi, b1], [bta])
        K.tt("dve", tb, nr, aim, ALU.mult, [bnr, b2], [btb])
        K.tt("dve", ta, ta, tb, ALU.subtract, [bta, btb], [bta])
        K.tt("dve", fi, ta, den, ALU.mult, [bta, bden], [bfi])
        Bbr, bBbr = pg("Bbr", [128, 32, 16]); Bbi, bBbi = pg("Bbi", [128, 32, 16])
        t16a, bt16a = pg("t16a", [128, 32, 16]); t16b, bt16b = pg("t16b", [128, 32, 16])
        frb = fr.unsqueeze(2).to_broadcast([128, 32, 16])
        fib = fi.unsqueeze(2).to_broadcast([128, 32, 16])
        K.tt("dve", t16a, Br, frb, ALU.mult, [bBr, bfr], [bt16a])
        K.tt("dve", t16b, Bi, fib, ALU.mult, [bBi, bfi], [bt16b])
        K.tt("dve", Bbr, t16a, t16b, ALU.subtract, [bt16a, bt16b], [bBbr])
        K.tt("dve", t16a, Bi, frb, ALU.mult, [bBi, bfr], [bt16a])
        K.tt("dve", t16b, Br, fib, ALU.mult, [bBr, bfi], [bt16b])
        K.tt("dve", Bbi, t16a, t16b, ALU.add, [bt16a, bt16b], [bBbi])
        def wtab(name, src0, bs0, o0, src1, bs1, o1):
            w, bw = pg(name, [128, 16, 2, 8])
            v0 = src0.rearrange("p (a d) n -> p a d n", d=2)
            v1 = src1.rearrange("p (a d) n -> p a d n", d=2)
            K.copy("pool", w[:, :, 0, :], v0[:, :, 0, o0:o0 + 8], [bs0], [bw])
            K.copy("pool", w[:, :, 1, :], v1[:, :, 1, o1:o1 + 8], [bs1], [bw])
            return w.rearrange("p a d n -> p (a d) n"), bw
        WBr, bWBr = wtab("WBr", Pdr, bPdr, 1, Par, bPar, 0)
        WBi, bWBi = wtab("WBi", Pdi, bPdi, 1, Pai, bPai, 0)
        WCr, bWCr = wtab("WCr", Qdr, bQdr, 1, Qar, bQar, 0)
        WCi, bWCi = wtab("WCi", Qdi, bQdi, 1, Qai, bQai, 0)
        WKr, bWKr = wtab("WKr", Par, bPar, 1, Pdr, bPdr, 0)
        WKi, bWKi = wtab("WKi", Pai, bPai, 1, Pdi, bPdi, 0)
        big = [128, 32, 8, 16]
        PBr, bPBr = pg("PBr", big); PBi, bPBi = pg("PBi", big)
        PCr, bPCr = pg("PCr", big); PCi, bPCi = pg("PCi", big)
        tg1, btg1 = pg("tg1", big); tg2, btg2 = pg("tg2", big)

        def cmul(outr, boutr, outi, bouti, Wr, bWr, Wi, bWi, Xr, bXr, Xi, bXi, neg_i=False):
            wr = Wr.unsqueeze(3).to_broadcast(big)
            wi = Wi.unsqueeze(3).to_broadcast(big)
            xr_ = Xr.unsqueeze(2).to_broadcast(big)
            xi_ = Xi.unsqueeze(2).to_broadcast(big)
            K.tt("dve", tg1, wr, xr_, ALU.mult, [bWr, bXr], [btg1])
            K.tt("dve", tg2, wi, xi_, ALU.mult, [bWi, bXi], [btg2])
            K.tt("dve", outr, tg1, tg2, ALU.subtract, [btg1, btg2], [boutr])
            K.tt("dve", tg1, wr, xi_, ALU.mult, [bWr, bXi], [btg1])
            K.tt("dve", tg2, wi, xr_, ALU.mult, [bWi, bXr], [btg2])
            if neg_i:
                K.stt(outi, tg1, -1.0, tg2, ALU.mult, ALU.subtract, [btg1, btg2], [bouti])
            else:
                K.tt("dve", outi, tg1, tg2, ALU.add, [btg1, btg2], [bouti])

        cmul(PBr, bPBr, PBi, bPBi, WBr, bWBr, WBi, bWBi, Bbr, bBbr, Bbi, bBbi)
        cmul(PCr, bPCr, PCi, bPCi, WCr, bWCr, WCi, bWCi, Cr, bCr, Ci, bCi, neg_i=True)
        if stopC == 'cmul':
            K.barrier(); scP.__exit__(None, None, None); return
        PBr3 = PBr.rearrange("p g n c -> p g (n c)")
        PBi3 = PBi.rearrange("p g n c -> p g (n c)")
        PCr3 = PCr.rearrange("p g n c -> p g (n c)")
        PCi3 = PCi.rearrange("p g n c -> p g (n c)")
        for gp in range(16):
            for d in range(2):
                pt, bp = K.psum()
                idx = 0
                for gpar in range(2):
                    for (src, bsrc) in ((PBr3, bPBr), (PBi3, bPBi)):
                        sl_ = slice(gpar * 64, gpar * 64 + 64)
                        K.mm(pt[:, idx * 64:(idx + 1) * 64], src[:, gp * 2 + d, :], ident[:, sl_], True, True,
                             [bsrc, b_ident], [bp])
                        idx += 1
                K.copy("act", S.PBT[:, gp, d].rearrange("p a b c -> p (a b c)"), pt[:, 0:256], [bp], [S.b_PBT])
        mt, bmt = pg("mt", [128, 128])
        tb1 = tg1.rearrange("p g n c -> p (g n c)").bitcast(BF16).rearrange("p (h g f) -> p h g f", h=2, g=32)
        tb2 = tg2.rearrange("p g n c -> p (g n c)").bitcast(BF16).rearrange("p (h g f) -> p h g f", h=2, g=32)
        PBrb, bPBrb, PBib, bPBib = tb1[:, 0], btg1, tb1[:, 1], btg1
        PCrb, bPCrb, PCib, bPCib = tb2[:, 0], btg2, tb2[:, 1], btg2
        K.copy("act", PBrb, PBr3, [bPBr], [bPBrb])
        K.copy("act", PBib, PBi3, [bPBi], [bPBib])
        K.copy("act", PCrb, PCr3, [bPCr], [bPCrb])
        K.copy("act", PCib, PCi3, [bPCi], [bPCib])
        for gp in range(16):
            for gpar in range(2):
                g = gp * 2 + gpar
                sl = slice(gpar * 64, gpar * 64 + 64)
                pt, bp = K.psum()
                for d in range(2):
                    o = pt[:, d * 128:(d + 1) * 128]
                    K.mm(o, PBrb[sl, gp * 2 + d, :], PCrb[sl, gp * 2 + d, :], True, False, [bPBrb, bPCrb], [bp])
                    K.mm(o, PBib[sl, gp * 2 + d, :], PCib[sl, gp * 2 + d, :], False, True, [bPBib, bPCib], [bp])
                K.tt("dve", mt, pt[:, 0:128], maskf, ALU.mult, [bp, b_maskf], [bmt])
                K.tt("dve", tmpc, pt[:, 128:256], maskb, ALU.mult, [bp, b_maskb], [b_tmpc])
                K.tt("dve", S.M0[:, g, :], mt, tmpc, ALU.add, [bmt, b_tmpc], [S.b_M0])

        PKr, bPKr, PKi, bPKi = PCr, bPCr, PCi, bPCi
        cmul(PKr, bPKr, PKi, bPKi, WKr, bWKr, WKi, bWKi, Cr, bCr, Ci, bCi, neg_i=True)
        K.copy("act", S.PCC[:, :, :, 0, :], PKr.rearrange("p (a d) n c -> p a d (n c)", d=2), [bPKr], [S.b_PCC])
        K.copy("act", S.PCC[:, :, :, 1, :], PKi.rearrange("p (a d) n c -> p a d (n c)", d=2), [bPKi], [S.b_PCC])
        K.barrier()
        scP.__exit__(None, None, None)

    def hs_scan(gp, d):
        col = gp * 2 + d
        cur = 0
        for k in range(NLEV):
            s = 1 << k
            src, bsrc = S.Hb[cur]
            dst, bdst = S.Hb[1 - cur]

            def scal(tab, j):
                return tab[:, k, j, col:col + 1]

            def region(lo, hi, tab, btab, two_seg=False):
                sh = -s if d == 0 else s
                if two_seg:
                    def v(t, ri, off):
                        return t[:, ri, :].rearrange("p (g n) -> p g n", g=2)[:, :, lo + off:hi + off]
                else:
                    def v(t, ri, off):
                        return t[:, ri, lo + off:hi + off]
                rd = [bsrc, btab]
                if two_seg:
                    def v2(t, off):
                        return t.rearrange("p r (g n) -> p r g n", g=2)[:, :, :, lo + off:hi + off]
                else:
                    def v2(t, off):
                        return t[:, :, lo + off:hi + off]
                if not two_seg:
                    K.stt(v2(dst, 0), v2(src, sh), scal(tab, 0), v2(src, 0), ALU.mult, ALU.add, rd, [bdst])
                else:
                    K.stt(v(dst, 0, 0), v(src, 0, sh), scal(tab, 0), v(src, 0, 0), ALU.mult, ALU.add, rd, [bdst])
                    K.stt(v(dst, 1, 0), v(src, 1, sh), scal(tab, 0), v(src, 1, 0), ALU.mult, ALU.add, rd, [bdst])
                K.stt(v(dst, 0, 0), v(src, 1, sh), scal(tab, 2), v(dst, 0, 0), ALU.mult, ALU.add, rd + [bdst], [bdst])
                K.stt(v(dst, 1, 0), v(src, 0, sh), scal(tab, 1), v(dst, 1, 0), ALU.mult, ALU.add, rd + [bdst], [bdst])

            if d == 0:
                K.copy("dve", dst[:, :, 0:min(s, NBS)], src[:, :, 0:min(s, NBS)], [bsrc], [bdst])
                if s < NBS:
                    region(s, NBS, S.SS, S.b_SS, two_seg=True)
                region(NBS, min(NBS + s, NB), S.SF, S.b_SF)
                if s >= NBS and s < NB:
                    pass
            else:
                lo0 = max(NB - s, NBS)
                K.copy("dve", dst[:, :, lo0:NB], src[:, :, lo0:NB], [bsrc], [bdst])
                if s < NBS:
                    region(0, NBS - s, S.SS, S.b_SS, two_seg=True)
                region(max(NBS - s, 0), NBS, S.SF, S.b_SF)
            cur = 1 - cur
        return cur

    def phaseC(l):
        stopC = os.environ.get('MK_STOPC', '')
        scC = K.scope()
        scC.__enter__()
        S.PBT, S.b_PBT = K.sb([128, 16, 2, 2, 2, 64], BF16, "PBT")
        S.PCC, S.b_PCC = K.sb([128, 16, 2, 2, 128], BF16, "PCC")
        S.M0, S.b_M0 = K.sb([128, 32, 128], BF16, "M0")
        S.SS, S.b_SS = K.sb([128, 11, 3, 32], F32, "SS")
        S.SF, S.b_SF = K.sb([128, 11, 3, 32], F32, "SF")
        ssm_precompute(l)
        if stopC in ('sin', 'pow', 'cmul', 'tr', 'pre'):
            K.barrier(); scC.__exit__(None, None, None); return
        S.Hb = [K.sb([128, 2, NB], F32, "H%d" % i) for i in range(2)]
        S.Hin, S.b_Hin = K.sb([128, 2, 2, NB], BF16, "Hin")
        S.xg_ = [K.sb([128, NB], BF16, "xg%d" % i) for i in range(4)]
        S.yg_ = [K.sb([128, NB], F32, "yg%d" % i) for i in range(2)]
        NBT = (NB + 511) // 512
        bw = min(512, NB)
        for gp in range(16):
            xg = []
            for gpar in range(2):
                g = gp * 2 + gpar
                xt, bx = S.xg_[cntC["x"] % 4]
                cntC["x"] += 1
                K.dma("sp", [(xt[s2 * 16:(s2 + 1) * 16, :], XL[s2, g * 16:(g + 1) * 16, :]) for s2 in range(8)],
                      [DB("XL", i) for i in range(NT)], [bx], bx)
                xg.append((xt, bx))
            for d in range(2):
                H0, bH0 = S.Hb[0]
                for nt in range(NBT):
                    for ri in range(2):
                        pt, bp = K.psum()
                        for gpar in range(2):
                            K.mm(pt[gpar * 64:(gpar + 1) * 64, 0:bw], S.PBT[:, gp, d, gpar, ri, :],
                                 xg[gpar][0][:, nt * 512:nt * 512 + bw], True, True, [S.b_PBT, xg[gpar][1]], [bp])
                        K.copy("act", H0[:, ri, nt * 512:nt * 512 + bw], pt[:, 0:bw], [bp], [bH0])
                cur = hs_scan(gp, d)
                Hf, bHf = S.Hb[cur]
                if d == 0:
                    K.memset("pool", S.Hin[:, 0, :, 0:1], 0.0, [S.b_Hin])
                    K.copy("act", S.Hin[:, 0].rearrange("p r (g n) -> p r g n", g=2)[:, :, :, 1:NBS],
                           Hf.rearrange("p r (g n) -> p r g n", g=2)[:, :, :, 0:NBS - 1], [bHf], [S.b_Hin])
                    K.ts("dve", S.Hin[:, 0, :, NBS:NBS + 1], Hf[:, :, NBS - 1:NBS], flag[:, 0:1], None, ALU.mult, None,
                         [bHf, b_flag], [S.b_Hin])
                else:
                    K.memset("pool", S.Hin[:, 1, :, NB - 1:NB], 0.0, [S.b_Hin])
                    K.copy("act", S.Hin[:, 1].rearrange("p r (g n) -> p r g n", g=2)[:, :, :, 0:NBS - 1],
                           Hf.rearrange("p r (g n) -> p r g n", g=2)[:, :, :, 1:NBS], [bHf], [S.b_Hin])
                    K.ts("dve", S.Hin[:, 1, :, NBS - 1:NBS], Hf[:, :, NBS:NBS + 1], flag[:, 0:1], None, ALU.mult, None,
                         [bHf, b_flag], [S.b_Hin])
            for gpar in range(2):
                g = gp * 2 + gpar
                sl = slice(gpar * 64, gpar * 64 + 64)
                yt, by = S.yg_[cntC["y"] % 2]
                cntC["y"] += 1
                for nt in range(NBT):
                    cs = slice(nt * 512, nt * 512 + bw)
                    pt, bp = K.psum()
                    K.mm(pt[:, 0:bw], S.M0[:, g, :], xg[gpar][0][:, cs], True, False, [S.b_M0, xg[gpar][1]], [bp])
                    for d in range(2):
                        for ri in range(2):
                            K.mm(pt[:, 0:bw], S.PCC[sl, gp, d, ri, :], S.Hin[sl, d, ri, cs], False,
                                 (d == 1 and ri == 1), [S.b_PCC, S.b_Hin], [bp])
                    K.copy("act", yt[:, cs], pt[:, 0:bw], [bp], [by])
                K.dma("act", [(YL[t2, g * 16:(g + 1) * 16, :], yt[t2 * 16:(t2 + 1) * 16, :]) for t2 in range(8)],
                      [by], [DB("YL", g)], by)
        K.barrier()
        scC.__exit__(None, None, None)

    cntD = {"i": 0}

    def phaseD(l, last):
        scD = K.scope()
        scD.__enter__()
        alloc_shared()
        xt_ = S.xt_
        oT_ = [K.sb([128, 4, 512], BF16, "oT%d" % i) for i in range(1)]
        sg_ = [K.sb([128, 2, 8, 512], BF16, "sg%d" % i) for i in range(1)]
        yl_ = [K.sb([128, 4, 8, 64], F32, "yl%d" % i) for i in range(1)]
        ud_ = [K.sb([128, 4, 512], F32, "ud%d" % i) for i in range(1)]
        zt_, b_zt = K.sb([128, 4, 512], BF16, "zt")
        m1_, b_m1 = K.sb([128, 8, 512], F32, "m1")
        mg_, b_mg = K.sb([128, 8, 512], BF16, "mg")
        h2_, b_h2 = K.sb([128, 8, 512], BF16, "h2")
        aT_, b_aT = K.sb([128, 22, 512], BF16, "aT")
        ga_, b_ga = K.sb([128, 512], F32, "ga")
        gb_, b_gb = K.sb([128, 512], F32, "gb")
        gc_, b_gc = K.sb([128, 512], F32, "gc")
        for i in range(NT):
            s = i // (NT // 2)
            t0 = i * 512
            par = 0
            cntD["i"] += 1
            xt, bx = xt_[cnt["x"] % 2]
            cnt["x"] += 1
            src = x_in if l == 0 else xs
            rd = [] if l == 0 else [DB("xs", i)]
            K.dma("sp", [(xt, src[:, t0:t0 + 512].rearrange("(k p) t -> p k t", p=128))], rd, [bx], bx)
            oT, boT = oT_[par]
            K.dma("sp", [(oT, OT[:, t0:t0 + 512].rearrange("(c p) t -> p c t", p=128))], [DB("OT", i)], [boT], boT)
            sg, bsg = sg_[par]
            K.dma("sp", [(sg[:, 0], SGA[:, t0:t0 + 512].rearrange("(c p) t -> p c t", p=128)),
                         (sg[:, 1], SGS[:, t0:t0 + 512].rearrange("(c p) t -> p c t", p=128))],
                  [DB("SGA", i * 2), DB("SGA", i * 2 + 1), DB("SGS", i * 2), DB("SGS", i * 2 + 1)], [bsg], bsg)
            yl, byl = yl_[par]
            K.dma("sp", [(yl[:, c], YL[:, c * 128:(c + 1) * 128, i * 64:(i + 1) * 64].rearrange("t p b -> p t b"))
                         for c in range(4)], [DB("YL", g) for g in range(NG)], [byl], byl)
            ud, bud = ud_[par]
            K.dma("sp", [(ud, UT[:, t0:t0 + 512].rearrange("(c p) t -> p c t", p=128))], [DB("UT", i)], [bud], bud)
            for c in range(4):
                yv = yl[:, c].rearrange("p t b -> p b t")
                g3 = ga_.rearrange("p (b t) -> p b t", t=8)
                K.stt(g3, ud[:, c, :].rearrange("p (b t) -> p b t", t=8), dsk[:, l, c:c + 1], yv, ALU.mult, ALU.add,
                      [bud, byl, b_dsk], [b_ga])
                K.actf(gb_, ga_, AF.Square, [b_ga], [b_gb])
                K.ts("dve", gb_, gb_, 0.044715, 1.0, ALU.mult, ALU.add, [b_gb], [b_gb])
                K.tt("dve", gb_, gb_, ga_, ALU.mult, [b_gb, b_ga], [b_gb])
                K.actf(gc_, gb_, AF.Sigmoid, [b_gb], [b_gc], scale=1.5957691216057308)
                K.tt("dve", zt_[:, c, :], ga_, gc_, ALU.mult, [b_ga, b_gc], [b_zt])
            for blk_ in range(2):
                wt, bw = load_w(wb_attn[l], 0, 4, [(blk_ * 512, 512)])
                for c in range(4):
                    pt, bp = K.psum()
                    for kc in range(4):
                        K.mm(pt, wt[:, kc, c * 128:(c + 1) * 128], oT[:, kc, :], kc == 0, kc == 3, [bw, boT], [bp])
                    K.tt("dve", m1_[:, blk_ * 4 + c, :], pt, sg[:, 0, blk_ * 4 + c, :], ALU.mult, [bp, bsg], [b_m1])
            for blk_ in range(4):
                wt, bw = load_w(wb_glu[l], 0, 4, [(blk_ * 256, 256), (1024 + blk_ * 256, 256)])
                for c in range(2):
                    j = blk_ * 2 + c
                    pl, bpl = K.psum()
                    for kc in range(4):
                        K.mm(pl, wt[:, kc, c * 128:(c + 1) * 128], zt_[:, kc, :], kc == 0, kc == 3, [bw, b_zt], [bpl])
                    pg_, bpg = K.psum()
                    for kc in range(4):
                        K.mm(pg_, wt[:, kc, 256 + c * 128:256 + (c + 1) * 128], zt_[:, kc, :], kc == 0, kc == 3,
                             [bw, b_zt], [bpg])
                    K.actf(ga_, pg_, AF.Sigmoid, [bpg, b_bglu], [b_ga], bias=bglu[:, l, 8 + j:9 + j], scale=1.0)
                    K.stt(gb_, pl, bglu[:, l, j:j + 1], ga_, ALU.add, ALU.mult, [bpl, b_bglu, b_ga], [b_gb])
                    K.tt("dve", gb_, gb_, sg[:, 1, j, :], ALU.mult, [b_gb, bsg], [b_gb])
                    K.tt("dve", mg_[:, j, :], gb_, m1_[:, j, :], ALU.add, [b_gb, b_m1], [b_mg])
            for blk_ in range(2):
                wt, bw = load_w(wb_o[l], 0, 8, [(blk_ * 512, 512)])
                for c in range(4):
                    j = blk_ * 4 + c
                    pt, bp = K.psum()
                    for kc in range(8):
                        K.mm(pt, wt[:, kc, c * 128:(c + 1) * 128], mg_[:, kc, :], kc == 0, kc == 7, [bw, b_mg], [bp])
                    K.stt(xt[:, j, :], pt, GT1(l, j, s), xt[:, j, :], ALU.mult, ALU.add, [bp, b_modT, bx], [bx])
            norm_mod(xt, bx, lambda kc: A2[:, l, kc, s:s + 1], lambda kc: SH2(l, kc, s), h2_, b_h2)
            for blk_ in range(11):
                wt, bw = load_w(wb_ffi[l], 0, 8, [(blk_ * 256, 256), (DFF + blk_ * 256, 256)])
                for c in range(2):
                    j = blk_ * 2 + c
                    pgt, bpg = K.psum()
                    for kc in range(8):
                        K.mm(pgt, wt[:, kc, c * 128:(c + 1) * 128], h2_[:, kc, :], kc == 0, kc == 7, [bw, b_h2], [bpg])
                    pu, bpu = K.psum()
                    for kc in range(8):
                        K.mm(pu, wt[:, kc, 256 + c * 128:256 + (c + 1) * 128], h2_[:, kc, :], kc == 0, kc == 7,
                             [bw, b_h2], [bpu])
                    K.actf(ga_, pgt, AF.Silu, [bpg], [b_ga])
                    K.tt("dve", aT_[:, j, :], pu, ga_, ALU.mult, [bpu, b_ga], [b_aT])
            for half in range(2):
                wa, bwa = load_w(wb_ffo[l], 0, 11, [(half * 512, 512)])
                wb2, bwb2 = load_w(wb_ffo[l], 11, 11, [(half * 512, 512)])
                for c in range(4):
                    j = half * 4 + c
                    pt, bp = K.psum()
                    for kc in range(22):
                        w_, bw_ = (wa, bwa) if kc < 11 else (wb2, bwb2)
                        K.mm(pt, w_[:, kc % 11, c * 128:(c + 1) * 128], aT_[:, kc, :], kc == 0, kc == 21,
                             [bw_, b_aT], [bp])
                    K.stt(xt[:, j, :], pt, GT2(l, j, s), xt[:, j, :], ALU.mult, ALU.add, [bp, b_modT, bx], [bx])
            if not last:
                K.dma("act", [(xs[:, t0:t0 + 512].rearrange("(k p) t -> p k t", p=128), xt)], [bx], [DB("xs", i)], bx)
            else:
                rstd_only(xt, bx)
                for kc in range(8):
                    K.stt(xt[:, kc, :], xt[:, kc, :], gfT[:, kc:kc + 1], S.rstd, ALU.mult, ALU.mult,
                          [bx, S.b_rstd, b_gfT], [bx])
                K.dma("act", [(y_out[:, t0:t0 + 512].rearrange("(k p) t -> p k t", p=128), xt)], [bx], [DB("y", i)], bx)

        K.barrier()
        scD.__exit__(None, None, None)

    import os
    stop = os.environ.get("MK_STOP", "")
    for l in range(L):
        if l == 0:
            for cb in cvb:
                b_wb.w.update(cb.w)
        if stop == "pro":
            break
        phaseA(l)
        K.barrier()
        if stop == "A":
            break
        phaseB(l)
        K.barrier()
        if stop == "B":
            break
        phaseC(l)
        K.barrier()
        if stop == "C":
            break
        phaseD(l, l == L - 1)
        K.barrier()
    K.barrier()
    return nc


def _rope_tables(T_seq):
    inv = 1.0 / (10000.0 ** (np.arange(0, 64, 2, dtype=np.float32) / 64.0))
    ang = np.arange(T_seq, dtype=np.float32)[:, None] * inv[None, :].astype(np.float32)
    ang = np.concatenate([ang, ang], axis=-1).astype(np.float32)
    return np.cos(ang).astype(np.float32), np.sin(ang).astype(np.float32)


def _consts():
    perm = np.zeros((128, 128), np.float32)
    for m in range(2):
        for d in range(64):
            perm[m * 64 + (d + 32) % 64, m * 64 + d] = 1.0
    ident = np.eye(128, dtype=np.float32)
    s2 = np.arange(128) // 16
    maskf = (s2[None, :] >= s2[:, None]).astype(np.float32)
    maskb = (s2[None, :] <= s2[:, None]).astype(np.float32)
    return perm, ident, maskf, maskb


def _pl(a, L):
    rest = a.shape[4:]
    a = a.reshape((L, 2, 16, 2, 64) + rest)
    a = np.moveaxis(a, (3, 4, 0, 2, 1), (0, 1, 2, 3, 4))
    return np.ascontiguousarray(a.reshape((128, L, 32) + rest)).astype(np.float32)


def make_in_maps(inp, cfg, core_seqs):
    L, T = cfg.L, cfg.T
    perm, ident, maskf, maskb = _consts()

    def fm(v, nch):
        v = np.asarray(v, np.float32)
        lead = v.shape[:-1]
        v = v.reshape(lead + (nch, 128))
        v = np.moveaxis(v, -1, 0)
        return np.ascontiguousarray(v)

    shared = {
        "perm": perm, "ident": ident, "maskf": maskf, "maskb": maskb,
        "w_mod": np.ascontiguousarray(inp["w_mod"][:L], np.float32),
        "b_modT": fm(inp["b_mod"][:L], 48),
        "g1T": fm(inp["norm1_g"][:L], 8), "g2T": fm(inp["norm2_g"][:L], 8), "gfT": fm(inp["final_g"], 8),
        "w_in": np.ascontiguousarray(inp["w_in"][:L], np.float32),
        "lamT": np.ascontiguousarray(np.broadcast_to(
            np.stack([inp["lam_q1"][:L], inp["lam_k1"][:L], inp["lam_q2"][:L], inp["lam_k2"][:L]], axis=1)[None],
            (128, L, 4, 64)), np.float32),
        "subgT": np.ascontiguousarray(np.asarray(inp["subln_g"][:L], np.float32).T),
        "w_attn": np.ascontiguousarray(inp["w_attn_br"][:L], np.float32),
        "ssm_dT": fm(inp["ssm_d"][:L], 4),
        "w_glu": np.ascontiguousarray(inp["w_glu"][:L], np.float32),
        "b_gluT": fm(inp["b_glu"][:L], 16),
        "w_o": np.ascontiguousarray(inp["w_o"][:L], np.float32),
        "w_ffi": np.ascontiguousarray(inp["w_ffn_in"][:L], np.float32),
        "w_ffo": np.ascontiguousarray(inp["w_ffn_out"][:L], np.float32),
    }
    are = np.asarray(inp["ssm_a_re"][:L], np.float32)
    aim = np.asarray(inp["ssm_a_im"][:L], np.float32)
    ldt = np.broadcast_to(np.asarray(inp["ssm_log_dt"][:L], np.float32)[..., None], (L, 2, 32, 64))
    shared["a_reP"] = _pl(are, L)
    shared["a_imP"] = _pl(aim, L)
    shared["ldtP"] = _pl(np.ascontiguousarray(ldt), L)
    shared["b_reP"] = _pl(np.asarray(inp["ssm_b_re"][:L], np.float32), L)
    shared["b_imP"] = _pl(np.asarray(inp["ssm_b_im"][:L], np.float32), L)
    shared["c_reP"] = _pl(np.swapaxes(np.asarray(inp["ssm_c_re"][:L], np.float32), 3, 4), L)
    shared["c_imP"] = _pl(np.swapaxes(np.asarray(inp["ssm_c_im"][:L], np.float32), 3, 4), L)
    maps = []
    for (x, c, pos, split) in core_seqs:
        cos, sin = _rope_tables(int(pos.max()) + 1)
        cosT = np.ascontiguousarray(np.tile(cos[pos].T, (2, 1)))
        sgn = np.where(np.arange(64) < 32, -1.0, 1.0).astype(np.float32)
        sinT = np.ascontiguousarray(np.tile((sin[pos] * sgn[None, :]).T, (2, 1)))
        m = dict(shared)
        m["xT"] = np.ascontiguousarray(x.T)
        m["cT"] = np.ascontiguousarray(np.moveaxis(np.asarray(c, np.float32).reshape(2, 8, 128), (0, 1, 2), (2, 1, 0)))
        m["cosT"] = cosT.astype(np.float32)
        m["sinT"] = sinT.astype(np.float32)
        m["crossbias"] = np.full((128, 1), 0.0 if split else -30000.0, np.float32)
        m["flag"] = np.full((128, 1), 1.0 if split else 0.0, np.float32)
        maps.append(m)
    return maps


_NC_CACHE = {}


def run(inp, cfg, core_seqs):
    key = (cfg.T, cfg.L)
    if key not in _NC_CACHE:
        _NC_CACHE[key] = build(cfg)
    nc = _NC_CACHE[key]
    maps = make_in_maps(inp, cfg, core_seqs)
    res = run_bass_kernel_spmd(nc, maps, core_ids=list(range(len(maps))))
    return [np.asarray(r["yT"]).T for r in res.results]


def kernel(**inp):
    cfg = Cfg(8192, 4)
    xp = np.asarray(inp["x_prompt"], np.float32)
    xsm = np.asarray(inp["x_sample"], np.float32)
    cp = np.asarray(inp["c_prompt"], np.float32)
    csm = np.asarray(inp["c_sample"], np.float32)
    cores = []
    pos_p = np.concatenate([np.arange(4096), np.arange(4096)])
    for c in range(4):
        cores.append((np.concatenate([xp[2 * c], xp[2 * c + 1]], axis=0), np.stack([cp[2 * c], cp[2 * c + 1]]),
                      pos_p, False))
    for c in range(4):
        cores.append((xsm[c], np.stack([csm[c], csm[c]]), np.arange(8192), True))
    outs = run(inp, cfg, cores)
    yp = np.empty((8, 4096, D), np.float32)
    for c in range(4):
        yp[2 * c] = outs[c][:4096]
        yp[2 * c + 1] = outs[c][4096:]
    ys = np.stack([outs[4 + c] for c in range(4)], axis=0).astype(np.float32)
    return (yp, ys)
```

```python
import math
import contextlib
import numpy as np
import concourse.bass as bass
import concourse.mybir as mybir
from concourse.bass_utils import run_bass_kernel_spmd

F32 = mybir.dt.float32
BF16 = mybir.dt.bfloat16
I32 = mybir.dt.int32
ALU = mybir.AluOpType
AF = mybir.ActivationFunctionType

D = 1024
NH = 4
DFF = 2816
DSSM = 512
NG = 32
EPS = 1e-6
TWO_PI = 2.0 * math.pi


def lam_init_fn(layer):
    return 0.8 - 0.6 * math.exp(-0.3 * layer)


class Buf:
    __slots__ = ("name", "w", "r", "sem", "cnt", "excl")

    def __init__(self, name, excl=False):
        self.name = name
        self.excl = excl
        self.w = {}
        self.r = {}
        self.sem = None
        self.cnt = 0


class EngState:
    def __init__(self, eng, sem, self_sync):
        self.eng = eng
        self.sem = sem
        self.cnt = 0
        self.waited = {}
        self.self_sync = self_sync


def _merge(d, src):
    for k, (s, v) in src.items():
        if k not in d or d[k][1] < v:
            d[k] = (s, v)


class KB:
    def __init__(self, nc):
        self.nc = nc
        self.engs = {}
        for name, e in (("pe", nc.tensor), ("dve", nc.vector), ("act", nc.scalar),
                        ("pool", nc.gpsimd), ("sp", nc.sync)):
            self.engs[name] = EngState(e, nc.alloc_semaphore("e_" + name), name != "pe")
        self.dma_bufs = []
        self.nalloc = 0
        self.stacks = []
        self.scope_bufs = []
        self.free_sems = []
        self.retired = {}
        self.nsem = 0
        self.ps = []
        for i in range(8):
            t = nc.alloc_psum_tensor("psb%d" % i, [128, 512], F32)
            self.ps.append((t.ap(), Buf("ps%d" % i, excl=True)))
        self.ps_i = 0

    def sb(self, shape, dt, name=None):
        self.nalloc += 1
        nm = "%s_%d" % (name or "t", self.nalloc)
        if self.stacks:
            t = self.stacks[-1].enter_context(self.nc.sbuf_tensor(nm, list(shape), dt))
        else:
            t = self.nc.alloc_sbuf_tensor(nm, list(shape), dt)
        b = Buf(name or "t")
        if self.scope_bufs:
            self.scope_bufs[-1].append(b)
        return (t.ap() if hasattr(t, "ap") and callable(t.ap) else t[:]), b

    @contextlib.contextmanager
    def scope(self):
        st = contextlib.ExitStack()
        self.stacks.append(st)
        self.scope_bufs.append([])
        try:
            yield
        finally:
            self.stacks.pop()
            for b in self.scope_bufs.pop():
                if b.sem is not None:
                    self.free_sems.append((b.sem, b.cnt))
                    self.dma_bufs.remove(b)
                    self.retired[id(b.sem)] = (b.sem, b.cnt)
                    b.sem = None
            st.close()

    def dram(self, name, shape, dt):
        return self.nc.dram_tensor(name, list(shape), dt, kind="Internal").ap()

    def psum(self):
        p = self.ps.pop(0)
        self.ps.append(p)
        return p

    def psum_hold(self):
        return self.ps.pop(0)

    def psum_release(self, p):
        self.ps.append(p)

    def _deps(self, reads, writes):
        d = {}
        for b in reads:
            _merge(d, b.w)
            if b.excl:
                _merge(d, b.r)
        for b in writes:
            _merge(d, b.w)
            _merge(d, b.r)
        return d

    def _wait(self, E, deps):
        for k, (sem, val) in deps.items():
            if sem is E.sem and not E.self_sync:
                continue
            if E.waited.get(k, 0) < val:
                E.eng.wait_ge(sem, val)
                E.waited[k] = val

    def _record(self, tok, reads, writes):
        k = id(tok[0])
        for b in reads:
            if k not in b.r or b.r[k][1] < tok[1]:
                b.r[k] = tok
        for b in writes:
            b.w = {k: tok}
            b.r = {}

    def op(self, ename, fn, reads=(), writes=()):
        E = self.engs[ename]
        self._wait(E, self._deps(reads, writes))
        ins = fn(E.eng)
        E.cnt += 1
        ins.then_inc(E.sem, 1)
        self._record((E.sem, E.cnt), reads, writes)

    def dma(self, ename, pairs, reads, writes, sbuf):
        E = self.engs[ename]
        if sbuf.sem is None:
            if self.free_sems:
                sbuf.sem, sbuf.cnt = self.free_sems.pop()
                self.retired.pop(id(sbuf.sem), None)
            else:
                self.nsem += 1
                sbuf.sem = self.nc.alloc_semaphore("d_%d" % self.nsem)
            self.dma_bufs.append(sbuf)
        deps = self._deps(reads, writes)
        if sbuf.cnt > 0:
            _merge(deps, {id(sbuf.sem): (sbuf.sem, sbuf.cnt)})
        self._wait(E, deps)
        for (o, i) in pairs:
            E.eng.dma_start(out=o, in_=i).then_inc(sbuf.sem, 16)
            sbuf.cnt += 16
        self._record((sbuf.sem, sbuf.cnt), reads, writes)

    def barrier(self):
        toks = {}
        for E in self.engs.values():
            if E.cnt:
                toks[id(E.sem)] = (E.sem, E.cnt)
        for b in self.dma_bufs:
            if b.cnt:
                toks[id(b.sem)] = (b.sem, b.cnt)
        for k, tok in self.retired.items():
            toks.setdefault(k, tok)
        for E in self.engs.values():
            for k, (sem, val) in toks.items():
                if sem is E.sem:
                    continue
                if E.waited.get(k, 0) < val:
                    E.eng.wait_ge(sem, val)
                    E.waited[k] = val

    def mm(self, out, lhsT, rhs, start, stop, reads, writes):
        self.op("pe", lambda e: e.matmul(out, lhsT, rhs, start=start, stop=stop), reads, writes)

    def tt(self, eng, out, in0, in1, op, reads, writes):
        self.op(eng, lambda e: e.tensor_tensor(out=out, in0=in0, in1=in1, op=op), reads, writes)

    def ts(self, eng, out, in0, s1, s2, op0, op1, reads, writes):
        if op1 is None:
            self.op(eng, lambda e: e.tensor_scalar(out=out, in0=in0, scalar1=s1, scalar2=None, op0=op0), reads, writes)
        else:
            self.op(eng, lambda e: e.tensor_scalar(out=out, in0=in0, scalar1=s1, scalar2=s2, op0=op0, op1=op1), reads, writes)

    def stt(self, out, in0, scalar, in1, op0, op1, reads, writes):
        self.op("dve", lambda e: e.scalar_tensor_tensor(out=out, in0=in0, scalar=scalar, in1=in1, op0=op0, op1=op1), reads, writes)

    def actf(self, out, in_, func, reads, writes, bias=None, scale=None):
        kw = {}
        if bias is not None:
            kw["bias"] = bias
        if scale is not None:
            kw["scale"] = scale
        self.op("act", lambda e: e.activation(out=out, in_=in_, func=func, **kw), reads, writes)

    def copy(self, eng, out, in_, reads, writes):
        if eng == "act":
            self.op("act", lambda e: e.activation(out=out, in_=in_, func=AF.Identity), reads, writes)
        else:
            self.op(eng, lambda e: e.tensor_copy(out=out, in_=in_), reads, writes)

    def memset(self, eng, ap, val, writes):
        self.op(eng, lambda e: e.memset(ap, val), (), writes)


class Cfg:
    def __init__(self, T, L):
        self.T = T
        self.L = L
        self.NT = T // 512
        self.SEG = T // 2
        self.NB = T // 8
        self.NBS = self.NB // 2
        self.NKC = T // 128


def build(cfg):
    T, L, NT, NB, NBS = cfg.T, cfg.L, cfg.NT, cfg.NB, cfg.NBS
    nc = bass.Bass("TRN2", target_bir_lowering=False)
    K = KB(nc)

    def din(name, shape, dt=F32):
        return nc.dram_tensor(name, list(shape), dt, kind="ExternalInput").ap()

    x_in = din("xT", [D, T])
    y_out = nc.dram_tensor("yT", [D, T], F32, kind="ExternalOutput").ap()
    cT_in = din("cT", [128, 8, 2])
    cos_in = din("cosT", [128, T])
    sin_in = din("sinT", [128, T])
    perm_in = din("perm", [128, 128])
    ident_in = din("ident", [128, 128])
    maskf_in = din("maskf", [128, 128])
    maskb_in = din("maskb", [128, 128])
    cb_in = din("crossbias", [128, 1])
    flag_in = din("flag", [128, 1])
    w_mod = din("w_mod", [L, D, 6 * D])
    bmod_in = din("b_modT", [128, L, 48])
    g1_in = din("g1T", [128, L, 8])
    g2_in = din("g2T", [128, L, 8])
    gf_in = din("gfT", [128, 8])
    w_in = din("w_in", [L, D, 4096])
    lam_in = din("lamT", [128, L, 4, 64])
    subg_in = din("subgT", [128, L])
    w_attn = din("w_attn", [L, 512, D])
    are_in = din("a_reP", [128, L, 32])
    aim_in = din("a_imP", [128, L, 32])
    ldt_in = din("ldtP", [128, L, 32])
    bre_in = din("b_reP", [128, L, 32, 16])
    bim_in = din("b_imP", [128, L, 32, 16])
    cre_in = din("c_reP", [128, L, 32, 16])
    cim_in = din("c_imP", [128, L, 32, 16])
    dsk_in = din("ssm_dT", [128, L, 4])
    w_glu = din("w_glu", [L, 512, 2 * D])
    bglu_in = din("b_gluT", [128, L, 16])
    w_o = din("w_o", [L, D, D])
    w_ffi = din("w_ffi", [L, D, 2 * DFF])
    w_ffo = din("w_ffo", [L, DFF, D])

    wb_in = K.dram("wb_in", [L, D, 4096], BF16)
    wb_attn = K.dram("wb_attn", [L, 512, D], BF16)
    wb_glu = K.dram("wb_glu", [L, 512, 2 * D], BF16)
    wb_o = K.dram("wb_o", [L, D, D], BF16)
    wb_ffi = K.dram("wb_ffi", [L, D, 2 * DFF], BF16)
    wb_ffo = K.dram("wb_ffo", [L, DFF, D], BF16)
    xs = K.dram("xs", [D, T], F32)
    QT = K.dram("QT", [4, 128, T], BF16)
    KT = K.dram("KT", [4, 128, T], BF16)
    VS = K.dram("VS", [T // 128, 128, 512], BF16)
    SGA = K.dram("SGA", [D, T], BF16)
    SGS = K.dram("SGS", [D, T], BF16)
    UT = K.dram("UT", [512, T], F32)
    XL = K.dram("XL", [8, 512, NB], BF16)
    YL = K.dram("YL", [8, 512, NB], F32)
    OT = K.dram("OT", [512, T], BF16)
    dbuf = {}

    def DB(name, i=0):
        key = (name, i)
        if key not in dbuf:
            dbuf[key] = Buf("%s%d" % (name, i))
        return dbuf[key]

    ones32, b_ones32 = K.sb([128, 128], F32, "ones32")
    perm_b, b_perm = K.sb([128, 128], BF16, "perm")
    ident, b_ident = K.sb([128, 128], F32, "ident")
    maskf, b_maskf = K.sb([128, 128], F32, "maskf")
    maskb, b_maskb = K.sb([128, 128], F32, "maskb")
    crossb, b_crossb = K.sb([128, 1], F32, "crossb")
    zerob, b_zerob = K.sb([128, 1], F32, "zerob")
    flag, b_flag = K.sb([128, 1], F32, "flag")
    tmpc, b_tmpc = K.sb([128, 128], F32, "tmpc")
    K.memset("dve", ones32, 1.0, [b_ones32])
    K.memset("dve", zerob, 0.0, [b_zerob])
    K.dma("sp", [(tmpc, perm_in)], [], [b_tmpc], b_tmpc)
    K.copy("dve", perm_b, tmpc, [b_tmpc], [b_perm])
    K.dma("sp", [(ident, ident_in)], [], [b_ident], b_ident)
    K.dma("sp", [(maskf, maskf_in)], [], [b_maskf], b_maskf)
    K.dma("sp", [(maskb, maskb_in)], [], [b_maskb], b_maskb)
    K.dma("sp", [(crossb, cb_in)], [], [b_crossb], b_crossb)
    K.dma("sp", [(flag, flag_in)], [], [b_flag], b_flag)

    g1T, b_g1T = K.sb([128, L, 8], F32, "g1T")
    g2T, b_g2T = K.sb([128, L, 8], F32, "g2T")
    gfT, b_gfT = K.sb([128, 8], F32, "gfT")
    bglu, b_bglu = K.sb([128, L, 16], F32, "bglu")
    dsk, b_dsk = K.sb([128, L, 4], F32, "dsk")
    subg, b_subg = K.sb([128, L], F32, "subg")
    bmodT, b_bmodT = K.sb([128, L, 48], F32, "bmodT")
    K.dma("sp", [(g1T, g1_in)], [], [b_g1T], b_g1T)
    K.dma("sp", [(g2T, g2_in)], [], [b_g2T], b_g2T)
    K.dma("sp", [(gfT, gf_in)], [], [b_gfT], b_gfT)
    K.dma("sp", [(bglu, bglu_in)], [], [b_bglu], b_bglu)
    K.dma("sp", [(dsk, dsk_in)], [], [b_dsk], b_dsk)
    K.dma("sp", [(subg, subg_in)], [], [b_subg], b_subg)
    K.dma("sp", [(bmodT, bmod_in)], [], [b_bmodT], b_bmodT)

    cvb = [Buf("cv%d" % i) for i in range(4)]
    cvi = [0]
    b_wb = Buf("wb_all")

    def convert(src, dst, nelem):
        rows = nelem // 2048
        s2 = src.rearrange("(r c) -> r c", c=2048)
        d2 = dst.rearrange("(r c) -> r c", c=2048)
        r0 = 0
        while r0 < rows:
            r1 = min(rows, r0 + 1024)
            cb = cvb[cvi[0] % 4]
            cvi[0] += 1
            K.dma("pool", [(d2[r0:r1, :], s2[r0:r1, :])], [], [cb], cb)
            r0 = r1

    b_wbL = [Buf("wbL%d" % l) for l in range(L)]
    cur_layer = [0]

    def convert_layer(l):
        convert(w_in[l].rearrange("a b -> (a b)"), wb_in[l].rearrange("a b -> (a b)"), D * 4096)
        convert(w_attn[l].rearrange("a b -> (a b)"), wb_attn[l].rearrange("a b -> (a b)"), 512 * D)
        convert(w_glu[l].rearrange("a b -> (a b)"), wb_glu[l].rearrange("a b -> (a b)"), 512 * 2 * D)
        convert(w_o[l].rearrange("a b -> (a b)"), wb_o[l].rearrange("a b -> (a b)"), D * D)
        convert(w_ffi[l].rearrange("a b -> (a b)"), wb_ffi[l].rearrange("a b -> (a b)"), D * 2 * DFF)
        convert(w_ffo[l].rearrange("a b -> (a b)"), wb_ffo[l].rearrange("a b -> (a b)"), DFF * D)
        for cb in cvb:
            b_wbL[l].w.update(cb.w)

    convert_layer(0)

    modT, b_modT = K.sb([128, L, 48, 2], F32, "modT")
    A1, b_A1 = K.sb([128, L, 8, 2], F32, "A1")
    A2, b_A2 = K.sb([128, L, 8, 2], F32, "A2")
    cT, b_cT = K.sb([128, 8, 2], F32, "cT")
    sc_, b_sc = K.sb([128, 8, 2], F32, "silu_c")
    K.dma("sp", [(cT, cT_in)], [], [b_cT], b_cT)
    K.actf(sc_, cT, AF.Silu, [b_cT], [b_sc])
    blk = 0
    mod_scope = K.scope()
    mod_scope.__enter__()
    wm = [K.sb([128, 8, 512], F32, "wm%d" % i) for i in range(2)]
    for l in range(L):
        for cb_ in range(12):
            wt, bw = wm[blk % 2]
            blk += 1
            K.dma("sp", [(wt, w_mod[l, :, cb_ * 512:(cb_ + 1) * 512].rearrange("(k p) c -> p k c", p=128))],
                  [], [bw], bw)
            pt, bp = K.psum()
            for c in range(4):
                for kc in range(8):
                    K.mm(pt[:, c * 2:c * 2 + 2], wt[:, kc, c * 128:(c + 1) * 128], sc_[:, kc, :],
                         kc == 0, kc == 7, [bw, b_sc], [bp])
            K.tt("dve", modT[:, l, cb_ * 4:(cb_ + 1) * 4, :],
                 pt[:, 0:8].rearrange("p (c s) -> p c s", s=2),
                 bmodT[:, l, cb_ * 4:(cb_ + 1) * 4].unsqueeze(2).to_broadcast([128, 4, 2]),
                 ALU.add, [bp, b_bmodT], [b_modT])
    K.barrier()
    mod_scope.__exit__(None, None, None)
    for l in range(L):
        K.stt(A1[:, l], modT[:, l, 8:16, :], 1.0, g1T[:, l, :].unsqueeze(2).to_broadcast([128, 8, 2]),
              ALU.add, ALU.mult, [b_modT, b_g1T], [b_A1])
        K.stt(A2[:, l], modT[:, l, 32:40, :], 1.0, g2T[:, l, :].unsqueeze(2).to_broadcast([128, 8, 2]),
              ALU.add, ALU.mult, [b_modT, b_g2T], [b_A2])

    def SH1(l, kc, s):
        return modT[:, l, 0 + kc, s:s + 1]

    def GT1(l, kc, s):
        return modT[:, l, 16 + kc, s:s + 1]

    def SH2(l, kc, s):
        return modT[:, l, 24 + kc, s:s + 1]

    def GT2(l, kc, s):
        return modT[:, l, 40 + kc, s:s + 1]

    lamT, b_lamT = K.sb([128, L, 4, 64], F32, "lamT")
    lamv, b_lamv = K.sb([128, L], F32, "lamv")
    neglam, b_neglam = K.sb([128, L], F32, "neglam")
    lsum, b_lsum = K.sb([128, L, 2], F32, "lsum")
    lprod, b_lprod = K.sb([128, L, 2, 64], F32, "lprod")
    K.dma("sp", [(lamT, lam_in)], [], [b_lamT], b_lamT)
    K.tt("dve", lprod[:, :, 0, :], lamT[:, :, 0, :], lamT[:, :, 1, :], ALU.mult, [b_lamT], [b_lprod])
    K.tt("dve", lprod[:, :, 1, :], lamT[:, :, 2, :], lamT[:, :, 3, :], ALU.mult, [b_lamT], [b_lprod])
    K.op("dve", lambda e: e.tensor_reduce(out=lsum, in_=lprod, op=ALU.add, axis=mybir.AxisListType.X),
         [b_lprod], [b_lsum])
    K.actf(lsum, lsum, AF.Exp, [b_lsum], [b_lsum])
    K.tt("dve", lamv, lsum[:, :, 0], lsum[:, :, 1], ALU.subtract, [b_lsum], [b_lamv])
    for l in range(L):
        K.ts("dve", lamv[:, l:l + 1], lamv[:, l:l + 1], float(lam_init_fn(l)), None, ALU.add, None, [b_lamv], [b_lamv])
    K.ts("dve", neglam, lamv, -1.0, None, ALU.mult, None, [b_lamv], [b_neglam])
    subgs, b_subgs = K.sb([128, L], F32, "subgs")
    for l in range(L):
        K.ts("dve", subgs[:, l:l + 1], subg[:, l:l + 1], float(1.0 - lam_init_fn(l)), None, ALU.mult, None,
             [b_subg], [b_subgs])

    K.barrier()

    class NS:
        pass
    S = NS()

    def alloc_shared():
        S.xt_ = [K.sb([128, 8, 512], F32, "xt%d" % i) for i in range(2)]
        S.hT, S.b_hT = K.sb([128, 8, 512], BF16, "hT")
        S.sq, S.b_sq = K.sb([128, 8, 512], F32, "sq")
        S.rstd, S.b_rstd = K.sb([128, 512], F32, "rstd")
        S.tmpn, S.b_tmpn = K.sb([128, 512], F32, "tmpn")
        S.wsl = [K.sb([128, 11, 512], BF16, "w%d" % i) for i in range(3)]
    wsl_i = [0]

    def wslot():
        s_ = S.wsl[wsl_i[0] % 3]
        wsl_i[0] += 1
        return s_

    def rstd_only(xt, bx):
        K.actf(S.sq, xt, AF.Square, [bx], [S.b_sq])
        pt, bp = K.psum()
        for kc in range(8):
            K.mm(pt, ones32, S.sq[:, kc, :], kc == 0, kc == 7, [b_ones32, S.b_sq], [bp])
        K.ts("dve", S.tmpn, pt, 1.0 / D, EPS, ALU.mult, ALU.add, [bp], [S.b_tmpn])
        K.actf(S.tmpn, S.tmpn, AF.Sqrt, [S.b_tmpn], [S.b_tmpn])
        K.op("dve", lambda e: e.reciprocal(out=S.rstd, in_=S.tmpn), [S.b_tmpn], [S.b_rstd])

    def norm_mod(xt, bx, Acol, Bcol, out_bf, b_out):
        rstd_only(xt, bx)
        for kc in range(8):
            K.stt(S.sq[:, kc, :], xt[:, kc, :], Acol(kc), S.rstd, ALU.mult, ALU.mult,
                  [bx, S.b_rstd, b_A1, b_A2], [S.b_sq])
            K.actf(out_bf[:, kc, :], S.sq[:, kc, :], AF.Identity, [S.b_sq, b_modT], [b_out], bias=Bcol(kc), scale=1.0)

    def load_w(src2d, k0, nk, col_runs):
        wt, bw = wslot()
        pairs = []
        off = 0
        for (c0, n) in col_runs:
            pairs.append((wt[:, 0:nk, off:off + n],
                          src2d[k0 * 128:(k0 + nk) * 128, c0:c0 + n].rearrange("(k p) c -> p k c", p=128)))
            off += n
        K.dma("sp", pairs, [b_wbL[cur_layer[0]]], [bw], bw)
        return wt, bw

    cnt = {"qo": 0, "vo": 0, "uo": 0, "go": 0, "x": 0, "cs": 0}

    def phaseA(l):
        import os
        stopA = os.environ.get('MK_STOPA', '')
        cur_layer[0] = l
        scA = K.scope()
        scA.__enter__()
        alloc_shared()
        xt_ = S.xt_
        hT, b_hT = S.hT, S.b_hT
        csl = [K.sb([128, 2, 512], F32, "cs%d" % i) for i in range(2)]
        qb_, b_qb = K.sb([128, 512], BF16, "qb")
        qc_, b_qc = K.sb([128, 512], F32, "qc")
        qs_, b_qs = K.sb([128, 512], F32, "qs")
        qo_ = [K.sb([128, 4, 512], BF16, "qo%d" % i) for i in range(2)]
        vo_ = [K.sb([128, 4, 512], BF16, "vo%d" % i) for i in range(2)]
        uo_ = [K.sb([128, 4, 512], F32, "uo%d" % i) for i in range(2)]
        ul_ = [K.sb([128, 4, 8, 64], BF16, "ul%d" % i) for i in range(2)]
        go_ = [K.sb([128, 4, 512], BF16, "go%d" % i) for i in range(2)]
        wsrc = wb_in[l]
        for i in range(NT):
            s = i // (NT // 2)
            t0 = i * 512
            xt, bx = xt_[cnt["x"] % 2]
            cnt["x"] += 1
            src = x_in if l == 0 else xs
            rd = [] if l == 0 else [DB("xs", i)]
            K.dma("sp", [(xt, src[:, t0:t0 + 512].rearrange("(k p) t -> p k t", p=128))], rd, [bx], bx)
            cs, bcs = csl[cnt["cs"] % 2]
            cnt["cs"] += 1
            K.dma("sp", [(cs[:, 0, :], cos_in[:, t0:t0 + 512]), (cs[:, 1, :], sin_in[:, t0:t0 + 512])], [], [bcs], bcs)
            norm_mod(xt, bx, lambda kc: A1[:, l, kc, s:s + 1], lambda kc: SH1(l, kc, s), hT, b_hT)
            if stopA == 'n':
                continue
            for qk in range(2):
                wt, bw = load_w(wsrc, 0, 8, [(qk * 512, 512)])
                qo, bqo = qo_[cnt["qo"] % 2]
                cnt["qo"] += 1
                Y = int(os.environ.get("MK_Y", "9"))
                for c in range(4):
                    if Y < 1:
                        continue
                    pt, bp = K.psum()
                    for kc in range(8):
                        K.mm(pt, wt[:, kc, c * 128:(c + 1) * 128], hT[:, kc, :], kc == 0, kc == 7, [bw, b_hT], [bp])
                    if Y < 2:
                        continue
                    K.copy("act", qb_, pt, [bp], [b_qb])
                    if Y < 3:
                        continue
                    K.tt("dve", qc_, pt, cs[:, 0, :], ALU.mult, [bp, bcs] + ([b_qb] if os.environ.get("MK_Z") == "1" else []), [b_qc])
                    if Y < 4:
                        continue
                    p2, bp2 = K.psum()
                    K.mm(p2, perm_b, qb_, True, True, [b_perm, b_qb], [bp2])
                    if Y < 5:
                        continue
                    K.tt("dve", qs_, p2, cs[:, 1, :], ALU.mult, [bp2, bcs], [b_qs])
                    K.tt(os.environ.get("MK_QE", "pool"), qo[:, c, :], qc_, qs_, ALU.add, [b_qc, b_qs], [bqo])
                dst = QT if qk == 0 else KT
                if os.environ.get("MK_X") != "1":
                    K.dma("act", [(dst[:, :, t0:t0 + 512].rearrange("h p t -> p h t"), qo)], [bqo],
                          [DB("QT" if qk == 0 else "KT", i)], bqo)
            if stopA == 'qk':
                continue
            wt, bw = load_w(wsrc, 0, 8, [(1024, 512)])
            vo, bvo = vo_[cnt["vo"] % 2]
            cnt["vo"] += 1
            for tc in range(4):
                pt, bp = K.psum()
                for kc in range(8):
                    K.mm(pt, hT[:, kc, tc * 128:(tc + 1) * 128], wt[:, kc, 0:512], kc == 0, kc == 7, [bw, b_hT], [bp])
                K.copy("act", vo[:, tc, :], pt, [bp], [bvo])
            K.dma("act", [(VS[i * 4:(i + 1) * 4].rearrange("c p e -> p c e"), vo)], [bvo], [DB("VS", i)], bvo)
            if stopA == 'v':
                continue
            wt, bw = load_w(wsrc, 0, 8, [(1536, 512)])
            uo, buo = uo_[cnt["uo"] % 2]
            ul, bul = ul_[cnt["uo"] % 2]
            cnt["uo"] += 1
            for c in range(4):
                pt, bp = K.psum()
                for kc in range(8):
                    K.mm(pt, wt[:, kc, c * 128:(c + 1) * 128], hT[:, kc, :], kc == 0, kc == 7, [bw, b_hT], [bp])
                K.copy("act", uo[:, c, :], pt, [bp], [buo])
                K.copy("dve", ul[:, c], pt.rearrange("p (b s) -> p s b", s=8), [bp], [bul])
            K.dma("act", [(UT[:, t0:t0 + 512].rearrange("(c p) t -> p c t", p=128), uo)], [buo], [DB("UT", i)], buo)
            K.dma("act", [(XL[:, c * 128:(c + 1) * 128, i * 64:(i + 1) * 64].rearrange("s p b -> p s b"), ul[:, c])
                         for c in range(4)], [bul], [DB("XL", i)], bul)
            if stopA == 'u':
                continue
            for gb in range(4):
                wt, bw = load_w(wsrc, 0, 8, [(2048 + gb * 512, 512)])
                go, bgo = go_[cnt["go"] % 2]
                cnt["go"] += 1
                for c in range(4):
                    pt, bp = K.psum()
                    for kc in range(8):
                        K.mm(pt, wt[:, kc, c * 128:(c + 1) * 128], hT[:, kc, :], kc == 0, kc == 7, [bw, b_hT], [bp])
                    K.actf(go[:, c, :], pt, AF.Sigmoid, [bp], [bgo])
                dst = SGA if gb < 2 else SGS
                r0 = (gb % 2) * 512
                K.dma("act", [(dst[r0:r0 + 512, t0:t0 + 512].rearrange("(c p) t -> p c t", p=128), go)], [bgo],
                      [DB("SGA" if gb < 2 else "SGS", i * 2 + gb % 2)], bgo)

        K.barrier()
        scA.__exit__(None, None, None)

    cntB = {"q": 0, "p": 0, "ob": 0}
    scale = 64 ** -0.5

    def phaseB(l):
        scB = K.scope()
        scB.__enter__()
        kz0, b_kz0 = K.sb([128, T], BF16, "kz0")
        kz1, b_kz1 = K.sb([128, T], BF16, "kz1")
        vh_, b_vh = K.sb([128, T // 128, 128], BF16, "vh")
        qt_ = [K.sb([128, 512], BF16, "qt%d" % i) for i in range(2)]
        pT_ = [K.sb([128, 512], BF16, "pT%d" % i) for i in range(6)]
        rr_, b_rr = K.sb([128, 512], F32, "rr")
        t1_, b_t1 = K.sb([128, 512], F32, "t1")
        t2_, b_t2 = K.sb([128, 512], F32, "t2")
        od_, b_od = K.sb([128, 512], F32, "od")
        o2_, b_o2 = K.sb([128, 512], F32, "o2")
        ob_ = [K.sb([128, 512], BF16, "ob%d" % i) for i in range(2)]
        acs_, b_acs = K.sb([128, 512], F32, "acs")
        accP, b_accP = K.sb([128, 512], F32, "accP")
        NKC = T // 128
        for h in range(NH):
            if h == 0:
                K.memset("pool", kz0[64:128, :], 0.0, [b_kz0])
                K.memset("pool", kz1[0:64, :], 0.0, [b_kz1])
            K.dma("sp", [(kz0[0:64, :], KT[h, 0:64, :])], [DB("KT", i) for i in range(NT)], [b_kz0], b_kz0)
            K.dma("sp", [(kz1[64:128, :], KT[h, 64:128, :])], [DB("KT", i) for i in range(NT)], [b_kz1], b_kz1)
            K.dma("sp", [(vh_, VS[:, :, h * 128:(h + 1) * 128].rearrange("c p e -> p c e"))],
                  [DB("VS", i) for i in range(NT)], [b_vh], b_vh)
            for j in range(NT):
                sj = j // (NT // 2)
                qt, bq = qt_[cntB["q"] % 2]
                cntB["q"] += 1
                K.dma("sp", [(qt, QT[h, :, j * 512:(j + 1) * 512])], [DB("QT", j)], [bq], bq)
                for m in range(2):
                    hpo = K.psum_hold()
                    hpa = K.psum_hold()
                    po, bpo = hpo
                    pa, bpa = hpa

                    def emit_s(c):
                        ps_, bps = K.psum()
                        kz, bkz = (kz0, b_kz0) if m == 0 else (kz1, b_kz1)
                        K.mm(ps_, kz[:, c * 128:(c + 1) * 128], qt, True, True, [bkz, bq], [bps])
                        return ps_, bps
                    LA = 2
                    pend = [emit_s(c_) for c_ in range(min(LA, NKC))]
                    for c in range(NKC):
                        sc = (c * 128) // cfg.SEG
                        ps_, bps = pend.pop(0)
                        if c + LA < NKC:
                            pend.append(emit_s(c + LA))
                        pT, bpT = pT_[cntB["p"] % 6]
                        cntB["p"] += 1
                        K.actf(pT, ps_, AF.Exp, [bps, b_crossb, b_zerob], [bpT],
                               bias=(zerob if sc == sj else crossb), scale=scale)
                        K.mm(po, vh_[:, c, :], pT, c == 0, c == NKC - 1, [b_vh, bpT], [bpo])
                        if c % 5 in (1, 3):
                            if c == 1:
                                K.copy("pool", accP, pT, [bpT], [b_accP])
                            else:
                                K.tt("pool", accP, accP, pT, ALU.add, [bpT, b_accP], [b_accP])
                        elif c == 0:
                            K.copy("dve", pa, pT, [bpT], [bpa])
                        else:
                            K.tt("dve", pa, pa, pT, ALU.add, [bpT, bpa], [bpa])
                    K.tt("dve", acs_, pa, accP, ALU.add, [bpa, b_accP], [b_acs])
                    pr, bpr = K.psum()
                    K.mm(pr, ones32, acs_, True, True, [b_ones32, b_acs], [bpr])
                    K.op("dve", lambda e: e.reciprocal(out=rr_, in_=pr), [bpr], [b_rr])
                    if m == 0:
                        K.tt("dve", t1_, po, rr_, ALU.mult, [bpo, b_rr], [b_t1])
                    else:
                        K.tt("dve", t2_, po, rr_, ALU.mult, [bpo, b_rr], [b_t2])
                    K.psum_release(hpo)
                    K.psum_release(hpa)
                K.stt(od_, t2_, neglam[:, l:l + 1], t1_, ALU.mult, ALU.add, [b_t1, b_t2, b_neglam], [b_od])
                K.actf(o2_, od_, AF.Square, [b_od], [b_o2])
                pq, bpq = K.psum()
                K.mm(pq, ones32, o2_, True, True, [b_ones32, b_o2], [bpq])
                K.ts("dve", rr_, pq, 1.0 / 128, EPS, ALU.mult, ALU.add, [bpq], [b_rr])
                K.actf(rr_, rr_, AF.Sqrt, [b_rr], [b_rr])
                K.op("dve", lambda e: e.reciprocal(out=t1_, in_=rr_), [b_rr], [b_t1])
                ob, bob = ob_[cntB["ob"] % 2]
                cntB["ob"] += 1
                K.stt(ob, od_, subgs[:, l:l + 1], t1_, ALU.mult, ALU.mult, [b_od, b_t1, b_subgs], [bob])
                K.dma("act", [(OT[h * 128:(h + 1) * 128, j * 512:(j + 1) * 512], ob)], [bob], [DB("OT", j)], bob)

        K.barrier()
        scB.__exit__(None, None, None)

    PG = [128, 32]
    sA = {}

    def pg(name, shape=None, dt=F32):
        if name not in sA:
            sA[name] = K.sb(shape or PG, dt, name)
        return sA[name]

    NLEV = int(math.log2(NB))
    cntC = {"x": 0, "y": 0}

    def ssm_precompute(l):
        sA.clear()
        scP = K.scope()
        scP.__enter__()
        are, b1 = pg("are"); aim, b2 = pg("aim"); ldt, b3 = pg("ldt")
        K.dma("sp", [(are, are_in[:, l, :])], [], [b1], b1)
        K.dma("sp", [(aim, aim_in[:, l, :])], [], [b2], b2)
        K.dma("sp", [(ldt, ldt_in[:, l, :])], [], [b3], b3)
        Br, bBr = pg("Br", [128, 32, 16]); Bi, bBi = pg("Bi", [128, 32, 16])
        Cr, bCr = pg("Cr", [128, 32, 16]); Ci, bCi = pg("Ci", [128, 32, 16])
        K.dma("sp", [(Br, bre_in[:, l])], [], [bBr], bBr)
        K.dma("sp", [(Bi, bim_in[:, l])], [], [bBi], bBi)
        K.dma("sp", [(Cr, cre_in[:, l])], [], [bCr], bCr)
        K.dma("sp", [(Ci, cim_in[:, l])], [], [bCi], bCi)
        dt_, bdt = pg("dt"); xr, bxr = pg("xr"); th, bth = pg("th"); mag, bmag = pg("mag")
        K.actf(dt_, ldt, AF.Exp, [b3], [bdt])
        K.tt("dve", xr, dt_, are, ALU.mult, [bdt, b1], [bxr])
        K.tt("dve", th, dt_, aim, ALU.mult, [bdt, b2], [bth])
        K.actf(mag, xr, AF.Exp, [bxr], [bmag])
        yv, byv = pg("yv"); ki, bki = pg("ki", PG, I32); kf, bkf = pg("kf"); mk, bmk = pg("mk")
        sn, bsn = pg("sn"); cs_, bcs_ = pg("cs")
        for (dst, bdst, off) in ((sn, bsn, 1.5), (cs_, bcs_, 1.75)):
            K.ts("dve", yv, th, 1.0 / TWO_PI, off, ALU.mult, ALU.add, [bth], [byv])
            K.copy("dve", ki, yv, [byv], [bki])
            K.copy("dve", kf, ki, [bki], [bkf])
            K.tt("dve", mk, kf, yv, ALU.is_gt, [bkf, byv], [bmk])
            K.tt("dve", kf, kf, mk, ALU.subtract, [bkf, bmk], [bkf])
            K.tt("dve", yv, yv, kf, ALU.subtract, [byv, bkf], [byv])
            K.ts("dve", yv, yv, -0.5, TWO_PI, ALU.add, ALU.mult, [byv], [byv])
            K.ts("dve", yv, yv, math.pi, -math.pi, ALU.min, ALU.max, [byv], [byv])
            K.actf(dst, yv, AF.Sin, [byv], [bdst])
        Ar, bAr = pg("Ar"); Ai, bAi = pg("Ai")
        stopC = os.environ.get('MK_STOPC', '')
        if stopC == 'sin':
            K.barrier(); scP.__exit__(None, None, None); return
        K.tt("dve", Ar, mag, cs_, ALU.mult, [bmag, bcs_], [bAr])
        K.tt("dve", Ai, mag, sn, ALU.mult, [bmag, bsn], [bAi])
        Par, bPar = pg("Par", [128, 32, 9]); Pai, bPai = pg("Pai", [128, 32, 9])
        Pdr, bPdr = pg("Pdr", [128, 32, 9]); Pdi, bPdi = pg("Pdi", [128, 32, 9])
        Qar, bQar = pg("Qar", [128, 32, 9]); Qai, bQai = pg("Qai", [128, 32, 9])
        Qdr, bQdr = pg("Qdr", [128, 32, 9]); Qdi, bQdi = pg("Qdi", [128, 32, 9])
        ta, bta = pg("ta"); tb, btb = pg("tb")
        K.memset("dve", Par[:, :, 0], 1.0, [bPar])
        K.memset("dve", Pai[:, :, 0], 0.0, [bPai])
        for n in range(1, 9):
            K.tt("dve", ta, Par[:, :, n - 1], Ar, ALU.mult, [bPar, bAr], [bta])
            K.tt("dve", tb, Pai[:, :, n - 1], Ai, ALU.mult, [bPai, bAi], [btb])
            K.tt("dve", Par[:, :, n], ta, tb, ALU.subtract, [bta, btb], [bPar])
            K.tt("dve", ta, Par[:, :, n - 1], Ai, ALU.mult, [bPar, bAi], [bta])
            K.tt("dve", tb, Pai[:, :, n - 1], Ar, ALU.mult, [bPai, bAr], [btb])
            K.tt("dve", Pai[:, :, n], ta, tb, ALU.add, [bta, btb], [bPai])
        e2, be2 = pg("e2")
        for n in range(9):
            K.actf(e2, xr, AF.Exp, [bxr], [be2], scale=-2.0 * n)
            K.tt("dve", Qar[:, :, n], Par[:, :, n], e2, ALU.mult, [bPar, be2], [bQar])
            K.stt(Qai[:, :, n], Pai[:, :, n], -1.0, e2, ALU.mult, ALU.mult, [bPai, be2], [bQai])
        for n in range(9):
            K.copy("pool", Pdr[:, :, 8 - n], Par[:, :, n], [bPar], [bPdr])
            K.copy("pool", Pdi[:, :, 8 - n], Pai[:, :, n], [bPai], [bPdi])
            K.copy("pool", Qdr[:, :, 8 - n], Qar[:, :, n], [bQar], [bQdr])
            K.copy("pool", Qdi[:, :, 8 - n], Qai[:, :, n], [bQai], [bQdi])
        K.copy("dve", S.SS[:, 0, 0, :], Par[:, :, 8], [bPar], [S.b_SS])
        K.copy("dve", S.SS[:, 0, 1, :], Pai[:, :, 8], [bPai], [S.b_SS])
        for k in range(NLEV):
            if k > 0:
                K.tt("dve", ta, S.SS[:, k - 1, 0, :], S.SS[:, k - 1, 0, :], ALU.mult, [S.b_SS], [bta])
                K.tt("dve", tb, S.SS[:, k - 1, 1, :], S.SS[:, k - 1, 1, :], ALU.mult, [S.b_SS], [btb])
                K.tt("dve", S.SS[:, k, 0, :], ta, tb, ALU.subtract, [bta, btb], [S.b_SS])
                K.stt(S.SS[:, k, 1, :], S.SS[:, k - 1, 0, :], 2.0, S.SS[:, k - 1, 1, :], ALU.mult, ALU.mult, [S.b_SS], [S.b_SS])
            K.ts("dve", S.SS[:, k, 2, :], S.SS[:, k, 1, :], -1.0, None, ALU.mult, None, [S.b_SS], [S.b_SS])
        K.ts("dve", S.SF, S.SS, flag[:, 0:1], None, ALU.mult, None, [S.b_SS, b_flag], [S.b_SF])
        if stopC == 'pow':
            K.barrier(); scP.__exit__(None, None, None); return
        nr, bnr = pg("nr"); den, bden = pg("den"); fr, bfr = pg("fr"); fi, bfi = pg("fi")
        K.ts("dve", nr, Ar, -1.0, None, ALU.add, None, [bAr], [bnr])
        K.tt("dve", ta, are, are, ALU.mult, [b1], [bta])
        K.tt("dve", tb, aim, aim, ALU.mult, [b2], [btb])
        K.tt("dve", den, ta, tb, ALU.add, [bta, btb], [bden])
        K.op("dve", lambda e: e.reciprocal(out=den, in_=den), [bden], [bden])
        K.tt("dve", ta, nr, are, ALU.mult, [bnr, b1], [bta])
        K.tt("dve", tb, Ai, aim, ALU.mult, [bAi, b2], [btb])
        K.tt("dve", ta, ta, tb, ALU.add, [bta, btb], [bta])
        K.tt("dve", fr, ta, den, ALU.mult, [bta, bden], [bfr])
        K.tt("dve", ta, Ai, are, ALU.mult, [bAi, b1], [bta])
        K.tt("dve", tb, nr, aim, ALU.mult, [bnr, b2], [btb])
        K.tt("dve", ta, ta, tb, ALU.subtract, [bta, btb], [bta])
        K.tt("dve", fi, ta, den, ALU.mult, [bta, bden], [bfi])
        Bbr, bBbr = pg("Bbr", [128, 32, 16]); Bbi, bBbi = pg("Bbi", [128, 32, 16])
        t16a, bt16a = pg("t16a", [128, 32, 16]); t16b, bt16b = pg("t16b", [128, 32, 16])
        frb = fr.unsqueeze(2).to_broadcast([128, 32, 16])
        fib = fi.unsqueeze(2).to_broadcast([128, 32, 16])
        K.tt("dve", t16a, Br, frb, ALU.mult, [bBr, bfr], [bt16a])
        K.tt("dve", t16b, Bi, fib, ALU.mult, [bBi, bfi], [bt16b])
        K.tt("dve", Bbr, t16a, t16b, ALU.subtract, [bt16a, bt16b], [bBbr])
        K.tt("dve", t16a, Bi, frb, ALU.mult, [bBi, bfr], [bt16a])
        K.tt("dve", t16b, Br, fib, ALU.mult, [bBr, bfi], [bt16b])
        K.tt("dve", Bbi, t16a, t16b, ALU.add, [bt16a, bt16b], [bBbi])
        def wtab(name, src0, bs0, o0, src1, bs1, o1):
            w, bw = pg(name, [128, 16, 2, 8])
            v0 = src0.rearrange("p (a d) n -> p a d n", d=2)
            v1 = src1.rearrange("p (a d) n -> p a d n", d=2)
            K.copy("pool", w[:, :, 0, :], v0[:, :, 0, o0:o0 + 8], [bs0], [bw])
            K.copy("pool", w[:, :, 1, :], v1[:, :, 1, o1:o1 + 8], [bs1], [bw])
            return w.rearrange("p a d n -> p (a d) n"), bw
        WBr, bWBr = wtab("WBr", Pdr, bPdr, 1, Par, bPar, 0)
        WBi, bWBi = wtab("WBi", Pdi, bPdi, 1, Pai, bPai, 0)
        WCr, bWCr = wtab("WCr", Qdr, bQdr, 1, Qar, bQar, 0)
        WCi, bWCi = wtab("WCi", Qdi, bQdi, 1, Qai, bQai, 0)
        WKr, bWKr = wtab("WKr", Par, bPar, 1, Pdr, bPdr, 0)
        WKi, bWKi = wtab("WKi", Pai, bPai, 1, Pdi, bPdi, 0)
        big = [128, 32, 8, 16]
        PBr, bPBr = pg("PBr", big); PBi, bPBi = pg("PBi", big)
        PCr, bPCr = pg("PCr", big); PCi, bPCi = pg("PCi", big)
        tg1, btg1 = pg("tg1", big); tg2, btg2 = pg("tg2", big)

        def cmul(outr, boutr, outi, bouti, Wr, bWr, Wi, bWi, Xr, bXr, Xi, bXi, neg_i=False):
            wr = Wr.unsqueeze(3).to_broadcast(big)
            wi = Wi.unsqueeze(3).to_broadcast(big)
            xr_ = Xr.unsqueeze(2).to_broadcast(big)
            xi_ = Xi.unsqueeze(2).to_broadcast(big)
            K.tt("dve", tg1, wr, xr_, ALU.mult, [bWr, bXr], [btg1])
            K.tt("dve", tg2, wi, xi_, ALU.mult, [bWi, bXi], [btg2])
            K.tt("dve", outr, tg1, tg2, ALU.subtract, [btg1, btg2], [boutr])
            K.tt("dve", tg1, wr, xi_, ALU.mult, [bWr, bXi], [btg1])
            K.tt("dve", tg2, wi, xr_, ALU.mult, [bWi, bXr], [btg2])
            if neg_i:
                K.stt(outi, tg1, -1.0, tg2, ALU.mult, ALU.subtract, [btg1, btg2], [bouti])
            else:
                K.tt("dve", outi, tg1, tg2, ALU.add, [btg1, btg2], [bouti])

        cmul(PBr, bPBr, PBi, bPBi, WBr, bWBr, WBi, bWBi, Bbr, bBbr, Bbi, bBbi)
        cmul(PCr, bPCr, PCi, bPCi, WCr, bWCr, WCi, bWCi, Cr, bCr, Ci, bCi, neg_i=True)
        if stopC == 'cmul':
            K.barrier(); scP.__exit__(None, None, None); return
        PBr3 = PBr.rearrange("p g n c -> p g (n c)")
        PBi3 = PBi.rearrange("p g n c -> p g (n c)")
        PCr3 = PCr.rearrange("p g n c -> p g (n c)")
        PCi3 = PCi.rearrange("p g n c -> p g (n c)")
        for gp in range(16):
            for d in range(2):
                pt, bp = K.psum()
                idx = 0
                for gpar in range(2):
                    for (src, bsrc) in ((PBr3, bPBr), (PBi3, bPBi)):
                        sl_ = slice(gpar * 64, gpar * 64 + 64)
                        K.mm(pt[:, idx * 64:(idx + 1) * 64], src[:, gp * 2 + d, :], ident[:, sl_], True, True,
                             [bsrc, b_ident], [bp])
                        idx += 1
                K.copy("act", S.PBT[:, gp, d].rearrange("p a b c -> p (a b c)"), pt[:, 0:256], [bp], [S.b_PBT])
        mt, bmt = pg("mt", [128, 128])
        tb1 = tg1.rearrange("p g n c -> p (g n c)").bitcast(BF16).rearrange("p (h g f) -> p h g f", h=2, g=32)
        tb2 = tg2.rearrange("p g n c -> p (g n c)").bitcast(BF16).rearrange("p (h g f) -> p h g f", h=2, g=32)
        PBrb, bPBrb, PBib, bPBib = tb1[:, 0], btg1, tb1[:, 1], btg1
        PCrb, bPCrb, PCib, bPCib = tb2[:, 0], btg2, tb2[:, 1], btg2
        K.copy("act", PBrb, PBr3, [bPBr], [bPBrb])
        K.copy("act", PBib, PBi3, [bPBi], [bPBib])
        K.copy("act", PCrb, PCr3, [bPCr], [bPCrb])
        K.copy("act", PCib, PCi3, [bPCi], [bPCib])
        for gp in range(16):
            for gpar in range(2):
                g = gp * 2 + gpar
                sl = slice(gpar * 64, gpar * 64 + 64)
                pt, bp = K.psum()
                for d in range(2):
                    o = pt[:, d * 128:(d + 1) * 128]
                    K.mm(o, PBrb[sl, gp * 2 + d, :], PCrb[sl, gp * 2 + d, :], True, False, [bPBrb, bPCrb], [bp])
                    K.mm(o, PBib[sl, gp * 2 + d, :], PCib[sl, gp * 2 + d, :], False, True, [bPBib, bPCib], [bp])
                K.tt("dve", mt, pt[:, 0:128], maskf, ALU.mult, [bp, b_maskf], [bmt])
                K.tt("dve", tmpc, pt[:, 128:256], maskb, ALU.mult, [bp, b_maskb], [b_tmpc])
                K.tt("dve", S.M0[:, g, :], mt, tmpc, ALU.add, [bmt, b_tmpc], [S.b_M0])

        PKr, bPKr, PKi, bPKi = PCr, bPCr, PCi, bPCi
        cmul(PKr, bPKr, PKi, bPKi, WKr, bWKr, WKi, bWKi, Cr, bCr, Ci, bCi, neg_i=True)
        K.copy("act", S.PCC[:, :, :, 0, :], PKr.rearrange("p (a d) n c -> p a d (n c)", d=2), [bPKr], [S.b_PCC])
        K.copy("act", S.PCC[:, :, :, 1, :], PKi.rearrange("p (a d) n c -> p a d (n c)", d=2), [bPKi], [S.b_PCC])
        K.barrier()
        scP.__exit__(None, None, None)

    def hs_scan(gp, d):
        col = gp * 2 + d
        cur = 0
        for k in range(NLEV):
            s = 1 << k
            src, bsrc = S.Hb[cur]
            dst, bdst = S.Hb[1 - cur]

            def scal(tab, j):
                return tab[:, k, j, col:col + 1]

            def region(lo, hi, tab, btab, two_seg=False):
                sh = -s if d == 0 else s
                if two_seg:
                    def v(t, ri, off):
                        return t[:, ri, :].rearrange("p (g n) -> p g n", g=2)[:, :, lo + off:hi + off]
                else:
                    def v(t, ri, off):
                        return t[:, ri, lo + off:hi + off]
                rd = [bsrc, btab]
                if two_seg:
                    def v2(t, off):
                        return t.rearrange("p r (g n) -> p r g n", g=2)[:, :, :, lo + off:hi + off]
                else:
                    def v2(t, off):
                        return t[:, :, lo + off:hi + off]
                if not two_seg:
                    K.stt(v2(dst, 0), v2(src, sh), scal(tab, 0), v2(src, 0), ALU.mult, ALU.add, rd, [bdst])
                else:
                    K.stt(v(dst, 0, 0), v(src, 0, sh), scal(tab, 0), v(src, 0, 0), ALU.mult, ALU.add, rd, [bdst])
                    K.stt(v(dst, 1, 0), v(src, 1, sh), scal(tab, 0), v(src, 1, 0), ALU.mult, ALU.add, rd, [bdst])
                K.stt(v(dst, 0, 0), v(src, 1, sh), scal(tab, 2), v(dst, 0, 0), ALU.mult, ALU.add, rd + [bdst], [bdst])
                K.stt(v(dst, 1, 0), v(src, 0, sh), scal(tab, 1), v(dst, 1, 0), ALU.mult, ALU.add, rd + [bdst], [bdst])

            if d == 0:
                K.copy("dve", dst[:, :, 0:min(s, NBS)], src[:, :, 0:min(s, NBS)], [bsrc], [bdst])
                if s < NBS:
                    region(s, NBS, S.SS, S.b_SS, two_seg=True)
                region(NBS, min(NBS + s, NB), S.SF, S.b_SF)
                if s >= NBS and s < NB:
                    pass
            else:
                lo0 = max(NB - s, NBS)
                K.copy("dve", dst[:, :, lo0:NB], src[:, :, lo0:NB], [bsrc], [bdst])
                if s < NBS:
                    region(0, NBS - s, S.SS, S.b_SS, two_seg=True)
                region(max(NBS - s, 0), NBS, S.SF, S.b_SF)
            cur = 1 - cur
        return cur

    def phaseC(l):
        stopC = os.environ.get('MK_STOPC', '')
        scC = K.scope()
        scC.__enter__()
        if l + 1 < L:
            convert_layer(l + 1)
        S.PBT, S.b_PBT = K.sb([128, 16, 2, 2, 2, 64], BF16, "PBT")
        S.PCC, S.b_PCC = K.sb([128, 16, 2, 2, 128], BF16, "PCC")
        S.M0, S.b_M0 = K.sb([128, 32, 128], BF16, "M0")
        S.SS, S.b_SS = K.sb([128, 11, 3, 32], F32, "SS")
        S.SF, S.b_SF = K.sb([128, 11, 3, 32], F32, "SF")
        ssm_precompute(l)
        if stopC in ('sin', 'pow', 'cmul', 'tr', 'pre'):
            K.barrier(); scC.__exit__(None, None, None); return
        S.Hb = [K.sb([128, 2, NB], F32, "H%d" % i) for i in range(2)]
        S.Hin, S.b_Hin = K.sb([128, 2, 2, NB], BF16, "Hin")
        S.xg_ = [K.sb([128, NB], BF16, "xg%d" % i) for i in range(4)]
        S.yg_ = [K.sb([128, NB], F32, "yg%d" % i) for i in range(2)]
        NBT = (NB + 511) // 512
        bw = min(512, NB)
        for gp in range(16):
            xg = []
            for gpar in range(2):
                g = gp * 2 + gpar
                xt, bx = S.xg_[cntC["x"] % 4]
                cntC["x"] += 1
                K.dma("sp", [(xt[s2 * 16:(s2 + 1) * 16, :], XL[s2, g * 16:(g + 1) * 16, :]) for s2 in range(8)],
                      [DB("XL", i) for i in range(NT)], [bx], bx)
                xg.append((xt, bx))
            for d in range(2):
                H0, bH0 = S.Hb[0]
                for nt in range(NBT):
                    for ri in range(2):
                        pt, bp = K.psum()
                        for gpar in range(2):
                            K.mm(pt[gpar * 64:(gpar + 1) * 64, 0:bw], S.PBT[:, gp, d, gpar, ri, :],
                                 xg[gpar][0][:, nt * 512:nt * 512 + bw], True, True, [S.b_PBT, xg[gpar][1]], [bp])
                        K.copy("act", H0[:, ri, nt * 512:nt * 512 + bw], pt[:, 0:bw], [bp], [bH0])
                cur = hs_scan(gp, d)
                Hf, bHf = S.Hb[cur]
                if d == 0:
                    K.memset("pool", S.Hin[:, 0, :, 0:1], 0.0, [S.b_Hin])
                    K.copy("act", S.Hin[:, 0].rearrange("p r (g n) -> p r g n", g=2)[:, :, :, 1:NBS],
                           Hf.rearrange("p r (g n) -> p r g n", g=2)[:, :, :, 0:NBS - 1], [bHf], [S.b_Hin])
                    K.ts("dve", S.Hin[:, 0, :, NBS:NBS + 1], Hf[:, :, NBS - 1:NBS], flag[:, 0:1], None, ALU.mult, None,
                         [bHf, b_flag], [S.b_Hin])
                else:
                    K.memset("pool", S.Hin[:, 1, :, NB - 1:NB], 0.0, [S.b_Hin])
                    K.copy("act", S.Hin[:, 1].rearrange("p r (g n) -> p r g n", g=2)[:, :, :, 0:NBS - 1],
                           Hf.rearrange("p r (g n) -> p r g n", g=2)[:, :, :, 1:NBS], [bHf], [S.b_Hin])
                    K.ts("dve", S.Hin[:, 1, :, NBS - 1:NBS], Hf[:, :, NBS:NBS + 1], flag[:, 0:1], None, ALU.mult, None,
                         [bHf, b_flag], [S.b_Hin])
            for gpar in range(2):
                g = gp * 2 + gpar
                sl = slice(gpar * 64, gpar * 64 + 64)
                yt, by = S.yg_[cntC["y"] % 2]
                cntC["y"] += 1
                for nt in range(NBT):
                    cs = slice(nt * 512, nt * 512 + bw)
                    pt, bp = K.psum()
                    K.mm(pt[:, 0:bw], S.M0[:, g, :], xg[gpar][0][:, cs], True, False, [S.b_M0, xg[gpar][1]], [bp])
                    for d in range(2):
                        for ri in range(2):
                            K.mm(pt[:, 0:bw], S.PCC[sl, gp, d, ri, :], S.Hin[sl, d, ri, cs], False,
                                 (d == 1 and ri == 1), [S.b_PCC, S.b_Hin], [bp])
                    K.copy("act", yt[:, cs], pt[:, 0:bw], [bp], [by])
                K.dma("act", [(YL[t2, g * 16:(g + 1) * 16, :], yt[t2 * 16:(t2 + 1) * 16, :]) for t2 in range(8)],
                      [by], [DB("YL", g)], by)
        K.barrier()
        scC.__exit__(None, None, None)

    cntD = {"i": 0}

    def phaseD(l, last):
        cur_layer[0] = l
        scD = K.scope()
        scD.__enter__()
        alloc_shared()
        xt_ = S.xt_
        oT_ = [K.sb([128, 4, 512], BF16, "oT%d" % i) for i in range(1)]
        sg_ = [K.sb([128, 2, 8, 512], BF16, "sg%d" % i) for i in range(1)]
        yl_ = [K.sb([128, 4, 8, 64], F32, "yl%d" % i) for i in range(1)]
        ud_ = [K.sb([128, 4, 512], F32, "ud%d" % i) for i in range(1)]
        zt_, b_zt = K.sb([128, 4, 512], BF16, "zt")
        m1_, b_m1 = K.sb([128, 8, 512], F32, "m1")
        mg_, b_mg = K.sb([128, 8, 512], BF16, "mg")
        h2_, b_h2 = K.sb([128, 8, 512], BF16, "h2")
        aT_, b_aT = K.sb([128, 22, 512], BF16, "aT")
        ga_, b_ga = K.sb([128, 512], F32, "ga")
        gb_, b_gb = K.sb([128, 512], F32, "gb")
        gc_, b_gc = K.sb([128, 512], F32, "gc")
        for i in range(NT):
            s = i // (NT // 2)
            t0 = i * 512
            par = 0
            cntD["i"] += 1
            xt, bx = xt_[cnt["x"] % 2]
            cnt["x"] += 1
            src = x_in if l == 0 else xs
            rd = [] if l == 0 else [DB("xs", i)]
            K.dma("sp", [(xt, src[:, t0:t0 + 512].rearrange("(k p) t -> p k t", p=128))], rd, [bx], bx)
            oT, boT = oT_[par]
            K.dma("sp", [(oT, OT[:, t0:t0 + 512].rearrange("(c p) t -> p c t", p=128))], [DB("OT", i)], [boT], boT)
            sg, bsg = sg_[par]
            K.dma("sp", [(sg[:, 0], SGA[:, t0:t0 + 512].rearrange("(c p) t -> p c t", p=128)),
                         (sg[:, 1], SGS[:, t0:t0 + 512].rearrange("(c p) t -> p c t", p=128))],
                  [DB("SGA", i * 2), DB("SGA", i * 2 + 1), DB("SGS", i * 2), DB("SGS", i * 2 + 1)], [bsg], bsg)
            yl, byl = yl_[par]
            K.dma("sp", [(yl[:, c], YL[:, c * 128:(c + 1) * 128, i * 64:(i + 1) * 64].rearrange("t p b -> p t b"))
                         for c in range(4)], [DB("YL", g) for g in range(NG)], [byl], byl)
            ud, bud = ud_[par]
            K.dma("sp", [(ud, UT[:, t0:t0 + 512].rearrange("(c p) t -> p c t", p=128))], [DB("UT", i)], [bud], bud)
            for c in range(4):
                yv = yl[:, c].rearrange("p t b -> p b t")
                g3 = ga_.rearrange("p (b t) -> p b t", t=8)
                K.stt(g3, ud[:, c, :].rearrange("p (b t) -> p b t", t=8), dsk[:, l, c:c + 1], yv, ALU.mult, ALU.add,
                      [bud, byl, b_dsk], [b_ga])
                K.actf(gb_, ga_, AF.Square, [b_ga], [b_gb])
                K.ts("dve", gb_, gb_, 0.044715, 1.0, ALU.mult, ALU.add, [b_gb], [b_gb])
                K.tt("dve", gb_, gb_, ga_, ALU.mult, [b_gb, b_ga], [b_gb])
                K.actf(gc_, gb_, AF.Sigmoid, [b_gb], [b_gc], scale=1.5957691216057308)
                K.tt("dve", zt_[:, c, :], ga_, gc_, ALU.mult, [b_ga, b_gc], [b_zt])
            for blk_ in range(2):
                wt, bw = load_w(wb_attn[l], 0, 4, [(blk_ * 512, 512)])
                for c in range(4):
                    pt, bp = K.psum()
                    for kc in range(4):
                        K.mm(pt, wt[:, kc, c * 128:(c + 1) * 128], oT[:, kc, :], kc == 0, kc == 3, [bw, boT], [bp])
                    K.tt("dve", m1_[:, blk_ * 4 + c, :], pt, sg[:, 0, blk_ * 4 + c, :], ALU.mult, [bp, bsg], [b_m1])
            for blk_ in range(4):
                wt, bw = load_w(wb_glu[l], 0, 4, [(blk_ * 256, 256), (1024 + blk_ * 256, 256)])
                for c in range(2):
                    j = blk_ * 2 + c
                    pl, bpl = K.psum()
                    for kc in range(4):
                        K.mm(pl, wt[:, kc, c * 128:(c + 1) * 128], zt_[:, kc, :], kc == 0, kc == 3, [bw, b_zt], [bpl])
                    pg_, bpg = K.psum()
                    for kc in range(4):
                        K.mm(pg_, wt[:, kc, 256 + c * 128:256 + (c + 1) * 128], zt_[:, kc, :], kc == 0, kc == 3,
                             [bw, b_zt], [bpg])
                    K.actf(ga_, pg_, AF.Sigmoid, [bpg, b_bglu], [b_ga], bias=bglu[:, l, 8 + j:9 + j], scale=1.0)
                    K.stt(gb_, pl, bglu[:, l, j:j + 1], ga_, ALU.add, ALU.mult, [bpl, b_bglu, b_ga], [b_gb])
                    K.tt("dve", gb_, gb_, sg[:, 1, j, :], ALU.mult, [b_gb, bsg], [b_gb])
                    K.tt("dve", mg_[:, j, :], gb_, m1_[:, j, :], ALU.add, [b_gb, b_m1], [b_mg])
            for blk_ in range(2):
                wt, bw = load_w(wb_o[l], 0, 8, [(blk_ * 512, 512)])
                for c in range(4):
                    j = blk_ * 4 + c
                    pt, bp = K.psum()
                    for kc in range(8):
                        K.mm(pt, wt[:, kc, c * 128:(c + 1) * 128], mg_[:, kc, :], kc == 0, kc == 7, [bw, b_mg], [bp])
                    K.stt(xt[:, j, :], pt, GT1(l, j, s), xt[:, j, :], ALU.mult, ALU.add, [bp, b_modT, bx], [bx])
            norm_mod(xt, bx, lambda kc: A2[:, l, kc, s:s + 1], lambda kc: SH2(l, kc, s), h2_, b_h2)
            for blk_ in range(11):
                wt, bw = load_w(wb_ffi[l], 0, 8, [(blk_ * 256, 256), (DFF + blk_ * 256, 256)])
                for c in range(2):
                    j = blk_ * 2 + c
                    pgt, bpg = K.psum()
                    for kc in range(8):
                        K.mm(pgt, wt[:, kc, c * 128:(c + 1) * 128], h2_[:, kc, :], kc == 0, kc == 7, [bw, b_h2], [bpg])
                    pu, bpu = K.psum()
                    for kc in range(8):
                        K.mm(pu, wt[:, kc, 256 + c * 128:256 + (c + 1) * 128], h2_[:, kc, :], kc == 0, kc == 7,
                             [bw, b_h2], [bpu])
                    K.actf(ga_, pgt, AF.Silu, [bpg], [b_ga])
                    K.tt("dve", aT_[:, j, :], pu, ga_, ALU.mult, [bpu, b_ga], [b_aT])
            for half in range(2):
                wa, bwa = load_w(wb_ffo[l], 0, 11, [(half * 512, 512)])
                wb2, bwb2 = load_w(wb_ffo[l], 11, 11, [(half * 512, 512)])
                for c in range(4):
                    j = half * 4 + c
                    pt, bp = K.psum()
                    for kc in range(22):
                        w_, bw_ = (wa, bwa) if kc < 11 else (wb2, bwb2)
                        K.mm(pt, w_[:, kc % 11, c * 128:(c + 1) * 128], aT_[:, kc, :], kc == 0, kc == 21,
                             [bw_, b_aT], [bp])
                    K.stt(xt[:, j, :], pt, GT2(l, j, s), xt[:, j, :], ALU.mult, ALU.add, [bp, b_modT, bx], [bx])
            if not last:
                K.dma("act", [(xs[:, t0:t0 + 512].rearrange("(k p) t -> p k t", p=128), xt)], [bx], [DB("xs", i)], bx)
            else:
                rstd_only(xt, bx)
                for kc in range(8):
                    K.stt(xt[:, kc, :], xt[:, kc, :], gfT[:, kc:kc + 1], S.rstd, ALU.mult, ALU.mult,
                          [bx, S.b_rstd, b_gfT], [bx])
                K.dma("act", [(y_out[:, t0:t0 + 512].rearrange("(k p) t -> p k t", p=128), xt)], [bx], [DB("y", i)], bx)

        K.barrier()
        scD.__exit__(None, None, None)

    import os
    stop = os.environ.get("MK_STOP", "")
    for l in range(L):
        if stop == "pro":
            break
        phaseA(l)
        K.barrier()
        if stop == "A":
            break
        phaseB(l)
        K.barrier()
        if stop == "B":
            break
        phaseC(l)
        K.barrier()
        if stop == "C":
            break
        phaseD(l, l == L - 1)
        K.barrier()
    K.barrier()
    return nc


def _rope_tables(T_seq):
    inv = 1.0 / (10000.0 ** (np.arange(0, 64, 2, dtype=np.float32) / 64.0))
    ang = np.arange(T_seq, dtype=np.float32)[:, None] * inv[None, :].astype(np.float32)
    ang = np.concatenate([ang, ang], axis=-1).astype(np.float32)
    return np.cos(ang).astype(np.float32), np.sin(ang).astype(np.float32)


def _consts():
    perm = np.zeros((128, 128), np.float32)
    for m in range(2):
        for d in range(64):
            perm[m * 64 + (d + 32) % 64, m * 64 + d] = 1.0
    ident = np.eye(128, dtype=np.float32)
    s2 = np.arange(128) // 16
    maskf = (s2[None, :] >= s2[:, None]).astype(np.float32)
    maskb = (s2[None, :] <= s2[:, None]).astype(np.float32)
    return perm, ident, maskf, maskb


def _pl(a, L):
    rest = a.shape[4:]
    a = a.reshape((L, 2, 16, 2, 64) + rest)
    a = np.moveaxis(a, (3, 4, 0, 2, 1), (0, 1, 2, 3, 4))
    return np.ascontiguousarray(a.reshape((128, L, 32) + rest)).astype(np.float32)


def make_in_maps(inp, cfg, core_seqs):
    L, T = cfg.L, cfg.T
    perm, ident, maskf, maskb = _consts()

    def fm(v, nch):
        v = np.asarray(v, np.float32)
        lead = v.shape[:-1]
        v = v.reshape(lead + (nch, 128))
        v = np.moveaxis(v, -1, 0)
        return np.ascontiguousarray(v)

    shared = {
        "perm": perm, "ident": ident, "maskf": maskf, "maskb": maskb,
        "w_mod": np.ascontiguousarray(inp["w_mod"][:L], np.float32),
        "b_modT": fm(inp["b_mod"][:L], 48),
        "g1T": fm(inp["norm1_g"][:L], 8), "g2T": fm(inp["norm2_g"][:L], 8), "gfT": fm(inp["final_g"], 8),
        "w_in": np.ascontiguousarray(inp["w_in"][:L], np.float32),
        "lamT": np.ascontiguousarray(np.broadcast_to(
            np.stack([inp["lam_q1"][:L], inp["lam_k1"][:L], inp["lam_q2"][:L], inp["lam_k2"][:L]], axis=1)[None],
            (128, L, 4, 64)), np.float32),
        "subgT": np.ascontiguousarray(np.asarray(inp["subln_g"][:L], np.float32).T),
        "w_attn": np.ascontiguousarray(inp["w_attn_br"][:L], np.float32),
        "ssm_dT": fm(inp["ssm_d"][:L], 4),
        "w_glu": np.ascontiguousarray(inp["w_glu"][:L], np.float32),
        "b_gluT": fm(inp["b_glu"][:L], 16),
        "w_o": np.ascontiguousarray(inp["w_o"][:L], np.float32),
        "w_ffi": np.ascontiguousarray(inp["w_ffn_in"][:L], np.float32),
        "w_ffo": np.ascontiguousarray(inp["w_ffn_out"][:L], np.float32),
    }
    are = np.asarray(inp["ssm_a_re"][:L], np.float32)
    aim = np.asarray(inp["ssm_a_im"][:L], np.float32)
    ldt = np.broadcast_to(np.asarray(inp["ssm_log_dt"][:L], np.float32)[..., None], (L, 2, 32, 64))
    shared["a_reP"] = _pl(are, L)
    shared["a_imP"] = _pl(aim, L)
    shared["ldtP"] = _pl(np.ascontiguousarray(ldt), L)
    shared["b_reP"] = _pl(np.asarray(inp["ssm_b_re"][:L], np.float32), L)
    shared["b_imP"] = _pl(np.asarray(inp["ssm_b_im"][:L], np.float32), L)
    shared["c_reP"] = _pl(np.swapaxes(np.asarray(inp["ssm_c_re"][:L], np.float32), 3, 4), L)
    shared["c_imP"] = _pl(np.swapaxes(np.asarray(inp["ssm_c_im"][:L], np.float32), 3, 4), L)
    maps = []
    for (x, c, pos, split) in core_seqs:
        cos, sin = _rope_tables(int(pos.max()) + 1)
        cosT = np.ascontiguousarray(np.tile(cos[pos].T, (2, 1)))
        sgn = np.where(np.arange(64) < 32, -1.0, 1.0).astype(np.float32)
        sinT = np.ascontiguousarray(np.tile((sin[pos] * sgn[None, :]).T, (2, 1)))
        m = dict(shared)
        m["xT"] = np.ascontiguousarray(x.T)
        m["cT"] = np.ascontiguousarray(np.moveaxis(np.asarray(c, np.float32).reshape(2, 8, 128), (0, 1, 2), (2, 1, 0)))
        m["cosT"] = cosT.astype(np.float32)
        m["sinT"] = sinT.astype(np.float32)
        m["crossbias"] = np.full((128, 1), 0.0 if split else -30000.0, np.float32)
        m["flag"] = np.full((128, 1), 1.0 if split else 0.0, np.float32)
        maps.append(m)
    return maps


_NC_CACHE = {}


def run(inp, cfg, core_seqs):
    key = (cfg.T, cfg.L)
    if key not in _NC_CACHE:
        _NC_CACHE[key] = build(cfg)
    nc = _NC_CACHE[key]
    maps = make_in_maps(inp, cfg, core_seqs)
    res = run_bass_kernel_spmd(nc, maps, core_ids=list(range(len(maps))))
    return [np.asarray(r["yT"]).T for r in res.results]


def kernel(**inp):
    cfg = Cfg(8192, 4)
    xp = np.asarray(inp["x_prompt"], np.float32)
    xsm = np.asarray(inp["x_sample"], np.float32)
    cp = np.asarray(inp["c_prompt"], np.float32)
    csm = np.asarray(inp["c_sample"], np.float32)
    cores = []
    pos_p = np.concatenate([np.arange(4096), np.arange(4096)])
    for c in range(4):
        cores.append((np.concatenate([xp[2 * c], xp[2 * c + 1]], axis=0), np.stack([cp[2 * c], cp[2 * c + 1]]),
                      pos_p, False))
    for c in range(4):
        cores.append((xsm[c], np.stack([csm[c], csm[c]]), np.arange(8192), True))
    outs = run(inp, cfg, cores)
    yp = np.empty((8, 4096, D), np.float32)
    for c in range(4):
        yp[2 * c] = outs[c][:4096]
        yp[2 * c + 1] = outs[c][4096:]
    ys = np.stack([outs[4 + c] for c in range(4)], axis=0).astype(np.float32)
    return (yp, ys)
```
